# Optimizing a Trainium2 kernel written in Bass

```python
import jax, jax.numpy as jnp
from jax import lax
import numpy as np

D_MODEL = 1024
BATCH = 32
SEQ = 2048
DEPTH = 2

N_MIXERS = 2
N_ATTN_LAYERS = (DEPTH + 1) // 2
N_REC_LAYERS = DEPTH // 2

HEAD_DIM = 64
N_HEADS = D_MODEL // HEAD_DIM
N_KV_HEADS = 2
GROUP = N_HEADS // N_KV_HEADS
ATTN_WIDTH = N_HEADS * HEAD_DIM
KV_WIDTH = N_KV_HEADS * HEAD_DIM
ATTN_IN = 2 * ATTN_WIDTH + 2 * KV_WIDTH
WINDOW = 128
ATTN_BLOCK = 128
ROPE_THETA = 500000.0
ROPE_DIM = HEAD_DIM // 4

REC_HEADS = 8
REC_KEY_DIM = 128
REC_VALUE_DIM = D_MODEL // REC_HEADS
FORGET_DIM = REC_HEADS * REC_KEY_DIM
REC_WIDTH = REC_HEADS * REC_VALUE_DIM
REC_IN = 2 * FORGET_DIM + 2 * REC_WIDTH
REC_CHUNK = 32

NORM_EPS = 1e-6

kernel_name = "hybrid_swa_sink_hgrn2_interleaved"


def rmsnorm(x, w):
    xf = x.astype(jnp.float32)
    y = xf * lax.rsqrt(jnp.mean(xf * xf, axis=-1, keepdims=True) + NORM_EPS)
    return (y * w.astype(jnp.float32)).astype(x.dtype)


def partial_rope(x, positions):
    half = ROPE_DIM // 2
    inv_freq = ROPE_THETA ** (-(jnp.arange(half, dtype=jnp.float32) * 2.0 / ROPE_DIM))
    ang = positions.astype(jnp.float32)[..., None] * inv_freq
    cos = jnp.cos(ang)[:, :, None, :]
    sin = jnp.sin(ang)[:, :, None, :]
    x1 = x[..., :half].astype(jnp.float32)
    x2 = x[..., half:ROPE_DIM].astype(jnp.float32)
    r1 = x1 * cos - x2 * sin
    r2 = x2 * cos + x1 * sin
    return jnp.concatenate([r1.astype(x.dtype), r2.astype(x.dtype), x[..., ROPE_DIM:]], axis=-1)


def sliding_window_gqa(q, k, v, sinks):
    B, T = q.shape[0], q.shape[1]
    nb = T // ATTN_BLOCK
    qb = q.reshape(B, nb, ATTN_BLOCK, N_KV_HEADS, GROUP, HEAD_DIM).transpose(1, 0, 2, 3, 4, 5)

    def span(t):
        tb = t.reshape(B, nb, ATTN_BLOCK, N_KV_HEADS, HEAD_DIM)
        prev = jnp.pad(tb, ((0, 0), (1, 0), (0, 0), (0, 0), (0, 0)))[:, :-1]
        return jnp.concatenate([prev, tb], axis=2).transpose(1, 0, 2, 3, 4)

    kk, vv = span(k), span(v)
    q_rel = jnp.arange(ATTN_BLOCK)[:, None] + ATTN_BLOCK
    k_rel = jnp.arange(2 * ATTN_BLOCK)[None, :]
    band = (k_rel <= q_rel) & (q_rel - k_rel < WINDOW)
    sink = sinks.astype(jnp.float32).reshape(N_KV_HEADS, GROUP)[None, :, :, None]
    scale = HEAD_DIM ** -0.5

    def block(args):
        qi, ki, vi, idx = args
        s = jnp.einsum('bqhgd,bkhd->bhgqk', qi, ki, preferred_element_type=jnp.float32) * scale
        valid = band & ((idx > 0) | (k_rel >= ATTN_BLOCK))
        s = jnp.where(valid, s, -jnp.inf)
        m = jnp.maximum(jnp.max(s, axis=-1), sink)
        p = jnp.exp(s - m[..., None])
        denom = jnp.sum(p, axis=-1) + jnp.exp(sink - m)
        p = (p / denom[..., None]).astype(vi.dtype)
        return jnp.einsum('bhgqk,bkhd->bqhgd', p, vi)

    out = lax.map(block, (qb, kk, vv, jnp.arange(nb)))
    return out.transpose(1, 0, 2, 3, 4, 5).reshape(B, T, ATTN_WIDTH)


def attention_mixer(h, positions, w_in, b_in, sinks, w_out, b_out):
    B, T = h.shape[0], h.shape[1]
    proj = h @ w_in + b_in
    q, k, v, z = jnp.split(proj, [ATTN_WIDTH, ATTN_WIDTH + KV_WIDTH, ATTN_WIDTH + 2 * KV_WIDTH], axis=-1)
    q = partial_rope(q.reshape(B, T, N_HEADS, HEAD_DIM), positions)
    k = partial_rope(k.reshape(B, T, N_KV_HEADS, HEAD_DIM), positions)
    v = v.reshape(B, T, N_KV_HEADS, HEAD_DIM)
    o = sliding_window_gqa(q, k, v, sinks)
    return (o * jax.nn.silu(z)) @ w_out + b_out


def chunked_gated_recurrence(q, k, v, log_f):
    B, T, H, K = q.shape
    V = v.shape[-1]
    nc = T // REC_CHUNK

    def to_chunks(t):
        return t.reshape(B, nc, REC_CHUNK, H, t.shape[-1]).transpose(1, 0, 3, 2, 4)

    qc, kc, vc, gc = to_chunks(q), to_chunks(k), to_chunks(v), to_chunks(log_f)
    causal = jnp.tril(jnp.ones((REC_CHUNK, REC_CHUNK), dtype=bool))[:, :, None]

    def step(S, inp):
        qi, ki, vi, gi = inp
        b = jnp.cumsum(gi, axis=2)
        b_last = b[:, :, -1:, :]
        o_inter = jnp.einsum('bhck,bhkv->bhcv', qi * jnp.exp(b), S)
        diff = b[:, :, :, None, :] - b[:, :, None, :, :]
        decay = jnp.exp(jnp.where(causal, diff, -jnp.inf))
        scores = jnp.einsum('bhtk,bhtsk->bhts', qi, decay * ki[:, :, None, :, :])
        o_intra = jnp.einsum('bhts,bhsv->bhtv', scores, vi)
        S_new = S * jnp.exp(b_last)[:, :, 0, :, None] + jnp.einsum(
            'bhck,bhcv->bhkv', ki * jnp.exp(b_last - b), vi)
        return S_new, o_inter + o_intra

    S0 = jnp.zeros((B, H, K, V), jnp.float32)
    _, o = lax.scan(step, S0, (qc, kc, vc, gc))
    return o.transpose(1, 0, 3, 2, 4).reshape(B, T, H, V)


def hgrn2_mixer(h, lower_bound, w_in, gnorm_w, w_out):
    B, T = h.shape[0], h.shape[1]
    proj = h @ w_in
    q, f, i, z = jnp.split(proj, [FORGET_DIM, 2 * FORGET_DIM, 2 * FORGET_DIM + REC_WIDTH], axis=-1)
    q = jax.nn.silu(q.astype(jnp.float32)).reshape(B, T, REC_HEADS, REC_KEY_DIM)
    lb = lower_bound.astype(jnp.float32)
    log_f = jnp.logaddexp(jnp.log(lb), jnp.log1p(-lb) + jax.nn.log_sigmoid(f.astype(jnp.float32)))
    k = -jnp.expm1(log_f)
    log_f = log_f.reshape(B, T, REC_HEADS, REC_KEY_DIM)
    k = k.reshape(B, T, REC_HEADS, REC_KEY_DIM)
    v = i.astype(jnp.float32).reshape(B, T, REC_HEADS, REC_VALUE_DIM)
    o = chunked_gated_recurrence(q, k, v, log_f)
    o = o * lax.rsqrt(jnp.mean(o * o, axis=-1, keepdims=True) + NORM_EPS) * gnorm_w.astype(jnp.float32)
    o = o.reshape(B, T, REC_WIDTH) * jax.nn.silu(z.astype(jnp.float32))
    return o.astype(h.dtype) @ w_out


def setup_inputs(seed: int = 0) -> dict:
    key = jax.random.key(seed)
    ks = jax.random.split(key, 16)
    f32 = jnp.float32
    x = jax.random.normal(ks[0], (BATCH, SEQ, D_MODEL), f32)
    offsets = jax.random.randint(ks[1], (BATCH, 1), 0, 4096, dtype=jnp.int32)
    positions = offsets + jnp.arange(SEQ, dtype=jnp.int32)[None, :]
    pre_norm_w = 1.0 + 0.02 * jax.random.normal(ks[2], (DEPTH, D_MODEL), f32)
    post_norm_w = 1.0 + 0.02 * jax.random.normal(ks[3], (DEPTH, D_MODEL), f32)
    attn_w_in = jax.random.normal(ks[4], (N_ATTN_LAYERS, D_MODEL, ATTN_IN), f32) * D_MODEL ** -0.5
    attn_b_in = 0.02 * jax.random.normal(ks[5], (N_ATTN_LAYERS, ATTN_IN), f32)
    attn_sinks = 0.5 * jax.random.normal(ks[6], (N_ATTN_LAYERS, N_HEADS), f32)
    attn_w_out = jax.random.normal(ks[7], (N_ATTN_LAYERS, ATTN_WIDTH, D_MODEL), f32) * ATTN_WIDTH ** -0.5
    attn_b_out = 0.02 * jax.random.normal(ks[8], (N_ATTN_LAYERS, D_MODEL), f32)
    rec_w_in = jax.random.normal(ks[9], (N_REC_LAYERS, D_MODEL, REC_IN), f32) * D_MODEL ** -0.5
    rec_lb_logits = 0.5 * jax.random.normal(ks[10], (DEPTH, FORGET_DIM), f32)
    rec_gnorm_w = 1.0 + 0.02 * jax.random.normal(ks[11], (N_REC_LAYERS, REC_VALUE_DIM), f32)
    rec_w_out = jax.random.normal(ks[12], (N_REC_LAYERS, REC_WIDTH, D_MODEL), f32) * REC_WIDTH ** -0.5
    return {"x": x, "positions": positions, "pre_norm_w": pre_norm_w, "post_norm_w": post_norm_w,
            "attn_w_in": attn_w_in, "attn_b_in": attn_b_in, "attn_sinks": attn_sinks,
            "attn_w_out": attn_w_out, "attn_b_out": attn_b_out, "rec_w_in": rec_w_in,
            "rec_lb_logits": rec_lb_logits, "rec_gnorm_w": rec_gnorm_w, "rec_w_out": rec_w_out}


def reference(x, positions, pre_norm_w, post_norm_w, attn_w_in, attn_b_in, attn_sinks,
              attn_w_out, attn_b_out, rec_w_in, rec_lb_logits, rec_gnorm_w, rec_w_out):
    probs = jax.nn.softmax(rec_lb_logits.astype(jnp.float32), axis=0)
    cum = jnp.cumsum(probs, axis=0)
    lower_bounds = cum - cum[0:1]
    for layer in range(DEPTH):
        h = rmsnorm(x, pre_norm_w[layer])
        j = layer // N_MIXERS
        if layer % N_MIXERS == 0:
            y = attention_mixer(h, positions, attn_w_in[j], attn_b_in[j], attn_sinks[j],
                                attn_w_out[j], attn_b_out[j])
        else:
            y = hgrn2_mixer(h, lower_bounds[layer], rec_w_in[j], rec_gnorm_w[j], rec_w_out[j])
        x = x + rmsnorm(y, post_norm_w[layer])
    return x
```

```python
import math
from contextlib import ExitStack

import numpy as np
import concourse.bass as bass
import concourse.mybir as mybir
from concourse.bass_utils import run_bass_kernel_spmd

F32 = mybir.dt.float32
BF16 = mybir.dt.bfloat16
I32 = mybir.dt.int32
AF = mybir.ActivationFunctionType
ALU = mybir.AluOpType

N_CORES = 8
D = 1024
EPS = 1e-6
TWO_PI = 2.0 * math.pi
C1 = 6.28125
C2 = TWO_PI - C1
PI_LO = 3.1415925


class _Op:
    __slots__ = ("eng", "fn", "idx", "deps", "signal", "sem", "count", "dma", "epoch")


class Sched:
    def __init__(self, nc):
        self.nc = nc
        self.ops = []
        self.last_w = {}
        self.readers = {}
        self.epoch = 0

    def new_epoch(self):
        self.epoch += 1

    def op(self, eng, fn, reads=(), writes=(), dma=None):
        o = _Op()
        o.eng, o.fn, o.idx, o.dma, o.epoch = eng, fn, len(self.ops), dma, self.epoch
        o.signal = False
        deps = set()
        for k in reads:
            w = self.last_w.get(k)
            if w is not None:
                deps.add(w)
        for k in writes:
            w = self.last_w.get(k)
            if w is not None:
                deps.add(w)
            for r in self.readers.get(k, ()):
                deps.add(r)
        deps.discard(o.idx)
        o.deps = deps
        for k in writes:
            self.last_w[k] = o.idx
            self.readers[k] = []
        for k in reads:
            if k not in writes:
                self.readers.setdefault(k, []).append(o.idx)
        self.ops.append(o)
        return o

    def emit(self, stack):
        nc = self.nc
        ops = self.ops
        for o in ops:
            for d in o.deps:
                p = ops[d]
                if p.dma is None and p.eng == "pe" and o.eng == "pe" and o.dma is None:
                    continue
                p.signal = True
        counts = {}
        for o in ops:
            if o.dma is not None:
                key = ("dma", o.dma)
                counts[key] = counts.get(key, 0) + 16
                o.sem, o.count = key, counts[key]
            elif o.signal:
                key = (o.eng, o.epoch)
                counts[key] = counts.get(key, 0) + 1
                o.sem, o.count = key, counts[key]
        sems = {}
        for key in counts:
            sems[key] = stack.enter_context(nc.semaphore("s_%s_%s" % key))
        final = dict(counts)

        def stream(eng_name):
            def body(eng):
                seen = {}
                for o in ops:
                    if o.eng != eng_name:
                        continue
                    need = {}
                    for d in o.deps:
                        p = ops[d]
                        if p.dma is None and p.eng == "pe" and eng_name == "pe" and o.dma is None:
                            continue
                        if p.dma is not None:
                            skey, val = ("dma", p.dma), (0, p.count)
                        else:
                            skey, val = ("eng", p.eng), (p.epoch, p.count)
                        if val > need.get(skey, (-1, -1)):
                            need[skey] = val
                    for skey, val in need.items():
                        if val <= seen.get(skey, (-1, -1)):
                            continue
                        seen[skey] = val
                        if skey[0] == "dma":
                            eng.wait_ge(sems[("dma", skey[1])], val[1])
                        else:
                            eng.wait_ge(sems[(skey[1], val[0])], val[1])
                    ins = o.fn(eng)
                    if o.dma is not None:
                        ins.then_inc(sems[o.sem], 16)
                    elif o.signal:
                        ins.then_inc(sems[o.sem], 1)
                if eng_name == "sp":
                    for key, c in final.items():
                        if key[0] == "dma":
                            eng.wait_ge(sems[key], c)
            return body

        with nc.Block() as block:
            block.tensor(stream("pe"))
            block.scalar(stream("act"))
            block.vector(stream("dve"))
            block.gpsimd(stream("pool"))
            block.sync(stream("sp"))


CST_W = 128 * 4 + 512 + 8


def make_consts():
    c = np.zeros((128, CST_W), np.float32)
    i = np.arange(128)
    c[:, 0:128] = np.eye(128, dtype=np.float32)
    c[:, 128:256] = (i[:, None] <= i[None, :]).astype(np.float32)
    c[:, 256:384] = (i[:, None] > i[None, :]).astype(np.float32)
    c[:, 384:512] = ((i[:, None] <= i[None, :]) & ((i[:, None] // 64) == (i[None, :] // 64))).astype(np.float32)
    sm = np.zeros(512, np.float32)
    sm[::64] = 1.0
    c[:, 512:1024] = sm[None, :]
    invf = (np.float32(500000.0) ** (-(np.arange(8, dtype=np.float32) * np.float32(2.0) / np.float32(16.0)))).astype(np.float32)
    c[:, 1024:1032] = invf[None, :]
    return c


def build_program(NSEQ, SEQ):
    NT = NSEQ * SEQ
    NB = NT // 128
    NMT = NT // 512
    MT_PER_SEQ = SEQ // 512
    nc = bass.Bass("TRN2", target_bir_lowering=False)

    def din(name, shape, dt=F32):
        return nc.dram_tensor(name, list(shape), dt, kind="ExternalInput").ap()

    x_d = din("x", [NT, D])
    pos_d = din("pos", [128, NB], I32)
    cst_d = din("cst", [128, CST_W])
    prew_d = din("pre_norm_w", [2, D])
    postw_d = din("post_norm_w", [2, D])
    w0_d = din("attn_w_in", [D, 2304])
    b0_d = din("attn_b_in", [1, 2304])
    sink_d = din("attn_sinks", [1, 16])
    wo0_d = din("attn_w_out", [D, D])
    bo0_d = din("attn_b_out", [1, D])
    w1_d = din("rec_w_in", [D, 4096])
    lb_d = din("rec_lb_logits", [2, D])
    gnw_d = din("rec_gnorm_w", [1, 128])
    wo1_d = din("rec_w_out", [D, D])
    out_d = nc.dram_tensor("out", [NT, D], F32, kind="ExternalOutput").ap()
    w1s_d = nc.dram_tensor("w1s", [8, 128, 8, 512], BF16, kind="Internal").ap()

    with ExitStack() as st:
        def sb(name, shape, dt):
            return st.enter_context(nc.sbuf_tensor(name, list(shape), dt))

        def ps(name, shape, dt):
            return st.enter_context(nc.psum_tensor(name, list(shape), dt))

        W0 = sb("W0", [128, 8, 2304], BF16)
        Wo0 = sb("Wo0", [128, 8, 1024], BF16)
        Wo1 = sb("Wo1", [128, 8, 1024], BF16)
        W1h = [sb("W1h%d" % i, [128, 8, 512], BF16) for i in range(2)]
        xb = [sb("xb%d" % i, [128, 4, 1024], F32) for i in range(2)]
        FF = sb("FF", [128, 4096], F32)
        hT = sb("hT", [128, 8, 512], BF16)
        og_all = sb("og_all", [128, 4, 1024], BF16)
        ident = sb("ident", [128, 128], BF16)
        mask_cur = sb("mask_cur", [128, 128], BF16)
        mask_prev = sb("mask_prev", [128, 128], BF16)
        maskbd = sb("maskbd", [128, 128], BF16)
        startmask = sb("startmask", [128, 512], F32)
        invf = sb("invf", [128, 8], F32)
        cosT = sb("cosT", [128, NB, 8], F32)
        sinT = sb("sinT", [128, NB, 8], F32)
        wpost = sb("wpost", [128, 2, 1024], F32)
        browA = sb("browA", [65, 1024], BF16)
        posi = sb("posi", [128, NB], I32)
        browB = sb("browB", [1, 1024], BF16)
        ones = sb("ones", [65, 128], BF16)
        small = sb("small", [128, 128], F32)
        S32 = sb("S32", [128, 8, 128], F32)
        Sbf = sb("Sbf", [128, 8, 128], BF16)
        hb = sb("hb", [128, 1024], BF16)
        junk = sb("junk", [128, 1024], BF16)
        qkb = sb("qkb", [128, 18, 64], BF16)
        qkr = sb("qkr", [128, 18, 16], F32)
        rt = sb("rt", [128, 4, 18, 8], F32)
        qT = sb("qT", [64, 16, 128], BF16)
        kT = sb("kT", [64, 2, 2, 128], BF16)
        vaug = sb("vaug", [128, 2, 2, 65], BF16)
        PT = [sb("PT%d" % i, [128, 512], BF16) for i in range(4)]
        ogT = sb("ogT", [128, 8, 128], BF16)
        qeT = [sb("qeT%d" % i, [128, 512], BF16) for i in range(2)]
        keT = [sb("keT%d" % i, [128, 512], BF16) for i in range(2)]
        kdT = [sb("kdT%d" % i, [128, 512], BF16) for i in range(2)]
        kd_tm = sb("kd_tm", [128, 4, 128], BF16)
        vb = [sb("vb%d" % i, [128, 4, 128], BF16) for i in range(2)]
        smk = sb("smk", [128, 4, 128], BF16)

        PREW0, PREW1 = 0, 8
        SCQ0, SCH0, SCH1 = 16, 24, 32
        LBA, LBB, LBNB = 40, 48, 56
        GNW = 64
        ESINK = 65
        SS = 81
        RSTD = 85
        DEN = 89
        RDEN = 105
        NHALF = 121
        LBT = 125

        def sv(c, n=1):
            return small[:, c:c + n]

        ptb = [ps("ptb%d" % i, [128, 1024], BF16) for i in range(2)]
        pab = [ps("pab%d" % i, [128, 512], F32) for i in range(6)]
        rr = {"pa": 0, "pt": 0}

        def next_pa():
            i = rr["pa"]
            rr["pa"] = (i + 1) % 6
            return i

        def next_pt():
            i = rr["pt"]
            rr["pt"] = (i + 1) % 2
            return i

        S = Sched(nc)
        op = S.op
        FK = ["FF0", "FF1", "FF2", "FF3"]

        def F(i):
            return FF[:, i * 1024:(i + 1) * 1024]

        def Fh(i):
            return FF[:, i * 512:(i + 1) * 512]

        def FhK(i):
            return "FH%d" % i

        ALLF = FK + [FhK(i) for i in range(8)]

        op("sp", lambda q: q.dma_start(out=FF[:, 0:CST_W], in_=cst_d), writes=ALLF, dma="cst")
        op("dve", lambda q: q.tensor_copy(out=ident[:], in_=FF[:, 0:128]), reads=ALLF, writes=["ident"])
        op("dve", lambda q: q.tensor_copy(out=mask_cur[:], in_=FF[:, 128:256]), reads=ALLF, writes=["mask_cur"])
        op("dve", lambda q: q.tensor_copy(out=mask_prev[:], in_=FF[:, 256:384]), reads=ALLF, writes=["mask_prev"])
        op("dve", lambda q: q.tensor_copy(out=maskbd[:], in_=FF[:, 384:512]), reads=ALLF, writes=["maskbd"])
        op("dve", lambda q: q.tensor_copy(out=startmask[:], in_=FF[:, 512:1024]), reads=ALLF, writes=["startmask"])
        op("dve", lambda q: q.tensor_copy(out=invf[:], in_=FF[:, 1024:1032]), reads=ALLF, writes=["invf"])
        op("pool", lambda q: q.memset(ones[:], 1.0), writes=["ones"])
        op("pool", lambda q: q.memset(small[:, NHALF:NHALF + 4], -0.5), writes=["nhalf"])
        op("pool", lambda q: q.memset(vaug[:], 1.0), writes=["vaug0", "vaug1"])
        op("sp", lambda q: q.dma_start(out=small[:, PREW0:PREW0 + 8], in_=prew_d[0:1, :].rearrange("o (k p) -> p (o k)", p=128),
                                       allow_slow_non_contiguous=True), writes=["prew", "smq"], dma="sm")
        op("sp", lambda q: q.dma_start(out=small[:, PREW1:PREW1 + 8], in_=prew_d[1:2, :].rearrange("o (k p) -> p (o k)", p=128),
                                       allow_slow_non_contiguous=True), writes=["prew", "smq"], dma="sm")
        op("sp", lambda q: q.dma_start(out=small[:, LBA:LBA + 8], in_=lb_d[0:1, :].rearrange("o (k p) -> p (o k)", p=128),
                                       allow_slow_non_contiguous=True), writes=["lb0", "smq"], dma="sm")
        op("sp", lambda q: q.dma_start(out=small[:, LBB:LBB + 8], in_=lb_d[1:2, :].rearrange("o (k p) -> p (o k)", p=128),
                                       allow_slow_non_contiguous=True), writes=["lb1", "smq"], dma="sm")
        op("sp", lambda q: q.dma_start(out=small[:, GNW:GNW + 1], in_=gnw_d.rearrange("o p -> p o"),
                                       allow_slow_non_contiguous=True), writes=["gnw", "smq"], dma="sm")
        op("sp", lambda q: q.dma_start(out=small[:, ESINK:ESINK + 16], in_=sink_d.partition_broadcast(128)), writes=["esink", "smq"], dma="sm")
        op("sp", lambda q: q.dma_start(out=wpost[:, 0, :], in_=postw_d[0:1, :].partition_broadcast(128)), writes=["wpost", "smq"], dma="sm")
        op("sp", lambda q: q.dma_start(out=wpost[:, 1, :], in_=postw_d[1:2, :].partition_broadcast(128)), writes=["wpost", "smq"], dma="sm")
        op("sp", lambda q: q.dma_start(out=posi[:], in_=pos_d), writes=["posi", "smq"], dma="sm")
        op("dve", lambda q: q.tensor_scalar(out=sv(SCQ0, 8), in0=sv(PREW0, 8), scalar1=0.125, scalar2=None, op0=ALU.mult), reads=["prew"], writes=["scq0"])
        op("dve", lambda q: q.tensor_scalar(out=sv(SCH0, 8), in0=sv(PREW0, 8), scalar1=0.5, scalar2=None, op0=ALU.mult), reads=["prew"], writes=["sch0"])
        op("dve", lambda q: q.tensor_scalar(out=sv(SCH1, 8), in0=sv(PREW1, 8), scalar1=0.5, scalar2=None, op0=ALU.mult), reads=["prew"], writes=["sch1"])
        op("act", lambda q: q.activation(out=sv(ESINK, 16), in_=sv(ESINK, 16), func=AF.Exp), reads=["esink"], writes=["esink"])
        op("dve", lambda q: q.tensor_tensor(out=sv(LBNB, 8), in0=sv(LBB, 8), in1=sv(LBA, 8), op=ALU.subtract), reads=["lb0", "lb1"], writes=["lbt"])
        op("act", lambda q: q.activation(out=sv(LBNB, 8), in_=sv(LBNB, 8), func=AF.Tanh, scale=0.5), reads=["lbt"], writes=["lbt"])
        op("dve", lambda q: q.tensor_scalar(out=sv(LBA, 8), in0=sv(LBNB, 8), scalar1=0.25, scalar2=0.75, op0=ALU.mult, op1=ALU.add), reads=["lbt"], writes=["lb0"])
        op("dve", lambda q: q.tensor_scalar(out=sv(LBB, 8), in0=sv(LBNB, 8), scalar1=-0.25, scalar2=0.25, op0=ALU.mult, op1=ALU.add), reads=["lbt"], writes=["lb1"])
        op("dve", lambda q: q.tensor_scalar(out=sv(LBNB, 8), in0=sv(LBB, 8), scalar1=-1.0, scalar2=None, op0=ALU.mult), reads=["lb1", "lbt"], writes=["lbt"])
        LBK = ["lb0", "lb1", "lbt"]

        o0 = 1040
        posf = FF[:, o0:o0 + NB]
        ang = FF[:, o0 + NB:o0 + NB + NB * 8]
        tmpu = FF[:, o0 + 9 * NB:o0 + 17 * NB]
        tmpk = FF[:, o0 + 17 * NB:o0 + 25 * NB]
        assert o0 + 25 * NB <= 4096
        tmpi = hb[:].bitcast(I32)[:, 0:NB * 8]
        op("dve", lambda q: q.tensor_copy(out=posf, in_=posi[:]), reads=["posi"] + ALLF, writes=ALLF)
        op("dve", lambda q: q.tensor_tensor(out=ang.rearrange("p (b i) -> p b i", i=8),
                                            in0=posf.unsqueeze(2).to_broadcast([128, NB, 8]),
                                            in1=invf[:].unsqueeze(1).to_broadcast([128, NB, 8]), op=ALU.mult),
           reads=ALLF + ["invf"], writes=ALLF)
        for which, tab in ((0, sinT), (1, cosT)):
            if which == 1:
                op("dve", lambda q: q.tensor_scalar(out=ang, in0=ang, scalar1=math.pi / 2, scalar2=None, op0=ALU.add), reads=ALLF, writes=ALLF)
            op("dve", lambda q: q.tensor_scalar(out=tmpu, in0=ang, scalar1=1.0 / TWO_PI, scalar2=None, op0=ALU.mult), reads=ALLF, writes=ALLF)
            op("dve", lambda q: q.tensor_copy(out=tmpi, in_=tmpu), reads=ALLF, writes=["hb"])
            op("dve", lambda q: q.tensor_copy(out=tmpk, in_=tmpi), reads=["hb"], writes=ALLF)
            op("dve", lambda q: q.scalar_tensor_tensor(out=tmpu, in0=tmpk, scalar=-C1, in1=ang, op0=ALU.mult, op1=ALU.add), reads=ALLF, writes=ALLF)
            op("dve", lambda q: q.scalar_tensor_tensor(out=tmpu, in0=tmpk, scalar=-C2, in1=tmpu, op0=ALU.mult, op1=ALU.add), reads=ALLF, writes=ALLF)
            op("dve", lambda q: q.tensor_scalar(out=tmpu, in0=tmpu, scalar1=-PI_LO, scalar2=PI_LO, op0=ALU.max, op1=ALU.min), reads=ALLF, writes=ALLF)
            op("act", lambda q, tab=tab: q.activation(out=tab[:].rearrange("p b i -> p (b i)"), in_=tmpu, func=AF.Sin),
               reads=ALLF, writes=["cosT" if which == 1 else "sinT"])

        BROWS = [(0, 1024, 0), (1024, 1792, 32), (1792, 2304, 64)]
        for (c0, c1, p) in BROWS:
            op("sp", lambda q, c0=c0, c1=c1, p=p: q.dma_start(out=FF[p:p + 1, 0:c1 - c0], in_=b0_d[0:1, c0:c1]),
               reads=["cosT", "sinT"], writes=ALLF + ["smq"], dma="sm")
        op("sp", lambda q: q.dma_start(out=FF[0:1, 1024:2048], in_=bo0_d), writes=ALLF + ["smq"], dma="sm")
        op("dve", lambda q: q.tensor_scalar(out=browA[0:1, 0:1024], in0=FF[0:1, 0:1024], scalar1=0.125, scalar2=None, op0=ALU.mult), reads=ALLF, writes=["browA"])
        op("dve", lambda q: q.tensor_copy(out=browA[32:33, 0:256], in_=FF[32:33, 0:256]), reads=ALLF, writes=["browA"])
        op("dve", lambda q: q.tensor_scalar(out=browA[32:33, 256:768], in0=FF[32:33, 256:768], scalar1=0.5, scalar2=None, op0=ALU.mult), reads=ALLF, writes=["browA"])
        op("dve", lambda q: q.tensor_scalar(out=browA[64:65, 0:512], in0=FF[64:65, 0:512], scalar1=0.5, scalar2=None, op0=ALU.mult), reads=ALLF, writes=["browA"])
        op("dve", lambda q: q.tensor_copy(out=browB[:], in_=FF[0:1, 1024:2048]), reads=ALLF, writes=["browB"])

        def brow(c0, n):
            for (r0, r1, p) in BROWS:
                if r0 <= c0 and c0 + n <= r1:
                    return ones[p:p + 1, :], browA[p:p + 1, c0 - r0:c0 - r0 + n]
            raise AssertionError((c0, n))

        stageA = FF
        stageB = xb[1][:].rearrange("p j d -> p (j d)")
        XB1K = [("x", 1, j) for j in range(4)]
        stg = [(stageA, ALLF), (stageB, XB1K)]
        stb = [(og_all[:].rearrange("p j d -> p (j d)"), [("og", j) for j in range(4)]), (hT[:].rearrange("p k t -> p (k t)"), [("hT", j) for j in range(4)])]
        cnt = [0]

        def conv(eng, out, in_, scal, rd, wr):
            if eng == "dve":
                op("dve", lambda q: q.tensor_scalar(out=out, in0=in_, scalar1=scal, scalar2=None, op0=ALU.mult), reads=rd, writes=wr)
            else:
                op("act", lambda q: q.activation(out=out, in_=in_, func=AF.Copy, scale=scal), reads=rd, writes=wr)

        for kc in range(8):
            sg, sk = stg[cnt[0] % 2]
            cnt[0] += 1
            op("sp", lambda q, sg=sg, kc=kc: q.dma_start(out=sg[:, 0:2304], in_=w0_d[kc * 128:(kc + 1) * 128, :]),
               reads=["browA", "browB"], writes=sk, dma="wl%d" % (cnt[0] % 2))
            conv("dve", W0[:, kc, 0:1024], sg[:, 0:1024], sv(SCQ0 + kc), sk + ["scq0"], ["W0"])
            conv("act", W0[:, kc, 1024:1280], sg[:, 1024:1280], sv(PREW0 + kc), sk + ["prew"], ["W0"])
            conv("act", W0[:, kc, 1280:2304], sg[:, 1280:2304], sv(SCH0 + kc), sk + ["sch0"], ["W0"])
        for kc in range(8):
            sg, sk = stg[cnt[0] % 2]
            cnt[0] += 1
            op("sp", lambda q, sg=sg, kc=kc: q.dma_start(out=sg[:, 0:1024], in_=wo0_d[kc * 128:(kc + 1) * 128, :]),
               writes=sk, dma="wl%d" % (cnt[0] % 2))
            op("sp", lambda q, sg=sg, kc=kc: q.dma_start(out=sg[:, 1024:2048], in_=wo1_d[kc * 128:(kc + 1) * 128, :]),
               writes=sk, dma="wl%d" % (cnt[0] % 2))
            op("act", lambda q, sg=sg, kc=kc: q.activation(out=Wo0[:, kc, :], in_=sg[:, 0:1024], func=AF.Copy), reads=sk, writes=["Wo0"])
            conv("dve", Wo1[:, kc, :], sg[:, 1024:2048], sv(GNW), sk + ["gnw"], ["Wo1"])
        for kc in range(8):
            sg, sk = stg[cnt[0] % 2]
            sbt, sbk = stb[cnt[0] % 2]
            cnt[0] += 1
            op("sp", lambda q, sg=sg, kc=kc: q.dma_start(out=sg[:, 0:4096], in_=w1_d[kc * 128:(kc + 1) * 128, :]),
               writes=sk, dma="wl%d" % (cnt[0] % 2))
            for t in range(4):
                o_ap = sbt.rearrange("p (h t c) -> p t h c", h=8, t=4, c=128)[:, t, :, :]
                i_ap = sg[:, t * 1024:(t + 1) * 1024].rearrange("p (h c) -> p h c", h=8)
                scal = sv(PREW1 + kc) if t == 2 else sv(SCH1 + kc)
                conv("dve" if t % 2 == 0 else "act", o_ap, i_ap, scal, sk + ["prew", "sch1"], sbk)
            op("sp", lambda q, sbt=sbt, kc=kc: q.dma_start(out=w1s_d[:, :, kc, :].rearrange("h p c -> p h c"),
                                                          in_=sbt.rearrange("p (h c) -> p h c", h=8)),
               reads=sbk, writes=["w1s"], dma="ws%d" % (cnt[0] % 2))

        def xk(buf, j):
            return ("x", buf, j)

        def load_x(m):
            buf = m % 2
            op("sp", lambda q: q.dma_start(out=xb[buf][:], in_=x_d[m * 512:(m + 1) * 512, :].rearrange("(j p) d -> p j d", p=128)),
               writes=[xk(buf, j) for j in range(4)], dma="xl%d" % buf)

        def load_w1h(h, slot):
            op("sp", lambda q: q.dma_start(out=W1h[slot][:], in_=w1s_d[h]), reads=["w1s"], writes=["W1h%d" % slot], dma="w1l%d" % slot)

        def rms_rstd(src_ap, src_keys, col):
            op("act", lambda q: q.activation(out=junk[:], in_=src_ap, func=AF.Square, accum_out=sv(SS + col)),
               reads=src_keys, writes=["junk", ("ss", col)])
            op("dve", lambda q: q.tensor_scalar(out=sv(SS + col), in0=sv(SS + col), scalar1=1.0 / 1024, scalar2=EPS, op0=ALU.mult, op1=ALU.add),
               reads=[("ss", col)], writes=[("ss", col)])
            op("pool", lambda q: q.tensor_tensor(out=sv(RSTD + col), in0=sv(SS + col), in1=sv(NHALF), op=ALU.pow),
               reads=[("ss", col), "nhalf"], writes=[("rstd", col)])

        def make_hT(buf, j):
            xs = xb[buf][:, j, :]
            rms_rstd(xs, [xk(buf, j)], j)
            op("act", lambda q: q.activation(out=hb[:], in_=xs, func=AF.Copy, scale=sv(RSTD + j)),
               reads=[xk(buf, j), ("rstd", j)], writes=["hb"])
            pt = next_pt()
            for kc in range(8):
                op("pe", lambda q, kc=kc: q.transpose(out=ptb[pt][:, kc * 128:(kc + 1) * 128], in_=hb[:, kc * 128:(kc + 1) * 128], identity=ident[:]),
                   reads=["hb", "ident"], writes=[("pt", pt)])
            op("dve", lambda q: q.tensor_copy(out=hT[:, :, j * 128:(j + 1) * 128], in_=ptb[pt][:].rearrange("p (k t) -> p k t", k=8)),
               reads=[("pt", pt)], writes=[("hT", j)])

        def post_norm_residual(buf, j, layer, pbanks):
            for hf in range(2):
                b = pbanks[hf]
                op("act", lambda q, b=b, hf=hf: q.activation(out=junk[:, 0:512], in_=pab[b][:], func=AF.Square, accum_out=sv(SS + hf)),
                   reads=[("pa", b)], writes=["junk", ("ss", hf)])
            op("dve", lambda q: q.tensor_scalar(out=sv(SS, 2), in0=sv(SS, 2), scalar1=1.0 / 1024, scalar2=EPS / 2, op0=ALU.mult, op1=ALU.add),
               reads=[("ss", 0), ("ss", 1)], writes=[("ss", 0), ("ss", 1)])
            op("dve", lambda q: q.tensor_tensor(out=sv(SS), in0=sv(SS), in1=sv(SS + 1), op=ALU.add),
               reads=[("ss", 0), ("ss", 1)], writes=[("ss", 0)])
            op("pool", lambda q: q.tensor_tensor(out=sv(RSTD), in0=sv(SS), in1=sv(NHALF), op=ALU.pow),
               reads=[("ss", 0), "nhalf"], writes=[("rstd", 0)])
            for hf in range(2):
                b = pbanks[hf]
                op("dve", lambda q, b=b, hf=hf: q.scalar_tensor_tensor(out=Fh(6 + hf), in0=pab[b][:], scalar=sv(RSTD),
                                                                        in1=wpost[:, layer, hf * 512:(hf + 1) * 512], op0=ALU.mult, op1=ALU.mult),
                   reads=[("pa", b), ("rstd", 0), "wpost"], writes=[FhK(6 + hf)])
            op("pool", lambda q: q.tensor_tensor(out=xb[buf][:, j, :], in0=xb[buf][:, j, :], in1=F(3), op=ALU.add),
               reads=[xk(buf, j), FhK(6), FhK(7)], writes=[xk(buf, j)])

        def out_proj(src_bf16_ap, src_keys, Wo, wo_key, bias, ybanks):
            pt = next_pt()
            for kc in range(8):
                op("pe", lambda q, kc=kc: q.transpose(out=ptb[pt][:, kc * 128:(kc + 1) * 128], in_=src_bf16_ap[:, kc * 128:(kc + 1) * 128], identity=ident[:]),
                   reads=src_keys + ["ident"], writes=[("pt", pt)])
            op("act", lambda q: q.activation(out=ogT[:].rearrange("p k t -> p (k t)"), in_=ptb[pt][:], func=AF.Copy),
               reads=[("pt", pt)], writes=["ogT"])
            banks = list(ybanks)
            for hf in range(2):
                b = banks[hf]
                for kc in range(8):
                    op("pe", lambda q, b=b, kc=kc, hf=hf: q.matmul(pab[b][:], lhsT=ogT[:, kc, :], rhs=Wo[:, kc, hf * 512:(hf + 1) * 512],
                                                                   start=(kc == 0), stop=(kc == 7 and not bias)),
                       reads=["ogT", wo_key], writes=[("pa", b)])
                if bias:
                    op("pe", lambda q, b=b, hf=hf: q.matmul(pab[b][:], lhsT=ones[0:1, :], rhs=browB[0:1, hf * 512:(hf + 1) * 512], start=False, stop=True),
                       reads=["ones", "browB"], writes=[("pa", b)])
            return banks

        def proj_tm(b, j, c0, n, Wt, wkey, bias=True):
            for kc in range(8):
                op("pe", lambda q, kc=kc: q.matmul(pab[b][:, 0:n], lhsT=hT[:, kc, j * 128:(j + 1) * 128], rhs=Wt[:, kc, c0:c0 + n],
                                                   start=(kc == 0), stop=(kc == 7 and not bias)),
                   reads=[("hT", j), wkey], writes=[("pa", b)])
            if bias:
                o1, br = brow(c0, n)
                op("pe", lambda q: q.matmul(pab[b][:, 0:n], lhsT=o1, rhs=br, start=False, stop=True),
                   reads=["ones", "browA"], writes=[("pa", b)])
            return b

        def layer0_sub(m, j):
            buf = m % 2
            blk = m * 4 + j
            first = (blk % (SEQ // 128) == 0)
            slot = blk % 2
            make_hT(buf, j)
            bq = [proj_tm(0, j, 0, 512, W0, "W0"), proj_tm(1, j, 512, 512, W0, "W0")]
            bkv = proj_tm(2, j, 1024, 256, W0, "W0")
            for a in range(2):
                op("act", lambda q, a=a: q.activation(out=qkb[:, 8 * a:8 * a + 8, :], in_=pab[bq[a]][:].rearrange("p (h d) -> p h d", h=8), func=AF.Copy),
                   reads=[("pa", bq[a])], writes=["qkb"])
                op("act", lambda q, a=a: q.activation(out=qkr[:, 8 * a:8 * a + 8, :], in_=pab[bq[a]][:].rearrange("p (h d) -> p h d", h=8)[:, :, 0:16], func=AF.Copy),
                   reads=[("pa", bq[a])], writes=["qkr"])
            op("act", lambda q: q.activation(out=qkb[:, 16:18, :], in_=pab[bkv][:, 0:128].rearrange("p (h d) -> p h d", h=2), func=AF.Copy),
               reads=[("pa", bkv)], writes=["qkb"])
            op("act", lambda q: q.activation(out=qkr[:, 16:18, :], in_=pab[bkv][:, 0:128].rearrange("p (h d) -> p h d", h=2)[:, :, 0:16], func=AF.Copy),
               reads=[("pa", bkv)], writes=["qkr"])
            op("act", lambda q: q.activation(out=vaug[:, slot, :, 0:64], in_=pab[bkv][:, 128:256].rearrange("p (g d) -> p g d", g=2), func=AF.Copy),
               reads=[("pa", bkv)], writes=["vaug%d" % slot])
            cb = cosT[:, blk, :].unsqueeze(1).to_broadcast([128, 18, 8])
            sbb = sinT[:, blk, :].unsqueeze(1).to_broadcast([128, 18, 8])
            x1 = qkr[:, :, 0:8]
            x2 = qkr[:, :, 8:16]
            op("pool", lambda q: q.tensor_tensor(out=rt[:, 0], in0=x1, in1=cb, op=ALU.mult), reads=["qkr", "cosT"], writes=["rt0"])
            op("pool", lambda q: q.tensor_tensor(out=rt[:, 1], in0=x2, in1=sbb, op=ALU.mult), reads=["qkr", "sinT"], writes=["rt1"])
            op("pool", lambda q: q.tensor_tensor(out=rt[:, 2], in0=x2, in1=cb, op=ALU.mult), reads=["qkr", "cosT"], writes=["rt2"])
            op("pool", lambda q: q.tensor_tensor(out=rt[:, 3], in0=x1, in1=sbb, op=ALU.mult), reads=["qkr", "sinT"], writes=["rt3"])
            op("dve", lambda q: q.tensor_tensor(out=qkb[:, :, 0:8], in0=rt[:, 0], in1=rt[:, 1], op=ALU.subtract), reads=["rt0", "rt1"], writes=["qkb"])
            op("dve", lambda q: q.tensor_tensor(out=qkb[:, :, 8:16], in0=rt[:, 2], in1=rt[:, 3], op=ALU.add), reads=["rt2", "rt3"], writes=["qkb"])
            for a in range(2):
                for hh in range(8):
                    op("pe", lambda q, a=a, hh=hh: q.transpose(out=ptb[a][0:64, hh * 128:(hh + 1) * 128], in_=qkb[:, 8 * a + hh, :], identity=ident[:]),
                       reads=["qkb", "ident"], writes=[("pt", a)])
                op("dve", lambda q, a=a: q.tensor_copy(out=qT[:, 8 * a:8 * a + 8, :], in_=ptb[a][0:64, :].rearrange("p (h t) -> p h t", h=8)),
                   reads=[("pt", a)], writes=["qT"])
            for g in range(2):
                op("pe", lambda q, g=g: q.transpose(out=ptb[0][0:64, g * 128:(g + 1) * 128], in_=qkb[:, 16 + g, :], identity=ident[:]),
                   reads=["qkb", "ident"], writes=[("pt", 0)])
            op("dve", lambda q: q.tensor_copy(out=kT[:, slot, :, :], in_=ptb[0][0:64, 0:256].rearrange("p (g t) -> p g t", g=2)),
               reads=[("pt", 0)], writes=["kT%d" % slot])
            rr["pt"] = 1
            oa = [0, 1, 2]
            pti = 0
            sbank = [0]
            for g in range(2):
                for a in range(2):
                    kbs = ([] if first else [(1 - slot, mask_prev, "mask_prev")]) + [(slot, mask_cur, "mask_cur")]
                    pts = []
                    for (ks, msk, mkey) in kbs:
                        b = 3 + sbank[0] % 3
                        sbank[0] += 1
                        op("pe", lambda q, b=b, ks=ks, g=g, a=a: q.matmul(pab[b][:], lhsT=kT[:, ks, g, :],
                                                                          rhs=qT[:, 8 * g + 4 * a:8 * g + 4 * a + 4, :].rearrange("p h t -> p (h t)"),
                                                                          start=True, stop=True),
                           reads=["kT%d" % ks, "qT"], writes=[("pa", b)])
                        p = pti % 4
                        pti += 1
                        op("act", lambda q, b=b, p=p: q.activation(out=PT[p][:], in_=pab[b][:], func=AF.Exp), reads=[("pa", b)], writes=[("PT", p)])
                        op("dve", lambda q, p=p, msk=msk: q.tensor_tensor(out=PT[p][:].rearrange("p (h t) -> p h t", h=4),
                                                                         in0=PT[p][:].rearrange("p (h t) -> p h t", h=4),
                                                                         in1=msk[:].unsqueeze(1).to_broadcast([128, 4, 128]), op=ALU.mult),
                           reads=[("PT", p), mkey], writes=[("PT", p)])
                        pts.append((p, ks))
                    for hh in range(4):
                        h = 8 * g + 4 * a + hh
                        ob, oo = oa[h // 7], (h % 7) * 65
                        for i, (p, ks) in enumerate(pts):
                            op("pe", lambda q, p=p, ks=ks, hh=hh, ob=ob, oo=oo, i=i, n=len(pts), g=g: q.matmul(
                                pab[ob][:, oo:oo + 65], lhsT=PT[p][:, hh * 128:(hh + 1) * 128], rhs=vaug[:, ks, g, :],
                                start=(i == 0), stop=(i == n - 1)),
                               reads=[("PT", p), "vaug%d" % ks], writes=[("pa", ob)])
            for bi in range(3):
                h0 = 7 * bi
                nh = min(7, 16 - h0)
                ov = pab[oa[bi]][:, 0:nh * 65].rearrange("p (h d) -> p h d", d=65)
                op("dve", lambda q, ov=ov, h0=h0, nh=nh: q.tensor_tensor(out=sv(DEN + h0, nh), in0=ov[:, :, 64], in1=sv(ESINK + h0, nh), op=ALU.add),
                   reads=[("pa", oa[bi]), "esink"], writes=[("den", bi)])
            op("dve", lambda q: q.reciprocal(out=sv(RDEN, 16), in_=sv(DEN, 16)), reads=[("den", 0), ("den", 1), ("den", 2)], writes=["rden"])
            for bi in range(3):
                h0 = 7 * bi
                nh = min(7, 16 - h0)
                ov = pab[oa[bi]][:, 0:nh * 65].rearrange("p (h d) -> p h d", d=65)
                op("dve", lambda q, ov=ov, h0=h0, nh=nh: q.tensor_tensor(out=F(0).rearrange("p (h d) -> p h d", d=64)[:, h0:h0 + nh, :], in0=ov[:, :, 0:64],
                                                                         in1=sv(RDEN + h0, nh).unsqueeze(2).to_broadcast([128, nh, 64]), op=ALU.mult),
                   reads=[("pa", oa[bi]), "rden"], writes=[FhK(0), FhK(1)])
            bz = [proj_tm(3, j, 1280, 512, W0, "W0"), proj_tm(4, j, 1792, 512, W0, "W0")]
            for hf in range(2):
                op("act", lambda q, hf=hf: q.activation(out=Fh(2 + hf), in_=pab[bz[hf]][:], func=AF.Tanh), reads=[("pa", bz[hf])], writes=[FhK(2 + hf)])
                op("dve", lambda q, hf=hf: q.scalar_tensor_tensor(out=Fh(2 + hf), in0=Fh(2 + hf), scalar=1.0, in1=pab[bz[hf]][:], op0=ALU.add, op1=ALU.mult),
                   reads=[FhK(2 + hf), ("pa", bz[hf])], writes=[FhK(2 + hf)])
            op("dve", lambda q: q.tensor_tensor(out=og_all[:, 0, :], in0=F(0), in1=F(1), op=ALU.mult),
               reads=[FhK(0), FhK(1), FhK(2), FhK(3)], writes=[("og", 0)])
            yb = out_proj(og_all[:, 0, :], [("og", 0)], Wo0, "Wo0", True, (5, 0))
            post_norm_residual(buf, j, 0, yb)

        def layer1_tile(m):
            buf = m % 2
            seq_start = (m % MT_PER_SEQ == 0)
            for j in range(4):
                make_hT(buf, j)
            hTk = [("hT", j) for j in range(4)]
            if seq_start:
                op("pool", lambda q: q.memset(S32[:], 0.0), writes=[("S32", h) for h in range(8)])
            def head(h):
                ws = h % 2
                Wt = W1h[ws]
                wk = "W1h%d" % ws
                db = h % 2
                bk = (0, 1, 2, 3, 4, 5, 0, 1) if h % 2 == 0 else (2, 3, 4, 5, 0, 1, 2, 3)
                bq, bf = bk[0], bk[1]
                for (b, c0) in ((bq, 0), (bf, 128)):
                    for kc in range(8):
                        op("pe", lambda q, b=b, c0=c0, kc=kc: q.matmul(pab[b][:], lhsT=Wt[:, kc, c0:c0 + 128], rhs=hT[:, kc, :], start=(kc == 0), stop=(kc == 7)),
                           reads=hTk + [wk], writes=[("pa", b)])
                bv, bzz = bk[2], bk[3]
                for (b, c0) in ((bv, 256), (bzz, 384)):
                    for j in range(4):
                        for kc in range(8):
                            op("pe", lambda q, b=b, c0=c0, kc=kc, j=j: q.matmul(pab[b][:, j * 128:(j + 1) * 128], lhsT=hT[:, kc, j * 128:(j + 1) * 128],
                                                                                rhs=Wt[:, kc, c0:c0 + 128], start=(kc == 0), stop=(kc == 7)),
                               reads=[("hT", j), wk], writes=[("pa", b)])
                A0, A1, A2, A3, A4, A5, A6 = [Fh(i) for i in range(7)]
                K0, K1, K2, K3, K4, K5, K6 = [FhK(i) for i in range(7)]
                op("act", lambda q: q.activation(out=A0, in_=pab[bq][:], func=AF.Tanh), reads=[("pa", bq)], writes=[K0])
                op("dve", lambda q: q.scalar_tensor_tensor(out=A0, in0=A0, scalar=1.0, in1=pab[bq][:], op0=ALU.add, op1=ALU.mult),
                   reads=[K0, ("pa", bq)], writes=[K0])
                op("act", lambda q: q.activation(out=A1, in_=pab[bf][:], func=AF.Tanh), reads=[("pa", bf)], writes=[K1])
                op("act", lambda q: q.activation(out=A2, in_=A1, func=AF.Identity, scale=sv(LBB + h), bias=sv(LBA + h)), reads=[K1] + LBK, writes=[K2])
                op("act", lambda q: q.activation(out=A3, in_=A1, func=AF.Identity, scale=sv(LBNB + h), bias=sv(LBB + h)), reads=[K1] + LBK, writes=[K3])
                op("dve", lambda q: q.tensor_tensor_scan(out=A4, data0=startmask[:], data1=A2, initial=0.0, op0=ALU.max, op1=ALU.mult),
                   reads=["startmask", K2], writes=[K4])
                op("dve", lambda q: q.tensor_tensor(out=qeT[db][:], in0=A0, in1=A4, op=ALU.mult), reads=[K0, K4], writes=[("qeT", db)])
                op("dve", lambda q: q.reciprocal(out=A5, in_=A4), reads=[K4], writes=[K5])
                op("dve", lambda q: q.tensor_tensor(out=A6, in0=A3, in1=A5, op=ALU.mult), reads=[K3, K5], writes=[K6])
                op("act", lambda q: q.activation(out=keT[db][:], in_=A6, func=AF.Copy), reads=[K6], writes=[("keT", db)])
                op("dve", lambda q: q.tensor_tensor(out=kdT[db][:].rearrange("p (c t) -> p c t", t=64), in0=A6.rearrange("p (c t) -> p c t", t=64),
                                                    in1=A4.rearrange("p (c t) -> p c t", t=64)[:, :, 63:64].to_broadcast([128, 8, 64]), op=ALU.mult),
                   reads=[K6, K4], writes=[("kdT", db)])
                op("act", lambda q: q.activation(out=vb[db][:].rearrange("p j v -> p (j v)"), in_=pab[bv][:], func=AF.Copy), reads=[("pa", bv)], writes=[("vb", db)])
                op("act", lambda q: q.activation(out=Fh(7), in_=pab[bzz][:], func=AF.Tanh), reads=[("pa", bzz)], writes=[FhK(7)])
                op("dve", lambda q: q.scalar_tensor_tensor(out=Fh(7), in0=Fh(7), scalar=1.0, in1=pab[bzz][:], op0=ALU.add, op1=ALU.mult),
                   reads=[FhK(7), ("pa", bzz)], writes=[FhK(7)])
                pt = next_pt()
                for j in range(4):
                    op("pe", lambda q, j=j: q.transpose(out=ptb[pt][:, j * 128:(j + 1) * 128], in_=kdT[db][:, j * 128:(j + 1) * 128], identity=ident[:]),
                       reads=[("kdT", db), "ident"], writes=[("pt", pt)])
                op("dve", lambda q: q.tensor_copy(out=kd_tm[:].rearrange("p j k -> p (j k)"), in_=ptb[pt][:, 0:512]), reads=[("pt", pt)], writes=["kd_tm"])
                ub = [bk[4], bk[5]]
                for j in range(4):
                    for c in range(2):
                        op("pe", lambda q, j=j, c=c: q.matmul(pab[ub[c]][:, j * 128:(j + 1) * 128], lhsT=kd_tm[64 * c:64 * c + 64, j, :],
                                                              rhs=vb[db][64 * c:64 * c + 64, j, :], start=True, stop=True),
                           reads=["kd_tm", ("vb", db)], writes=[("pa", ub[c])])
                for k in range(8):
                    j, c = k // 2, k % 2
                    op("act", lambda q, k=k: q.activation(out=Sbf[:, k, :], in_=S32[:, h, :], func=AF.Copy), reads=[("S32", h)], writes=[("Sbf", k)])
                    op("dve", lambda q, k=k, j=j, c=c: q.scalar_tensor_tensor(out=S32[:, h, :], in0=S32[:, h, :], scalar=A4[:, 64 * k + 63:64 * k + 64],
                                                                                 in1=pab[ub[c]][:, j * 128:(j + 1) * 128], op0=ALU.mult, op1=ALU.add),
                       reads=[("S32", h), K4, ("pa", ub[c])], writes=[("S32", h)])
                bs = bk[6]
                for j in range(4):
                    op("pe", lambda q, j=j: q.matmul(pab[bs][:, j * 128:(j + 1) * 128], lhsT=keT[db][:, j * 128:(j + 1) * 128], rhs=qeT[db][:, j * 128:(j + 1) * 128],
                                                     start=True, stop=True),
                       reads=[("keT", db), ("qeT", db)], writes=[("pa", bs)])
                op("dve", lambda q: q.tensor_tensor(out=smk[:], in0=pab[bs][:].rearrange("p (j t) -> p j t", j=4),
                                                    in1=maskbd[:].unsqueeze(1).to_broadcast([128, 4, 128]), op=ALU.mult),
                   reads=[("pa", bs), "maskbd"], writes=["smk"])
                bo = bk[7]
                for j in range(4):
                    op("pe", lambda q, j=j: q.matmul(pab[bo][:, j * 128:(j + 1) * 128], lhsT=smk[:, j, :], rhs=vb[db][:, j, :], start=True, stop=True),
                       reads=["smk", ("vb", db)], writes=[("pa", bo)])
                    for c in range(2):
                        k = 2 * j + c
                        op("pe", lambda q, j=j, c=c, k=k: q.matmul(pab[bo][64 * c:64 * c + 64, j * 128:(j + 1) * 128],
                                                                   lhsT=qeT[db][:, j * 128 + 64 * c:j * 128 + 64 * c + 64], rhs=Sbf[:, k, :],
                                                                   start=False, stop=True, skip_group_check=True),
                           reads=[("qeT", db), ("Sbf", k)], writes=[("pa", bo)])
                for j in range(4):
                    op("act", lambda q, j=j: q.activation(out=junk[:, 0:128], in_=pab[bo][:, j * 128:(j + 1) * 128], func=AF.Square, accum_out=sv(SS + j)),
                       reads=[("pa", bo)], writes=["junk", ("ss", j)])
                op("dve", lambda q: q.tensor_scalar(out=sv(SS, 4), in0=sv(SS, 4), scalar1=1.0 / 128, scalar2=EPS, op0=ALU.mult, op1=ALU.add),
                   reads=[("ss", j) for j in range(4)], writes=[("ss", j) for j in range(4)])
                op("pool", lambda q: q.tensor_tensor(out=sv(RSTD, 4), in0=sv(SS, 4), in1=sv(NHALF, 4), op=ALU.pow),
                   reads=[("ss", j) for j in range(4)] + ["nhalf"], writes=[("rstd", j) for j in range(4)])
                for j in range(4):
                    op("dve", lambda q, j=j: q.scalar_tensor_tensor(out=og_all[:, j, h * 128:(h + 1) * 128], in0=pab[bo][:, j * 128:(j + 1) * 128],
                                                                     scalar=sv(RSTD + j), in1=Fh(7)[:, j * 128:(j + 1) * 128], op0=ALU.mult, op1=ALU.mult),
                       reads=[("pa", bo), ("rstd", j), FhK(7)], writes=[("og", j)])
                if h + 2 < 8:
                    load_w1h(h + 2, ws)
            for h in range(8):
                head(h)
            for j in range(4):
                yb = out_proj(og_all[:, j, :], [("og", j)], Wo1, "Wo1", False, ((0, 1), (2, 3), (4, 5))[j % 3])
                post_norm_residual(buf, j, 1, yb)

        load_x(0)
        for m in range(NMT):
            S.new_epoch()
            if m + 1 < NMT:
                load_x(m + 1)
            load_w1h(0, 0)
            load_w1h(1, 1)
            for j in range(4):
                layer0_sub(m, j)
            layer1_tile(m)
            buf = m % 2
            op("sp", lambda q, m=m, buf=buf: q.dma_start(out=out_d[m * 512:(m + 1) * 512, :].rearrange("(j p) d -> p j d", p=128), in_=xb[buf][:]),
               reads=[xk(buf, j) for j in range(4)], dma="xs%d" % buf)
        S.emit(st)
    return nc


_PROG_CACHE = {}


def _get_prog(nseq, seq):
    key = (nseq, seq)
    if key not in _PROG_CACHE:
        _PROG_CACHE[key] = build_program(nseq, seq)
    return _PROG_CACHE[key]


def make_in_maps(inputs, n_cores, nseq, seq):
    x = np.ascontiguousarray(inputs["x"], dtype=np.float32)
    pos = np.ascontiguousarray(inputs["positions"], dtype=np.int32)
    cst = make_consts()
    maps = []
    for c in range(n_cores):
        xs = x[c * nseq:(c + 1) * nseq, :seq].reshape(nseq * seq, D)
        ps = pos[c * nseq:(c + 1) * nseq, :seq].reshape(nseq * seq // 128, 128).T
        maps.append({
            "x": np.ascontiguousarray(xs),
            "pos": np.ascontiguousarray(ps),
            "cst": cst,
            "pre_norm_w": np.ascontiguousarray(inputs["pre_norm_w"], dtype=np.float32),
            "post_norm_w": np.ascontiguousarray(inputs["post_norm_w"], dtype=np.float32),
            "attn_w_in": np.ascontiguousarray(inputs["attn_w_in"][0], dtype=np.float32),
            "attn_b_in": np.ascontiguousarray(inputs["attn_b_in"], dtype=np.float32).reshape(1, 2304),
            "attn_sinks": np.ascontiguousarray(inputs["attn_sinks"], dtype=np.float32).reshape(1, 16),
            "attn_w_out": np.ascontiguousarray(inputs["attn_w_out"][0], dtype=np.float32),
            "attn_b_out": np.ascontiguousarray(inputs["attn_b_out"], dtype=np.float32).reshape(1, D),
            "rec_w_in": np.ascontiguousarray(inputs["rec_w_in"][0], dtype=np.float32),
            "rec_lb_logits": np.ascontiguousarray(inputs["rec_lb_logits"], dtype=np.float32),
            "rec_gnorm_w": np.ascontiguousarray(inputs["rec_gnorm_w"], dtype=np.float32).reshape(1, 128),
            "rec_w_out": np.ascontiguousarray(inputs["rec_w_out"][0], dtype=np.float32),
        })
    return maps


def kernel(**inputs):
    B, T, _ = inputs["x"].shape
    nseq = B // N_CORES
    nc = _get_prog(nseq, T)
    maps = make_in_maps(inputs, N_CORES, nseq, T)
    res = run_bass_kernel_spmd(nc, maps, core_ids=list(range(N_CORES)))
    outs = [np.asarray(r["out"], dtype=np.float32).reshape(nseq, T, D) for r in res.results]
    return np.concatenate(outs, axis=0)
```

```python
import math
from contextlib import ExitStack

import numpy as np
import concourse.bass as bass
import concourse.mybir as mybir
from concourse.bass_utils import run_bass_kernel_spmd

F32 = mybir.dt.float32
BF16 = mybir.dt.bfloat16
I32 = mybir.dt.int32
AF = mybir.ActivationFunctionType
ALU = mybir.AluOpType

N_CORES = 8
D = 1024
EPS = 1e-6
TWO_PI = 2.0 * math.pi
C1 = 6.28125
C2 = TWO_PI - C1
PI_LO = 3.1415925


class _Op:
    __slots__ = ("eng", "fn", "idx", "deps", "signal", "sem", "count", "dma", "epoch", "tag")


class Sched:
    def __init__(self, nc):
        self.nc = nc
        self.ops = []
        self.last_w = {}
        self.readers = {}
        self.epoch = 0
        self.tag = ""

    def new_epoch(self):
        self.epoch += 1

    def op(self, eng, fn, reads=(), writes=(), dma=None):
        o = _Op()
        o.eng, o.fn, o.idx, o.dma, o.epoch = eng, fn, len(self.ops), dma, self.epoch
        o.signal = False
        o.tag = self.tag
        deps = set()
        for k in reads:
            w = self.last_w.get(k)
            if w is not None:
                deps.add(w)
        for k in writes:
            w = self.last_w.get(k)
            if w is not None:
                deps.add(w)
            for r in self.readers.get(k, ()):
                deps.add(r)
        deps.discard(o.idx)
        o.deps = deps
        for k in writes:
            self.last_w[k] = o.idx
            self.readers[k] = []
        for k in reads:
            if k not in writes:
                self.readers.setdefault(k, []).append(o.idx)
        self.ops.append(o)
        return o

    def emit(self, stack):
        nc = self.nc
        ops = self.ops
        for o in ops:
            for d in o.deps:
                p = ops[d]
                if p.dma is None and p.eng == "pe" and o.eng == "pe" and o.dma is None:
                    continue
                p.signal = True
        counts = {}
        for o in ops:
            if o.dma is not None:
                key = ("dma", o.dma)
                counts[key] = counts.get(key, 0) + 16
                o.sem, o.count = key, counts[key]
            elif o.signal:
                key = (o.eng, o.epoch)
                counts[key] = counts.get(key, 0) + 1
                o.sem, o.count = key, counts[key]
        sems = {}
        for key in counts:
            sems[key] = stack.enter_context(nc.semaphore("s_%s_%s" % key))
        final = dict(counts)

        def stream(eng_name):
            def body(eng):
                seen = {}
                for o in ops:
                    if o.eng != eng_name:
                        continue
                    need = {}
                    for d in o.deps:
                        p = ops[d]
                        if p.dma is None and p.eng == "pe" and eng_name == "pe" and o.dma is None:
                            continue
                        if p.dma is not None:
                            skey, val = ("dma", p.dma), (0, p.count)
                        else:
                            skey, val = ("eng", p.eng), (p.epoch, p.count)
                        if val > need.get(skey, (-1, -1)):
                            need[skey] = val
                    for skey, val in need.items():
                        if val <= seen.get(skey, (-1, -1)):
                            continue
                        seen[skey] = val
                        if skey[0] == "dma":
                            eng.wait_ge(sems[("dma", skey[1])], val[1])
                        else:
                            eng.wait_ge(sems[(skey[1], val[0])], val[1])
                    ins = o.fn(eng)
                    if o.dma is not None:
                        ins.then_inc(sems[o.sem], 16)
                    elif o.signal:
                        ins.then_inc(sems[o.sem], 1)
                if eng_name == "sp":
                    for key, c in final.items():
                        if key[0] == "dma":
                            eng.wait_ge(sems[key], c)
            return body

        with nc.Block() as block:
            block.tensor(stream("pe"))
            block.scalar(stream("act"))
            block.vector(stream("dve"))
            block.gpsimd(stream("pool"))
            block.sync(stream("sp"))


CST_W = 128 * 4 + 512 + 8


def make_consts():
    c = np.zeros((128, CST_W), np.float32)
    i = np.arange(128)
    c[:, 0:128] = np.eye(128, dtype=np.float32)
    c[:, 128:256] = (i[:, None] <= i[None, :]).astype(np.float32)
    c[:, 256:384] = (i[:, None] > i[None, :]).astype(np.float32)
    c[:, 384:512] = ((i[:, None] <= i[None, :]) & ((i[:, None] // 64) == (i[None, :] // 64))).astype(np.float32)
    sm = np.zeros(512, np.float32)
    sm[::64] = 1.0
    c[:, 512:1024] = sm[None, :]
    invf = (np.float32(500000.0) ** (-(np.arange(8, dtype=np.float32) * np.float32(2.0) / np.float32(16.0)))).astype(np.float32)
    c[:, 1024:1032] = invf[None, :]
    return c


def build_program(NSEQ, SEQ):
    NT = NSEQ * SEQ
    NB = NT // 128
    NMT = NT // 512
    MT_PER_SEQ = SEQ // 512
    nc = bass.Bass("TRN2", target_bir_lowering=False)

    def din(name, shape, dt=F32):
        return nc.dram_tensor(name, list(shape), dt, kind="ExternalInput").ap()

    x_d = din("x", [NT, D])
    pos_d = din("pos", [128, NB], I32)
    cst_d = din("cst", [128, CST_W])
    prew_d = din("pre_norm_w", [2, D])
    postw_d = din("post_norm_w", [2, D])
    w0_d = din("attn_w_in", [D, 2304])
    b0_d = din("attn_b_in", [1, 2304])
    sink_d = din("attn_sinks", [1, 16])
    wo0_d = din("attn_w_out", [D, D])
    bo0_d = din("attn_b_out", [1, D])
    w1_d = din("rec_w_in", [D, 4096])
    lb_d = din("rec_lb_logits", [2, D])
    gnw_d = din("rec_gnorm_w", [1, 128])
    wo1_d = din("rec_w_out", [D, D])
    out_d = nc.dram_tensor("out", [NT, D], F32, kind="ExternalOutput").ap()
    w1s_d = nc.dram_tensor("w1s", [8, 128, 8, 512], BF16, kind="Internal").ap()

    with ExitStack() as st:
        def sb(name, shape, dt):
            return st.enter_context(nc.sbuf_tensor(name, list(shape), dt))

        def ps(name, shape, dt):
            return st.enter_context(nc.psum_tensor(name, list(shape), dt))

        W0 = sb("W0", [128, 8, 2304], BF16)
        Wo0 = sb("Wo0", [128, 8, 1024], BF16)
        Wo1 = sb("Wo1", [128, 8, 1024], BF16)
        W1h = [sb("W1h%d" % i, [128, 8, 512], BF16) for i in range(2)]
        xb = [sb("xb%d" % i, [128, 4, 1024], F32) for i in range(2)]
        FF = sb("FF", [128, 4096], F32)
        hT = sb("hT", [128, 8, 512], BF16)
        og_all = sb("og_all", [128, 4, 1024], BF16)
        ident = sb("ident", [128, 128], BF16)
        mask_cur = sb("mask_cur", [128, 128], BF16)
        mask_prev = sb("mask_prev", [128, 128], BF16)
        maskbd = sb("maskbd", [128, 128], BF16)
        startmask = sb("startmask", [128, 512], F32)
        invf = sb("invf", [128, 8], F32)
        cosT = sb("cosT", [128, NB, 8], F32)
        sinT = sb("sinT", [128, NB, 8], F32)
        wpost = sb("wpost", [128, 2, 1024], F32)
        browA = sb("browA", [65, 1024], BF16)
        posi = sb("posi", [128, NB], I32)
        browB = sb("browB", [1, 1024], BF16)
        ones = sb("ones", [65, 128], BF16)
        small = sb("small", [128, 144], F32)
        S32 = sb("S32", [128, 8, 128], F32)
        Sbf = sb("Sbf", [128, 8, 128], BF16)
        hb = sb("hb", [128, 1024], BF16)
        junk = sb("junk", [128, 1024], BF16)
        qkb = sb("qkb", [128, 18, 64], BF16)
        qkr = sb("qkr", [128, 18, 16], F32)
        rt = sb("rt", [128, 4, 18, 8], F32)
        qT = sb("qT", [64, 16, 128], BF16)
        kT = sb("kT", [64, 2, 2, 128], BF16)
        vaug = sb("vaug", [128, 2, 2, 65], BF16)
        PT = [sb("PT%d" % i, [128, 512], BF16) for i in range(4)]
        ogT = sb("ogT", [128, 8, 128], BF16)
        qeT = [sb("qeT%d" % i, [128, 512], BF16) for i in range(2)]
        keT = [sb("keT%d" % i, [128, 512], BF16) for i in range(2)]
        kdT = [sb("kdT%d" % i, [128, 512], BF16) for i in range(2)]
        kd_tm = sb("kd_tm", [128, 4, 128], BF16)
        vb = [sb("vb%d" % i, [128, 4, 128], BF16) for i in range(2)]
        smk = sb("smk", [128, 4, 128], BF16)

        PREW0, PREW1 = 0, 8
        SCQ0, SCH0, SCH1 = 16, 24, 32
        LBA, LBB, LBNB = 40, 48, 56
        GNW = 64
        ESINK = 65
        SS = 81
        RSTD = 85
        DEN = 89
        RDEN = 105
        NHALF = 121
        SS2, RSTD2 = 125, 127
        SS3, RSTD3 = 128, 132

        def sv(c, n=1):
            return small[:, c:c + n]

        pab = [ps("pab%d" % i, [128, 512], F32) for i in range(8)]
        GZ = [sb("gz0", [128, 512], F32)]
        glast = sb("glast", [128, 2, 8], F32)

        def ptv(i):
            return pab[i][:].bitcast(BF16)

        S = Sched(nc)
        op = S.op
        FK = ["FF0", "FF1", "FF2", "FF3"]

        def F(i):
            return FF[:, i * 1024:(i + 1) * 1024]

        def Fh(i):
            return FF[:, i * 512:(i + 1) * 512]

        def FhK(i):
            return "FH%d" % i

        ALLF = FK + [FhK(i) for i in range(8)]

        op("sp", lambda q: q.dma_start(out=FF[:, 0:CST_W], in_=cst_d), writes=ALLF, dma="cst")
        op("dve", lambda q: q.tensor_copy(out=ident[:], in_=FF[:, 0:128]), reads=ALLF, writes=["ident"])
        op("dve", lambda q: q.tensor_copy(out=mask_cur[:], in_=FF[:, 128:256]), reads=ALLF, writes=["mask_cur"])
        op("dve", lambda q: q.tensor_copy(out=mask_prev[:], in_=FF[:, 256:384]), reads=ALLF, writes=["mask_prev"])
        op("dve", lambda q: q.tensor_copy(out=maskbd[:], in_=FF[:, 384:512]), reads=ALLF, writes=["maskbd"])
        op("dve", lambda q: q.tensor_copy(out=startmask[:], in_=FF[:, 512:1024]), reads=ALLF, writes=["startmask"])
        op("dve", lambda q: q.tensor_copy(out=invf[:], in_=FF[:, 1024:1032]), reads=ALLF, writes=["invf"])
        op("pool", lambda q: q.memset(ones[:], 1.0), writes=["ones"])
        op("pool", lambda q: q.memset(small[:, NHALF:NHALF + 4], -0.5), writes=["nhalf"])
        op("pool", lambda q: q.memset(vaug[:], 1.0), writes=["vaug0", "vaug1"])
        op("sp", lambda q: q.dma_start(out=small[:, PREW0:PREW0 + 8], in_=prew_d[0:1, :].rearrange("o (k p) -> p (o k)", p=128),
                                       allow_slow_non_contiguous=True), writes=["prew", "smq"], dma="sm")
        op("sp", lambda q: q.dma_start(out=small[:, PREW1:PREW1 + 8], in_=prew_d[1:2, :].rearrange("o (k p) -> p (o k)", p=128),
                                       allow_slow_non_contiguous=True), writes=["prew", "smq"], dma="sm")
        op("sp", lambda q: q.dma_start(out=small[:, LBA:LBA + 8], in_=lb_d[0:1, :].rearrange("o (k p) -> p (o k)", p=128),
                                       allow_slow_non_contiguous=True), writes=["lb0", "smq"], dma="sm")
        op("sp", lambda q: q.dma_start(out=small[:, LBB:LBB + 8], in_=lb_d[1:2, :].rearrange("o (k p) -> p (o k)", p=128),
                                       allow_slow_non_contiguous=True), writes=["lb1", "smq"], dma="sm")
        op("sp", lambda q: q.dma_start(out=small[:, GNW:GNW + 1], in_=gnw_d.rearrange("o p -> p o"),
                                       allow_slow_non_contiguous=True), writes=["gnw", "smq"], dma="sm")
        op("sp", lambda q: q.dma_start(out=small[:, ESINK:ESINK + 16], in_=sink_d.partition_broadcast(128)), writes=["esink", "smq"], dma="sm")
        op("sp", lambda q: q.dma_start(out=wpost[:, 0, :], in_=postw_d[0:1, :].partition_broadcast(128)), writes=["wpost", "smq"], dma="sm")
        op("sp", lambda q: q.dma_start(out=wpost[:, 1, :], in_=postw_d[1:2, :].partition_broadcast(128)), writes=["wpost", "smq"], dma="sm")
        op("sp", lambda q: q.dma_start(out=posi[:], in_=pos_d), writes=["posi", "smq"], dma="sm")
        op("dve", lambda q: q.tensor_scalar(out=sv(SCQ0, 8), in0=sv(PREW0, 8), scalar1=0.125, scalar2=None, op0=ALU.mult), reads=["prew"], writes=["scq0"])
        op("dve", lambda q: q.tensor_scalar(out=sv(SCH0, 8), in0=sv(PREW0, 8), scalar1=0.5, scalar2=None, op0=ALU.mult), reads=["prew"], writes=["sch0"])
        op("dve", lambda q: q.tensor_scalar(out=sv(SCH1, 8), in0=sv(PREW1, 8), scalar1=0.5, scalar2=None, op0=ALU.mult), reads=["prew"], writes=["sch1"])
        op("act", lambda q: q.activation(out=sv(ESINK, 16), in_=sv(ESINK, 16), func=AF.Exp), reads=["esink"], writes=["esink"])
        op("dve", lambda q: q.tensor_tensor(out=sv(LBNB, 8), in0=sv(LBB, 8), in1=sv(LBA, 8), op=ALU.subtract), reads=["lb0", "lb1"], writes=["lbt"])
        op("act", lambda q: q.activation(out=sv(LBNB, 8), in_=sv(LBNB, 8), func=AF.Tanh, scale=0.5), reads=["lbt"], writes=["lbt"])
        op("dve", lambda q: q.tensor_scalar(out=sv(LBA, 8), in0=sv(LBNB, 8), scalar1=0.25, scalar2=0.75, op0=ALU.mult, op1=ALU.add), reads=["lbt"], writes=["lb0"])
        op("dve", lambda q: q.tensor_scalar(out=sv(LBB, 8), in0=sv(LBNB, 8), scalar1=-0.25, scalar2=0.25, op0=ALU.mult, op1=ALU.add), reads=["lbt"], writes=["lb1"])
        op("dve", lambda q: q.tensor_scalar(out=sv(LBNB, 8), in0=sv(LBB, 8), scalar1=-1.0, scalar2=None, op0=ALU.mult), reads=["lb1", "lbt"], writes=["lbt"])
        LBK = ["lb0", "lb1", "lbt"]

        o0 = 1040
        posf = FF[:, o0:o0 + NB]
        ang = FF[:, o0 + NB:o0 + NB + NB * 8]
        tmpu = FF[:, o0 + 9 * NB:o0 + 17 * NB]
        tmpk = FF[:, o0 + 17 * NB:o0 + 25 * NB]
        assert o0 + 25 * NB <= 4096
        tmpi = hb[:].bitcast(I32)[:, 0:NB * 8]
        op("dve", lambda q: q.tensor_copy(out=posf, in_=posi[:]), reads=["posi"] + ALLF, writes=ALLF)
        op("dve", lambda q: q.tensor_tensor(out=ang.rearrange("p (b i) -> p b i", i=8),
                                            in0=posf.unsqueeze(2).to_broadcast([128, NB, 8]),
                                            in1=invf[:].unsqueeze(1).to_broadcast([128, NB, 8]), op=ALU.mult),
           reads=ALLF + ["invf"], writes=ALLF)
        for which, tab in ((0, sinT), (1, cosT)):
            if which == 1:
                op("dve", lambda q: q.tensor_scalar(out=ang, in0=ang, scalar1=math.pi / 2, scalar2=None, op0=ALU.add), reads=ALLF, writes=ALLF)
            op("dve", lambda q: q.tensor_scalar(out=tmpu, in0=ang, scalar1=1.0 / TWO_PI, scalar2=None, op0=ALU.mult), reads=ALLF, writes=ALLF)
            op("dve", lambda q: q.tensor_copy(out=tmpi, in_=tmpu), reads=ALLF, writes=["hb"])
            op("dve", lambda q: q.tensor_copy(out=tmpk, in_=tmpi), reads=["hb"], writes=ALLF)
            op("dve", lambda q: q.scalar_tensor_tensor(out=tmpu, in0=tmpk, scalar=-C1, in1=ang, op0=ALU.mult, op1=ALU.add), reads=ALLF, writes=ALLF)
            op("dve", lambda q: q.scalar_tensor_tensor(out=tmpu, in0=tmpk, scalar=-C2, in1=tmpu, op0=ALU.mult, op1=ALU.add), reads=ALLF, writes=ALLF)
            op("dve", lambda q: q.tensor_scalar(out=tmpu, in0=tmpu, scalar1=-PI_LO, scalar2=PI_LO, op0=ALU.max, op1=ALU.min), reads=ALLF, writes=ALLF)
            op("act", lambda q, tab=tab: q.activation(out=tab[:].rearrange("p b i -> p (b i)"), in_=tmpu, func=AF.Sin),
               reads=ALLF, writes=["cosT" if which == 1 else "sinT"])

        BROWS = [(0, 1024, 0), (1024, 1792, 32), (1792, 2304, 64)]
        for (c0, c1, p) in BROWS:
            op("sp", lambda q, c0=c0, c1=c1, p=p: q.dma_start(out=FF[p:p + 1, 0:c1 - c0], in_=b0_d[0:1, c0:c1]),
               reads=["cosT", "sinT"], writes=ALLF + ["smq"], dma="sm")
        op("sp", lambda q: q.dma_start(out=FF[0:1, 1024:2048], in_=bo0_d), writes=ALLF + ["smq"], dma="sm")
        op("dve", lambda q: q.tensor_scalar(out=browA[0:1, 0:1024], in0=FF[0:1, 0:1024], scalar1=0.125, scalar2=None, op0=ALU.mult), reads=ALLF, writes=["browA"])
        op("dve", lambda q: q.tensor_copy(out=browA[32:33, 0:256], in_=FF[32:33, 0:256]), reads=ALLF, writes=["browA"])
        op("dve", lambda q: q.tensor_scalar(out=browA[32:33, 256:768], in0=FF[32:33, 256:768], scalar1=0.5, scalar2=None, op0=ALU.mult), reads=ALLF, writes=["browA"])
        op("dve", lambda q: q.tensor_scalar(out=browA[64:65, 0:512], in0=FF[64:65, 0:512], scalar1=0.5, scalar2=None, op0=ALU.mult), reads=ALLF, writes=["browA"])
        op("dve", lambda q: q.tensor_copy(out=browB[:], in_=FF[0:1, 1024:2048]), reads=ALLF, writes=["browB"])

        def brow(c0, n):
            for (r0, r1, p) in BROWS:
                if r0 <= c0 and c0 + n <= r1:
                    return ones[p:p + 1, :], browA[p:p + 1, c0 - r0:c0 - r0 + n]
            raise AssertionError((c0, n))

        stageA = FF
        stageB = xb[1][:].rearrange("p j d -> p (j d)")
        XB1K = [("x", 1, j) for j in range(4)]
        stg = [(stageA, ALLF), (stageB, XB1K)]
        stb = [(og_all[:].rearrange("p j d -> p (j d)"), [("og", j) for j in range(4)]), (hT[:].rearrange("p k t -> p (k t)"), [("hT", j) for j in range(4)])]
        cnt = [0]

        def conv(eng, out, in_, scal, rd, wr):
            if eng == "dve":
                op("dve", lambda q: q.tensor_scalar(out=out, in0=in_, scalar1=scal, scalar2=None, op0=ALU.mult), reads=rd, writes=wr)
            else:
                op("act", lambda q: q.activation(out=out, in_=in_, func=AF.Copy, scale=scal), reads=rd, writes=wr)

        for kc in range(8):
            sg, sk = stg[cnt[0] % 2]
            cnt[0] += 1
            op("sp", lambda q, sg=sg, kc=kc: q.dma_start(out=sg[:, 0:2304], in_=w0_d[kc * 128:(kc + 1) * 128, :]),
               reads=["browA", "browB"], writes=sk, dma="wl%d" % (cnt[0] % 2))
            conv("dve", W0[:, kc, 0:1024], sg[:, 0:1024], sv(SCQ0 + kc), sk + ["scq0"], ["W0"])
            conv("act", W0[:, kc, 1024:1280], sg[:, 1024:1280], sv(PREW0 + kc), sk + ["prew"], ["W0"])
            conv("act", W0[:, kc, 1280:2304], sg[:, 1280:2304], sv(SCH0 + kc), sk + ["sch0"], ["W0"])
        for kc in range(8):
            sg, sk = stg[cnt[0] % 2]
            cnt[0] += 1
            op("sp", lambda q, sg=sg, kc=kc: q.dma_start(out=sg[:, 0:1024], in_=wo0_d[kc * 128:(kc + 1) * 128, :]),
               writes=sk, dma="wl%d" % (cnt[0] % 2))
            op("sp", lambda q, sg=sg, kc=kc: q.dma_start(out=sg[:, 1024:2048], in_=wo1_d[kc * 128:(kc + 1) * 128, :]),
               writes=sk, dma="wl%d" % (cnt[0] % 2))
            op("act", lambda q, sg=sg, kc=kc: q.activation(out=Wo0[:, kc, :], in_=sg[:, 0:1024], func=AF.Copy), reads=sk, writes=["Wo0"])
            conv("dve", Wo1[:, kc, :], sg[:, 1024:2048], sv(GNW), sk + ["gnw"], ["Wo1"])
        for kc in range(8):
            sg, sk = stg[cnt[0] % 2]
            sbt, sbk = stb[cnt[0] % 2]
            cnt[0] += 1
            op("sp", lambda q, sg=sg, kc=kc: q.dma_start(out=sg[:, 0:4096], in_=w1_d[kc * 128:(kc + 1) * 128, :]),
               writes=sk, dma="wl%d" % (cnt[0] % 2))
            for t in range(4):
                o_ap = sbt.rearrange("p (h t c) -> p t h c", h=8, t=4, c=128)[:, t, :, :]
                i_ap = sg[:, t * 1024:(t + 1) * 1024].rearrange("p (h c) -> p h c", h=8)
                scal = sv(PREW1 + kc) if t == 2 else sv(SCH1 + kc)
                conv("dve" if t % 2 == 0 else "act", o_ap, i_ap, scal, sk + ["prew", "sch1"], sbk)
            op("sp", lambda q, sbt=sbt, kc=kc: q.dma_start(out=w1s_d[:, :, kc, :].rearrange("h p c -> p h c"),
                                                          in_=sbt.rearrange("p (h c) -> p h c", h=8)),
               reads=sbk, writes=["w1s"], dma="ws%d" % (cnt[0] % 2))

        def xk(buf, j):
            return ("x", buf, j)

        def load_x(m):
            buf = m % 2
            op("sp", lambda q: q.dma_start(out=xb[buf][:], in_=x_d[m * 512:(m + 1) * 512, :].rearrange("(j p) d -> p j d", p=128)),
               writes=[xk(buf, j) for j in range(4)], dma="xl%d" % buf)

        def load_w1h(h, slot):
            op("sp", lambda q: q.dma_start(out=W1h[slot][:], in_=w1s_d[h]), reads=["w1s"], writes=["W1h%d" % slot], dma="w1l%d" % slot)

        def rms_rstd(src_ap, src_keys, col):
            op("act", lambda q: q.activation(out=junk[:], in_=src_ap, func=AF.Square, accum_out=sv(SS + col)),
               reads=src_keys, writes=["junk", ("ss", col)])
            op("dve", lambda q: q.tensor_scalar(out=sv(SS + col), in0=sv(SS + col), scalar1=1.0 / 1024, scalar2=EPS, op0=ALU.mult, op1=ALU.add),
               reads=[("ss", col)], writes=[("ss", col)])
            op("pool", lambda q: q.tensor_tensor(out=sv(RSTD + col), in0=sv(SS + col), in1=sv(NHALF), op=ALU.pow),
               reads=[("ss", col), "nhalf"], writes=[("rstd", col)])

        TB = 7

        def make_hT(buf, j, col):
            xs = xb[buf][:, j, :]
            rms_rstd(xs, [xk(buf, j)], col)
            op("act", lambda q: q.activation(out=hb[:], in_=xs, func=AF.Copy, scale=sv(RSTD + col)),
               reads=[xk(buf, j), ("rstd", col)], writes=["hb"])
            for kc in range(8):
                op("pe", lambda q, kc=kc: q.transpose(out=ptv(TB)[:, kc * 128:(kc + 1) * 128], in_=hb[:, kc * 128:(kc + 1) * 128], identity=ident[:]),
                   reads=["hb", "ident"], writes=[("pa", TB)])
            op("dve", lambda q: q.tensor_copy(out=hT[:, :, j * 128:(j + 1) * 128], in_=ptv(TB).rearrange("p (k t) -> p k t", k=8)),
               reads=[("pa", TB)], writes=[("hT", j)])

        def post_norm_residual(buf, j, layer, pbanks):
            c0 = 4
            for hf in range(2):
                b = pbanks[hf]
                op("act", lambda q, b=b, hf=hf: q.activation(out=junk[:, 0:512], in_=pab[b][:], func=AF.Square, accum_out=sv(SS2 + hf)),
                   reads=[("pa", b)], writes=["junk", ("ss2", hf)])
            op("dve", lambda q: q.tensor_scalar(out=sv(SS2, 2), in0=sv(SS2, 2), scalar1=1.0 / 1024, scalar2=EPS / 2, op0=ALU.mult, op1=ALU.add),
               reads=[("ss2", 0), ("ss2", 1)], writes=[("ss2", 0), ("ss2", 1)])
            op("dve", lambda q: q.tensor_tensor(out=sv(SS2), in0=sv(SS2), in1=sv(SS2 + 1), op=ALU.add),
               reads=[("ss2", 0), ("ss2", 1)], writes=[("ss2", 0)])
            op("pool", lambda q: q.tensor_tensor(out=sv(RSTD2), in0=sv(SS2), in1=sv(NHALF), op=ALU.pow),
               reads=[("ss2", 0), "nhalf"], writes=["rstd2"])
            for hf in range(2):
                b = pbanks[hf]
                op("dve", lambda q, b=b, hf=hf: q.scalar_tensor_tensor(out=Fh(6 + hf), in0=pab[b][:], scalar=sv(RSTD2),
                                                                        in1=wpost[:, layer, hf * 512:(hf + 1) * 512], op0=ALU.mult, op1=ALU.mult),
                   reads=[("pa", b), "rstd2", "wpost"], writes=[FhK(6 + hf)])
            op("pool", lambda q: q.tensor_tensor(out=xb[buf][:, j, :], in0=xb[buf][:, j, :], in1=F(3), op=ALU.add),
               reads=[xk(buf, j), FhK(6), FhK(7)], writes=[xk(buf, j)])

        def out_proj(src_bf16_ap, src_keys, Wo, wo_key, bias, ybanks):
            for kc in range(8):
                op("pe", lambda q, kc=kc: q.transpose(out=ptv(TB)[:, kc * 128:(kc + 1) * 128], in_=src_bf16_ap[:, kc * 128:(kc + 1) * 128], identity=ident[:]),
                   reads=src_keys + ["ident"], writes=[("pa", TB)])
            op("act", lambda q: q.activation(out=ogT[:].rearrange("p k t -> p (k t)"), in_=ptv(TB), func=AF.Copy),
               reads=[("pa", TB)], writes=["ogT"])
            banks = list(ybanks)
            for hf in range(2):
                b = banks[hf]
                for kc in range(8):
                    op("pe", lambda q, b=b, kc=kc, hf=hf: q.matmul(pab[b][:], lhsT=ogT[:, kc, :], rhs=Wo[:, kc, hf * 512:(hf + 1) * 512],
                                                                   start=(kc == 0), stop=(kc == 7 and not bias)),
                       reads=["ogT", wo_key], writes=[("pa", b)])
                if bias:
                    op("pe", lambda q, b=b, hf=hf: q.matmul(pab[b][:], lhsT=ones[0:1, :], rhs=browB[0:1, hf * 512:(hf + 1) * 512], start=False, stop=True),
                       reads=["ones", "browB"], writes=[("pa", b)])
            return banks

        def proj_tm(b, j, c0, n, Wt, wkey, bias=True):
            for kc in range(8):
                op("pe", lambda q, kc=kc: q.matmul(pab[b][:, 0:n], lhsT=hT[:, kc, j * 128:(j + 1) * 128], rhs=Wt[:, kc, c0:c0 + n],
                                                   start=(kc == 0), stop=(kc == 7 and not bias)),
                   reads=[("hT", j), wkey], writes=[("pa", b)])
            if bias:
                o1, br = brow(c0, n)
                op("pe", lambda q: q.matmul(pab[b][:, 0:n], lhsT=o1, rhs=br, start=False, stop=True),
                   reads=["ones", "browA"], writes=[("pa", b)])
            return b

        def l0_s1(m, j):
            buf = m % 2
            blk = m * 4 + j
            slot = blk % 2
            S.tag = "m%d.L0.%d.s1" % (m, j)
            make_hT(buf, j, j)
            bq = [proj_tm(0, j, 0, 512, W0, "W0"), proj_tm(1, j, 512, 512, W0, "W0")]
            bkv = proj_tm(2, j, 1024, 256, W0, "W0")
            for a in range(2):
                op("act", lambda q, a=a: q.activation(out=qkb[:, 8 * a:8 * a + 8, :], in_=pab[bq[a]][:].rearrange("p (h d) -> p h d", h=8), func=AF.Copy),
                   reads=[("pa", bq[a])], writes=["qkb"])
                op("act", lambda q, a=a: q.activation(out=qkr[:, 8 * a:8 * a + 8, :], in_=pab[bq[a]][:].rearrange("p (h d) -> p h d", h=8)[:, :, 0:16], func=AF.Copy),
                   reads=[("pa", bq[a])], writes=["qkr"])
            op("act", lambda q: q.activation(out=qkb[:, 16:18, :], in_=pab[bkv][:, 0:128].rearrange("p (h d) -> p h d", h=2), func=AF.Copy),
               reads=[("pa", bkv)], writes=["qkb"])
            op("act", lambda q: q.activation(out=qkr[:, 16:18, :], in_=pab[bkv][:, 0:128].rearrange("p (h d) -> p h d", h=2)[:, :, 0:16], func=AF.Copy),
               reads=[("pa", bkv)], writes=["qkr"])
            op("act", lambda q: q.activation(out=vaug[:, slot, :, 0:64], in_=pab[bkv][:, 128:256].rearrange("p (g d) -> p g d", g=2), func=AF.Copy),
               reads=[("pa", bkv)], writes=["vaug%d" % slot])
            cb = cosT[:, blk, :].unsqueeze(1).to_broadcast([128, 18, 8])
            sbb = sinT[:, blk, :].unsqueeze(1).to_broadcast([128, 18, 8])
            x1 = qkr[:, :, 0:8]
            x2 = qkr[:, :, 8:16]
            op("pool", lambda q: q.tensor_tensor(out=rt[:, 0], in0=x1, in1=cb, op=ALU.mult), reads=["qkr", "cosT"], writes=["rt0"])
            op("pool", lambda q: q.tensor_tensor(out=rt[:, 1], in0=x2, in1=sbb, op=ALU.mult), reads=["qkr", "sinT"], writes=["rt1"])
            op("pool", lambda q: q.tensor_tensor(out=rt[:, 2], in0=x2, in1=cb, op=ALU.mult), reads=["qkr", "cosT"], writes=["rt2"])
            op("pool", lambda q: q.tensor_tensor(out=rt[:, 3], in0=x1, in1=sbb, op=ALU.mult), reads=["qkr", "sinT"], writes=["rt3"])
            op("dve", lambda q: q.tensor_tensor(out=qkb[:, :, 0:8], in0=rt[:, 0], in1=rt[:, 1], op=ALU.subtract), reads=["rt0", "rt1"], writes=["qkb"])
            op("dve", lambda q: q.tensor_tensor(out=qkb[:, :, 8:16], in0=rt[:, 2], in1=rt[:, 3], op=ALU.add), reads=["rt2", "rt3"], writes=["qkb"])
            tbk = (6, 7)
            for a in range(2):
                for hh in range(8):
                    op("pe", lambda q, a=a, hh=hh: q.transpose(out=ptv(tbk[a])[0:64, hh * 128:(hh + 1) * 128], in_=qkb[:, 8 * a + hh, :], identity=ident[:]),
                       reads=["qkb", "ident"], writes=[("pa", tbk[a])])
                op("dve", lambda q, a=a: q.tensor_copy(out=qT[:, 8 * a:8 * a + 8, :], in_=ptv(tbk[a])[0:64, :].rearrange("p (h t) -> p h t", h=8)),
                   reads=[("pa", tbk[a])], writes=["qT"])
            for g in range(2):
                op("pe", lambda q, g=g: q.transpose(out=ptv(6)[0:64, g * 128:(g + 1) * 128], in_=qkb[:, 16 + g, :], identity=ident[:]),
                   reads=["qkb", "ident"], writes=[("pa", 6)])
            op("dve", lambda q: q.tensor_copy(out=kT[:, slot, :, :], in_=ptv(6)[0:64, 0:256].rearrange("p (g t) -> p g t", g=2)),
               reads=[("pa", 6)], writes=["kT%d" % slot])

        def l0_s2(m, j):
            blk = m * 4 + j
            first = (blk % (SEQ // 128) == 0)
            slot = blk % 2
            S.tag = "m%d.L0.%d.s2" % (m, j)
            oa = [3, 4, 5]
            pti = 0
            sb_i = 0
            for g in range(2):
                for a in range(2):
                    kbs = ([] if first else [(1 - slot, mask_prev, "mask_prev")]) + [(slot, mask_cur, "mask_cur")]
                    pts = []
                    for (ks, msk, mkey) in kbs:
                        b = 6 + sb_i % 2
                        sb_i += 1
                        op("pe", lambda q, b=b, ks=ks, g=g, a=a: q.matmul(pab[b][:], lhsT=kT[:, ks, g, :],
                                                                          rhs=qT[:, 8 * g + 4 * a:8 * g + 4 * a + 4, :].rearrange("p h t -> p (h t)"),
                                                                          start=True, stop=True),
                           reads=["kT%d" % ks, "qT"], writes=[("pa", b)])
                        p = pti % 4
                        pti += 1
                        op("act", lambda q, b=b, p=p: q.activation(out=PT[p][:], in_=pab[b][:], func=AF.Exp), reads=[("pa", b)], writes=[("PT", p)])
                        op("dve", lambda q, p=p, msk=msk: q.tensor_tensor(out=PT[p][:].rearrange("p (h t) -> p h t", h=4),
                                                                         in0=PT[p][:].rearrange("p (h t) -> p h t", h=4),
                                                                         in1=msk[:].unsqueeze(1).to_broadcast([128, 4, 128]), op=ALU.mult),
                           reads=[("PT", p), mkey], writes=[("PT", p)])
                        pts.append((p, ks))
                    for hh in range(4):
                        h = 8 * g + 4 * a + hh
                        ob, oo = oa[h // 7], (h % 7) * 65
                        for i, (p, ks) in enumerate(pts):
                            op("pe", lambda q, p=p, ks=ks, hh=hh, ob=ob, oo=oo, i=i, n=len(pts), g=g: q.matmul(
                                pab[ob][:, oo:oo + 65], lhsT=PT[p][:, hh * 128:(hh + 1) * 128], rhs=vaug[:, ks, g, :],
                                start=(i == 0), stop=(i == n - 1)),
                               reads=[("PT", p), "vaug%d" % ks], writes=[("pa", ob)])
            for bi in range(3):
                h0 = 7 * bi
                nh = min(7, 16 - h0)
                ov = pab[oa[bi]][:, 0:nh * 65].rearrange("p (h d) -> p h d", d=65)
                op("dve", lambda q, ov=ov, h0=h0, nh=nh: q.tensor_tensor(out=sv(DEN + h0, nh), in0=ov[:, :, 64], in1=sv(ESINK + h0, nh), op=ALU.add),
                   reads=[("pa", oa[bi]), "esink"], writes=[("den", bi)])
            op("dve", lambda q: q.reciprocal(out=sv(RDEN, 16), in_=sv(DEN, 16)), reads=[("den", 0), ("den", 1), ("den", 2)], writes=["rden"])
            for bi in range(3):
                h0 = 7 * bi
                nh = min(7, 16 - h0)
                ov = pab[oa[bi]][:, 0:nh * 65].rearrange("p (h d) -> p h d", d=65)
                op("dve", lambda q, ov=ov, h0=h0, nh=nh: q.tensor_tensor(out=F(0).rearrange("p (h d) -> p h d", d=64)[:, h0:h0 + nh, :], in0=ov[:, :, 0:64],
                                                                         in1=sv(RDEN + h0, nh).unsqueeze(2).to_broadcast([128, nh, 64]), op=ALU.mult),
                   reads=[("pa", oa[bi]), "rden"], writes=[FhK(0), FhK(1)])

        def l0_s3(m, j):
            buf = m % 2
            S.tag = "m%d.L0.%d.s3" % (m, j)
            bz = [proj_tm(3, j, 1280, 512, W0, "W0"), proj_tm(4, j, 1792, 512, W0, "W0")]
            for hf in range(2):
                op("act", lambda q, hf=hf: q.activation(out=Fh(2 + hf), in_=pab[bz[hf]][:], func=AF.Tanh), reads=[("pa", bz[hf])], writes=[FhK(2 + hf)])
                op("dve", lambda q, hf=hf: q.scalar_tensor_tensor(out=Fh(2 + hf), in0=Fh(2 + hf), scalar=1.0, in1=pab[bz[hf]][:], op0=ALU.add, op1=ALU.mult),
                   reads=[FhK(2 + hf), ("pa", bz[hf])], writes=[FhK(2 + hf)])
            op("dve", lambda q: q.tensor_tensor(out=og_all[:, 0, :], in0=F(0), in1=F(1), op=ALU.mult),
               reads=[FhK(0), FhK(1), FhK(2), FhK(3)], writes=[("og", 0)])
            yb = out_proj(og_all[:, 0, :], [("og", 0)], Wo0, "Wo0", True, (5, 3))
            post_norm_residual(buf, j, 0, yb)

        def l1_A(m, h):
            ws = h % 2
            Wt = W1h[ws]
            wk = "W1h%d" % ws
            db = h % 2
            S.tag = "m%d.L1.h%d.A" % (m, h)
            bq, bf, bv, bzz = 0, 1, 2, 3
            hTk = [("hT", j) for j in range(4)]
            for (b, c0) in ((bq, 0), (bf, 128)):
                for kc in range(8):
                    op("pe", lambda q, b=b, c0=c0, kc=kc: q.matmul(pab[b][:], lhsT=Wt[:, kc, c0:c0 + 128], rhs=hT[:, kc, :], start=(kc == 0), stop=(kc == 7)),
                       reads=hTk + [wk], writes=[("pa", b)])
            for (b, c0) in ((bv, 256), (bzz, 384)):
                for j in range(4):
                    for kc in range(8):
                        op("pe", lambda q, b=b, c0=c0, kc=kc, j=j: q.matmul(pab[b][:, j * 128:(j + 1) * 128], lhsT=hT[:, kc, j * 128:(j + 1) * 128],
                                                                            rhs=Wt[:, kc, c0:c0 + 128], start=(kc == 0), stop=(kc == 7)),
                           reads=[("hT", j), wk], writes=[("pa", b)])
            if h + 2 < 8:
                load_w1h(h + 2, ws)
            A0, A1, A2, A3, A4, A5, A6 = [Fh(i) for i in range(7)]
            K0, K1, K2, K3, K4, K5, K6 = [FhK(i) for i in range(7)]
            gz = Fh(7) if db == 0 else GZ[0][:]
            gzk = FhK(7) if db == 0 else "gz0"
            op("act", lambda q: q.activation(out=A0, in_=pab[bq][:], func=AF.Tanh), reads=[("pa", bq)], writes=[K0])
            op("dve", lambda q: q.scalar_tensor_tensor(out=A0, in0=A0, scalar=1.0, in1=pab[bq][:], op0=ALU.add, op1=ALU.mult),
               reads=[K0, ("pa", bq)], writes=[K0])
            op("act", lambda q: q.activation(out=A1, in_=pab[bf][:], func=AF.Tanh), reads=[("pa", bf)], writes=[K1])
            op("act", lambda q: q.activation(out=A2, in_=A1, func=AF.Identity, scale=sv(LBB + h), bias=sv(LBA + h)), reads=[K1] + LBK, writes=[K2])
            op("act", lambda q: q.activation(out=A3, in_=A1, func=AF.Identity, scale=sv(LBNB + h), bias=sv(LBB + h)), reads=[K1] + LBK, writes=[K3])
            op("dve", lambda q: q.tensor_tensor_scan(out=A4, data0=startmask[:], data1=A2, initial=0.0, op0=ALU.max, op1=ALU.mult),
               reads=["startmask", K2], writes=[K4])
            op("dve", lambda q: q.tensor_tensor(out=qeT[db][:], in0=A0, in1=A4, op=ALU.mult), reads=[K0, K4], writes=[("qeT", db)])
            op("dve", lambda q: q.reciprocal(out=A5, in_=A4), reads=[K4], writes=[K5])
            op("dve", lambda q: q.tensor_tensor(out=A6, in0=A3, in1=A5, op=ALU.mult), reads=[K3, K5], writes=[K6])
            op("act", lambda q: q.activation(out=keT[db][:], in_=A6, func=AF.Copy), reads=[K6], writes=[("keT", db)])
            op("dve", lambda q: q.tensor_tensor(out=kdT[db][:].rearrange("p (c t) -> p c t", t=64), in0=A6.rearrange("p (c t) -> p c t", t=64),
                                                in1=A4.rearrange("p (c t) -> p c t", t=64)[:, :, 63:64].to_broadcast([128, 8, 64]), op=ALU.mult),
               reads=[K6, K4], writes=[("kdT", db)])
            op("act", lambda q: q.activation(out=glast[:, db, :], in_=A4.rearrange("p (c t) -> p c t", t=64)[:, :, 63], func=AF.Copy),
               reads=[K4], writes=[("glast", db)])
            op("act", lambda q: q.activation(out=vb[db][:].rearrange("p j v -> p (j v)"), in_=pab[bv][:], func=AF.Copy), reads=[("pa", bv)], writes=[("vb", db)])
            op("act", lambda q: q.activation(out=gz, in_=pab[bzz][:], func=AF.Tanh), reads=[("pa", bzz)], writes=[gzk])
            op("dve", lambda q: q.scalar_tensor_tensor(out=gz, in0=gz, scalar=1.0, in1=pab[bzz][:], op0=ALU.add, op1=ALU.mult),
               reads=[gzk, ("pa", bzz)], writes=[gzk])

        def l1_B(m, h):
            db = h % 2
            S.tag = "m%d.L1.h%d.B" % (m, h)
            gz = Fh(7) if db == 0 else GZ[0][:]
            gzk = FhK(7) if db == 0 else "gz0"
            ub = [4, 5]
            bso = 6
            for j in range(4):
                op("pe", lambda q, j=j: q.transpose(out=ptv(TB)[:, j * 128:(j + 1) * 128], in_=kdT[db][:, j * 128:(j + 1) * 128], identity=ident[:]),
                   reads=[("kdT", db), "ident"], writes=[("pa", TB)])
            op("dve", lambda q: q.tensor_copy(out=kd_tm[:].rearrange("p j k -> p (j k)"), in_=ptv(TB)[:, 0:512]), reads=[("pa", TB)], writes=["kd_tm"])
            for j in range(4):
                for c in range(2):
                    op("pe", lambda q, j=j, c=c: q.matmul(pab[ub[c]][:, j * 128:(j + 1) * 128], lhsT=kd_tm[64 * c:64 * c + 64, j, :],
                                                          rhs=vb[db][64 * c:64 * c + 64, j, :], start=True, stop=True),
                       reads=["kd_tm", ("vb", db)], writes=[("pa", ub[c])])
            for j in range(4):
                op("pe", lambda q, j=j: q.matmul(pab[bso][:, j * 128:(j + 1) * 128], lhsT=keT[db][:, j * 128:(j + 1) * 128], rhs=qeT[db][:, j * 128:(j + 1) * 128],
                                                 start=True, stop=True),
                   reads=[("keT", db), ("qeT", db)], writes=[("pa", bso)])
            op("dve", lambda q: q.tensor_tensor(out=smk[:], in0=pab[bso][:].rearrange("p (j t) -> p j t", j=4),
                                                in1=maskbd[:].unsqueeze(1).to_broadcast([128, 4, 128]), op=ALU.mult),
               reads=[("pa", bso), "maskbd"], writes=["smk"])
            for k in range(8):
                j, c = k // 2, k % 2
                op("act", lambda q, k=k: q.activation(out=Sbf[:, k, :], in_=S32[:, h, :], func=AF.Copy), reads=[("S32", h)], writes=[("Sbf", k)])
                op("dve", lambda q, k=k, j=j, c=c: q.scalar_tensor_tensor(out=S32[:, h, :], in0=S32[:, h, :], scalar=glast[:, db, k:k + 1],
                                                                             in1=pab[ub[c]][:, j * 128:(j + 1) * 128], op0=ALU.mult, op1=ALU.add),
                   reads=[("S32", h), ("glast", db), ("pa", ub[c])], writes=[("S32", h)])
            for j in range(4):
                op("pe", lambda q, j=j: q.matmul(pab[bso][:, j * 128:(j + 1) * 128], lhsT=smk[:, j, :], rhs=vb[db][:, j, :], start=True, stop=True),
                   reads=["smk", ("vb", db)], writes=[("pa", bso)])
                for c in range(2):
                    k = 2 * j + c
                    op("pe", lambda q, j=j, c=c, k=k: q.matmul(pab[bso][64 * c:64 * c + 64, j * 128:(j + 1) * 128],
                                                               lhsT=qeT[db][:, j * 128 + 64 * c:j * 128 + 64 * c + 64], rhs=Sbf[:, k, :],
                                                               start=False, stop=True, skip_group_check=True),
                       reads=[("qeT", db), ("Sbf", k)], writes=[("pa", bso)])
            for j in range(4):
                op("act", lambda q, j=j: q.activation(out=junk[:, 0:128], in_=pab[bso][:, j * 128:(j + 1) * 128], func=AF.Square, accum_out=sv(SS3 + j)),
                   reads=[("pa", bso)], writes=["junk", ("ss3", j)])
            op("dve", lambda q: q.tensor_scalar(out=sv(SS3, 4), in0=sv(SS3, 4), scalar1=1.0 / 128, scalar2=EPS, op0=ALU.mult, op1=ALU.add),
               reads=[("ss3", j) for j in range(4)], writes=[("ss3", j) for j in range(4)])
            op("pool", lambda q: q.tensor_tensor(out=sv(RSTD3, 4), in0=sv(SS3, 4), in1=sv(NHALF, 4), op=ALU.pow),
               reads=[("ss3", j) for j in range(4)] + ["nhalf"], writes=[("rstd3", j) for j in range(4)])
            for j in range(4):
                op("dve", lambda q, j=j: q.scalar_tensor_tensor(out=og_all[:, j, h * 128:(h + 1) * 128], in0=pab[bso][:, j * 128:(j + 1) * 128],
                                                                 scalar=sv(RSTD3 + j), in1=gz[:, j * 128:(j + 1) * 128], op0=ALU.mult, op1=ALU.mult),
                   reads=[("pa", bso), ("rstd3", j), gzk], writes=[("og", j)])

        load_x(0)
        for m in range(NMT):
            S.new_epoch()
            buf = m % 2
            if m + 1 < NMT:
                load_x(m + 1)
            load_w1h(0, 0)
            load_w1h(1, 1)
            l0_s1(m, 0)
            for j in range(4):
                l0_s2(m, j)
                if j + 1 < 4:
                    l0_s1(m, j + 1)
                l0_s3(m, j)
            S.tag = "m%d.L1.pre" % m
            for j in range(4):
                make_hT(buf, j, j)
            if m % MT_PER_SEQ == 0:
                op("pool", lambda q: q.memset(S32[:], 0.0), writes=[("S32", h) for h in range(8)])
            l1_A(m, 0)
            for h in range(8):
                if h + 1 < 8:
                    l1_A(m, h + 1)
                l1_B(m, h)
            S.tag = "m%d.L1.out" % m
            for j in range(4):
                yb = out_proj(og_all[:, j, :], [("og", j)], Wo1, "Wo1", False, ((0, 1), (2, 3))[j % 2])
                post_norm_residual(buf, j, 1, yb)
            op("sp", lambda q, m=m, buf=buf: q.dma_start(out=out_d[m * 512:(m + 1) * 512, :].rearrange("(j p) d -> p j d", p=128), in_=xb[buf][:]),
               reads=[xk(buf, j) for j in range(4)], dma="xs%d" % buf)
        S.emit(st)
    build_program.last_sched = S
    return nc


_PROG_CACHE = {}


def _get_prog(nseq, seq):
    key = (nseq, seq)
    if key not in _PROG_CACHE:
        _PROG_CACHE[key] = build_program(nseq, seq)
    return _PROG_CACHE[key]


def make_in_maps(inputs, n_cores, nseq, seq):
    x = np.ascontiguousarray(inputs["x"], dtype=np.float32)
    pos = np.ascontiguousarray(inputs["positions"], dtype=np.int32)
    cst = make_consts()
    maps = []
    for c in range(n_cores):
        xs = x[c * nseq:(c + 1) * nseq, :seq].reshape(nseq * seq, D)
        ps = pos[c * nseq:(c + 1) * nseq, :seq].reshape(nseq * seq // 128, 128).T
        maps.append({
            "x": np.ascontiguousarray(xs),
            "pos": np.ascontiguousarray(ps),
            "cst": cst,
            "pre_norm_w": np.ascontiguousarray(inputs["pre_norm_w"], dtype=np.float32),
            "post_norm_w": np.ascontiguousarray(inputs["post_norm_w"], dtype=np.float32),
            "attn_w_in": np.ascontiguousarray(inputs["attn_w_in"][0], dtype=np.float32),
            "attn_b_in": np.ascontiguousarray(inputs["attn_b_in"], dtype=np.float32).reshape(1, 2304),
            "attn_sinks": np.ascontiguousarray(inputs["attn_sinks"], dtype=np.float32).reshape(1, 16),
            "attn_w_out": np.ascontiguousarray(inputs["attn_w_out"][0], dtype=np.float32),
            "attn_b_out": np.ascontiguousarray(inputs["attn_b_out"], dtype=np.float32).reshape(1, D),
            "rec_w_in": np.ascontiguousarray(inputs["rec_w_in"][0], dtype=np.float32),
            "rec_lb_logits": np.ascontiguousarray(inputs["rec_lb_logits"], dtype=np.float32),
            "rec_gnorm_w": np.ascontiguousarray(inputs["rec_gnorm_w"], dtype=np.float32).reshape(1, 128),
            "rec_w_out": np.ascontiguousarray(inputs["rec_w_out"][0], dtype=np.float32),
        })
    return maps


def kernel(**inputs):
    B, T, _ = inputs["x"].shape
    nseq = B // N_CORES
    nc = _get_prog(nseq, T)
    maps = make_in_maps(inputs, N_CORES, nseq, T)
    res = run_bass_kernel_spmd(nc, maps, core_ids=list(range(N_CORES)))
    outs = [np.asarray(r["out"], dtype=np.float32).reshape(nseq, T, D) for r in res.results]
    return np.concatenate(outs, axis=0)
```

```python
import math
from contextlib import ExitStack

import numpy as np
import concourse.bass as bass
import concourse.mybir as mybir
from concourse.bass_utils import run_bass_kernel_spmd

F32 = mybir.dt.float32
BF16 = mybir.dt.bfloat16
I32 = mybir.dt.int32
AF = mybir.ActivationFunctionType
ALU = mybir.AluOpType

N_CORES = 8
D = 1024
EPS = 1e-6
TWO_PI = 2.0 * math.pi
C1 = 6.28125
C2 = TWO_PI - C1
PI_LO = 3.1415925


class _Op:
    __slots__ = ("eng", "fn", "idx", "deps", "signal", "sem", "count", "dma", "epoch", "tag", "dur", "start")


class _Rec:
    def __init__(self):
        self.call = None

    def __getattr__(self, name):
        def f(*a, **k):
            self.call = (name, a, k)
            return self
        return f


def _nelem(ap):
    n = 1
    for d in ap.shape[1:]:
        n *= int(d)
    return n


def _estimate_us(eng, fn, is_dma):
    r = _Rec()
    try:
        fn(r)
    except Exception:
        return 0.5
    if r.call is None:
        return 0.3
    name, a, k = r.call
    out = k.get("out", a[0] if a else None)
    try:
        if is_dma:
            nbytes = _nelem(out) * int(out.shape[0]) * 4
            return 2.0 + nbytes / 150e3
        if eng == "pe":
            if name == "transpose":
                return 0.10
            rhs = k.get("rhs")
            n = _nelem(rhs)
            return 0.07 + n * 0.00055
        n = _nelem(out) if out is not None else 64
        if eng == "act":
            return 0.26 + n * 0.00085 + (0.1 if k.get("accum_out") is not None else 0.0)
        if eng == "dve":
            f = 1.0
            if name == "reciprocal":
                f = 4.2
            elif name == "tensor_tensor_scan":
                f = 2.0
            elif name == "tensor_tensor":
                f = 1.6
            return 0.12 + n * 0.00104 * f
        if eng == "pool":
            if name == "tensor_tensor" and k.get("op") == ALU.pow:
                return 0.65
            return 0.3 + n * 0.0021
    except Exception:
        pass
    return 0.4


class Sched:
    EPOCH = 1500

    def __init__(self, nc):
        self.nc = nc
        self.ops = []
        self.last_w = {}
        self.readers = {}
        self.epoch = 0
        self.tag = ""

    def new_epoch(self):
        pass

    def op(self, eng, fn, reads=(), writes=(), dma=None, dur=None):
        o = _Op()
        o.eng, o.fn, o.idx, o.dma, o.epoch = eng, fn, len(self.ops), dma, 0
        o.signal = False
        o.tag = self.tag
        o.dur = dur if dur is not None else _estimate_us(eng, fn, dma is not None)
        deps = set()
        for k in reads:
            w = self.last_w.get(k)
            if w is not None:
                deps.add(w)
        for k in writes:
            w = self.last_w.get(k)
            if w is not None:
                deps.add(w)
            for r in self.readers.get(k, ()):
                deps.add(r)
        deps.discard(o.idx)
        o.deps = deps
        for k in writes:
            self.last_w[k] = o.idx
            self.readers[k] = []
        for k in reads:
            if k not in writes:
                self.readers.setdefault(k, []).append(o.idx)
        self.ops.append(o)
        return o

    def list_schedule(self, reorder=True):
        import heapq
        ops = self.ops
        n = len(ops)
        if not reorder:
            return {e: [o for o in ops if o.eng == e] for e in ("pe", "act", "dve", "pool", "sp")}, 0.0
        succ = [[] for _ in range(n)]
        indeg = [0] * n
        for o in ops:
            indeg[o.idx] = len(o.deps)
            for d in o.deps:
                succ[d].append(o.idx)
        ready_t = [0.0] * n
        free = {e: 0.0 for e in ("pe", "act", "dve", "pool", "sp")}
        heaps = {e: [] for e in free}
        for o in ops:
            if indeg[o.idx] == 0:
                heapq.heappush(heaps[o.eng], (0.0, o.idx))
        order = {e: [] for e in free}
        done = 0
        SEM_LAT = 0.15
        while done < n:
            best = None
            for e, h in heaps.items():
                if not h:
                    continue
                t0 = max(free[e], h[0][0])
                cand = None
                tmp = []
                while h and h[0][0] <= t0 and len(tmp) < 24:
                    tmp.append(heapq.heappop(h))
                pick = min(tmp, key=lambda x: x[1])
                for x in tmp:
                    if x is not pick:
                        heapq.heappush(h, x)
                heapq.heappush(h, pick)
                cand = (t0, pick[1], e, pick)
                if best is None or cand[:2] < best[:2]:
                    best = cand
            t0, idx, e, pick = best
            h = heaps[e]
            h.remove(pick)
            heapq.heapify(h)
            o = ops[idx]
            o.start = t0
            if o.dma is not None:
                free[e] = t0 + 0.06
                fin = t0 + o.dur
            else:
                free[e] = t0 + o.dur
                fin = free[e]
            order[e].append(o)
            done += 1
            for sidx in succ[idx]:
                so = ops[sidx]
                lat = 0.0 if (so.eng == e and e == "pe" and o.dma is None) else SEM_LAT
                ready_t[sidx] = max(ready_t[sidx], fin + lat)
                indeg[sidx] -= 1
                if indeg[sidx] == 0:
                    heapq.heappush(heaps[so.eng], (ready_t[sidx], sidx))
        return order, max(free.values())

    def emit(self, stack, reorder=True):
        nc = self.nc
        ops = self.ops
        order, makespan = self.list_schedule(reorder)
        self.makespan = makespan
        pos = {}
        for e, lst in order.items():
            for i, o in enumerate(lst):
                pos[o.idx] = i
        for o in ops:
            for d in o.deps:
                p = ops[d]
                if p.dma is None and p.eng == "pe" and o.eng == "pe" and o.dma is None:
                    assert pos[p.idx] < pos[o.idx]
                    continue
                p.signal = True
        counts = {}
        nsig = {}
        for e, lst in order.items():
            for o in lst:
                if o.dma is not None:
                    key = ("dma", o.dma)
                    counts[key] = counts.get(key, 0) + 16
                    o.sem, o.count = key, counts[key]
                elif o.signal:
                    k = nsig.get(e, 0)
                    nsig[e] = k + 1
                    o.epoch = k // self.EPOCH
                    key = (e, o.epoch)
                    counts[key] = counts.get(key, 0) + 1
                    o.sem, o.count = key, counts[key]
        sems = {}
        for key in counts:
            sems[key] = stack.enter_context(nc.semaphore("s_%s_%s" % key))
        self.n_sems = len(sems)
        final = dict(counts)

        def stream(eng_name):
            def body(eng):
                seen = {}
                for o in order[eng_name]:
                    need = {}
                    for d in o.deps:
                        p = ops[d]
                        if p.dma is None and p.eng == "pe" and eng_name == "pe" and o.dma is None:
                            continue
                        if p.dma is not None:
                            skey, val = ("dma", p.dma), (0, p.count)
                        else:
                            skey, val = ("eng", p.eng), (p.epoch, p.count)
                        if val > need.get(skey, (-1, -1)):
                            need[skey] = val
                    for skey, val in need.items():
                        if val <= seen.get(skey, (-1, -1)):
                            continue
                        seen[skey] = val
                        if skey[0] == "dma":
                            eng.wait_ge(sems[("dma", skey[1])], val[1])
                        else:
                            eng.wait_ge(sems[(skey[1], val[0])], val[1])
                    ins = o.fn(eng)
                    if o.dma is not None:
                        ins.then_inc(sems[o.sem], 16)
                    elif o.signal:
                        ins.then_inc(sems[o.sem], 1)
                if eng_name == "sp":
                    for key, c in final.items():
                        if key[0] == "dma":
                            eng.wait_ge(sems[key], c)
            return body

        with nc.Block() as block:
            block.tensor(stream("pe"))
            block.scalar(stream("act"))
            block.vector(stream("dve"))
            block.gpsimd(stream("pool"))
            block.sync(stream("sp"))


CST_W = 128 * 4 + 512 + 8


def make_consts():
    c = np.zeros((128, CST_W), np.float32)
    i = np.arange(128)
    c[:, 0:128] = np.eye(128, dtype=np.float32)
    c[:, 128:256] = (i[:, None] <= i[None, :]).astype(np.float32)
    c[:, 256:384] = (i[:, None] > i[None, :]).astype(np.float32)
    c[:, 384:512] = ((i[:, None] <= i[None, :]) & ((i[:, None] // 64) == (i[None, :] // 64))).astype(np.float32)
    sm = np.zeros(512, np.float32)
    sm[::64] = 1.0
    c[:, 512:1024] = sm[None, :]
    invf = (np.float32(500000.0) ** (-(np.arange(8, dtype=np.float32) * np.float32(2.0) / np.float32(16.0)))).astype(np.float32)
    c[:, 1024:1032] = invf[None, :]
    return c


def build_program(NSEQ, SEQ):
    NT = NSEQ * SEQ
    NB = NT // 128
    NMT = NT // 512
    MT_PER_SEQ = SEQ // 512
    nc = bass.Bass("TRN2", target_bir_lowering=False)

    def din(name, shape, dt=F32):
        return nc.dram_tensor(name, list(shape), dt, kind="ExternalInput").ap()

    x_d = din("x", [NT, D])
    pos_d = din("pos", [128, NB], I32)
    cst_d = din("cst", [128, CST_W])
    prew_d = din("pre_norm_w", [2, D])
    postw_d = din("post_norm_w", [2, D])
    w0_d = din("attn_w_in", [D, 2304])
    b0_d = din("attn_b_in", [1, 2304])
    sink_d = din("attn_sinks", [1, 16])
    wo0_d = din("attn_w_out", [D, D])
    bo0_d = din("attn_b_out", [1, D])
    w1_d = din("rec_w_in", [D, 4096])
    lb_d = din("rec_lb_logits", [2, D])
    gnw_d = din("rec_gnorm_w", [1, 128])
    wo1_d = din("rec_w_out", [D, D])
    out_d = nc.dram_tensor("out", [NT, D], F32, kind="ExternalOutput").ap()
    w1s_d = nc.dram_tensor("w1s", [8, 128, 8, 512], BF16, kind="Internal").ap()

    with ExitStack() as st:
        def sb(name, shape, dt):
            return st.enter_context(nc.sbuf_tensor(name, list(shape), dt))

        def ps(name, shape, dt):
            return st.enter_context(nc.psum_tensor(name, list(shape), dt))

        W0 = sb("W0", [128, 8, 2304], BF16)
        Wo0 = sb("Wo0", [128, 8, 1024], BF16)
        Wo1 = sb("Wo1", [128, 8, 1024], BF16)
        W1h = [sb("W1h%d" % i, [128, 8, 512], BF16) for i in range(2)]
        xb = [sb("xb%d" % i, [128, 4, 1024], F32) for i in range(2)]
        FF = sb("FF", [128, 4096], F32)
        hT = sb("hT", [128, 8, 512], BF16)
        og_all = sb("og_all", [128, 4, 1024], BF16)
        ident = sb("ident", [128, 128], BF16)
        mask_cur = sb("mask_cur", [128, 128], BF16)
        mask_prev = sb("mask_prev", [128, 128], BF16)
        maskbd = sb("maskbd", [128, 128], BF16)
        startmask = sb("startmask", [128, 512], F32)
        invf = sb("invf", [128, 8], F32)
        cosT = sb("cosT", [128, NB, 8], F32)
        sinT = sb("sinT", [128, NB, 8], F32)
        wpost = sb("wpost", [128, 2, 1024], F32)
        browA = sb("browA", [65, 1024], BF16)
        posi = sb("posi", [128, NB], I32)
        browB = sb("browB", [1, 1024], BF16)
        ones = sb("ones", [65, 128], BF16)
        small = sb("small", [128, 144], F32)
        S32 = sb("S32", [128, 8, 128], F32)
        Sbf = sb("Sbf", [128, 8, 128], BF16)
        hb = sb("hb", [128, 1024], BF16)
        junk = sb("junk", [128, 1024], BF16)
        qkb = sb("qkb", [128, 18, 64], BF16)
        qkr = sb("qkr", [128, 18, 16], F32)
        rt = sb("rt", [128, 4, 18, 8], F32)
        qT = sb("qT", [64, 16, 128], BF16)
        kT = sb("kT", [64, 2, 2, 128], BF16)
        vaug = sb("vaug", [128, 2, 2, 65], BF16)
        PT = [sb("PT%d" % i, [128, 512], BF16) for i in range(4)]
        ogT = sb("ogT", [128, 8, 128], BF16)
        qeT = [sb("qeT%d" % i, [128, 512], BF16) for i in range(2)]
        keT = [sb("keT%d" % i, [128, 512], BF16) for i in range(2)]
        kdT = [sb("kdT%d" % i, [128, 512], BF16) for i in range(2)]
        kd_tm = sb("kd_tm", [128, 4, 128], BF16)
        vb = [sb("vb%d" % i, [128, 4, 128], BF16) for i in range(2)]
        smk = sb("smk", [128, 4, 128], BF16)

        PREW0, PREW1 = 0, 8
        SCQ0, SCH0, SCH1 = 16, 24, 32
        LBA, LBB, LBNB = 40, 48, 56
        GNW = 64
        ESINK = 65
        SS = 81
        RSTD = 85
        DEN = 89
        RDEN = 105
        NHALF = 121
        SS2, RSTD2 = 125, 127
        SS3, RSTD3 = 128, 132

        def sv(c, n=1):
            return small[:, c:c + n]

        pab = [ps("pab%d" % i, [128, 512], F32) for i in range(8)]
        GZ = [sb("gz0", [128, 512], F32)]
        glast = sb("glast", [128, 2, 8], F32)

        def ptv(i):
            return pab[i][:].bitcast(BF16)

        S = Sched(nc)
        op = S.op
        FK = ["FF0", "FF1", "FF2", "FF3"]

        def F(i):
            return FF[:, i * 1024:(i + 1) * 1024]

        def Fh(i):
            return FF[:, i * 512:(i + 1) * 512]

        def FhK(i):
            return "FH%d" % i

        ALLF = FK + [FhK(i) for i in range(8)]

        op("sp", lambda q: q.dma_start(out=FF[:, 0:CST_W], in_=cst_d), writes=ALLF, dma="cst")
        op("dve", lambda q: q.tensor_copy(out=ident[:], in_=FF[:, 0:128]), reads=ALLF, writes=["ident"])
        op("dve", lambda q: q.tensor_copy(out=mask_cur[:], in_=FF[:, 128:256]), reads=ALLF, writes=["mask_cur"])
        op("dve", lambda q: q.tensor_copy(out=mask_prev[:], in_=FF[:, 256:384]), reads=ALLF, writes=["mask_prev"])
        op("dve", lambda q: q.tensor_copy(out=maskbd[:], in_=FF[:, 384:512]), reads=ALLF, writes=["maskbd"])
        op("dve", lambda q: q.tensor_copy(out=startmask[:], in_=FF[:, 512:1024]), reads=ALLF, writes=["startmask"])
        op("dve", lambda q: q.tensor_copy(out=invf[:], in_=FF[:, 1024:1032]), reads=ALLF, writes=["invf"])
        op("pool", lambda q: q.memset(ones[:], 1.0), writes=["ones"])
        op("pool", lambda q: q.memset(small[:, NHALF:NHALF + 4], -0.5), writes=["nhalf"])
        op("pool", lambda q: q.memset(vaug[:], 1.0), writes=["vaug0", "vaug1"])
        op("sp", lambda q: q.dma_start(out=small[:, PREW0:PREW0 + 8], in_=prew_d[0:1, :].rearrange("o (k p) -> p (o k)", p=128),
                                       allow_slow_non_contiguous=True), writes=["prew", "smq"], dma="sm")
        op("sp", lambda q: q.dma_start(out=small[:, PREW1:PREW1 + 8], in_=prew_d[1:2, :].rearrange("o (k p) -> p (o k)", p=128),
                                       allow_slow_non_contiguous=True), writes=["prew", "smq"], dma="sm")
        op("sp", lambda q: q.dma_start(out=small[:, LBA:LBA + 8], in_=lb_d[0:1, :].rearrange("o (k p) -> p (o k)", p=128),
                                       allow_slow_non_contiguous=True), writes=["lb0", "smq"], dma="sm")
        op("sp", lambda q: q.dma_start(out=small[:, LBB:LBB + 8], in_=lb_d[1:2, :].rearrange("o (k p) -> p (o k)", p=128),
                                       allow_slow_non_contiguous=True), writes=["lb1", "smq"], dma="sm")
        op("sp", lambda q: q.dma_start(out=small[:, GNW:GNW + 1], in_=gnw_d.rearrange("o p -> p o"),
                                       allow_slow_non_contiguous=True), writes=["gnw", "smq"], dma="sm")
        op("sp", lambda q: q.dma_start(out=small[:, ESINK:ESINK + 16], in_=sink_d.partition_broadcast(128)), writes=["esink", "smq"], dma="sm")
        op("sp", lambda q: q.dma_start(out=wpost[:, 0, :], in_=postw_d[0:1, :].partition_broadcast(128)), writes=["wpost", "smq"], dma="sm")
        op("sp", lambda q: q.dma_start(out=wpost[:, 1, :], in_=postw_d[1:2, :].partition_broadcast(128)), writes=["wpost", "smq"], dma="sm")
        op("sp", lambda q: q.dma_start(out=posi[:], in_=pos_d), writes=["posi", "smq"], dma="sm")
        op("dve", lambda q: q.tensor_scalar(out=sv(SCQ0, 8), in0=sv(PREW0, 8), scalar1=0.125, scalar2=None, op0=ALU.mult), reads=["prew"], writes=["scq0"])
        op("dve", lambda q: q.tensor_scalar(out=sv(SCH0, 8), in0=sv(PREW0, 8), scalar1=0.5, scalar2=None, op0=ALU.mult), reads=["prew"], writes=["sch0"])
        op("dve", lambda q: q.tensor_scalar(out=sv(SCH1, 8), in0=sv(PREW1, 8), scalar1=0.5, scalar2=None, op0=ALU.mult), reads=["prew"], writes=["sch1"])
        op("act", lambda q: q.activation(out=sv(ESINK, 16), in_=sv(ESINK, 16), func=AF.Exp), reads=["esink"], writes=["esink"])
        op("dve", lambda q: q.tensor_tensor(out=sv(LBNB, 8), in0=sv(LBB, 8), in1=sv(LBA, 8), op=ALU.subtract), reads=["lb0", "lb1"], writes=["lbt"])
        op("act", lambda q: q.activation(out=sv(LBNB, 8), in_=sv(LBNB, 8), func=AF.Tanh, scale=0.5), reads=["lbt"], writes=["lbt"])
        op("dve", lambda q: q.tensor_scalar(out=sv(LBA, 8), in0=sv(LBNB, 8), scalar1=0.25, scalar2=0.75, op0=ALU.mult, op1=ALU.add), reads=["lbt"], writes=["lb0"])
        op("dve", lambda q: q.tensor_scalar(out=sv(LBB, 8), in0=sv(LBNB, 8), scalar1=-0.25, scalar2=0.25, op0=ALU.mult, op1=ALU.add), reads=["lbt"], writes=["lb1"])
        op("dve", lambda q: q.tensor_scalar(out=sv(LBNB, 8), in0=sv(LBB, 8), scalar1=-1.0, scalar2=None, op0=ALU.mult), reads=["lb1", "lbt"], writes=["lbt"])
        LBK = ["lb0", "lb1", "lbt"]

        o0 = 1040
        posf = FF[:, o0:o0 + NB]
        ang = FF[:, o0 + NB:o0 + NB + NB * 8]
        tmpu = FF[:, o0 + 9 * NB:o0 + 17 * NB]
        tmpk = FF[:, o0 + 17 * NB:o0 + 25 * NB]
        assert o0 + 25 * NB <= 4096
        tmpi = hb[:].bitcast(I32)[:, 0:NB * 8]
        op("dve", lambda q: q.tensor_copy(out=posf, in_=posi[:]), reads=["posi"] + ALLF, writes=ALLF)
        op("dve", lambda q: q.tensor_tensor(out=ang.rearrange("p (b i) -> p b i", i=8),
                                            in0=posf.unsqueeze(2).to_broadcast([128, NB, 8]),
                                            in1=invf[:].unsqueeze(1).to_broadcast([128, NB, 8]), op=ALU.mult),
           reads=ALLF + ["invf"], writes=ALLF)
        for which, tab in ((0, sinT), (1, cosT)):
            if which == 1:
                op("dve", lambda q: q.tensor_scalar(out=ang, in0=ang, scalar1=math.pi / 2, scalar2=None, op0=ALU.add), reads=ALLF, writes=ALLF)
            op("dve", lambda q: q.tensor_scalar(out=tmpu, in0=ang, scalar1=1.0 / TWO_PI, scalar2=None, op0=ALU.mult), reads=ALLF, writes=ALLF)
            op("dve", lambda q: q.tensor_copy(out=tmpi, in_=tmpu), reads=ALLF, writes=["hb"])
            op("dve", lambda q: q.tensor_copy(out=tmpk, in_=tmpi), reads=["hb"], writes=ALLF)
            op("dve", lambda q: q.scalar_tensor_tensor(out=tmpu, in0=tmpk, scalar=-C1, in1=ang, op0=ALU.mult, op1=ALU.add), reads=ALLF, writes=ALLF)
            op("dve", lambda q: q.scalar_tensor_tensor(out=tmpu, in0=tmpk, scalar=-C2, in1=tmpu, op0=ALU.mult, op1=ALU.add), reads=ALLF, writes=ALLF)
            op("dve", lambda q: q.tensor_scalar(out=tmpu, in0=tmpu, scalar1=-PI_LO, scalar2=PI_LO, op0=ALU.max, op1=ALU.min), reads=ALLF, writes=ALLF)
            op("act", lambda q, tab=tab: q.activation(out=tab[:].rearrange("p b i -> p (b i)"), in_=tmpu, func=AF.Sin),
               reads=ALLF, writes=["cosT" if which == 1 else "sinT"])

        BROWS = [(0, 1024, 0), (1024, 1792, 32), (1792, 2304, 64)]
        for (c0, c1, p) in BROWS:
            op("sp", lambda q, c0=c0, c1=c1, p=p: q.dma_start(out=FF[p:p + 1, 0:c1 - c0], in_=b0_d[0:1, c0:c1]),
               reads=["cosT", "sinT"], writes=ALLF + ["smq"], dma="sm")
        op("sp", lambda q: q.dma_start(out=FF[0:1, 1024:2048], in_=bo0_d), writes=ALLF + ["smq"], dma="sm")
        op("dve", lambda q: q.tensor_scalar(out=browA[0:1, 0:1024], in0=FF[0:1, 0:1024], scalar1=0.125, scalar2=None, op0=ALU.mult), reads=ALLF, writes=["browA"])
        op("dve", lambda q: q.tensor_copy(out=browA[32:33, 0:256], in_=FF[32:33, 0:256]), reads=ALLF, writes=["browA"])
        op("dve", lambda q: q.tensor_scalar(out=browA[32:33, 256:768], in0=FF[32:33, 256:768], scalar1=0.5, scalar2=None, op0=ALU.mult), reads=ALLF, writes=["browA"])
        op("dve", lambda q: q.tensor_scalar(out=browA[64:65, 0:512], in0=FF[64:65, 0:512], scalar1=0.5, scalar2=None, op0=ALU.mult), reads=ALLF, writes=["browA"])
        op("dve", lambda q: q.tensor_copy(out=browB[:], in_=FF[0:1, 1024:2048]), reads=ALLF, writes=["browB"])

        def brow(c0, n):
            for (r0, r1, p) in BROWS:
                if r0 <= c0 and c0 + n <= r1:
                    return ones[p:p + 1, :], browA[p:p + 1, c0 - r0:c0 - r0 + n]
            raise AssertionError((c0, n))

        stageA = FF
        stageB = xb[1][:].rearrange("p j d -> p (j d)")
        XB1K = [("x", 1, j) for j in range(4)]
        stg = [(stageA, ALLF), (stageB, XB1K)]
        stb = [(og_all[:].rearrange("p j d -> p (j d)"), [("og", j) for j in range(4)]), (hT[:].rearrange("p k t -> p (k t)"), [("hT", j) for j in range(4)])]
        cnt = [0]

        def conv(eng, out, in_, scal, rd, wr):
            if eng == "dve":
                op("dve", lambda q: q.tensor_scalar(out=out, in0=in_, scalar1=scal, scalar2=None, op0=ALU.mult), reads=rd, writes=wr)
            else:
                op("act", lambda q: q.activation(out=out, in_=in_, func=AF.Copy, scale=scal), reads=rd, writes=wr)

        for kc in range(8):
            sg, sk = stg[cnt[0] % 2]
            cnt[0] += 1
            op("sp", lambda q, sg=sg, kc=kc: q.dma_start(out=sg[:, 0:2304], in_=w0_d[kc * 128:(kc + 1) * 128, :]),
               reads=["browA", "browB"], writes=sk, dma="wl%d" % (cnt[0] % 2))
            conv("dve", W0[:, kc, 0:1024], sg[:, 0:1024], sv(SCQ0 + kc), sk + ["scq0"], ["W0"])
            conv("act", W0[:, kc, 1024:1280], sg[:, 1024:1280], sv(PREW0 + kc), sk + ["prew"], ["W0"])
            conv("act", W0[:, kc, 1280:2304], sg[:, 1280:2304], sv(SCH0 + kc), sk + ["sch0"], ["W0"])
        for kc in range(8):
            sg, sk = stg[cnt[0] % 2]
            cnt[0] += 1
            op("sp", lambda q, sg=sg, kc=kc: q.dma_start(out=sg[:, 0:1024], in_=wo0_d[kc * 128:(kc + 1) * 128, :]),
               writes=sk, dma="wl%d" % (cnt[0] % 2))
            op("sp", lambda q, sg=sg, kc=kc: q.dma_start(out=sg[:, 1024:2048], in_=wo1_d[kc * 128:(kc + 1) * 128, :]),
               writes=sk, dma="wl%d" % (cnt[0] % 2))
            op("act", lambda q, sg=sg, kc=kc: q.activation(out=Wo0[:, kc, :], in_=sg[:, 0:1024], func=AF.Copy), reads=sk, writes=["Wo0"])
            conv("dve", Wo1[:, kc, :], sg[:, 1024:2048], sv(GNW), sk + ["gnw"], ["Wo1"])
        for kc in range(8):
            sg, sk = stg[cnt[0] % 2]
            sbt, sbk = stb[cnt[0] % 2]
            cnt[0] += 1
            op("sp", lambda q, sg=sg, kc=kc: q.dma_start(out=sg[:, 0:4096], in_=w1_d[kc * 128:(kc + 1) * 128, :]),
               writes=sk, dma="wl%d" % (cnt[0] % 2))
            for t in range(4):
                o_ap = sbt.rearrange("p (h t c) -> p t h c", h=8, t=4, c=128)[:, t, :, :]
                i_ap = sg[:, t * 1024:(t + 1) * 1024].rearrange("p (h c) -> p h c", h=8)
                scal = sv(PREW1 + kc) if t == 2 else sv(SCH1 + kc)
                conv("dve" if t % 2 == 0 else "act", o_ap, i_ap, scal, sk + ["prew", "sch1"], sbk)
            op("sp", lambda q, sbt=sbt, kc=kc: q.dma_start(out=w1s_d[:, :, kc, :].rearrange("h p c -> p h c"),
                                                          in_=sbt.rearrange("p (h c) -> p h c", h=8)),
               reads=sbk, writes=["w1s"], dma="ws%d" % (cnt[0] % 2))

        def xk(buf, j):
            return ("x", buf, j)

        def load_x(m):
            buf = m % 2
            op("sp", lambda q: q.dma_start(out=xb[buf][:], in_=x_d[m * 512:(m + 1) * 512, :].rearrange("(j p) d -> p j d", p=128)),
               writes=[xk(buf, j) for j in range(4)], dma="xl%d" % buf)

        def load_w1h(h, slot):
            op("sp", lambda q: q.dma_start(out=W1h[slot][:], in_=w1s_d[h]), reads=["w1s"], writes=["W1h%d" % slot], dma="w1l%d" % slot)

        def rms_rstd(src_ap, src_keys, col):
            op("act", lambda q: q.activation(out=junk[:], in_=src_ap, func=AF.Square, accum_out=sv(SS + col)),
               reads=src_keys, writes=["junk", ("ss", col)])
            op("dve", lambda q: q.tensor_scalar(out=sv(SS + col), in0=sv(SS + col), scalar1=1.0 / 1024, scalar2=EPS, op0=ALU.mult, op1=ALU.add),
               reads=[("ss", col)], writes=[("ss", col)])
            op("pool", lambda q: q.tensor_tensor(out=sv(RSTD + col), in0=sv(SS + col), in1=sv(NHALF), op=ALU.pow),
               reads=[("ss", col), "nhalf"], writes=[("rstd", col)])

        TB = 7

        def make_hT(buf, j, col):
            xs = xb[buf][:, j, :]
            rms_rstd(xs, [xk(buf, j)], col)
            op("act", lambda q: q.activation(out=hb[:], in_=xs, func=AF.Copy, scale=sv(RSTD + col)),
               reads=[xk(buf, j), ("rstd", col)], writes=["hb"])
            for kc in range(8):
                op("pe", lambda q, kc=kc: q.transpose(out=ptv(TB)[:, kc * 128:(kc + 1) * 128], in_=hb[:, kc * 128:(kc + 1) * 128], identity=ident[:]),
                   reads=["hb", "ident"], writes=[("pa", TB)])
            op("dve", lambda q: q.tensor_copy(out=hT[:, :, j * 128:(j + 1) * 128], in_=ptv(TB).rearrange("p (k t) -> p k t", k=8)),
               reads=[("pa", TB)], writes=[("hT", j)])

        def post_norm_residual(buf, j, layer, pbanks):
            c0 = 4
            for hf in range(2):
                b = pbanks[hf]
                op("act", lambda q, b=b, hf=hf: q.activation(out=junk[:, 0:512], in_=pab[b][:], func=AF.Square, accum_out=sv(SS2 + hf)),
                   reads=[("pa", b)], writes=["junk", ("ss2", hf)])
            op("dve", lambda q: q.tensor_scalar(out=sv(SS2, 2), in0=sv(SS2, 2), scalar1=1.0 / 1024, scalar2=EPS / 2, op0=ALU.mult, op1=ALU.add),
               reads=[("ss2", 0), ("ss2", 1)], writes=[("ss2", 0), ("ss2", 1)])
            op("dve", lambda q: q.tensor_tensor(out=sv(SS2), in0=sv(SS2), in1=sv(SS2 + 1), op=ALU.add),
               reads=[("ss2", 0), ("ss2", 1)], writes=[("ss2", 0)])
            op("pool", lambda q: q.tensor_tensor(out=sv(RSTD2), in0=sv(SS2), in1=sv(NHALF), op=ALU.pow),
               reads=[("ss2", 0), "nhalf"], writes=["rstd2"])
            for hf in range(2):
                b = pbanks[hf]
                op("dve", lambda q, b=b, hf=hf: q.scalar_tensor_tensor(out=Fh(6 + hf), in0=pab[b][:], scalar=sv(RSTD2),
                                                                        in1=wpost[:, layer, hf * 512:(hf + 1) * 512], op0=ALU.mult, op1=ALU.mult),
                   reads=[("pa", b), "rstd2", "wpost"], writes=[FhK(6 + hf)])
            op("pool", lambda q: q.tensor_tensor(out=xb[buf][:, j, :], in0=xb[buf][:, j, :], in1=F(3), op=ALU.add),
               reads=[xk(buf, j), FhK(6), FhK(7)], writes=[xk(buf, j)])

        def out_proj(src_bf16_ap, src_keys, Wo, wo_key, bias, ybanks):
            for kc in range(8):
                op("pe", lambda q, kc=kc: q.transpose(out=ptv(TB)[:, kc * 128:(kc + 1) * 128], in_=src_bf16_ap[:, kc * 128:(kc + 1) * 128], identity=ident[:]),
                   reads=src_keys + ["ident"], writes=[("pa", TB)])
            op("act", lambda q: q.activation(out=ogT[:].rearrange("p k t -> p (k t)"), in_=ptv(TB), func=AF.Copy),
               reads=[("pa", TB)], writes=["ogT"])
            banks = list(ybanks)
            for hf in range(2):
                b = banks[hf]
                for kc in range(8):
                    op("pe", lambda q, b=b, kc=kc, hf=hf: q.matmul(pab[b][:], lhsT=ogT[:, kc, :], rhs=Wo[:, kc, hf * 512:(hf + 1) * 512],
                                                                   start=(kc == 0), stop=(kc == 7 and not bias)),
                       reads=["ogT", wo_key], writes=[("pa", b)])
                if bias:
                    op("pe", lambda q, b=b, hf=hf: q.matmul(pab[b][:], lhsT=ones[0:1, :], rhs=browB[0:1, hf * 512:(hf + 1) * 512], start=False, stop=True),
                       reads=["ones", "browB"], writes=[("pa", b)])
            return banks

        def proj_tm(b, j, c0, n, Wt, wkey, bias=True):
            for kc in range(8):
                op("pe", lambda q, kc=kc: q.matmul(pab[b][:, 0:n], lhsT=hT[:, kc, j * 128:(j + 1) * 128], rhs=Wt[:, kc, c0:c0 + n],
                                                   start=(kc == 0), stop=(kc == 7 and not bias)),
                   reads=[("hT", j), wkey], writes=[("pa", b)])
            if bias:
                o1, br = brow(c0, n)
                op("pe", lambda q: q.matmul(pab[b][:, 0:n], lhsT=o1, rhs=br, start=False, stop=True),
                   reads=["ones", "browA"], writes=[("pa", b)])
            return b

        def l0_s1(m, j):
            buf = m % 2
            blk = m * 4 + j
            slot = blk % 2
            S.tag = "m%d.L0.%d.s1" % (m, j)
            make_hT(buf, j, j)
            bq = [proj_tm(0, j, 0, 512, W0, "W0"), proj_tm(1, j, 512, 512, W0, "W0")]
            bkv = proj_tm(2, j, 1024, 256, W0, "W0")
            for a in range(2):
                op("act", lambda q, a=a: q.activation(out=qkb[:, 8 * a:8 * a + 8, :], in_=pab[bq[a]][:].rearrange("p (h d) -> p h d", h=8), func=AF.Copy),
                   reads=[("pa", bq[a])], writes=["qkb"])
                op("act", lambda q, a=a: q.activation(out=qkr[:, 8 * a:8 * a + 8, :], in_=pab[bq[a]][:].rearrange("p (h d) -> p h d", h=8)[:, :, 0:16], func=AF.Copy),
                   reads=[("pa", bq[a])], writes=["qkr"])
            op("act", lambda q: q.activation(out=qkb[:, 16:18, :], in_=pab[bkv][:, 0:128].rearrange("p (h d) -> p h d", h=2), func=AF.Copy),
               reads=[("pa", bkv)], writes=["qkb"])
            op("act", lambda q: q.activation(out=qkr[:, 16:18, :], in_=pab[bkv][:, 0:128].rearrange("p (h d) -> p h d", h=2)[:, :, 0:16], func=AF.Copy),
               reads=[("pa", bkv)], writes=["qkr"])
            op("act", lambda q: q.activation(out=vaug[:, slot, :, 0:64], in_=pab[bkv][:, 128:256].rearrange("p (g d) -> p g d", g=2), func=AF.Copy),
               reads=[("pa", bkv)], writes=["vaug%d" % slot])
            cb = cosT[:, blk, :].unsqueeze(1).to_broadcast([128, 18, 8])
            sbb = sinT[:, blk, :].unsqueeze(1).to_broadcast([128, 18, 8])
            x1 = qkr[:, :, 0:8]
            x2 = qkr[:, :, 8:16]
            op("pool", lambda q: q.tensor_tensor(out=rt[:, 0], in0=x1, in1=cb, op=ALU.mult), reads=["qkr", "cosT"], writes=["rt0"])
            op("pool", lambda q: q.tensor_tensor(out=rt[:, 1], in0=x2, in1=sbb, op=ALU.mult), reads=["qkr", "sinT"], writes=["rt1"])
            op("pool", lambda q: q.tensor_tensor(out=rt[:, 2], in0=x2, in1=cb, op=ALU.mult), reads=["qkr", "cosT"], writes=["rt2"])
            op("pool", lambda q: q.tensor_tensor(out=rt[:, 3], in0=x1, in1=sbb, op=ALU.mult), reads=["qkr", "sinT"], writes=["rt3"])
            op("dve", lambda q: q.tensor_tensor(out=qkb[:, :, 0:8], in0=rt[:, 0], in1=rt[:, 1], op=ALU.subtract), reads=["rt0", "rt1"], writes=["qkb"])
            op("dve", lambda q: q.tensor_tensor(out=qkb[:, :, 8:16], in0=rt[:, 2], in1=rt[:, 3], op=ALU.add), reads=["rt2", "rt3"], writes=["qkb"])
            tbk = (6, 7)
            for a in range(2):
                for hh in range(8):
                    op("pe", lambda q, a=a, hh=hh: q.transpose(out=ptv(tbk[a])[0:64, hh * 128:(hh + 1) * 128], in_=qkb[:, 8 * a + hh, :], identity=ident[:]),
                       reads=["qkb", "ident"], writes=[("pa", tbk[a])])
                op("dve", lambda q, a=a: q.tensor_copy(out=qT[:, 8 * a:8 * a + 8, :], in_=ptv(tbk[a])[0:64, :].rearrange("p (h t) -> p h t", h=8)),
                   reads=[("pa", tbk[a])], writes=["qT"])
            for g in range(2):
                op("pe", lambda q, g=g: q.transpose(out=ptv(6)[0:64, g * 128:(g + 1) * 128], in_=qkb[:, 16 + g, :], identity=ident[:]),
                   reads=["qkb", "ident"], writes=[("pa", 6)])
            op("dve", lambda q: q.tensor_copy(out=kT[:, slot, :, :], in_=ptv(6)[0:64, 0:256].rearrange("p (g t) -> p g t", g=2)),
               reads=[("pa", 6)], writes=["kT%d" % slot])

        def l0_s2(m, j):
            blk = m * 4 + j
            first = (blk % (SEQ // 128) == 0)
            slot = blk % 2
            S.tag = "m%d.L0.%d.s2" % (m, j)
            oa = [3, 4, 5]
            pti = 0
            sb_i = 0
            for g in range(2):
                for a in range(2):
                    kbs = ([] if first else [(1 - slot, mask_prev, "mask_prev")]) + [(slot, mask_cur, "mask_cur")]
                    pts = []
                    for (ks, msk, mkey) in kbs:
                        b = 6 + sb_i % 2
                        sb_i += 1
                        op("pe", lambda q, b=b, ks=ks, g=g, a=a: q.matmul(pab[b][:], lhsT=kT[:, ks, g, :],
                                                                          rhs=qT[:, 8 * g + 4 * a:8 * g + 4 * a + 4, :].rearrange("p h t -> p (h t)"),
                                                                          start=True, stop=True),
                           reads=["kT%d" % ks, "qT"], writes=[("pa", b)])
                        p = pti % 4
                        pti += 1
                        op("act", lambda q, b=b, p=p: q.activation(out=PT[p][:], in_=pab[b][:], func=AF.Exp), reads=[("pa", b)], writes=[("PT", p)])
                        op("dve", lambda q, p=p, msk=msk: q.tensor_tensor(out=PT[p][:].rearrange("p (h t) -> p h t", h=4),
                                                                         in0=PT[p][:].rearrange("p (h t) -> p h t", h=4),
                                                                         in1=msk[:].unsqueeze(1).to_broadcast([128, 4, 128]), op=ALU.mult),
                           reads=[("PT", p), mkey], writes=[("PT", p)])
                        pts.append((p, ks))
                    for hh in range(4):
                        h = 8 * g + 4 * a + hh
                        ob, oo = oa[h // 7], (h % 7) * 65
                        for i, (p, ks) in enumerate(pts):
                            op("pe", lambda q, p=p, ks=ks, hh=hh, ob=ob, oo=oo, i=i, n=len(pts), g=g: q.matmul(
                                pab[ob][:, oo:oo + 65], lhsT=PT[p][:, hh * 128:(hh + 1) * 128], rhs=vaug[:, ks, g, :],
                                start=(i == 0), stop=(i == n - 1)),
                               reads=[("PT", p), "vaug%d" % ks], writes=[("pa", ob)])
            for bi in range(3):
                h0 = 7 * bi
                nh = min(7, 16 - h0)
                ov = pab[oa[bi]][:, 0:nh * 65].rearrange("p (h d) -> p h d", d=65)
                op("dve", lambda q, ov=ov, h0=h0, nh=nh: q.tensor_tensor(out=sv(DEN + h0, nh), in0=ov[:, :, 64], in1=sv(ESINK + h0, nh), op=ALU.add),
                   reads=[("pa", oa[bi]), "esink"], writes=[("den", bi)])
            op("dve", lambda q: q.reciprocal(out=sv(RDEN, 16), in_=sv(DEN, 16)), reads=[("den", 0), ("den", 1), ("den", 2)], writes=["rden"])
            for bi in range(3):
                h0 = 7 * bi
                nh = min(7, 16 - h0)
                ov = pab[oa[bi]][:, 0:nh * 65].rearrange("p (h d) -> p h d", d=65)
                op("dve", lambda q, ov=ov, h0=h0, nh=nh: q.tensor_tensor(out=F(0).rearrange("p (h d) -> p h d", d=64)[:, h0:h0 + nh, :], in0=ov[:, :, 0:64],
                                                                         in1=sv(RDEN + h0, nh).unsqueeze(2).to_broadcast([128, nh, 64]), op=ALU.mult),
                   reads=[("pa", oa[bi]), "rden"], writes=[FhK(0), FhK(1)])

        def l0_s3(m, j):
            buf = m % 2
            S.tag = "m%d.L0.%d.s3" % (m, j)
            bz = [proj_tm(3, j, 1280, 512, W0, "W0"), proj_tm(4, j, 1792, 512, W0, "W0")]
            for hf in range(2):
                op("act", lambda q, hf=hf: q.activation(out=Fh(2 + hf), in_=pab[bz[hf]][:], func=AF.Tanh), reads=[("pa", bz[hf])], writes=[FhK(2 + hf)])
                op("dve", lambda q, hf=hf: q.scalar_tensor_tensor(out=Fh(2 + hf), in0=Fh(2 + hf), scalar=1.0, in1=pab[bz[hf]][:], op0=ALU.add, op1=ALU.mult),
                   reads=[FhK(2 + hf), ("pa", bz[hf])], writes=[FhK(2 + hf)])
            op("dve", lambda q: q.tensor_tensor(out=og_all[:, 0, :], in0=F(0), in1=F(1), op=ALU.mult),
               reads=[FhK(0), FhK(1), FhK(2), FhK(3)], writes=[("og", 0)])
            yb = out_proj(og_all[:, 0, :], [("og", 0)], Wo0, "Wo0", True, (5, 3))
            post_norm_residual(buf, j, 0, yb)

        def l1_A(m, h):
            ws = h % 2
            Wt = W1h[ws]
            wk = "W1h%d" % ws
            db = h % 2
            S.tag = "m%d.L1.h%d.A" % (m, h)
            bq, bf, bv, bzz = 0, 1, 2, 3
            hTk = [("hT", j) for j in range(4)]
            for (b, c0) in ((bq, 0), (bf, 128)):
                for kc in range(8):
                    op("pe", lambda q, b=b, c0=c0, kc=kc: q.matmul(pab[b][:], lhsT=Wt[:, kc, c0:c0 + 128], rhs=hT[:, kc, :], start=(kc == 0), stop=(kc == 7)),
                       reads=hTk + [wk], writes=[("pa", b)])
            for (b, c0) in ((bv, 256), (bzz, 384)):
                for j in range(4):
                    for kc in range(8):
                        op("pe", lambda q, b=b, c0=c0, kc=kc, j=j: q.matmul(pab[b][:, j * 128:(j + 1) * 128], lhsT=hT[:, kc, j * 128:(j + 1) * 128],
                                                                            rhs=Wt[:, kc, c0:c0 + 128], start=(kc == 0), stop=(kc == 7)),
                           reads=[("hT", j), wk], writes=[("pa", b)])
            if h + 2 < 8:
                load_w1h(h + 2, ws)
            A0, A1, A2, A3, A4, A5, A6 = [Fh(i) for i in range(7)]
            K0, K1, K2, K3, K4, K5, K6 = [FhK(i) for i in range(7)]
            gz = Fh(7) if db == 0 else GZ[0][:]
            gzk = FhK(7) if db == 0 else "gz0"
            op("act", lambda q: q.activation(out=A0, in_=pab[bq][:], func=AF.Tanh), reads=[("pa", bq)], writes=[K0])
            op("dve", lambda q: q.scalar_tensor_tensor(out=A0, in0=A0, scalar=1.0, in1=pab[bq][:], op0=ALU.add, op1=ALU.mult),
               reads=[K0, ("pa", bq)], writes=[K0])
            op("act", lambda q: q.activation(out=A1, in_=pab[bf][:], func=AF.Tanh), reads=[("pa", bf)], writes=[K1])
            op("act", lambda q: q.activation(out=A2, in_=A1, func=AF.Identity, scale=sv(LBB + h), bias=sv(LBA + h)), reads=[K1] + LBK, writes=[K2])
            op("act", lambda q: q.activation(out=A3, in_=A1, func=AF.Identity, scale=sv(LBNB + h), bias=sv(LBB + h)), reads=[K1] + LBK, writes=[K3])
            op("dve", lambda q: q.tensor_tensor_scan(out=A4, data0=startmask[:], data1=A2, initial=0.0, op0=ALU.max, op1=ALU.mult),
               reads=["startmask", K2], writes=[K4])
            op("dve", lambda q: q.tensor_tensor(out=qeT[db][:], in0=A0, in1=A4, op=ALU.mult), reads=[K0, K4], writes=[("qeT", db)])
            op("dve", lambda q: q.reciprocal(out=A5, in_=A4), reads=[K4], writes=[K5])
            op("dve", lambda q: q.tensor_tensor(out=A6, in0=A3, in1=A5, op=ALU.mult), reads=[K3, K5], writes=[K6])
            op("act", lambda q: q.activation(out=keT[db][:], in_=A6, func=AF.Copy), reads=[K6], writes=[("keT", db)])
            op("dve", lambda q: q.tensor_tensor(out=kdT[db][:].rearrange("p (c t) -> p c t", t=64), in0=A6.rearrange("p (c t) -> p c t", t=64),
                                                in1=A4.rearrange("p (c t) -> p c t", t=64)[:, :, 63:64].to_broadcast([128, 8, 64]), op=ALU.mult),
               reads=[K6, K4], writes=[("kdT", db)])
            op("act", lambda q: q.activation(out=glast[:, db, :], in_=A4.rearrange("p (c t) -> p c t", t=64)[:, :, 63], func=AF.Copy),
               reads=[K4], writes=[("glast", db)])
            op("act", lambda q: q.activation(out=vb[db][:].rearrange("p j v -> p (j v)"), in_=pab[bv][:], func=AF.Copy), reads=[("pa", bv)], writes=[("vb", db)])
            op("act", lambda q: q.activation(out=gz, in_=pab[bzz][:], func=AF.Tanh), reads=[("pa", bzz)], writes=[gzk])
            op("dve", lambda q: q.scalar_tensor_tensor(out=gz, in0=gz, scalar=1.0, in1=pab[bzz][:], op0=ALU.add, op1=ALU.mult),
               reads=[gzk, ("pa", bzz)], writes=[gzk])

        def l1_B(m, h):
            db = h % 2
            S.tag = "m%d.L1.h%d.B" % (m, h)
            gz = Fh(7) if db == 0 else GZ[0][:]
            gzk = FhK(7) if db == 0 else "gz0"
            ub = [4, 5]
            bso = 6
            for j in range(4):
                op("pe", lambda q, j=j: q.transpose(out=ptv(TB)[:, j * 128:(j + 1) * 128], in_=kdT[db][:, j * 128:(j + 1) * 128], identity=ident[:]),
                   reads=[("kdT", db), "ident"], writes=[("pa", TB)])
            op("dve", lambda q: q.tensor_copy(out=kd_tm[:].rearrange("p j k -> p (j k)"), in_=ptv(TB)[:, 0:512]), reads=[("pa", TB)], writes=["kd_tm"])
            for j in range(4):
                for c in range(2):
                    op("pe", lambda q, j=j, c=c: q.matmul(pab[ub[c]][:, j * 128:(j + 1) * 128], lhsT=kd_tm[64 * c:64 * c + 64, j, :],
                                                          rhs=vb[db][64 * c:64 * c + 64, j, :], start=True, stop=True),
                       reads=["kd_tm", ("vb", db)], writes=[("pa", ub[c])])
            for j in range(4):
                op("pe", lambda q, j=j: q.matmul(pab[bso][:, j * 128:(j + 1) * 128], lhsT=keT[db][:, j * 128:(j + 1) * 128], rhs=qeT[db][:, j * 128:(j + 1) * 128],
                                                 start=True, stop=True),
                   reads=[("keT", db), ("qeT", db)], writes=[("pa", bso)])
            op("dve", lambda q: q.tensor_tensor(out=smk[:], in0=pab[bso][:].rearrange("p (j t) -> p j t", j=4),
                                                in1=maskbd[:].unsqueeze(1).to_broadcast([128, 4, 128]), op=ALU.mult),
               reads=[("pa", bso), "maskbd"], writes=["smk"])
            for k in range(8):
                j, c = k // 2, k % 2
                op("act", lambda q, k=k: q.activation(out=Sbf[:, k, :], in_=S32[:, h, :], func=AF.Copy), reads=[("S32", h)], writes=[("Sbf", k)])
                op("dve", lambda q, k=k, j=j, c=c: q.scalar_tensor_tensor(out=S32[:, h, :], in0=S32[:, h, :], scalar=glast[:, db, k:k + 1],
                                                                             in1=pab[ub[c]][:, j * 128:(j + 1) * 128], op0=ALU.mult, op1=ALU.add),
                   reads=[("S32", h), ("glast", db), ("pa", ub[c])], writes=[("S32", h)])
            for j in range(4):
                op("pe", lambda q, j=j: q.matmul(pab[bso][:, j * 128:(j + 1) * 128], lhsT=smk[:, j, :], rhs=vb[db][:, j, :], start=True, stop=True),
                   reads=["smk", ("vb", db)], writes=[("pa", bso)])
                for c in range(2):
                    k = 2 * j + c
                    op("pe", lambda q, j=j, c=c, k=k: q.matmul(pab[bso][64 * c:64 * c + 64, j * 128:(j + 1) * 128],
                                                               lhsT=qeT[db][:, j * 128 + 64 * c:j * 128 + 64 * c + 64], rhs=Sbf[:, k, :],
                                                               start=False, stop=True, skip_group_check=True),
                       reads=[("qeT", db), ("Sbf", k)], writes=[("pa", bso)])
            for j in range(4):
                op("act", lambda q, j=j: q.activation(out=junk[:, 0:128], in_=pab[bso][:, j * 128:(j + 1) * 128], func=AF.Square, accum_out=sv(SS3 + j)),
                   reads=[("pa", bso)], writes=["junk", ("ss3", j)])
            op("dve", lambda q: q.tensor_scalar(out=sv(SS3, 4), in0=sv(SS3, 4), scalar1=1.0 / 128, scalar2=EPS, op0=ALU.mult, op1=ALU.add),
               reads=[("ss3", j) for j in range(4)], writes=[("ss3", j) for j in range(4)])
            op("pool", lambda q: q.tensor_tensor(out=sv(RSTD3, 4), in0=sv(SS3, 4), in1=sv(NHALF, 4), op=ALU.pow),
               reads=[("ss3", j) for j in range(4)] + ["nhalf"], writes=[("rstd3", j) for j in range(4)])
            for j in range(4):
                op("dve", lambda q, j=j: q.scalar_tensor_tensor(out=og_all[:, j, h * 128:(h + 1) * 128], in0=pab[bso][:, j * 128:(j + 1) * 128],
                                                                 scalar=sv(RSTD3 + j), in1=gz[:, j * 128:(j + 1) * 128], op0=ALU.mult, op1=ALU.mult),
                   reads=[("pa", bso), ("rstd3", j), gzk], writes=[("og", j)])

        load_x(0)
        for m in range(NMT):
            S.new_epoch()
            buf = m % 2
            if m + 1 < NMT:
                load_x(m + 1)
            load_w1h(0, 0)
            load_w1h(1, 1)
            l0_s1(m, 0)
            for j in range(4):
                l0_s2(m, j)
                if j + 1 < 4:
                    l0_s1(m, j + 1)
                l0_s3(m, j)
            S.tag = "m%d.L1.pre" % m
            for j in range(4):
                make_hT(buf, j, j)
            if m % MT_PER_SEQ == 0:
                op("pool", lambda q: q.memset(S32[:], 0.0), writes=[("S32", h) for h in range(8)])
            l1_A(m, 0)
            for h in range(8):
                if h + 1 < 8:
                    l1_A(m, h + 1)
                l1_B(m, h)
            S.tag = "m%d.L1.out" % m
            for j in range(4):
                yb = out_proj(og_all[:, j, :], [("og", j)], Wo1, "Wo1", False, ((0, 1), (2, 3))[j % 2])
                post_norm_residual(buf, j, 1, yb)
            op("sp", lambda q, m=m, buf=buf: q.dma_start(out=out_d[m * 512:(m + 1) * 512, :].rearrange("(j p) d -> p j d", p=128), in_=xb[buf][:]),
               reads=[xk(buf, j) for j in range(4)], dma="xs%d" % buf)
        S.emit(st)
    build_program.last_sched = S
    return nc


_PROG_CACHE = {}


def _get_prog(nseq, seq):
    key = (nseq, seq)
    if key not in _PROG_CACHE:
        _PROG_CACHE[key] = build_program(nseq, seq)
    return _PROG_CACHE[key]


def make_in_maps(inputs, n_cores, nseq, seq):
    x = np.ascontiguousarray(inputs["x"], dtype=np.float32)
    pos = np.ascontiguousarray(inputs["positions"], dtype=np.int32)
    cst = make_consts()
    maps = []
    for c in range(n_cores):
        xs = x[c * nseq:(c + 1) * nseq, :seq].reshape(nseq * seq, D)
        ps = pos[c * nseq:(c + 1) * nseq, :seq].reshape(nseq * seq // 128, 128).T
        maps.append({
            "x": np.ascontiguousarray(xs),
            "pos": np.ascontiguousarray(ps),
            "cst": cst,
            "pre_norm_w": np.ascontiguousarray(inputs["pre_norm_w"], dtype=np.float32),
            "post_norm_w": np.ascontiguousarray(inputs["post_norm_w"], dtype=np.float32),
            "attn_w_in": np.ascontiguousarray(inputs["attn_w_in"][0], dtype=np.float32),
            "attn_b_in": np.ascontiguousarray(inputs["attn_b_in"], dtype=np.float32).reshape(1, 2304),
            "attn_sinks": np.ascontiguousarray(inputs["attn_sinks"], dtype=np.float32).reshape(1, 16),
            "attn_w_out": np.ascontiguousarray(inputs["attn_w_out"][0], dtype=np.float32),
            "attn_b_out": np.ascontiguousarray(inputs["attn_b_out"], dtype=np.float32).reshape(1, D),
            "rec_w_in": np.ascontiguousarray(inputs["rec_w_in"][0], dtype=np.float32),
            "rec_lb_logits": np.ascontiguousarray(inputs["rec_lb_logits"], dtype=np.float32),
            "rec_gnorm_w": np.ascontiguousarray(inputs["rec_gnorm_w"], dtype=np.float32).reshape(1, 128),
            "rec_w_out": np.ascontiguousarray(inputs["rec_w_out"][0], dtype=np.float32),
        })
    return maps


def kernel(**inputs):
    B, T, _ = inputs["x"].shape
    nseq = B // N_CORES
    nc = _get_prog(nseq, T)
    maps = make_in_maps(inputs, N_CORES, nseq, T)
    res = run_bass_kernel_spmd(nc, maps, core_ids=list(range(N_CORES)))
    outs = [np.asarray(r["out"], dtype=np.float32).reshape(nseq, T, D) for r in res.results]
    return np.concatenate(outs, axis=0)
```

```python
import math
from contextlib import ExitStack

import numpy as np
import concourse.bass as bass
import concourse.mybir as mybir
from concourse.bass_utils import run_bass_kernel_spmd

F32 = mybir.dt.float32
BF16 = mybir.dt.bfloat16
I32 = mybir.dt.int32
AF = mybir.ActivationFunctionType
ALU = mybir.AluOpType

N_CORES = 8
D = 1024
EPS = 1e-6
TWO_PI = 2.0 * math.pi
C1 = 6.28125
C2 = TWO_PI - C1
PI_LO = 3.1415925


class _Op:
    __slots__ = ("eng", "fn", "idx", "deps", "signal", "sem", "count", "dma", "epoch", "tag", "dur", "start")


class _Rec:
    def __init__(self):
        self.call = None

    def __getattr__(self, name):
        def f(*a, **k):
            self.call = (name, a, k)
            return self
        return f


def _nelem(ap):
    n = 1
    for d in ap.shape[1:]:
        n *= int(d)
    return n


def _estimate_us(eng, fn, is_dma):
    r = _Rec()
    try:
        fn(r)
    except Exception:
        return 0.5
    if r.call is None:
        return 0.3
    name, a, k = r.call
    out = k.get("out", a[0] if a else None)
    try:
        if is_dma:
            nbytes = _nelem(out) * int(out.shape[0]) * 4
            return 2.0 + nbytes / 150e3
        if eng == "pe":
            if name == "transpose":
                return 0.10
            rhs = k.get("rhs")
            n = _nelem(rhs)
            return 0.02 + n * 0.00058
        n = _nelem(out) if out is not None else 64
        if eng == "act":
            return 0.26 + n * 0.00085 + (0.1 if k.get("accum_out") is not None else 0.0)
        if eng == "dve":
            f = 1.0
            if name == "reciprocal":
                f = 4.2
            elif name == "tensor_tensor_scan":
                f = 2.0
            elif name == "tensor_tensor":
                f = 1.6
            return 0.12 + n * 0.00104 * f
        if eng == "pool":
            if name == "tensor_tensor" and k.get("op") == ALU.pow:
                return 0.65
            return 0.3 + n * 0.0021
    except Exception:
        pass
    return 0.4


class Sched:
    EPOCH = 1500

    def __init__(self, nc):
        self.nc = nc
        self.ops = []
        self.last_w = {}
        self.readers = {}
        self.epoch = 0
        self.tag = ""

    def new_epoch(self):
        pass

    def op(self, eng, fn, reads=(), writes=(), dma=None, dur=None):
        o = _Op()
        o.eng, o.fn, o.idx, o.dma, o.epoch = eng, fn, len(self.ops), dma, 0
        o.signal = False
        o.tag = self.tag
        o.dur = dur if dur is not None else _estimate_us(eng, fn, dma is not None)
        deps = set()
        for k in reads:
            w = self.last_w.get(k)
            if w is not None:
                deps.add(w)
        for k in writes:
            w = self.last_w.get(k)
            if w is not None:
                deps.add(w)
            for r in self.readers.get(k, ()):
                deps.add(r)
        deps.discard(o.idx)
        o.deps = deps
        for k in writes:
            self.last_w[k] = o.idx
            self.readers[k] = []
        for k in reads:
            if k not in writes:
                self.readers.setdefault(k, []).append(o.idx)
        self.ops.append(o)
        return o

    def list_schedule(self, reorder=True):
        import heapq
        ops = self.ops
        n = len(ops)
        if not reorder:
            return {e: [o for o in ops if o.eng == e] for e in ("pe", "act", "dve", "pool", "sp")}, 0.0
        succ = [[] for _ in range(n)]
        indeg = [0] * n
        for o in ops:
            indeg[o.idx] = len(o.deps)
            for d in o.deps:
                succ[d].append(o.idx)
        ready_t = [0.0] * n
        free = {e: 0.0 for e in ("pe", "act", "dve", "pool", "sp")}
        heaps = {e: [] for e in free}
        for o in ops:
            if indeg[o.idx] == 0:
                heapq.heappush(heaps[o.eng], (0.0, o.idx))
        order = {e: [] for e in free}
        done = 0
        SEM_LAT = 0.15
        while done < n:
            best = None
            for e, h in heaps.items():
                if not h:
                    continue
                t0 = max(free[e], h[0][0])
                cand = None
                tmp = []
                while h and h[0][0] <= t0 and len(tmp) < 24:
                    tmp.append(heapq.heappop(h))
                pick = min(tmp, key=lambda x: x[1])
                for x in tmp:
                    if x is not pick:
                        heapq.heappush(h, x)
                heapq.heappush(h, pick)
                cand = (t0, pick[1], e, pick)
                if best is None or cand[:2] < best[:2]:
                    best = cand
            t0, idx, e, pick = best
            h = heaps[e]
            h.remove(pick)
            heapq.heapify(h)
            o = ops[idx]
            o.start = t0
            if o.dma is not None:
                free[e] = t0 + 0.06
                fin = t0 + o.dur
            else:
                free[e] = t0 + o.dur
                fin = free[e]
            order[e].append(o)
            done += 1
            for sidx in succ[idx]:
                so = ops[sidx]
                lat = 0.0 if (so.eng == e and e == "pe" and o.dma is None) else SEM_LAT
                ready_t[sidx] = max(ready_t[sidx], fin + lat)
                indeg[sidx] -= 1
                if indeg[sidx] == 0:
                    heapq.heappush(heaps[so.eng], (ready_t[sidx], sidx))
        return order, max(free.values())

    def emit(self, stack, reorder=True):
        nc = self.nc
        ops = self.ops
        order, makespan = self.list_schedule(reorder)
        self.makespan = makespan
        pos = {}
        for e, lst in order.items():
            for i, o in enumerate(lst):
                pos[o.idx] = i
        for o in ops:
            for d in o.deps:
                p = ops[d]
                if p.dma is None and p.eng == "pe" and o.eng == "pe" and o.dma is None:
                    assert pos[p.idx] < pos[o.idx]
                    continue
                p.signal = True
        counts = {}
        nsig = {}
        for e, lst in order.items():
            for o in lst:
                if o.dma is not None:
                    key = ("dma", o.dma)
                    counts[key] = counts.get(key, 0) + 16
                    o.sem, o.count = key, counts[key]
                elif o.signal:
                    k = nsig.get(e, 0)
                    nsig[e] = k + 1
                    o.epoch = k // self.EPOCH
                    key = (e, o.epoch)
                    counts[key] = counts.get(key, 0) + 1
                    o.sem, o.count = key, counts[key]
        sems = {}
        for key in counts:
            sems[key] = stack.enter_context(nc.semaphore("s_%s_%s" % key))
        self.n_sems = len(sems)
        final = dict(counts)

        def stream(eng_name):
            def body(eng):
                seen = {}
                for o in order[eng_name]:
                    need = {}
                    for d in o.deps:
                        p = ops[d]
                        if p.dma is None and p.eng == "pe" and eng_name == "pe" and o.dma is None:
                            continue
                        if p.dma is not None:
                            skey, val = ("dma", p.dma), (0, p.count)
                        else:
                            skey, val = ("eng", p.eng), (p.epoch, p.count)
                        if val > need.get(skey, (-1, -1)):
                            need[skey] = val
                    for skey, val in need.items():
                        if val <= seen.get(skey, (-1, -1)):
                            continue
                        seen[skey] = val
                        if skey[0] == "dma":
                            eng.wait_ge(sems[("dma", skey[1])], val[1])
                        else:
                            eng.wait_ge(sems[(skey[1], val[0])], val[1])
                    ins = o.fn(eng)
                    if o.dma is not None:
                        ins.then_inc(sems[o.sem], 16)
                    elif o.signal:
                        ins.then_inc(sems[o.sem], 1)
                if eng_name == "sp":
                    for key, c in final.items():
                        if key[0] == "dma":
                            eng.wait_ge(sems[key], c)
            return body

        with nc.Block() as block:
            block.tensor(stream("pe"))
            block.scalar(stream("act"))
            block.vector(stream("dve"))
            block.gpsimd(stream("pool"))
            block.sync(stream("sp"))


CST_W = 128 * 4 + 512 + 8


def make_consts():
    c = np.zeros((128, CST_W), np.float32)
    i = np.arange(128)
    c[:, 0:128] = np.eye(128, dtype=np.float32)
    c[:, 128:256] = (i[:, None] <= i[None, :]).astype(np.float32)
    c[:, 256:384] = (i[:, None] > i[None, :]).astype(np.float32)
    c[:, 384:512] = ((i[:, None] <= i[None, :]) & ((i[:, None] // 64) == (i[None, :] // 64))).astype(np.float32)
    sm = np.zeros(512, np.float32)
    sm[::64] = 1.0
    c[:, 512:1024] = sm[None, :]
    invf = (np.float32(500000.0) ** (-(np.arange(8, dtype=np.float32) * np.float32(2.0) / np.float32(16.0)))).astype(np.float32)
    c[:, 1024:1032] = invf[None, :]
    return c


def build_program(NSEQ, SEQ):
    NT = NSEQ * SEQ
    NB = NT // 128
    NMT = NT // 512
    MT_PER_SEQ = SEQ // 512
    nc = bass.Bass("TRN2", target_bir_lowering=False)

    def din(name, shape, dt=F32):
        return nc.dram_tensor(name, list(shape), dt, kind="ExternalInput").ap()

    x_d = din("x", [NT, D])
    pos_d = din("pos", [128, NB], I32)
    cst_d = din("cst", [128, CST_W])
    prew_d = din("pre_norm_w", [2, D])
    postw_d = din("post_norm_w", [2, D])
    w0_d = din("attn_w_in", [D, 2304])
    b0_d = din("attn_b_in", [1, 2304])
    sink_d = din("attn_sinks", [1, 16])
    wo0_d = din("attn_w_out", [D, D])
    bo0_d = din("attn_b_out", [1, D])
    w1_d = din("rec_w_in", [D, 4096])
    lb_d = din("rec_lb_logits", [2, D])
    gnw_d = din("rec_gnorm_w", [1, 128])
    wo1_d = din("rec_w_out", [D, D])
    out_d = nc.dram_tensor("out", [NT, D], F32, kind="ExternalOutput").ap()
    w1s_d = nc.dram_tensor("w1s", [8, 128, 8, 512], BF16, kind="Internal").ap()

    with ExitStack() as st:
        def sb(name, shape, dt):
            return st.enter_context(nc.sbuf_tensor(name, list(shape), dt))

        def ps(name, shape, dt):
            return st.enter_context(nc.psum_tensor(name, list(shape), dt))

        W0 = sb("W0", [128, 8, 2304], BF16)
        Wo0 = sb("Wo0", [128, 8, 1024], BF16)
        Wo1 = sb("Wo1", [128, 8, 1024], BF16)
        W1h = [sb("W1h%d" % i, [128, 8, 512], BF16) for i in range(2)]
        xb = [sb("xb%d" % i, [128, 4, 1024], F32) for i in range(2)]
        FF = sb("FF", [128, 4096], F32)
        hT = sb("hT", [128, 8, 512], BF16)
        og_all = sb("og_all", [128, 4, 1024], BF16)
        ident = sb("ident", [128, 128], BF16)
        mask_cur = sb("mask_cur", [128, 128], BF16)
        mask_prev = sb("mask_prev", [128, 128], BF16)
        maskbd = sb("maskbd", [128, 128], BF16)
        startmask = sb("startmask", [128, 512], F32)
        invf = sb("invf", [128, 8], F32)
        cosT = sb("cosT", [128, NB, 8], F32)
        sinT = sb("sinT", [128, NB, 8], F32)
        wpost = sb("wpost", [128, 2, 1024], F32)
        browA = sb("browA", [65, 1024], BF16)
        posi = sb("posi", [128, NB], I32)
        browB = sb("browB", [1, 1024], BF16)
        ones = sb("ones", [65, 128], BF16)
        small = sb("small", [128, 144], F32)
        S32 = sb("S32", [128, 8, 128], F32)
        Sbf = sb("Sbf", [128, 8, 128], BF16)
        hb = sb("hb", [128, 1024], BF16)
        qkb = sb("qkb", [128, 18, 64], BF16)
        qkr = sb("qkr", [128, 18, 16], F32)
        rt = sb("rt", [128, 4, 18, 8], F32)
        qT = sb("qT", [64, 16, 128], BF16)
        kT = sb("kT", [64, 2, 2, 128], BF16)
        vaug = sb("vaug", [128, 2, 2, 65], BF16)
        PT = [sb("PT%d" % i, [128, 512], BF16) for i in range(4)]
        ogT = sb("ogT", [128, 8, 128], BF16)
        qeT = [sb("qeT%d" % i, [128, 512], BF16) for i in range(2)]
        keT = [sb("keT%d" % i, [128, 512], BF16) for i in range(2)]
        kdT = [sb("kdT%d" % i, [128, 512], BF16) for i in range(2)]
        kd_tm = sb("kd_tm", [128, 4, 128], BF16)
        vb = [sb("vb%d" % i, [128, 4, 128], BF16) for i in range(2)]
        smk = sb("smk", [128, 4, 128], BF16)

        PREW0, PREW1 = 0, 8
        SCQ0, SCH0, SCH1 = 16, 24, 32
        LBA, LBB, LBNB = 40, 48, 56
        GNW = 64
        ESINK = 65
        SS = 81
        RSTD = 85
        DEN = 89
        RDEN = 105
        NHALF = 121
        SS2, RSTD2 = 125, 127
        SS3, RSTD3 = 128, 132

        def sv(c, n=1):
            return small[:, c:c + n]

        pab = [ps("pab%d" % i, [128, 512], F32) for i in range(8)]
        GZ = [sb("gz0", [128, 512], F32)]
        glast = sb("glast", [128, 2, 8], F32)

        def ptv(i):
            return pab[i][:].bitcast(BF16)

        S = Sched(nc)
        op = S.op
        FK = ["FF0", "FF1", "FF2", "FF3"]

        def F(i):
            return FF[:, i * 1024:(i + 1) * 1024]

        def Fh(i):
            return FF[:, i * 512:(i + 1) * 512]

        def FhK(i):
            return "FH%d" % i

        ALLF = FK + [FhK(i) for i in range(8)]

        op("sp", lambda q: q.dma_start(out=FF[:, 0:CST_W], in_=cst_d), writes=ALLF, dma="cst")
        op("dve", lambda q: q.tensor_copy(out=ident[:], in_=FF[:, 0:128]), reads=ALLF, writes=["ident"])
        op("dve", lambda q: q.tensor_copy(out=mask_cur[:], in_=FF[:, 128:256]), reads=ALLF, writes=["mask_cur"])
        op("dve", lambda q: q.tensor_copy(out=mask_prev[:], in_=FF[:, 256:384]), reads=ALLF, writes=["mask_prev"])
        op("dve", lambda q: q.tensor_copy(out=maskbd[:], in_=FF[:, 384:512]), reads=ALLF, writes=["maskbd"])
        op("dve", lambda q: q.tensor_copy(out=startmask[:], in_=FF[:, 512:1024]), reads=ALLF, writes=["startmask"])
        op("dve", lambda q: q.tensor_copy(out=invf[:], in_=FF[:, 1024:1032]), reads=ALLF, writes=["invf"])
        op("pool", lambda q: q.memset(ones[:], 1.0), writes=["ones"])
        op("pool", lambda q: q.memset(small[:, NHALF:NHALF + 4], -0.5), writes=["nhalf"])
        op("pool", lambda q: q.memset(vaug[:], 1.0), writes=["vaug0", "vaug1"])
        op("sp", lambda q: q.dma_start(out=small[:, PREW0:PREW0 + 8], in_=prew_d[0:1, :].rearrange("o (k p) -> p (o k)", p=128),
                                       allow_slow_non_contiguous=True), writes=["prew", "smq"], dma="sm")
        op("sp", lambda q: q.dma_start(out=small[:, PREW1:PREW1 + 8], in_=prew_d[1:2, :].rearrange("o (k p) -> p (o k)", p=128),
                                       allow_slow_non_contiguous=True), writes=["prew", "smq"], dma="sm")
        op("sp", lambda q: q.dma_start(out=small[:, LBA:LBA + 8], in_=lb_d[0:1, :].rearrange("o (k p) -> p (o k)", p=128),
                                       allow_slow_non_contiguous=True), writes=["lb0", "smq"], dma="sm")
        op("sp", lambda q: q.dma_start(out=small[:, LBB:LBB + 8], in_=lb_d[1:2, :].rearrange("o (k p) -> p (o k)", p=128),
                                       allow_slow_non_contiguous=True), writes=["lb1", "smq"], dma="sm")
        op("sp", lambda q: q.dma_start(out=small[:, GNW:GNW + 1], in_=gnw_d.rearrange("o p -> p o"),
                                       allow_slow_non_contiguous=True), writes=["gnw", "smq"], dma="sm")
        op("sp", lambda q: q.dma_start(out=small[:, ESINK:ESINK + 16], in_=sink_d.partition_broadcast(128)), writes=["esink", "smq"], dma="sm")
        op("sp", lambda q: q.dma_start(out=wpost[:, 0, :], in_=postw_d[0:1, :].partition_broadcast(128)), writes=["wpost", "smq"], dma="sm")
        op("sp", lambda q: q.dma_start(out=wpost[:, 1, :], in_=postw_d[1:2, :].partition_broadcast(128)), writes=["wpost", "smq"], dma="sm")
        op("sp", lambda q: q.dma_start(out=posi[:], in_=pos_d), writes=["posi", "smq"], dma="sm")
        op("dve", lambda q: q.tensor_scalar(out=sv(SCQ0, 8), in0=sv(PREW0, 8), scalar1=0.125, scalar2=None, op0=ALU.mult), reads=["prew"], writes=["scq0"])
        op("dve", lambda q: q.tensor_scalar(out=sv(SCH0, 8), in0=sv(PREW0, 8), scalar1=0.5, scalar2=None, op0=ALU.mult), reads=["prew"], writes=["sch0"])
        op("dve", lambda q: q.tensor_scalar(out=sv(SCH1, 8), in0=sv(PREW1, 8), scalar1=0.5, scalar2=None, op0=ALU.mult), reads=["prew"], writes=["sch1"])
        op("act", lambda q: q.activation(out=sv(ESINK, 16), in_=sv(ESINK, 16), func=AF.Exp), reads=["esink"], writes=["esink"])
        op("dve", lambda q: q.tensor_tensor(out=sv(LBNB, 8), in0=sv(LBB, 8), in1=sv(LBA, 8), op=ALU.subtract), reads=["lb0", "lb1"], writes=["lbt"])
        op("act", lambda q: q.activation(out=sv(LBNB, 8), in_=sv(LBNB, 8), func=AF.Tanh, scale=0.5), reads=["lbt"], writes=["lbt"])
        op("dve", lambda q: q.tensor_scalar(out=sv(LBA, 8), in0=sv(LBNB, 8), scalar1=0.25, scalar2=0.75, op0=ALU.mult, op1=ALU.add), reads=["lbt"], writes=["lb0"])
        op("dve", lambda q: q.tensor_scalar(out=sv(LBB, 8), in0=sv(LBNB, 8), scalar1=-0.25, scalar2=0.25, op0=ALU.mult, op1=ALU.add), reads=["lbt"], writes=["lb1"])
        op("dve", lambda q: q.tensor_scalar(out=sv(LBNB, 8), in0=sv(LBB, 8), scalar1=-1.0, scalar2=None, op0=ALU.mult), reads=["lb1", "lbt"], writes=["lbt"])
        LBK = ["lb0", "lb1", "lbt"]

        o0 = 1040
        posf = FF[:, o0:o0 + NB]
        ang = FF[:, o0 + NB:o0 + NB + NB * 8]
        tmpu = FF[:, o0 + 9 * NB:o0 + 17 * NB]
        tmpk = FF[:, o0 + 17 * NB:o0 + 25 * NB]
        assert o0 + 25 * NB <= 4096
        tmpi = hb[:].bitcast(I32)[:, 0:NB * 8]
        op("dve", lambda q: q.tensor_copy(out=posf, in_=posi[:]), reads=["posi"] + ALLF, writes=ALLF)
        op("dve", lambda q: q.tensor_tensor(out=ang.rearrange("p (b i) -> p b i", i=8),
                                            in0=posf.unsqueeze(2).to_broadcast([128, NB, 8]),
                                            in1=invf[:].unsqueeze(1).to_broadcast([128, NB, 8]), op=ALU.mult),
           reads=ALLF + ["invf"], writes=ALLF)
        for which, tab in ((0, sinT), (1, cosT)):
            if which == 1:
                op("dve", lambda q: q.tensor_scalar(out=ang, in0=ang, scalar1=math.pi / 2, scalar2=None, op0=ALU.add), reads=ALLF, writes=ALLF)
            op("dve", lambda q: q.tensor_scalar(out=tmpu, in0=ang, scalar1=1.0 / TWO_PI, scalar2=None, op0=ALU.mult), reads=ALLF, writes=ALLF)
            op("dve", lambda q: q.tensor_copy(out=tmpi, in_=tmpu), reads=ALLF, writes=["hb"])
            op("dve", lambda q: q.tensor_copy(out=tmpk, in_=tmpi), reads=["hb"], writes=ALLF)
            op("dve", lambda q: q.scalar_tensor_tensor(out=tmpu, in0=tmpk, scalar=-C1, in1=ang, op0=ALU.mult, op1=ALU.add), reads=ALLF, writes=ALLF)
            op("dve", lambda q: q.scalar_tensor_tensor(out=tmpu, in0=tmpk, scalar=-C2, in1=tmpu, op0=ALU.mult, op1=ALU.add), reads=ALLF, writes=ALLF)
            op("dve", lambda q: q.tensor_scalar(out=tmpu, in0=tmpu, scalar1=-PI_LO, scalar2=PI_LO, op0=ALU.max, op1=ALU.min), reads=ALLF, writes=ALLF)
            op("act", lambda q, tab=tab: q.activation(out=tab[:].rearrange("p b i -> p (b i)"), in_=tmpu, func=AF.Sin),
               reads=ALLF, writes=["cosT" if which == 1 else "sinT"])

        BROWS = [(0, 1024, 0), (1024, 1792, 32), (1792, 2304, 64)]
        for (c0, c1, p) in BROWS:
            op("sp", lambda q, c0=c0, c1=c1, p=p: q.dma_start(out=FF[p:p + 1, 0:c1 - c0], in_=b0_d[0:1, c0:c1]),
               reads=["cosT", "sinT"], writes=ALLF + ["smq"], dma="sm")
        op("sp", lambda q: q.dma_start(out=FF[0:1, 1024:2048], in_=bo0_d), writes=ALLF + ["smq"], dma="sm")
        op("dve", lambda q: q.tensor_scalar(out=browA[0:1, 0:1024], in0=FF[0:1, 0:1024], scalar1=0.125, scalar2=None, op0=ALU.mult), reads=ALLF, writes=["browA"])
        op("dve", lambda q: q.tensor_copy(out=browA[32:33, 0:256], in_=FF[32:33, 0:256]), reads=ALLF, writes=["browA"])
        op("dve", lambda q: q.tensor_scalar(out=browA[32:33, 256:768], in0=FF[32:33, 256:768], scalar1=0.5, scalar2=None, op0=ALU.mult), reads=ALLF, writes=["browA"])
        op("dve", lambda q: q.tensor_scalar(out=browA[64:65, 0:512], in0=FF[64:65, 0:512], scalar1=0.5, scalar2=None, op0=ALU.mult), reads=ALLF, writes=["browA"])
        op("dve", lambda q: q.tensor_copy(out=browB[:], in_=FF[0:1, 1024:2048]), reads=ALLF, writes=["browB"])

        def brow(c0, n):
            for (r0, r1, p) in BROWS:
                if r0 <= c0 and c0 + n <= r1:
                    return ones[p:p + 1, :], browA[p:p + 1, c0 - r0:c0 - r0 + n]
            raise AssertionError((c0, n))

        stageA = FF
        stageB = xb[1][:].rearrange("p j d -> p (j d)")
        XB1K = [("x", 1, j) for j in range(4)]
        stg = [(stageA, ALLF), (stageB, XB1K)]
        stb = [(og_all[:].rearrange("p j d -> p (j d)"), [("og", j) for j in range(4)]), (hT[:].rearrange("p k t -> p (k t)"), [("hT", j) for j in range(4)])]
        cnt = [0]

        def conv(eng, out, in_, scal, rd, wr):
            if eng == "dve":
                op("dve", lambda q: q.tensor_scalar(out=out, in0=in_, scalar1=scal, scalar2=None, op0=ALU.mult), reads=rd, writes=wr)
            else:
                op("act", lambda q: q.activation(out=out, in_=in_, func=AF.Copy, scale=scal), reads=rd, writes=wr)

        for kc in range(8):
            sg, sk = stg[cnt[0] % 2]
            cnt[0] += 1
            op("sp", lambda q, sg=sg, kc=kc: q.dma_start(out=sg[:, 0:2304], in_=w0_d[kc * 128:(kc + 1) * 128, :]),
               reads=["browA", "browB"], writes=sk, dma="wl%d" % (cnt[0] % 2))
            conv("dve", W0[:, kc, 0:1024], sg[:, 0:1024], sv(SCQ0 + kc), sk + ["scq0"], ["W0"])
            conv("act", W0[:, kc, 1024:1280], sg[:, 1024:1280], sv(PREW0 + kc), sk + ["prew"], ["W0"])
            conv("act", W0[:, kc, 1280:2304], sg[:, 1280:2304], sv(SCH0 + kc), sk + ["sch0"], ["W0"])
        for kc in range(8):
            sg, sk = stg[cnt[0] % 2]
            cnt[0] += 1
            op("sp", lambda q, sg=sg, kc=kc: q.dma_start(out=sg[:, 0:1024], in_=wo0_d[kc * 128:(kc + 1) * 128, :]),
               writes=sk, dma="wl%d" % (cnt[0] % 2))
            op("sp", lambda q, sg=sg, kc=kc: q.dma_start(out=sg[:, 1024:2048], in_=wo1_d[kc * 128:(kc + 1) * 128, :]),
               writes=sk, dma="wl%d" % (cnt[0] % 2))
            op("act", lambda q, sg=sg, kc=kc: q.activation(out=Wo0[:, kc, :], in_=sg[:, 0:1024], func=AF.Copy), reads=sk, writes=["Wo0"])
            conv("dve", Wo1[:, kc, :], sg[:, 1024:2048], sv(GNW), sk + ["gnw"], ["Wo1"])
        for kc in range(8):
            sg, sk = stg[cnt[0] % 2]
            sbt, sbk = stb[cnt[0] % 2]
            cnt[0] += 1
            op("sp", lambda q, sg=sg, kc=kc: q.dma_start(out=sg[:, 0:4096], in_=w1_d[kc * 128:(kc + 1) * 128, :]),
               writes=sk, dma="wl%d" % (cnt[0] % 2))
            for t in range(4):
                o_ap = sbt.rearrange("p (h t c) -> p t h c", h=8, t=4, c=128)[:, t, :, :]
                i_ap = sg[:, t * 1024:(t + 1) * 1024].rearrange("p (h c) -> p h c", h=8)
                scal = sv(PREW1 + kc) if t == 2 else sv(SCH1 + kc)
                conv("dve" if t % 2 == 0 else "act", o_ap, i_ap, scal, sk + ["prew", "sch1"], sbk)
            op("sp", lambda q, sbt=sbt, kc=kc: q.dma_start(out=w1s_d[:, :, kc, :].rearrange("h p c -> p h c"),
                                                          in_=sbt.rearrange("p (h c) -> p h c", h=8)),
               reads=sbk, writes=["w1s"], dma="ws%d" % (cnt[0] % 2))

        def xk(buf, j):
            return ("x", buf, j)

        def load_x(m):
            buf = m % 2
            op("sp", lambda q: q.dma_start(out=xb[buf][:], in_=x_d[m * 512:(m + 1) * 512, :].rearrange("(j p) d -> p j d", p=128)),
               writes=[xk(buf, j) for j in range(4)], dma="xl%d" % buf)

        def load_w1h(h, slot):
            op("sp", lambda q: q.dma_start(out=W1h[slot][:], in_=w1s_d[h]), reads=["w1s"], writes=["W1h%d" % slot], dma="w1l%d" % slot)

        def rms_rstd(src_ap, src_keys, col):
            op("act", lambda q: q.activation(out=hb[:], in_=src_ap, func=AF.Square, accum_out=sv(SS + col)),
               reads=src_keys, writes=["hb", ("ss", col)])
            op("dve", lambda q: q.tensor_scalar(out=sv(SS + col), in0=sv(SS + col), scalar1=1.0 / 1024, scalar2=EPS, op0=ALU.mult, op1=ALU.add),
               reads=[("ss", col)], writes=[("ss", col)])
            op("pool", lambda q: q.tensor_tensor(out=sv(RSTD + col), in0=sv(SS + col), in1=sv(NHALF), op=ALU.pow),
               reads=[("ss", col), "nhalf"], writes=[("rstd", col)])

        TB = 7

        def make_hT(buf, j, col):
            xs = xb[buf][:, j, :]
            rms_rstd(xs, [xk(buf, j)], col)
            op("act", lambda q: q.activation(out=hb[:], in_=xs, func=AF.Copy, scale=sv(RSTD + col)),
               reads=[xk(buf, j), ("rstd", col)], writes=["hb"])
            for kc in range(8):
                op("pe", lambda q, kc=kc: q.transpose(out=ptv(TB)[:, kc * 128:(kc + 1) * 128], in_=hb[:, kc * 128:(kc + 1) * 128], identity=ident[:]),
                   reads=["hb", "ident"], writes=[("pa", TB)])
            op("dve", lambda q: q.tensor_copy(out=hT[:, :, j * 128:(j + 1) * 128], in_=ptv(TB).rearrange("p (k t) -> p k t", k=8)),
               reads=[("pa", TB)], writes=[("hT", j)])

        def post_norm_residual(buf, j, layer, pbanks):
            c0 = 4
            for hf in range(2):
                b = pbanks[hf]
                op("act", lambda q, b=b, hf=hf: q.activation(out=Fh(6 + hf), in_=pab[b][:], func=AF.Square, accum_out=sv(SS2 + hf)),
                   reads=[("pa", b)], writes=[FhK(6 + hf), ("ss2", hf)])
            op("dve", lambda q: q.tensor_scalar(out=sv(SS2, 2), in0=sv(SS2, 2), scalar1=1.0 / 1024, scalar2=EPS / 2, op0=ALU.mult, op1=ALU.add),
               reads=[("ss2", 0), ("ss2", 1)], writes=[("ss2", 0), ("ss2", 1)])
            op("dve", lambda q: q.tensor_tensor(out=sv(SS2), in0=sv(SS2), in1=sv(SS2 + 1), op=ALU.add),
               reads=[("ss2", 0), ("ss2", 1)], writes=[("ss2", 0)])
            op("pool", lambda q: q.tensor_tensor(out=sv(RSTD2), in0=sv(SS2), in1=sv(NHALF), op=ALU.pow),
               reads=[("ss2", 0), "nhalf"], writes=["rstd2"])
            for hf in range(2):
                b = pbanks[hf]
                op("dve", lambda q, b=b, hf=hf: q.scalar_tensor_tensor(out=Fh(6 + hf), in0=pab[b][:], scalar=sv(RSTD2),
                                                                        in1=wpost[:, layer, hf * 512:(hf + 1) * 512], op0=ALU.mult, op1=ALU.mult),
                   reads=[("pa", b), "rstd2", "wpost"], writes=[FhK(6 + hf)])
            op("pool", lambda q: q.tensor_tensor(out=xb[buf][:, j, :], in0=xb[buf][:, j, :], in1=F(3), op=ALU.add),
               reads=[xk(buf, j), FhK(6), FhK(7)], writes=[xk(buf, j)])

        def out_proj(src_bf16_ap, src_keys, Wo, wo_key, bias, ybanks):
            for kc in range(8):
                op("pe", lambda q, kc=kc: q.transpose(out=ptv(TB)[:, kc * 128:(kc + 1) * 128], in_=src_bf16_ap[:, kc * 128:(kc + 1) * 128], identity=ident[:]),
                   reads=src_keys + ["ident"], writes=[("pa", TB)])
            op("act", lambda q: q.activation(out=ogT[:].rearrange("p k t -> p (k t)"), in_=ptv(TB), func=AF.Copy),
               reads=[("pa", TB)], writes=["ogT"])
            banks = list(ybanks)
            for hf in range(2):
                b = banks[hf]
                for kc in range(8):
                    op("pe", lambda q, b=b, kc=kc, hf=hf: q.matmul(pab[b][:], lhsT=ogT[:, kc, :], rhs=Wo[:, kc, hf * 512:(hf + 1) * 512],
                                                                   start=(kc == 0), stop=(kc == 7 and not bias)),
                       reads=["ogT", wo_key], writes=[("pa", b)])
                if bias:
                    op("pe", lambda q, b=b, hf=hf: q.matmul(pab[b][:], lhsT=ones[0:1, :], rhs=browB[0:1, hf * 512:(hf + 1) * 512], start=False, stop=True),
                       reads=["ones", "browB"], writes=[("pa", b)])
            return banks

        def proj_tm(b, j, c0, n, Wt, wkey, bias=True):
            for kc in range(8):
                op("pe", lambda q, kc=kc: q.matmul(pab[b][:, 0:n], lhsT=hT[:, kc, j * 128:(j + 1) * 128], rhs=Wt[:, kc, c0:c0 + n],
                                                   start=(kc == 0), stop=(kc == 7 and not bias)),
                   reads=[("hT", j), wkey], writes=[("pa", b)])
            if bias:
                o1, br = brow(c0, n)
                op("pe", lambda q: q.matmul(pab[b][:, 0:n], lhsT=o1, rhs=br, start=False, stop=True),
                   reads=["ones", "browA"], writes=[("pa", b)])
            return b

        def l0_s1(m, j):
            buf = m % 2
            blk = m * 4 + j
            slot = blk % 2
            S.tag = "m%d.L0.%d.s1" % (m, j)
            make_hT(buf, j, j)
            bq = [proj_tm(0, j, 0, 512, W0, "W0"), proj_tm(1, j, 512, 512, W0, "W0")]
            bkv = proj_tm(2, j, 1024, 256, W0, "W0")
            for a in range(2):
                op("act", lambda q, a=a: q.activation(out=qkb[:, 8 * a:8 * a + 8, :], in_=pab[bq[a]][:].rearrange("p (h d) -> p h d", h=8), func=AF.Copy),
                   reads=[("pa", bq[a])], writes=["qkb"])
                op("act", lambda q, a=a: q.activation(out=qkr[:, 8 * a:8 * a + 8, :], in_=pab[bq[a]][:].rearrange("p (h d) -> p h d", h=8)[:, :, 0:16], func=AF.Copy),
                   reads=[("pa", bq[a])], writes=["qkr"])
            op("act", lambda q: q.activation(out=qkb[:, 16:18, :], in_=pab[bkv][:, 0:128].rearrange("p (h d) -> p h d", h=2), func=AF.Copy),
               reads=[("pa", bkv)], writes=["qkb"])
            op("act", lambda q: q.activation(out=qkr[:, 16:18, :], in_=pab[bkv][:, 0:128].rearrange("p (h d) -> p h d", h=2)[:, :, 0:16], func=AF.Copy),
               reads=[("pa", bkv)], writes=["qkr"])
            op("act", lambda q: q.activation(out=vaug[:, slot, :, 0:64], in_=pab[bkv][:, 128:256].rearrange("p (g d) -> p g d", g=2), func=AF.Copy),
               reads=[("pa", bkv)], writes=["vaug%d" % slot])
            cb = cosT[:, blk, :].unsqueeze(1).to_broadcast([128, 18, 8])
            sbb = sinT[:, blk, :].unsqueeze(1).to_broadcast([128, 18, 8])
            x1 = qkr[:, :, 0:8]
            x2 = qkr[:, :, 8:16]
            op("pool", lambda q: q.tensor_tensor(out=rt[:, 0], in0=x1, in1=cb, op=ALU.mult), reads=["qkr", "cosT"], writes=["rt0"])
            op("pool", lambda q: q.tensor_tensor(out=rt[:, 1], in0=x2, in1=sbb, op=ALU.mult), reads=["qkr", "sinT"], writes=["rt1"])
            op("pool", lambda q: q.tensor_tensor(out=rt[:, 2], in0=x2, in1=cb, op=ALU.mult), reads=["qkr", "cosT"], writes=["rt2"])
            op("pool", lambda q: q.tensor_tensor(out=rt[:, 3], in0=x1, in1=sbb, op=ALU.mult), reads=["qkr", "sinT"], writes=["rt3"])
            op("dve", lambda q: q.tensor_tensor(out=qkb[:, :, 0:8], in0=rt[:, 0], in1=rt[:, 1], op=ALU.subtract), reads=["rt0", "rt1"], writes=["qkb"])
            op("dve", lambda q: q.tensor_tensor(out=qkb[:, :, 8:16], in0=rt[:, 2], in1=rt[:, 3], op=ALU.add), reads=["rt2", "rt3"], writes=["qkb"])
            tbk = (6, 7)
            for a in range(2):
                for hh in range(8):
                    op("pe", lambda q, a=a, hh=hh: q.transpose(out=ptv(tbk[a])[0:64, hh * 128:(hh + 1) * 128], in_=qkb[:, 8 * a + hh, :], identity=ident[:]),
                       reads=["qkb", "ident"], writes=[("pa", tbk[a])])
                op("dve", lambda q, a=a: q.tensor_copy(out=qT[:, 8 * a:8 * a + 8, :], in_=ptv(tbk[a])[0:64, :].rearrange("p (h t) -> p h t", h=8)),
                   reads=[("pa", tbk[a])], writes=["qT"])
            for g in range(2):
                op("pe", lambda q, g=g: q.transpose(out=ptv(6)[0:64, g * 128:(g + 1) * 128], in_=qkb[:, 16 + g, :], identity=ident[:]),
                   reads=["qkb", "ident"], writes=[("pa", 6)])
            op("dve", lambda q: q.tensor_copy(out=kT[:, slot, :, :], in_=ptv(6)[0:64, 0:256].rearrange("p (g t) -> p g t", g=2)),
               reads=[("pa", 6)], writes=["kT%d" % slot])

        def l0_s2(m, j):
            blk = m * 4 + j
            first = (blk % (SEQ // 128) == 0)
            slot = blk % 2
            S.tag = "m%d.L0.%d.s2" % (m, j)
            oa = [3, 4, 5]
            pti = 0
            sb_i = 0
            for g in range(2):
                for a in range(2):
                    kbs = ([] if first else [(1 - slot, mask_prev, "mask_prev")]) + [(slot, mask_cur, "mask_cur")]
                    pts = []
                    for (ks, msk, mkey) in kbs:
                        b = 6 + sb_i % 2
                        sb_i += 1
                        op("pe", lambda q, b=b, ks=ks, g=g, a=a: q.matmul(pab[b][:], lhsT=kT[:, ks, g, :],
                                                                          rhs=qT[:, 8 * g + 4 * a:8 * g + 4 * a + 4, :].rearrange("p h t -> p (h t)"),
                                                                          start=True, stop=True),
                           reads=["kT%d" % ks, "qT"], writes=[("pa", b)])
                        p = pti % 4
                        pti += 1
                        op("act", lambda q, b=b, p=p: q.activation(out=PT[p][:], in_=pab[b][:], func=AF.Exp), reads=[("pa", b)], writes=[("PT", p)])
                        op("dve", lambda q, p=p, msk=msk: q.tensor_tensor(out=PT[p][:].rearrange("p (h t) -> p h t", h=4),
                                                                         in0=PT[p][:].rearrange("p (h t) -> p h t", h=4),
                                                                         in1=msk[:].unsqueeze(1).to_broadcast([128, 4, 128]), op=ALU.mult),
                           reads=[("PT", p), mkey], writes=[("PT", p)])
                        pts.append((p, ks))
                    for hh in range(4):
                        h = 8 * g + 4 * a + hh
                        ob, oo = oa[h // 7], (h % 7) * 65
                        for i, (p, ks) in enumerate(pts):
                            op("pe", lambda q, p=p, ks=ks, hh=hh, ob=ob, oo=oo, i=i, n=len(pts), g=g: q.matmul(
                                pab[ob][:, oo:oo + 65], lhsT=PT[p][:, hh * 128:(hh + 1) * 128], rhs=vaug[:, ks, g, :],
                                start=(i == 0), stop=(i == n - 1)),
                               reads=[("PT", p), "vaug%d" % ks], writes=[("pa", ob)])
            for bi in range(3):
                h0 = 7 * bi
                nh = min(7, 16 - h0)
                ov = pab[oa[bi]][:, 0:nh * 65].rearrange("p (h d) -> p h d", d=65)
                op("dve", lambda q, ov=ov, h0=h0, nh=nh: q.tensor_tensor(out=sv(DEN + h0, nh), in0=ov[:, :, 64], in1=sv(ESINK + h0, nh), op=ALU.add),
                   reads=[("pa", oa[bi]), "esink"], writes=[("den", bi)])
            op("dve", lambda q: q.reciprocal(out=sv(RDEN, 16), in_=sv(DEN, 16)), reads=[("den", 0), ("den", 1), ("den", 2)], writes=["rden"])
            for bi in range(3):
                h0 = 7 * bi
                nh = min(7, 16 - h0)
                ov = pab[oa[bi]][:, 0:nh * 65].rearrange("p (h d) -> p h d", d=65)
                op("dve", lambda q, ov=ov, h0=h0, nh=nh: q.tensor_tensor(out=F(0).rearrange("p (h d) -> p h d", d=64)[:, h0:h0 + nh, :], in0=ov[:, :, 0:64],
                                                                         in1=sv(RDEN + h0, nh).unsqueeze(2).to_broadcast([128, nh, 64]), op=ALU.mult),
                   reads=[("pa", oa[bi]), "rden"], writes=[FhK(0), FhK(1)])

        def l0_s3(m, j):
            buf = m % 2
            S.tag = "m%d.L0.%d.s3" % (m, j)
            bz = [proj_tm(3, j, 1280, 512, W0, "W0"), proj_tm(4, j, 1792, 512, W0, "W0")]
            for hf in range(2):
                op("act", lambda q, hf=hf: q.activation(out=Fh(2 + hf), in_=pab[bz[hf]][:], func=AF.Tanh), reads=[("pa", bz[hf])], writes=[FhK(2 + hf)])
                op("dve", lambda q, hf=hf: q.scalar_tensor_tensor(out=Fh(2 + hf), in0=Fh(2 + hf), scalar=1.0, in1=pab[bz[hf]][:], op0=ALU.add, op1=ALU.mult),
                   reads=[FhK(2 + hf), ("pa", bz[hf])], writes=[FhK(2 + hf)])
            op("dve", lambda q: q.tensor_tensor(out=og_all[:, 0, :], in0=F(0), in1=F(1), op=ALU.mult),
               reads=[FhK(0), FhK(1), FhK(2), FhK(3)], writes=[("og", 0)])
            yb = out_proj(og_all[:, 0, :], [("og", 0)], Wo0, "Wo0", True, (5, 3))
            post_norm_residual(buf, j, 0, yb)

        def l1_A(m, h):
            ws = h % 2
            Wt = W1h[ws]
            wk = "W1h%d" % ws
            db = h % 2
            S.tag = "m%d.L1.h%d.A" % (m, h)
            bq, bf, bv, bzz = 0, 1, 2, 3
            hTk = [("hT", j) for j in range(4)]
            for (b, c0) in ((bq, 0), (bf, 128)):
                for kc in range(8):
                    op("pe", lambda q, b=b, c0=c0, kc=kc: q.matmul(pab[b][:], lhsT=Wt[:, kc, c0:c0 + 128], rhs=hT[:, kc, :], start=(kc == 0), stop=(kc == 7)),
                       reads=hTk + [wk], writes=[("pa", b)])
            for (b, c0) in ((bv, 256), (bzz, 384)):
                for j in range(4):
                    for kc in range(8):
                        op("pe", lambda q, b=b, c0=c0, kc=kc, j=j: q.matmul(pab[b][:, j * 128:(j + 1) * 128], lhsT=hT[:, kc, j * 128:(j + 1) * 128],
                                                                            rhs=Wt[:, kc, c0:c0 + 128], start=(kc == 0), stop=(kc == 7)),
                           reads=[("hT", j), wk], writes=[("pa", b)])
            if h + 2 < 8:
                load_w1h(h + 2, ws)
            A0, A1, A2, A3, A4, A5, A6 = [Fh(i) for i in range(7)]
            K0, K1, K2, K3, K4, K5, K6 = [FhK(i) for i in range(7)]
            gz = Fh(7) if db == 0 else GZ[0][:]
            gzk = FhK(7) if db == 0 else "gz0"
            op("act", lambda q: q.activation(out=A0, in_=pab[bq][:], func=AF.Tanh), reads=[("pa", bq)], writes=[K0])
            op("dve", lambda q: q.scalar_tensor_tensor(out=A0, in0=A0, scalar=1.0, in1=pab[bq][:], op0=ALU.add, op1=ALU.mult),
               reads=[K0, ("pa", bq)], writes=[K0])
            op("act", lambda q: q.activation(out=A1, in_=pab[bf][:], func=AF.Tanh), reads=[("pa", bf)], writes=[K1])
            op("act", lambda q: q.activation(out=A2, in_=A1, func=AF.Identity, scale=sv(LBB + h), bias=sv(LBA + h)), reads=[K1] + LBK, writes=[K2])
            op("act", lambda q: q.activation(out=A3, in_=A1, func=AF.Identity, scale=sv(LBNB + h), bias=sv(LBB + h)), reads=[K1] + LBK, writes=[K3])
            op("dve", lambda q: q.tensor_tensor_scan(out=A4, data0=startmask[:], data1=A2, initial=0.0, op0=ALU.max, op1=ALU.mult),
               reads=["startmask", K2], writes=[K4])
            op("dve", lambda q: q.tensor_tensor(out=qeT[db][:], in0=A0, in1=A4, op=ALU.mult), reads=[K0, K4], writes=[("qeT", db)])
            op("dve", lambda q: q.reciprocal(out=A5, in_=A4), reads=[K4], writes=[K5])
            op("dve", lambda q: q.tensor_tensor(out=A6, in0=A3, in1=A5, op=ALU.mult), reads=[K3, K5], writes=[K6])
            op("act", lambda q: q.activation(out=keT[db][:], in_=A6, func=AF.Copy), reads=[K6], writes=[("keT", db)])
            op("dve", lambda q: q.tensor_tensor(out=kdT[db][:].rearrange("p (c t) -> p c t", t=64), in0=A6.rearrange("p (c t) -> p c t", t=64),
                                                in1=A4.rearrange("p (c t) -> p c t", t=64)[:, :, 63:64].to_broadcast([128, 8, 64]), op=ALU.mult),
               reads=[K6, K4], writes=[("kdT", db)])
            op("act", lambda q: q.activation(out=glast[:, db, :], in_=A4.rearrange("p (c t) -> p c t", t=64)[:, :, 63], func=AF.Copy),
               reads=[K4], writes=[("glast", db)])
            op("act", lambda q: q.activation(out=vb[db][:].rearrange("p j v -> p (j v)"), in_=pab[bv][:], func=AF.Copy), reads=[("pa", bv)], writes=[("vb", db)])
            op("act", lambda q: q.activation(out=gz, in_=pab[bzz][:], func=AF.Tanh), reads=[("pa", bzz)], writes=[gzk])
            op("dve", lambda q: q.scalar_tensor_tensor(out=gz, in0=gz, scalar=1.0, in1=pab[bzz][:], op0=ALU.add, op1=ALU.mult),
               reads=[gzk, ("pa", bzz)], writes=[gzk])

        def l1_B(m, h):
            db = h % 2
            S.tag = "m%d.L1.h%d.B" % (m, h)
            gz = Fh(7) if db == 0 else GZ[0][:]
            gzk = FhK(7) if db == 0 else "gz0"
            ub = [4, 5]
            bso = 6
            for j in range(4):
                op("pe", lambda q, j=j: q.transpose(out=ptv(TB)[:, j * 128:(j + 1) * 128], in_=kdT[db][:, j * 128:(j + 1) * 128], identity=ident[:]),
                   reads=[("kdT", db), "ident"], writes=[("pa", TB)])
            op("dve", lambda q: q.tensor_copy(out=kd_tm[:].rearrange("p j k -> p (j k)"), in_=ptv(TB)[:, 0:512]), reads=[("pa", TB)], writes=["kd_tm"])
            for j in range(4):
                for c in range(2):
                    op("pe", lambda q, j=j, c=c: q.matmul(pab[ub[c]][:, j * 128:(j + 1) * 128], lhsT=kd_tm[64 * c:64 * c + 64, j, :],
                                                          rhs=vb[db][64 * c:64 * c + 64, j, :], start=True, stop=True),
                       reads=["kd_tm", ("vb", db)], writes=[("pa", ub[c])])
            for j in range(4):
                op("pe", lambda q, j=j: q.matmul(pab[bso][:, j * 128:(j + 1) * 128], lhsT=keT[db][:, j * 128:(j + 1) * 128], rhs=qeT[db][:, j * 128:(j + 1) * 128],
                                                 start=True, stop=True),
                   reads=[("keT", db), ("qeT", db)], writes=[("pa", bso)])
            op("dve", lambda q: q.tensor_tensor(out=smk[:], in0=pab[bso][:].rearrange("p (j t) -> p j t", j=4),
                                                in1=maskbd[:].unsqueeze(1).to_broadcast([128, 4, 128]), op=ALU.mult),
               reads=[("pa", bso), "maskbd"], writes=["smk"])
            for k in range(8):
                j, c = k // 2, k % 2
                op("act", lambda q, k=k: q.activation(out=Sbf[:, k, :], in_=S32[:, h, :], func=AF.Copy), reads=[("S32", h)], writes=[("Sbf", k)])
                op("dve", lambda q, k=k, j=j, c=c: q.scalar_tensor_tensor(out=S32[:, h, :], in0=S32[:, h, :], scalar=glast[:, db, k:k + 1],
                                                                             in1=pab[ub[c]][:, j * 128:(j + 1) * 128], op0=ALU.mult, op1=ALU.add),
                   reads=[("S32", h), ("glast", db), ("pa", ub[c])], writes=[("S32", h)])
            for j in range(4):
                op("pe", lambda q, j=j: q.matmul(pab[bso][:, j * 128:(j + 1) * 128], lhsT=smk[:, j, :], rhs=vb[db][:, j, :], start=True, stop=True),
                   reads=["smk", ("vb", db)], writes=[("pa", bso)])
                for c in range(2):
                    k = 2 * j + c
                    op("pe", lambda q, j=j, c=c, k=k: q.matmul(pab[bso][64 * c:64 * c + 64, j * 128:(j + 1) * 128],
                                                               lhsT=qeT[db][:, j * 128 + 64 * c:j * 128 + 64 * c + 64], rhs=Sbf[:, k, :],
                                                               start=False, stop=True, skip_group_check=True),
                       reads=[("qeT", db), ("Sbf", k)], writes=[("pa", bso)])
            for j in range(4):
                op("act", lambda q, j=j: q.activation(out=og_all[:, j, h * 128:(h + 1) * 128], in_=pab[bso][:, j * 128:(j + 1) * 128], func=AF.Square, accum_out=sv(SS3 + j)),
                   reads=[("pa", bso)], writes=[("og", j), ("ss3", j)])
            op("dve", lambda q: q.tensor_scalar(out=sv(SS3, 4), in0=sv(SS3, 4), scalar1=1.0 / 128, scalar2=EPS, op0=ALU.mult, op1=ALU.add),
               reads=[("ss3", j) for j in range(4)], writes=[("ss3", j) for j in range(4)])
            op("pool", lambda q: q.tensor_tensor(out=sv(RSTD3, 4), in0=sv(SS3, 4), in1=sv(NHALF, 4), op=ALU.pow),
               reads=[("ss3", j) for j in range(4)] + ["nhalf"], writes=[("rstd3", j) for j in range(4)])
            for j in range(4):
                op("dve", lambda q, j=j: q.scalar_tensor_tensor(out=og_all[:, j, h * 128:(h + 1) * 128], in0=pab[bso][:, j * 128:(j + 1) * 128],
                                                                 scalar=sv(RSTD3 + j), in1=gz[:, j * 128:(j + 1) * 128], op0=ALU.mult, op1=ALU.mult),
                   reads=[("pa", bso), ("rstd3", j), gzk], writes=[("og", j)])

        load_x(0)
        for m in range(NMT):
            S.new_epoch()
            buf = m % 2
            if m + 1 < NMT:
                load_x(m + 1)
            load_w1h(0, 0)
            load_w1h(1, 1)
            l0_s1(m, 0)
            for j in range(4):
                l0_s2(m, j)
                if j + 1 < 4:
                    l0_s1(m, j + 1)
                l0_s3(m, j)
            S.tag = "m%d.L1.pre" % m
            for j in range(4):
                make_hT(buf, j, j)
            if m % MT_PER_SEQ == 0:
                op("pool", lambda q: q.memset(S32[:], 0.0), writes=[("S32", h) for h in range(8)])
            l1_A(m, 0)
            for h in range(8):
                if h + 1 < 8:
                    l1_A(m, h + 1)
                l1_B(m, h)
            S.tag = "m%d.L1.out" % m
            for j in range(4):
                yb = out_proj(og_all[:, j, :], [("og", j)], Wo1, "Wo1", False, ((0, 1), (2, 3))[j % 2])
                post_norm_residual(buf, j, 1, yb)
            op("sp", lambda q, m=m, buf=buf: q.dma_start(out=out_d[m * 512:(m + 1) * 512, :].rearrange("(j p) d -> p j d", p=128), in_=xb[buf][:]),
               reads=[xk(buf, j) for j in range(4)], dma="xs%d" % buf)
        S.emit(st)
    build_program.last_sched = S
    return nc


_PROG_CACHE = {}


def _get_prog(nseq, seq):
    key = (nseq, seq)
    if key not in _PROG_CACHE:
        _PROG_CACHE[key] = build_program(nseq, seq)
    return _PROG_CACHE[key]


def make_in_maps(inputs, n_cores, nseq, seq):
    x = np.ascontiguousarray(inputs["x"], dtype=np.float32)
    pos = np.ascontiguousarray(inputs["positions"], dtype=np.int32)
    cst = make_consts()
    maps = []
    for c in range(n_cores):
        xs = x[c * nseq:(c + 1) * nseq, :seq].reshape(nseq * seq, D)
        ps = pos[c * nseq:(c + 1) * nseq, :seq].reshape(nseq * seq // 128, 128).T
        maps.append({
            "x": np.ascontiguousarray(xs),
            "pos": np.ascontiguousarray(ps),
            "cst": cst,
            "pre_norm_w": np.ascontiguousarray(inputs["pre_norm_w"], dtype=np.float32),
            "post_norm_w": np.ascontiguousarray(inputs["post_norm_w"], dtype=np.float32),
            "attn_w_in": np.ascontiguousarray(inputs["attn_w_in"][0], dtype=np.float32),
            "attn_b_in": np.ascontiguousarray(inputs["attn_b_in"], dtype=np.float32).reshape(1, 2304),
            "attn_sinks": np.ascontiguousarray(inputs["attn_sinks"], dtype=np.float32).reshape(1, 16),
            "attn_w_out": np.ascontiguousarray(inputs["attn_w_out"][0], dtype=np.float32),
            "attn_b_out": np.ascontiguousarray(inputs["attn_b_out"], dtype=np.float32).reshape(1, D),
            "rec_w_in": np.ascontiguousarray(inputs["rec_w_in"][0], dtype=np.float32),
            "rec_lb_logits": np.ascontiguousarray(inputs["rec_lb_logits"], dtype=np.float32),
            "rec_gnorm_w": np.ascontiguousarray(inputs["rec_gnorm_w"], dtype=np.float32).reshape(1, 128),
            "rec_w_out": np.ascontiguousarray(inputs["rec_w_out"][0], dtype=np.float32),
        })
    return maps


def kernel(**inputs):
    B, T, _ = inputs["x"].shape
    nseq = B // N_CORES
    nc = _get_prog(nseq, T)
    maps = make_in_maps(inputs, N_CORES, nseq, T)
    res = run_bass_kernel_spmd(nc, maps, core_ids=list(range(N_CORES)))
    outs = [np.asarray(r["out"], dtype=np.float32).reshape(nseq, T, D) for r in res.results]
    return np.concatenate(outs, axis=0)
```

```python
import math
from contextlib import ExitStack

import numpy as np
import concourse.bass as bass
import concourse.mybir as mybir
from concourse.bass_utils import run_bass_kernel_spmd

F32 = mybir.dt.float32
BF16 = mybir.dt.bfloat16
I32 = mybir.dt.int32
AF = mybir.ActivationFunctionType
ALU = mybir.AluOpType

N_CORES = 8
D = 1024
EPS = 1e-6
TWO_PI = 2.0 * math.pi
C1 = 6.28125
C2 = TWO_PI - C1
PI_LO = 3.1415925


class _Op:
    __slots__ = ("eng", "fn", "idx", "deps", "signal", "sem", "count", "dma", "epoch", "tag", "dur", "start")


class _Rec:
    def __init__(self):
        self.call = None

    def __getattr__(self, name):
        def f(*a, **k):
            self.call = (name, a, k)
            return self
        return f


def _nelem(ap):
    n = 1
    for d in ap.shape[1:]:
        n *= int(d)
    return n


def _estimate_us(eng, fn, is_dma):
    r = _Rec()
    try:
        fn(r)
    except Exception:
        return 0.5
    if r.call is None:
        return 0.3
    name, a, k = r.call
    out = k.get("out", a[0] if a else None)
    try:
        if is_dma:
            esz = 2 if out.dtype == BF16 else 4
            nbytes = _nelem(out) * int(out.shape[0]) * esz
            return 2.0 + nbytes / 200e3
        if eng == "pe":
            if name == "transpose":
                return 0.10
            rhs = k.get("rhs")
            n = _nelem(rhs)
            return 0.02 + n * 0.00058
        n = _nelem(out) if out is not None else 64
        if eng == "act":
            return 0.26 + n * 0.00085 + (0.1 if k.get("accum_out") is not None else 0.0)
        if eng == "dve":
            f = 1.0
            if name == "reciprocal":
                f = 4.2
            elif name == "tensor_tensor_scan":
                f = 2.0
            elif name == "tensor_tensor":
                f = 1.6
            return 0.12 + n * 0.00104 * f
        if eng == "pool":
            if name == "tensor_tensor" and k.get("op") == ALU.pow:
                return 0.65
            return 0.3 + n * 0.0021
    except Exception:
        pass
    return 0.4


class Sched:
    EPOCH = 1500

    def __init__(self, nc):
        self.nc = nc
        self.ops = []
        self.last_w = {}
        self.readers = {}
        self.epoch = 0
        self.tag = ""

    def new_epoch(self):
        pass

    def op(self, eng, fn, reads=(), writes=(), dma=None, dur=None):
        o = _Op()
        o.eng, o.fn, o.idx, o.dma, o.epoch = eng, fn, len(self.ops), dma, 0
        o.signal = False
        o.tag = self.tag
        o.dur = dur if dur is not None else _estimate_us(eng, fn, dma is not None)
        deps = set()
        for k in reads:
            w = self.last_w.get(k)
            if w is not None:
                deps.add(w)
        for k in writes:
            w = self.last_w.get(k)
            if w is not None:
                deps.add(w)
            for r in self.readers.get(k, ()):
                deps.add(r)
        deps.discard(o.idx)
        o.deps = deps
        for k in writes:
            self.last_w[k] = o.idx
            self.readers[k] = []
        for k in reads:
            if k not in writes:
                self.readers.setdefault(k, []).append(o.idx)
        self.ops.append(o)
        return o

    def list_schedule(self, reorder=True):
        import heapq
        ops = self.ops
        n = len(ops)
        if not reorder:
            return {e: [o for o in ops if o.eng == e] for e in ("pe", "act", "dve", "pool", "sp")}, 0.0
        succ = [[] for _ in range(n)]
        indeg = [0] * n
        for o in ops:
            indeg[o.idx] = len(o.deps)
            for d in o.deps:
                succ[d].append(o.idx)
        ready_t = [0.0] * n
        free = {e: 0.0 for e in ("pe", "act", "dve", "pool", "sp")}
        heaps = {e: [] for e in free}
        for o in ops:
            if indeg[o.idx] == 0:
                heapq.heappush(heaps[o.eng], (0.0, o.idx))
        order = {e: [] for e in free}
        done = 0
        SEM_LAT = 0.15
        while done < n:
            best = None
            for e, h in heaps.items():
                if not h:
                    continue
                t0 = max(free[e], h[0][0])
                cand = None
                tmp = []
                while h and h[0][0] <= t0 and len(tmp) < 24:
                    tmp.append(heapq.heappop(h))
                pick = min(tmp, key=lambda x: x[1])
                for x in tmp:
                    if x is not pick:
                        heapq.heappush(h, x)
                heapq.heappush(h, pick)
                cand = (t0, pick[1], e, pick)
                if best is None or cand[:2] < best[:2]:
                    best = cand
            t0, idx, e, pick = best
            h = heaps[e]
            h.remove(pick)
            heapq.heapify(h)
            o = ops[idx]
            o.start = t0
            if o.dma is not None:
                free[e] = t0 + 0.06
                fin = t0 + o.dur
            else:
                free[e] = t0 + o.dur
                fin = free[e]
            order[e].append(o)
            done += 1
            for sidx in succ[idx]:
                so = ops[sidx]
                lat = 0.0 if (so.eng == e and e == "pe" and o.dma is None) else SEM_LAT
                ready_t[sidx] = max(ready_t[sidx], fin + lat)
                indeg[sidx] -= 1
                if indeg[sidx] == 0:
                    heapq.heappush(heaps[so.eng], (ready_t[sidx], sidx))
        return order, max(free.values())

    def emit(self, stack, reorder=True):
        nc = self.nc
        ops = self.ops
        order, makespan = self.list_schedule(reorder)
        self.makespan = makespan
        pos = {}
        for e, lst in order.items():
            for i, o in enumerate(lst):
                pos[o.idx] = i
        for o in ops:
            for d in o.deps:
                p = ops[d]
                if p.dma is None and p.eng == "pe" and o.eng == "pe" and o.dma is None:
                    assert pos[p.idx] < pos[o.idx]
                    continue
                p.signal = True
        counts = {}
        nsig = {}
        for e, lst in order.items():
            for o in lst:
                if o.dma is not None:
                    key = ("dma", o.dma)
                    counts[key] = counts.get(key, 0) + 16
                    o.sem, o.count = key, counts[key]
                elif o.signal:
                    k = nsig.get(e, 0)
                    nsig[e] = k + 1
                    o.epoch = k // self.EPOCH
                    key = (e, o.epoch)
                    counts[key] = counts.get(key, 0) + 1
                    o.sem, o.count = key, counts[key]
        sems = {}
        for key in counts:
            sems[key] = stack.enter_context(nc.semaphore("s_%s_%s" % key))
        self.n_sems = len(sems)
        final = dict(counts)

        def stream(eng_name):
            def body(eng):
                seen = {}
                for o in order[eng_name]:
                    need = {}
                    for d in o.deps:
                        p = ops[d]
                        if p.dma is None and p.eng == "pe" and eng_name == "pe" and o.dma is None:
                            continue
                        if p.dma is not None:
                            skey, val = ("dma", p.dma), (0, p.count)
                        else:
                            skey, val = ("eng", p.eng), (p.epoch, p.count)
                        if val > need.get(skey, (-1, -1)):
                            need[skey] = val
                    for skey, val in need.items():
                        if val <= seen.get(skey, (-1, -1)):
                            continue
                        seen[skey] = val
                        if skey[0] == "dma":
                            eng.wait_ge(sems[("dma", skey[1])], val[1])
                        else:
                            eng.wait_ge(sems[(skey[1], val[0])], val[1])
                    ins = o.fn(eng)
                    if o.dma is not None:
                        ins.then_inc(sems[o.sem], 16)
                    elif o.signal:
                        ins.then_inc(sems[o.sem], 1)
                if eng_name == "sp":
                    for key, c in final.items():
                        if key[0] == "dma":
                            eng.wait_ge(sems[key], c)
            return body

        with nc.Block() as block:
            block.tensor(stream("pe"))
            block.scalar(stream("act"))
            block.vector(stream("dve"))
            block.gpsimd(stream("pool"))
            block.sync(stream("sp"))


CST_W = 128 * 4 + 512 + 8


def make_consts():
    c = np.zeros((128, CST_W), np.float32)
    i = np.arange(128)
    c[:, 0:128] = np.eye(128, dtype=np.float32)
    c[:, 128:256] = (i[:, None] <= i[None, :]).astype(np.float32)
    c[:, 256:384] = (i[:, None] > i[None, :]).astype(np.float32)
    c[:, 384:512] = ((i[:, None] <= i[None, :]) & ((i[:, None] // 64) == (i[None, :] // 64))).astype(np.float32)
    sm = np.zeros(512, np.float32)
    sm[::64] = 1.0
    c[:, 512:1024] = sm[None, :]
    invf = (np.float32(500000.0) ** (-(np.arange(8, dtype=np.float32) * np.float32(2.0) / np.float32(16.0)))).astype(np.float32)
    c[:, 1024:1032] = invf[None, :]
    return c


def build_program(NSEQ, SEQ):
    NT = NSEQ * SEQ
    NB = NT // 128
    NMT = NT // 512
    MT_PER_SEQ = SEQ // 512
    nc = bass.Bass("TRN2", target_bir_lowering=False)

    def din(name, shape, dt=F32):
        return nc.dram_tensor(name, list(shape), dt, kind="ExternalInput").ap()

    x_d = din("x", [NT, D])
    pos_d = din("pos", [128, NB], I32)
    cst_d = din("cst", [128, CST_W])
    prew_d = din("pre_norm_w", [2, D])
    postw_d = din("post_norm_w", [2, D])
    w0_d = din("attn_w_in", [D, 2304])
    b0_d = din("attn_b_in", [1, 2304])
    sink_d = din("attn_sinks", [1, 16])
    wo0_d = din("attn_w_out", [D, D])
    bo0_d = din("attn_b_out", [1, D])
    w1_d = din("rec_w_in", [D, 4096])
    lb_d = din("rec_lb_logits", [2, D])
    gnw_d = din("rec_gnorm_w", [1, 128])
    wo1_d = din("rec_w_out", [D, D])
    out_d = nc.dram_tensor("out", [NT, D], F32, kind="ExternalOutput").ap()
    w1s_d = nc.dram_tensor("w1s", [8, 128, 8, 512], BF16, kind="Internal").ap()

    with ExitStack() as st:
        def sb(name, shape, dt):
            return st.enter_context(nc.sbuf_tensor(name, list(shape), dt))

        def ps(name, shape, dt):
            return st.enter_context(nc.psum_tensor(name, list(shape), dt))

        W0 = sb("W0", [128, 8, 2304], BF16)
        Wo0 = sb("Wo0", [128, 8, 1024], BF16)
        Wo1 = sb("Wo1", [128, 8, 1024], BF16)
        W1h = [sb("W1h%d" % i, [128, 8, 512], BF16) for i in range(2)]
        xb = [sb("xb%d" % i, [128, 4, 1024], F32) for i in range(2)]
        FF = sb("FF", [128, 4096], F32)
        hT = sb("hT", [128, 8, 512], BF16)
        og_all = sb("og_all", [128, 4, 1024], BF16)
        ident = sb("ident", [128, 128], BF16)
        mask_cur = sb("mask_cur", [128, 128], BF16)
        mask_prev = sb("mask_prev", [128, 128], BF16)
        maskbd = sb("maskbd", [128, 128], BF16)
        startmask = sb("startmask", [128, 512], F32)
        invf = sb("invf", [128, 8], F32)
        cosT = sb("cosT", [128, NB, 8], F32)
        sinT = sb("sinT", [128, NB, 8], F32)
        wpost = sb("wpost", [128, 2, 1024], F32)
        browA = sb("browA", [65, 1024], BF16)
        posi = sb("posi", [128, NB], I32)
        browB = sb("browB", [1, 1024], BF16)
        ones = sb("ones", [65, 128], BF16)
        small = sb("small", [128, 144], F32)
        S32 = sb("S32", [128, 8, 128], F32)
        Sbf = sb("Sbf", [128, 8, 128], BF16)
        hb = sb("hb", [128, 1024], BF16)
        qkb = sb("qkb", [128, 18, 64], BF16)
        qkr = sb("qkr", [128, 18, 16], F32)
        rt = sb("rt", [128, 4, 18, 8], F32)
        qT = sb("qT", [64, 16, 128], BF16)
        kT = sb("kT", [64, 2, 2, 128], BF16)
        vaug = sb("vaug", [128, 2, 2, 65], BF16)
        PT = [sb("PT%d" % i, [128, 512], BF16) for i in range(4)]
        ogT = sb("ogT", [128, 8, 128], BF16)
        qeT = [sb("qeT%d" % i, [128, 512], BF16) for i in range(2)]
        keT = [sb("keT%d" % i, [128, 512], BF16) for i in range(2)]
        kdT = [sb("kdT%d" % i, [128, 512], BF16) for i in range(2)]
        kd_tm = sb("kd_tm", [128, 4, 128], BF16)
        vb = [sb("vb%d" % i, [128, 4, 128], BF16) for i in range(2)]
        smk = sb("smk", [128, 4, 128], BF16)

        PREW0, PREW1 = 0, 8
        SCQ0, SCH0, SCH1 = 16, 24, 32
        LBA, LBB, LBNB = 40, 48, 56
        GNW = 64
        ESINK = 65
        SS = 81
        RSTD = 85
        DEN = 89
        RDEN = 105
        NHALF = 121
        SS2, RSTD2 = 125, 127
        SS3, RSTD3 = 128, 132

        def sv(c, n=1):
            return small[:, c:c + n]

        pab = [ps("pab%d" % i, [128, 512], F32) for i in range(8)]
        GZ = [sb("gz0", [128, 512], F32)]
        glast = sb("glast", [128, 2, 8], F32)

        def ptv(i):
            return pab[i][:].bitcast(BF16)

        S = Sched(nc)
        op = S.op
        FK = ["FF0", "FF1", "FF2", "FF3"]

        def F(i):
            return FF[:, i * 1024:(i + 1) * 1024]

        def Fh(i):
            return FF[:, i * 512:(i + 1) * 512]

        def FhK(i):
            return "FH%d" % i

        ALLF = FK + [FhK(i) for i in range(8)]

        op("sp", lambda q: q.dma_start(out=FF[:, 0:CST_W], in_=cst_d), writes=ALLF, dma="cst")
        op("dve", lambda q: q.tensor_copy(out=ident[:], in_=FF[:, 0:128]), reads=ALLF, writes=["ident"])
        op("dve", lambda q: q.tensor_copy(out=mask_cur[:], in_=FF[:, 128:256]), reads=ALLF, writes=["mask_cur"])
        op("dve", lambda q: q.tensor_copy(out=mask_prev[:], in_=FF[:, 256:384]), reads=ALLF, writes=["mask_prev"])
        op("dve", lambda q: q.tensor_copy(out=maskbd[:], in_=FF[:, 384:512]), reads=ALLF, writes=["maskbd"])
        op("dve", lambda q: q.tensor_copy(out=startmask[:], in_=FF[:, 512:1024]), reads=ALLF, writes=["startmask"])
        op("dve", lambda q: q.tensor_copy(out=invf[:], in_=FF[:, 1024:1032]), reads=ALLF, writes=["invf"])
        op("pool", lambda q: q.memset(ones[:], 1.0), writes=["ones"])
        op("pool", lambda q: q.memset(small[:, NHALF:NHALF + 4], -0.5), writes=["nhalf"])
        op("pool", lambda q: q.memset(vaug[:], 1.0), writes=["vaug0", "vaug1"])
        op("sp", lambda q: q.dma_start(out=small[:, PREW0:PREW0 + 8], in_=prew_d[0:1, :].rearrange("o (k p) -> p (o k)", p=128),
                                       allow_slow_non_contiguous=True), writes=["prew", "smq"], dma="sm")
        op("sp", lambda q: q.dma_start(out=small[:, PREW1:PREW1 + 8], in_=prew_d[1:2, :].rearrange("o (k p) -> p (o k)", p=128),
                                       allow_slow_non_contiguous=True), writes=["prew", "smq"], dma="sm")
        op("sp", lambda q: q.dma_start(out=small[:, LBA:LBA + 8], in_=lb_d[0:1, :].rearrange("o (k p) -> p (o k)", p=128),
                                       allow_slow_non_contiguous=True), writes=["lb0", "smq"], dma="sm")
        op("sp", lambda q: q.dma_start(out=small[:, LBB:LBB + 8], in_=lb_d[1:2, :].rearrange("o (k p) -> p (o k)", p=128),
                                       allow_slow_non_contiguous=True), writes=["lb1", "smq"], dma="sm")
        op("sp", lambda q: q.dma_start(out=small[:, GNW:GNW + 1], in_=gnw_d.rearrange("o p -> p o"),
                                       allow_slow_non_contiguous=True), writes=["gnw", "smq"], dma="sm")
        op("sp", lambda q: q.dma_start(out=small[:, ESINK:ESINK + 16], in_=sink_d.partition_broadcast(128)), writes=["esink", "smq"], dma="sm")
        op("sp", lambda q: q.dma_start(out=wpost[:, 0, :], in_=postw_d[0:1, :].partition_broadcast(128)), writes=["wpost", "smq"], dma="sm")
        op("sp", lambda q: q.dma_start(out=wpost[:, 1, :], in_=postw_d[1:2, :].partition_broadcast(128)), writes=["wpost", "smq"], dma="sm")
        op("sp", lambda q: q.dma_start(out=posi[:], in_=pos_d), writes=["posi", "smq"], dma="sm")
        op("dve", lambda q: q.tensor_scalar(out=sv(SCQ0, 8), in0=sv(PREW0, 8), scalar1=0.125, scalar2=None, op0=ALU.mult), reads=["prew"], writes=["scq0"])
        op("dve", lambda q: q.tensor_scalar(out=sv(SCH0, 8), in0=sv(PREW0, 8), scalar1=0.5, scalar2=None, op0=ALU.mult), reads=["prew"], writes=["sch0"])
        op("dve", lambda q: q.tensor_scalar(out=sv(SCH1, 8), in0=sv(PREW1, 8), scalar1=0.5, scalar2=None, op0=ALU.mult), reads=["prew"], writes=["sch1"])
        op("act", lambda q: q.activation(out=sv(ESINK, 16), in_=sv(ESINK, 16), func=AF.Exp), reads=["esink"], writes=["esink"])
        op("dve", lambda q: q.tensor_tensor(out=sv(LBNB, 8), in0=sv(LBB, 8), in1=sv(LBA, 8), op=ALU.subtract), reads=["lb0", "lb1"], writes=["lbt"])
        op("act", lambda q: q.activation(out=sv(LBNB, 8), in_=sv(LBNB, 8), func=AF.Tanh, scale=0.5), reads=["lbt"], writes=["lbt"])
        op("dve", lambda q: q.tensor_scalar(out=sv(LBA, 8), in0=sv(LBNB, 8), scalar1=0.25, scalar2=0.75, op0=ALU.mult, op1=ALU.add), reads=["lbt"], writes=["lb0"])
        op("dve", lambda q: q.tensor_scalar(out=sv(LBB, 8), in0=sv(LBNB, 8), scalar1=-0.25, scalar2=0.25, op0=ALU.mult, op1=ALU.add), reads=["lbt"], writes=["lb1"])
        op("dve", lambda q: q.tensor_scalar(out=sv(LBNB, 8), in0=sv(LBB, 8), scalar1=-1.0, scalar2=None, op0=ALU.mult), reads=["lb1", "lbt"], writes=["lbt"])
        LBK = ["lb0", "lb1", "lbt"]

        o0 = 1040
        posf = FF[:, o0:o0 + NB]
        ang = FF[:, o0 + NB:o0 + NB + NB * 8]
        tmpu = FF[:, o0 + 9 * NB:o0 + 17 * NB]
        tmpk = FF[:, o0 + 17 * NB:o0 + 25 * NB]
        assert o0 + 25 * NB <= 4096
        tmpi = hb[:].bitcast(I32)[:, 0:NB * 8]
        op("dve", lambda q: q.tensor_copy(out=posf, in_=posi[:]), reads=["posi"] + ALLF, writes=ALLF)
        op("dve", lambda q: q.tensor_tensor(out=ang.rearrange("p (b i) -> p b i", i=8),
                                            in0=posf.unsqueeze(2).to_broadcast([128, NB, 8]),
                                            in1=invf[:].unsqueeze(1).to_broadcast([128, NB, 8]), op=ALU.mult),
           reads=ALLF + ["invf"], writes=ALLF)
        for which, tab in ((0, sinT), (1, cosT)):
            if which == 1:
                op("dve", lambda q: q.tensor_scalar(out=ang, in0=ang, scalar1=math.pi / 2, scalar2=None, op0=ALU.add), reads=ALLF, writes=ALLF)
            op("dve", lambda q: q.tensor_scalar(out=tmpu, in0=ang, scalar1=1.0 / TWO_PI, scalar2=None, op0=ALU.mult), reads=ALLF, writes=ALLF)
            op("dve", lambda q: q.tensor_copy(out=tmpi, in_=tmpu), reads=ALLF, writes=["hb"])
            op("dve", lambda q: q.tensor_copy(out=tmpk, in_=tmpi), reads=["hb"], writes=ALLF)
            op("dve", lambda q: q.scalar_tensor_tensor(out=tmpu, in0=tmpk, scalar=-C1, in1=ang, op0=ALU.mult, op1=ALU.add), reads=ALLF, writes=ALLF)
            op("dve", lambda q: q.scalar_tensor_tensor(out=tmpu, in0=tmpk, scalar=-C2, in1=tmpu, op0=ALU.mult, op1=ALU.add), reads=ALLF, writes=ALLF)
            op("dve", lambda q: q.tensor_scalar(out=tmpu, in0=tmpu, scalar1=-PI_LO, scalar2=PI_LO, op0=ALU.max, op1=ALU.min), reads=ALLF, writes=ALLF)
            op("act", lambda q, tab=tab: q.activation(out=tab[:].rearrange("p b i -> p (b i)"), in_=tmpu, func=AF.Sin),
               reads=ALLF, writes=["cosT" if which == 1 else "sinT"])

        BROWS = [(0, 1024, 0), (1024, 1792, 32), (1792, 2304, 64)]
        for (c0, c1, p) in BROWS:
            op("sp", lambda q, c0=c0, c1=c1, p=p: q.dma_start(out=FF[p:p + 1, 0:c1 - c0], in_=b0_d[0:1, c0:c1]),
               reads=["cosT", "sinT"], writes=ALLF + ["smq"], dma="sm")
        op("sp", lambda q: q.dma_start(out=FF[0:1, 1024:2048], in_=bo0_d), writes=ALLF + ["smq"], dma="sm")
        op("dve", lambda q: q.tensor_scalar(out=browA[0:1, 0:1024], in0=FF[0:1, 0:1024], scalar1=0.125, scalar2=None, op0=ALU.mult), reads=ALLF, writes=["browA"])
        op("dve", lambda q: q.tensor_copy(out=browA[32:33, 0:256], in_=FF[32:33, 0:256]), reads=ALLF, writes=["browA"])
        op("dve", lambda q: q.tensor_scalar(out=browA[32:33, 256:768], in0=FF[32:33, 256:768], scalar1=0.5, scalar2=None, op0=ALU.mult), reads=ALLF, writes=["browA"])
        op("dve", lambda q: q.tensor_scalar(out=browA[64:65, 0:512], in0=FF[64:65, 0:512], scalar1=0.5, scalar2=None, op0=ALU.mult), reads=ALLF, writes=["browA"])
        op("dve", lambda q: q.tensor_copy(out=browB[:], in_=FF[0:1, 1024:2048]), reads=ALLF, writes=["browB"])

        def brow(c0, n):
            for (r0, r1, p) in BROWS:
                if r0 <= c0 and c0 + n <= r1:
                    return ones[p:p + 1, :], browA[p:p + 1, c0 - r0:c0 - r0 + n]
            raise AssertionError((c0, n))

        stageA = FF
        stageB = xb[1][:].rearrange("p j d -> p (j d)")
        XB1K = [("x", 1, j) for j in range(4)]
        stg = [(stageA, ALLF), (stageB, XB1K)]
        stb = [(og_all[:].rearrange("p j d -> p (j d)"), [("og", j) for j in range(4)]), (hT[:].rearrange("p k t -> p (k t)"), [("hT", j) for j in range(4)])]
        cnt = [0]

        def conv(eng, out, in_, scal, rd, wr):
            if eng == "dve":
                op("dve", lambda q: q.tensor_scalar(out=out, in0=in_, scalar1=scal, scalar2=None, op0=ALU.mult), reads=rd, writes=wr)
            else:
                op("act", lambda q: q.activation(out=out, in_=in_, func=AF.Copy, scale=scal), reads=rd, writes=wr)

        for kc in range(8):
            sg, sk = stg[cnt[0] % 2]
            cnt[0] += 1
            op("sp", lambda q, sg=sg, kc=kc: q.dma_start(out=sg[:, 0:2304], in_=w0_d[kc * 128:(kc + 1) * 128, :]),
               reads=["browA", "browB"], writes=sk, dma="wl%d" % (cnt[0] % 2))
            conv("dve", W0[:, kc, 0:1024], sg[:, 0:1024], sv(SCQ0 + kc), sk + ["scq0"], ["W0"])
            conv("act", W0[:, kc, 1024:1280], sg[:, 1024:1280], sv(PREW0 + kc), sk + ["prew"], ["W0"])
            conv("act", W0[:, kc, 1280:2304], sg[:, 1280:2304], sv(SCH0 + kc), sk + ["sch0"], ["W0"])
        for kc in range(8):
            sg, sk = stg[cnt[0] % 2]
            cnt[0] += 1
            op("sp", lambda q, sg=sg, kc=kc: q.dma_start(out=sg[:, 0:1024], in_=wo0_d[kc * 128:(kc + 1) * 128, :]),
               writes=sk, dma="wl%d" % (cnt[0] % 2))
            op("sp", lambda q, sg=sg, kc=kc: q.dma_start(out=sg[:, 1024:2048], in_=wo1_d[kc * 128:(kc + 1) * 128, :]),
               writes=sk, dma="wl%d" % (cnt[0] % 2))
            op("act", lambda q, sg=sg, kc=kc: q.activation(out=Wo0[:, kc, :], in_=sg[:, 0:1024], func=AF.Copy), reads=sk, writes=["Wo0"])
            conv("dve", Wo1[:, kc, :], sg[:, 1024:2048], sv(GNW), sk + ["gnw"], ["Wo1"])
        for kc in range(8):
            sg, sk = stg[cnt[0] % 2]
            sbt, sbk = stb[cnt[0] % 2]
            cnt[0] += 1
            op("sp", lambda q, sg=sg, kc=kc: q.dma_start(out=sg[:, 0:4096], in_=w1_d[kc * 128:(kc + 1) * 128, :]),
               writes=sk, dma="wl%d" % (cnt[0] % 2))
            for t in range(4):
                o_ap = sbt.rearrange("p (h t c) -> p t h c", h=8, t=4, c=128)[:, t, :, :]
                i_ap = sg[:, t * 1024:(t + 1) * 1024].rearrange("p (h c) -> p h c", h=8)
                scal = sv(PREW1 + kc) if t == 2 else sv(SCH1 + kc)
                conv("dve" if t % 2 == 0 else "act", o_ap, i_ap, scal, sk + ["prew", "sch1"], sbk)
            op("sp", lambda q, sbt=sbt, kc=kc: q.dma_start(out=w1s_d[:, :, kc, :].rearrange("h p c -> p h c"),
                                                          in_=sbt.rearrange("p (h c) -> p h c", h=8)),
               reads=sbk, writes=["w1s"], dma="ws%d" % (cnt[0] % 2))

        def xk(buf, j):
            return ("x", buf, j)

        def load_x(m):
            buf = m % 2
            op("sp", lambda q: q.dma_start(out=xb[buf][:], in_=x_d[m * 512:(m + 1) * 512, :].rearrange("(j p) d -> p j d", p=128)),
               writes=[xk(buf, j) for j in range(4)], dma="xl%d" % buf)

        def load_w1h(h, slot):
            op("sp", lambda q: q.dma_start(out=W1h[slot][:], in_=w1s_d[h]), reads=["w1s"], writes=["W1h%d" % slot], dma="w1l%d" % slot)

        def rms_rstd(src_ap, src_keys, col):
            op("act", lambda q: q.activation(out=hb[:], in_=src_ap, func=AF.Square, accum_out=sv(SS + col)),
               reads=src_keys, writes=["hb", ("ss", col)])
            op("dve", lambda q: q.tensor_scalar(out=sv(SS + col), in0=sv(SS + col), scalar1=1.0 / 1024, scalar2=EPS, op0=ALU.mult, op1=ALU.add),
               reads=[("ss", col)], writes=[("ss", col)])
            op("pool", lambda q: q.tensor_tensor(out=sv(RSTD + col), in0=sv(SS + col), in1=sv(NHALF), op=ALU.pow),
               reads=[("ss", col), "nhalf"], writes=[("rstd", col)])

        TB = 7

        def make_hT(buf, j, col, tb=TB):
            xs = xb[buf][:, j, :]
            rms_rstd(xs, [xk(buf, j)], col)
            op("act", lambda q: q.activation(out=hb[:], in_=xs, func=AF.Copy, scale=sv(RSTD + col)),
               reads=[xk(buf, j), ("rstd", col)], writes=["hb"])
            for kc in range(8):
                op("pe", lambda q, kc=kc: q.transpose(out=ptv(tb)[:, kc * 128:(kc + 1) * 128], in_=hb[:, kc * 128:(kc + 1) * 128], identity=ident[:]),
                   reads=["hb", "ident"], writes=[("pa", tb)])
            op("dve", lambda q: q.tensor_copy(out=hT[:, :, j * 128:(j + 1) * 128], in_=ptv(tb).rearrange("p (k t) -> p k t", k=8)),
               reads=[("pa", tb)], writes=[("hT", j)])

        def post_norm_residual(buf, j, layer, pbanks):
            c0 = 4
            for hf in range(2):
                b = pbanks[hf]
                op("act", lambda q, b=b, hf=hf: q.activation(out=Fh(6 + hf), in_=pab[b][:], func=AF.Square, accum_out=sv(SS2 + hf)),
                   reads=[("pa", b)], writes=[FhK(6 + hf), ("ss2", hf)])
            op("dve", lambda q: q.tensor_scalar(out=sv(SS2, 2), in0=sv(SS2, 2), scalar1=1.0 / 1024, scalar2=EPS / 2, op0=ALU.mult, op1=ALU.add),
               reads=[("ss2", 0), ("ss2", 1)], writes=[("ss2", 0), ("ss2", 1)])
            op("dve", lambda q: q.tensor_tensor(out=sv(SS2), in0=sv(SS2), in1=sv(SS2 + 1), op=ALU.add),
               reads=[("ss2", 0), ("ss2", 1)], writes=[("ss2", 0)])
            op("pool", lambda q: q.tensor_tensor(out=sv(RSTD2), in0=sv(SS2), in1=sv(NHALF), op=ALU.pow),
               reads=[("ss2", 0), "nhalf"], writes=["rstd2"])
            for hf in range(2):
                b = pbanks[hf]
                op("dve", lambda q, b=b, hf=hf: q.scalar_tensor_tensor(out=Fh(6 + hf), in0=pab[b][:], scalar=sv(RSTD2),
                                                                        in1=wpost[:, layer, hf * 512:(hf + 1) * 512], op0=ALU.mult, op1=ALU.mult),
                   reads=[("pa", b), "rstd2", "wpost"], writes=[FhK(6 + hf)])
            op("pool", lambda q: q.tensor_tensor(out=xb[buf][:, j, :], in0=xb[buf][:, j, :], in1=F(3), op=ALU.add),
               reads=[xk(buf, j), FhK(6), FhK(7)], writes=[xk(buf, j)])

        def out_proj(src_bf16_ap, src_keys, Wo, wo_key, bias, ybanks, tb=TB):
            for kc in range(8):
                op("pe", lambda q, kc=kc: q.transpose(out=ptv(tb)[:, kc * 128:(kc + 1) * 128], in_=src_bf16_ap[:, kc * 128:(kc + 1) * 128], identity=ident[:]),
                   reads=src_keys + ["ident"], writes=[("pa", tb)])
            op("act", lambda q: q.activation(out=ogT[:].rearrange("p k t -> p (k t)"), in_=ptv(tb), func=AF.Copy),
               reads=[("pa", tb)], writes=["ogT"])
            banks = list(ybanks)
            for hf in range(2):
                b = banks[hf]
                for kc in range(8):
                    op("pe", lambda q, b=b, kc=kc, hf=hf: q.matmul(pab[b][:], lhsT=ogT[:, kc, :], rhs=Wo[:, kc, hf * 512:(hf + 1) * 512],
                                                                   start=(kc == 0), stop=(kc == 7 and not bias)),
                       reads=["ogT", wo_key], writes=[("pa", b)])
                if bias:
                    op("pe", lambda q, b=b, hf=hf: q.matmul(pab[b][:], lhsT=ones[0:1, :], rhs=browB[0:1, hf * 512:(hf + 1) * 512], start=False, stop=True),
                       reads=["ones", "browB"], writes=[("pa", b)])
            return banks

        def proj_tm(b, j, c0, n, Wt, wkey, bias=True):
            for kc in range(8):
                op("pe", lambda q, kc=kc: q.matmul(pab[b][:, 0:n], lhsT=hT[:, kc, j * 128:(j + 1) * 128], rhs=Wt[:, kc, c0:c0 + n],
                                                   start=(kc == 0), stop=(kc == 7 and not bias)),
                   reads=[("hT", j), wkey], writes=[("pa", b)])
            if bias:
                o1, br = brow(c0, n)
                op("pe", lambda q: q.matmul(pab[b][:, 0:n], lhsT=o1, rhs=br, start=False, stop=True),
                   reads=["ones", "browA"], writes=[("pa", b)])
            return b

        def l0_s1(m, j):
            buf = m % 2
            blk = m * 4 + j
            slot = blk % 2
            S.tag = "m%d.L0.%d.s1" % (m, j)
            make_hT(buf, j, j, 2)
            bq = [proj_tm(0, j, 0, 512, W0, "W0"), proj_tm(1, j, 512, 512, W0, "W0")]
            for a in range(2):
                op("act", lambda q, a=a: q.activation(out=qkb[:, 8 * a:8 * a + 8, :], in_=pab[bq[a]][:].rearrange("p (h d) -> p h d", h=8), func=AF.Copy),
                   reads=[("pa", bq[a])], writes=["qkb"])
                op("act", lambda q, a=a: q.activation(out=qkr[:, 8 * a:8 * a + 8, :], in_=pab[bq[a]][:].rearrange("p (h d) -> p h d", h=8)[:, :, 0:16], func=AF.Copy),
                   reads=[("pa", bq[a])], writes=["qkr"])
            bkv = proj_tm(0, j, 1024, 256, W0, "W0")
            op("act", lambda q: q.activation(out=qkb[:, 16:18, :], in_=pab[bkv][:, 0:128].rearrange("p (h d) -> p h d", h=2), func=AF.Copy),
               reads=[("pa", bkv)], writes=["qkb"])
            op("act", lambda q: q.activation(out=qkr[:, 16:18, :], in_=pab[bkv][:, 0:128].rearrange("p (h d) -> p h d", h=2)[:, :, 0:16], func=AF.Copy),
               reads=[("pa", bkv)], writes=["qkr"])
            op("act", lambda q: q.activation(out=vaug[:, slot, :, 0:64], in_=pab[bkv][:, 128:256].rearrange("p (g d) -> p g d", g=2), func=AF.Copy),
               reads=[("pa", bkv)], writes=["vaug%d" % slot])
            cb = cosT[:, blk, :].unsqueeze(1).to_broadcast([128, 18, 8])
            sbb = sinT[:, blk, :].unsqueeze(1).to_broadcast([128, 18, 8])
            x1 = qkr[:, :, 0:8]
            x2 = qkr[:, :, 8:16]
            op("pool", lambda q: q.tensor_tensor(out=rt[:, 0], in0=x1, in1=cb, op=ALU.mult), reads=["qkr", "cosT"], writes=["rt0"])
            op("pool", lambda q: q.tensor_tensor(out=rt[:, 1], in0=x2, in1=sbb, op=ALU.mult), reads=["qkr", "sinT"], writes=["rt1"])
            op("pool", lambda q: q.tensor_tensor(out=rt[:, 2], in0=x2, in1=cb, op=ALU.mult), reads=["qkr", "cosT"], writes=["rt2"])
            op("pool", lambda q: q.tensor_tensor(out=rt[:, 3], in0=x1, in1=sbb, op=ALU.mult), reads=["qkr", "sinT"], writes=["rt3"])
            op("dve", lambda q: q.tensor_tensor(out=qkb[:, :, 0:8], in0=rt[:, 0], in1=rt[:, 1], op=ALU.subtract), reads=["rt0", "rt1"], writes=["qkb"])
            op("dve", lambda q: q.tensor_tensor(out=qkb[:, :, 8:16], in0=rt[:, 2], in1=rt[:, 3], op=ALU.add), reads=["rt2", "rt3"], writes=["qkb"])
            tbk = (2, 2)
            for a in range(2):
                for hh in range(8):
                    op("pe", lambda q, a=a, hh=hh: q.transpose(out=ptv(tbk[a])[0:64, hh * 128:(hh + 1) * 128], in_=qkb[:, 8 * a + hh, :], identity=ident[:]),
                       reads=["qkb", "ident"], writes=[("pa", tbk[a])])
                op("dve", lambda q, a=a: q.tensor_copy(out=qT[:, 8 * a:8 * a + 8, :], in_=ptv(tbk[a])[0:64, :].rearrange("p (h t) -> p h t", h=8)),
                   reads=[("pa", tbk[a])], writes=["qT"])
            for g in range(2):
                op("pe", lambda q, g=g: q.transpose(out=ptv(2)[0:64, g * 128:(g + 1) * 128], in_=qkb[:, 16 + g, :], identity=ident[:]),
                   reads=["qkb", "ident"], writes=[("pa", 2)])
            op("dve", lambda q: q.tensor_copy(out=kT[:, slot, :, :], in_=ptv(2)[0:64, 0:256].rearrange("p (g t) -> p g t", g=2)),
               reads=[("pa", 2)], writes=["kT%d" % slot])

        def l0_s2(m, j):
            blk = m * 4 + j
            first = (blk % (SEQ // 128) == 0)
            slot = blk % 2
            S.tag = "m%d.L0.%d.s2" % (m, j)
            oa = [3, 4, 5]
            pti = 0
            sb_i = 0
            for g in range(2):
                for a in range(2):
                    kbs = ([] if first else [(1 - slot, mask_prev, "mask_prev")]) + [(slot, mask_cur, "mask_cur")]
                    pts = []
                    for (ks, msk, mkey) in kbs:
                        b = 6 + sb_i % 2
                        sb_i += 1
                        op("pe", lambda q, b=b, ks=ks, g=g, a=a: q.matmul(pab[b][:], lhsT=kT[:, ks, g, :],
                                                                          rhs=qT[:, 8 * g + 4 * a:8 * g + 4 * a + 4, :].rearrange("p h t -> p (h t)"),
                                                                          start=True, stop=True),
                           reads=["kT%d" % ks, "qT"], writes=[("pa", b)])
                        p = pti % 4
                        pti += 1
                        op("act", lambda q, b=b, p=p: q.activation(out=PT[p][:], in_=pab[b][:], func=AF.Exp), reads=[("pa", b)], writes=[("PT", p)])
                        op("dve", lambda q, p=p, msk=msk: q.tensor_tensor(out=PT[p][:].rearrange("p (h t) -> p h t", h=4),
                                                                         in0=PT[p][:].rearrange("p (h t) -> p h t", h=4),
                                                                         in1=msk[:].unsqueeze(1).to_broadcast([128, 4, 128]), op=ALU.mult),
                           reads=[("PT", p), mkey], writes=[("PT", p)])
                        pts.append((p, ks))
                    for hh in range(4):
                        h = 8 * g + 4 * a + hh
                        ob, oo = oa[h // 7], (h % 7) * 65
                        for i, (p, ks) in enumerate(pts):
                            op("pe", lambda q, p=p, ks=ks, hh=hh, ob=ob, oo=oo, i=i, n=len(pts), g=g: q.matmul(
                                pab[ob][:, oo:oo + 65], lhsT=PT[p][:, hh * 128:(hh + 1) * 128], rhs=vaug[:, ks, g, :],
                                start=(i == 0), stop=(i == n - 1)),
                               reads=[("PT", p), "vaug%d" % ks], writes=[("pa", ob)])
            for bi in range(3):
                h0 = 7 * bi
                nh = min(7, 16 - h0)
                ov = pab[oa[bi]][:, 0:nh * 65].rearrange("p (h d) -> p h d", d=65)
                op("dve", lambda q, ov=ov, h0=h0, nh=nh: q.tensor_tensor(out=sv(DEN + h0, nh), in0=ov[:, :, 64], in1=sv(ESINK + h0, nh), op=ALU.add),
                   reads=[("pa", oa[bi]), "esink"], writes=[("den", bi)])
            op("dve", lambda q: q.reciprocal(out=sv(RDEN, 16), in_=sv(DEN, 16)), reads=[("den", 0), ("den", 1), ("den", 2)], writes=["rden"])
            for bi in range(3):
                h0 = 7 * bi
                nh = min(7, 16 - h0)
                ov = pab[oa[bi]][:, 0:nh * 65].rearrange("p (h d) -> p h d", d=65)
                op("dve", lambda q, ov=ov, h0=h0, nh=nh: q.tensor_tensor(out=F(0).rearrange("p (h d) -> p h d", d=64)[:, h0:h0 + nh, :], in0=ov[:, :, 0:64],
                                                                         in1=sv(RDEN + h0, nh).unsqueeze(2).to_broadcast([128, nh, 64]), op=ALU.mult),
                   reads=[("pa", oa[bi]), "rden"], writes=[FhK(0), FhK(1)])

        def l0_s3(m, j):
            buf = m % 2
            S.tag = "m%d.L0.%d.s3" % (m, j)
            bz = [proj_tm(3, j, 1280, 512, W0, "W0"), proj_tm(4, j, 1792, 512, W0, "W0")]
            for hf in range(2):
                op("act", lambda q, hf=hf: q.activation(out=Fh(2 + hf), in_=pab[bz[hf]][:], func=AF.Tanh), reads=[("pa", bz[hf])], writes=[FhK(2 + hf)])
                op("dve", lambda q, hf=hf: q.scalar_tensor_tensor(out=Fh(2 + hf), in0=Fh(2 + hf), scalar=1.0, in1=pab[bz[hf]][:], op0=ALU.add, op1=ALU.mult),
                   reads=[FhK(2 + hf), ("pa", bz[hf])], writes=[FhK(2 + hf)])
            op("dve", lambda q: q.tensor_tensor(out=og_all[:, 0, :], in0=F(0), in1=F(1), op=ALU.mult),
               reads=[FhK(0), FhK(1), FhK(2), FhK(3)], writes=[("og", 0)])
            yb = out_proj(og_all[:, 0, :], [("og", 0)], Wo0, "Wo0", True, (5, 3), 2)
            post_norm_residual(buf, j, 0, yb)

        def l1_A(m, h):
            ws = h % 2
            Wt = W1h[ws]
            wk = "W1h%d" % ws
            db = h % 2
            S.tag = "m%d.L1.h%d.A" % (m, h)
            bq, bf, bv, bzz = 0, 1, 2, 3
            hTk = [("hT", j) for j in range(4)]
            for (b, c0) in ((bq, 0), (bf, 128)):
                for kc in range(8):
                    op("pe", lambda q, b=b, c0=c0, kc=kc: q.matmul(pab[b][:], lhsT=Wt[:, kc, c0:c0 + 128], rhs=hT[:, kc, :], start=(kc == 0), stop=(kc == 7)),
                       reads=hTk + [wk], writes=[("pa", b)])
            for (b, c0) in ((bv, 256), (bzz, 384)):
                for j in range(4):
                    for kc in range(8):
                        op("pe", lambda q, b=b, c0=c0, kc=kc, j=j: q.matmul(pab[b][:, j * 128:(j + 1) * 128], lhsT=hT[:, kc, j * 128:(j + 1) * 128],
                                                                            rhs=Wt[:, kc, c0:c0 + 128], start=(kc == 0), stop=(kc == 7)),
                           reads=[("hT", j), wk], writes=[("pa", b)])
            if h + 2 < 8:
                load_w1h(h + 2, ws)
            A0, A1, A2, A3, A4, A5, A6 = [Fh(i) for i in range(7)]
            K0, K1, K2, K3, K4, K5, K6 = [FhK(i) for i in range(7)]
            gz = Fh(7) if db == 0 else GZ[0][:]
            gzk = FhK(7) if db == 0 else "gz0"
            op("act", lambda q: q.activation(out=A0, in_=pab[bq][:], func=AF.Tanh), reads=[("pa", bq)], writes=[K0])
            op("dve", lambda q: q.scalar_tensor_tensor(out=A0, in0=A0, scalar=1.0, in1=pab[bq][:], op0=ALU.add, op1=ALU.mult),
               reads=[K0, ("pa", bq)], writes=[K0])
            op("act", lambda q: q.activation(out=A1, in_=pab[bf][:], func=AF.Tanh), reads=[("pa", bf)], writes=[K1])
            op("act", lambda q: q.activation(out=A2, in_=A1, func=AF.Identity, scale=sv(LBB + h), bias=sv(LBA + h)), reads=[K1] + LBK, writes=[K2])
            op("act", lambda q: q.activation(out=A3, in_=A1, func=AF.Identity, scale=sv(LBNB + h), bias=sv(LBB + h)), reads=[K1] + LBK, writes=[K3])
            op("dve", lambda q: q.tensor_tensor_scan(out=A4, data0=startmask[:], data1=A2, initial=0.0, op0=ALU.max, op1=ALU.mult),
               reads=["startmask", K2], writes=[K4])
            op("dve", lambda q: q.tensor_tensor(out=qeT[db][:], in0=A0, in1=A4, op=ALU.mult), reads=[K0, K4], writes=[("qeT", db)])
            op("dve", lambda q: q.reciprocal(out=A5, in_=A4), reads=[K4], writes=[K5])
            op("dve", lambda q: q.tensor_tensor(out=A6, in0=A3, in1=A5, op=ALU.mult), reads=[K3, K5], writes=[K6])
            op("act", lambda q: q.activation(out=keT[db][:], in_=A6, func=AF.Copy), reads=[K6], writes=[("keT", db)])
            op("dve", lambda q: q.tensor_tensor(out=kdT[db][:].rearrange("p (c t) -> p c t", t=64), in0=A6.rearrange("p (c t) -> p c t", t=64),
                                                in1=A4.rearrange("p (c t) -> p c t", t=64)[:, :, 63:64].to_broadcast([128, 8, 64]), op=ALU.mult),
               reads=[K6, K4], writes=[("kdT", db)])
            op("act", lambda q: q.activation(out=glast[:, db, :], in_=A4.rearrange("p (c t) -> p c t", t=64)[:, :, 63], func=AF.Copy),
               reads=[K4], writes=[("glast", db)])
            op("act", lambda q: q.activation(out=vb[db][:].rearrange("p j v -> p (j v)"), in_=pab[bv][:], func=AF.Copy), reads=[("pa", bv)], writes=[("vb", db)])
            op("act", lambda q: q.activation(out=gz, in_=pab[bzz][:], func=AF.Tanh), reads=[("pa", bzz)], writes=[gzk])
            op("dve", lambda q: q.scalar_tensor_tensor(out=gz, in0=gz, scalar=1.0, in1=pab[bzz][:], op0=ALU.add, op1=ALU.mult),
               reads=[gzk, ("pa", bzz)], writes=[gzk])

        def l1_B(m, h):
            db = h % 2
            S.tag = "m%d.L1.h%d.B" % (m, h)
            gz = Fh(7) if db == 0 else GZ[0][:]
            gzk = FhK(7) if db == 0 else "gz0"
            ub = [4, 5]
            bso = 6
            for j in range(4):
                op("pe", lambda q, j=j: q.transpose(out=ptv(TB)[:, j * 128:(j + 1) * 128], in_=kdT[db][:, j * 128:(j + 1) * 128], identity=ident[:]),
                   reads=[("kdT", db), "ident"], writes=[("pa", TB)])
            op("dve", lambda q: q.tensor_copy(out=kd_tm[:].rearrange("p j k -> p (j k)"), in_=ptv(TB)[:, 0:512]), reads=[("pa", TB)], writes=["kd_tm"])
            for j in range(4):
                for c in range(2):
                    op("pe", lambda q, j=j, c=c: q.matmul(pab[ub[c]][:, j * 128:(j + 1) * 128], lhsT=kd_tm[64 * c:64 * c + 64, j, :],
                                                          rhs=vb[db][64 * c:64 * c + 64, j, :], start=True, stop=True),
                       reads=["kd_tm", ("vb", db)], writes=[("pa", ub[c])])
            for j in range(4):
                op("pe", lambda q, j=j: q.matmul(pab[bso][:, j * 128:(j + 1) * 128], lhsT=keT[db][:, j * 128:(j + 1) * 128], rhs=qeT[db][:, j * 128:(j + 1) * 128],
                                                 start=True, stop=True),
                   reads=[("keT", db), ("qeT", db)], writes=[("pa", bso)])
            op("dve", lambda q: q.tensor_tensor(out=smk[:], in0=pab[bso][:].rearrange("p (j t) -> p j t", j=4),
                                                in1=maskbd[:].unsqueeze(1).to_broadcast([128, 4, 128]), op=ALU.mult),
               reads=[("pa", bso), "maskbd"], writes=["smk"])
            for k in range(8):
                j, c = k // 2, k % 2
                op("act", lambda q, k=k: q.activation(out=Sbf[:, k, :], in_=S32[:, h, :], func=AF.Copy), reads=[("S32", h)], writes=[("Sbf", k)])
                op("dve", lambda q, k=k, j=j, c=c: q.scalar_tensor_tensor(out=S32[:, h, :], in0=S32[:, h, :], scalar=glast[:, db, k:k + 1],
                                                                             in1=pab[ub[c]][:, j * 128:(j + 1) * 128], op0=ALU.mult, op1=ALU.add),
                   reads=[("S32", h), ("glast", db), ("pa", ub[c])], writes=[("S32", h)])
            for j in range(4):
                op("pe", lambda q, j=j: q.matmul(pab[bso][:, j * 128:(j + 1) * 128], lhsT=smk[:, j, :], rhs=vb[db][:, j, :], start=True, stop=True),
                   reads=["smk", ("vb", db)], writes=[("pa", bso)])
                for c in range(2):
                    k = 2 * j + c
                    op("pe", lambda q, j=j, c=c, k=k: q.matmul(pab[bso][64 * c:64 * c + 64, j * 128:(j + 1) * 128],
                                                               lhsT=qeT[db][:, j * 128 + 64 * c:j * 128 + 64 * c + 64], rhs=Sbf[:, k, :],
                                                               start=False, stop=True, skip_group_check=True),
                       reads=[("qeT", db), ("Sbf", k)], writes=[("pa", bso)])
            for j in range(4):
                op("act", lambda q, j=j: q.activation(out=og_all[:, j, h * 128:(h + 1) * 128], in_=pab[bso][:, j * 128:(j + 1) * 128], func=AF.Square, accum_out=sv(SS3 + j)),
                   reads=[("pa", bso)], writes=[("og", j), ("ss3", j)])
            op("dve", lambda q: q.tensor_scalar(out=sv(SS3, 4), in0=sv(SS3, 4), scalar1=1.0 / 128, scalar2=EPS, op0=ALU.mult, op1=ALU.add),
               reads=[("ss3", j) for j in range(4)], writes=[("ss3", j) for j in range(4)])
            op("pool", lambda q: q.tensor_tensor(out=sv(RSTD3, 4), in0=sv(SS3, 4), in1=sv(NHALF, 4), op=ALU.pow),
               reads=[("ss3", j) for j in range(4)] + ["nhalf"], writes=[("rstd3", j) for j in range(4)])
            for j in range(4):
                op("dve", lambda q, j=j: q.scalar_tensor_tensor(out=og_all[:, j, h * 128:(h + 1) * 128], in0=pab[bso][:, j * 128:(j + 1) * 128],
                                                                 scalar=sv(RSTD3 + j), in1=gz[:, j * 128:(j + 1) * 128], op0=ALU.mult, op1=ALU.mult),
                   reads=[("pa", bso), ("rstd3", j), gzk], writes=[("og", j)])

        load_x(0)
        for m in range(NMT):
            S.new_epoch()
            buf = m % 2
            if m + 1 < NMT:
                load_x(m + 1)
            load_w1h(0, 0)
            load_w1h(1, 1)
            l0_s1(m, 0)
            for j in range(4):
                l0_s2(m, j)
                if j + 1 < 4:
                    l0_s1(m, j + 1)
                l0_s3(m, j)
            S.tag = "m%d.L1.pre" % m
            for j in range(4):
                make_hT(buf, j, j)
            if m % MT_PER_SEQ == 0:
                op("pool", lambda q: q.memset(S32[:], 0.0), writes=[("S32", h) for h in range(8)])
            l1_A(m, 0)
            for h in range(8):
                if h + 1 < 8:
                    l1_A(m, h + 1)
                l1_B(m, h)
            S.tag = "m%d.L1.out" % m
            for j in range(4):
                yb = out_proj(og_all[:, j, :], [("og", j)], Wo1, "Wo1", False, ((0, 1), (2, 3))[j % 2])
                post_norm_residual(buf, j, 1, yb)
            op("sp", lambda q, m=m, buf=buf: q.dma_start(out=out_d[m * 512:(m + 1) * 512, :].rearrange("(j p) d -> p j d", p=128), in_=xb[buf][:]),
               reads=[xk(buf, j) for j in range(4)], dma="xs%d" % buf)
        S.emit(st)
    build_program.last_sched = S
    return nc


_PROG_CACHE = {}


def _get_prog(nseq, seq):
    key = (nseq, seq)
    if key not in _PROG_CACHE:
        _PROG_CACHE[key] = build_program(nseq, seq)
    return _PROG_CACHE[key]


def make_in_maps(inputs, n_cores, nseq, seq):
    x = np.ascontiguousarray(inputs["x"], dtype=np.float32)
    pos = np.ascontiguousarray(inputs["positions"], dtype=np.int32)
    cst = make_consts()
    maps = []
    for c in range(n_cores):
        xs = x[c * nseq:(c + 1) * nseq, :seq].reshape(nseq * seq, D)
        ps = pos[c * nseq:(c + 1) * nseq, :seq].reshape(nseq * seq // 128, 128).T
        maps.append({
            "x": np.ascontiguousarray(xs),
            "pos": np.ascontiguousarray(ps),
            "cst": cst,
            "pre_norm_w": np.ascontiguousarray(inputs["pre_norm_w"], dtype=np.float32),
            "post_norm_w": np.ascontiguousarray(inputs["post_norm_w"], dtype=np.float32),
            "attn_w_in": np.ascontiguousarray(inputs["attn_w_in"][0], dtype=np.float32),
            "attn_b_in": np.ascontiguousarray(inputs["attn_b_in"], dtype=np.float32).reshape(1, 2304),
            "attn_sinks": np.ascontiguousarray(inputs["attn_sinks"], dtype=np.float32).reshape(1, 16),
            "attn_w_out": np.ascontiguousarray(inputs["attn_w_out"][0], dtype=np.float32),
            "attn_b_out": np.ascontiguousarray(inputs["attn_b_out"], dtype=np.float32).reshape(1, D),
            "rec_w_in": np.ascontiguousarray(inputs["rec_w_in"][0], dtype=np.float32),
            "rec_lb_logits": np.ascontiguousarray(inputs["rec_lb_logits"], dtype=np.float32),
            "rec_gnorm_w": np.ascontiguousarray(inputs["rec_gnorm_w"], dtype=np.float32).reshape(1, 128),
            "rec_w_out": np.ascontiguousarray(inputs["rec_w_out"][0], dtype=np.float32),
        })
    return maps


def kernel(**inputs):
    B, T, _ = inputs["x"].shape
    nseq = B // N_CORES
    nc = _get_prog(nseq, T)
    maps = make_in_maps(inputs, N_CORES, nseq, T)
    res = run_bass_kernel_spmd(nc, maps, core_ids=list(range(N_CORES)))
    outs = [np.asarray(r["out"], dtype=np.float32).reshape(nseq, T, D) for r in res.results]
    return np.concatenate(outs, axis=0)
```

```python
import math
from contextlib import ExitStack

import numpy as np
import concourse.bass as bass
import concourse.mybir as mybir
from concourse.bass_utils import run_bass_kernel_spmd

F32 = mybir.dt.float32
BF16 = mybir.dt.bfloat16
I32 = mybir.dt.int32
AF = mybir.ActivationFunctionType
ALU = mybir.AluOpType

N_CORES = 8
L1_BANKS = (4, 5, 6, 7, 2, 3, 0, 1)
D = 1024
EPS = 1e-6
TWO_PI = 2.0 * math.pi
C1 = 6.28125
C2 = TWO_PI - C1
PI_LO = 3.1415925


class _Op:
    __slots__ = ("eng", "fn", "idx", "deps", "signal", "sem", "count", "dma", "epoch", "tag", "dur", "start")


class _Rec:
    def __init__(self):
        self.call = None

    def __getattr__(self, name):
        def f(*a, **k):
            self.call = (name, a, k)
            return self
        return f


def _nelem(ap):
    n = 1
    for d in ap.shape[1:]:
        n *= int(d)
    return n


def _estimate_us(eng, fn, is_dma):
    r = _Rec()
    try:
        fn(r)
    except Exception:
        return 0.5
    if r.call is None:
        return 0.3
    name, a, k = r.call
    out = k.get("out", a[0] if a else None)
    try:
        if is_dma:
            esz = 2 if out.dtype == BF16 else 4
            nbytes = _nelem(out) * int(out.shape[0]) * esz
            return 2.0 + nbytes / 200e3
        if eng == "pe":
            if name == "transpose":
                return 0.10
            rhs = k.get("rhs")
            n = _nelem(rhs)
            return 0.02 + n * 0.00058
        n = _nelem(out) if out is not None else 64
        if eng == "act":
            return 0.26 + n * 0.00085 + (0.1 if k.get("accum_out") is not None else 0.0)
        if eng == "dve":
            f = 1.0
            if name == "reciprocal":
                f = 4.2
            elif name == "tensor_tensor_scan":
                f = 2.0
            elif name == "tensor_tensor":
                f = 1.6
            return 0.12 + n * 0.00104 * f
        if eng == "pool":
            if name == "tensor_tensor" and k.get("op") == ALU.pow:
                return 0.65
            return 0.3 + n * 0.0021
    except Exception:
        pass
    return 0.4


class Sched:
    EPOCH = 1500

    def __init__(self, nc):
        self.nc = nc
        self.ops = []
        self.last_w = {}
        self.readers = {}
        self.epoch = 0
        self.tag = ""

    def new_epoch(self):
        pass

    def op(self, eng, fn, reads=(), writes=(), dma=None, dur=None):
        o = _Op()
        o.eng, o.fn, o.idx, o.dma, o.epoch = eng, fn, len(self.ops), dma, 0
        o.signal = False
        o.tag = self.tag
        o.dur = dur if dur is not None else _estimate_us(eng, fn, dma is not None)
        deps = set()
        for k in reads:
            w = self.last_w.get(k)
            if w is not None:
                deps.add(w)
        for k in writes:
            w = self.last_w.get(k)
            if w is not None:
                deps.add(w)
            for r in self.readers.get(k, ()):
                deps.add(r)
        deps.discard(o.idx)
        o.deps = deps
        for k in writes:
            self.last_w[k] = o.idx
            self.readers[k] = []
        for k in reads:
            if k not in writes:
                self.readers.setdefault(k, []).append(o.idx)
        self.ops.append(o)
        return o

    def list_schedule(self, reorder=True):
        import heapq
        ops = self.ops
        n = len(ops)
        if not reorder:
            return {e: [o for o in ops if o.eng == e] for e in ("pe", "act", "dve", "pool", "sp")}, 0.0
        succ = [[] for _ in range(n)]
        indeg = [0] * n
        for o in ops:
            indeg[o.idx] = len(o.deps)
            for d in o.deps:
                succ[d].append(o.idx)
        ready_t = [0.0] * n
        free = {e: 0.0 for e in ("pe", "act", "dve", "pool", "sp")}
        heaps = {e: [] for e in free}
        for o in ops:
            if indeg[o.idx] == 0:
                heapq.heappush(heaps[o.eng], (0.0, o.idx))
        order = {e: [] for e in free}
        done = 0
        SEM_LAT = 0.15
        while done < n:
            best = None
            for e, h in heaps.items():
                if not h:
                    continue
                t0 = max(free[e], h[0][0])
                cand = None
                tmp = []
                while h and h[0][0] <= t0 and len(tmp) < 24:
                    tmp.append(heapq.heappop(h))
                pick = min(tmp, key=lambda x: x[1])
                for x in tmp:
                    if x is not pick:
                        heapq.heappush(h, x)
                heapq.heappush(h, pick)
                cand = (t0, pick[1], e, pick)
                if best is None or cand[:2] < best[:2]:
                    best = cand
            t0, idx, e, pick = best
            h = heaps[e]
            h.remove(pick)
            heapq.heapify(h)
            o = ops[idx]
            o.start = t0
            if o.dma is not None:
                free[e] = t0 + 0.06
                fin = t0 + o.dur
            else:
                free[e] = t0 + o.dur
                fin = free[e]
            order[e].append(o)
            done += 1
            for sidx in succ[idx]:
                so = ops[sidx]
                lat = 0.0 if (so.eng == e and e == "pe" and o.dma is None) else SEM_LAT
                ready_t[sidx] = max(ready_t[sidx], fin + lat)
                indeg[sidx] -= 1
                if indeg[sidx] == 0:
                    heapq.heappush(heaps[so.eng], (ready_t[sidx], sidx))
        return order, max(free.values())

    def emit(self, stack, reorder=True):
        nc = self.nc
        ops = self.ops
        order, makespan = self.list_schedule(reorder)
        self.makespan = makespan
        pos = {}
        for e, lst in order.items():
            for i, o in enumerate(lst):
                pos[o.idx] = i
        for o in ops:
            for d in o.deps:
                p = ops[d]
                if p.dma is None and p.eng == "pe" and o.eng == "pe" and o.dma is None:
                    assert pos[p.idx] < pos[o.idx]
                    continue
                p.signal = True
        counts = {}
        nsig = {}
        for e, lst in order.items():
            for o in lst:
                if o.dma is not None:
                    key = ("dma", o.dma)
                    counts[key] = counts.get(key, 0) + 16
                    o.sem, o.count = key, counts[key]
                elif o.signal:
                    k = nsig.get(e, 0)
                    nsig[e] = k + 1
                    o.epoch = k // self.EPOCH
                    key = (e, o.epoch)
                    counts[key] = counts.get(key, 0) + 1
                    o.sem, o.count = key, counts[key]
        sems = {}
        for key in counts:
            sems[key] = stack.enter_context(nc.semaphore("s_%s_%s" % key))
        self.n_sems = len(sems)
        final = dict(counts)

        def stream(eng_name):
            def body(eng):
                seen = {}
                for o in order[eng_name]:
                    need = {}
                    for d in o.deps:
                        p = ops[d]
                        if p.dma is None and p.eng == "pe" and eng_name == "pe" and o.dma is None:
                            continue
                        if p.dma is not None:
                            skey, val = ("dma", p.dma), (0, p.count)
                        else:
                            skey, val = ("eng", p.eng), (p.epoch, p.count)
                        if val > need.get(skey, (-1, -1)):
                            need[skey] = val
                    for skey, val in need.items():
                        if val <= seen.get(skey, (-1, -1)):
                            continue
                        seen[skey] = val
                        if skey[0] == "dma":
                            eng.wait_ge(sems[("dma", skey[1])], val[1])
                        else:
                            eng.wait_ge(sems[(skey[1], val[0])], val[1])
                    ins = o.fn(eng)
                    if o.dma is not None:
                        ins.then_inc(sems[o.sem], 16)
                    elif o.signal:
                        ins.then_inc(sems[o.sem], 1)
                if eng_name == "sp":
                    for key, c in final.items():
                        if key[0] == "dma":
                            eng.wait_ge(sems[key], c)
            return body

        with nc.Block() as block:
            block.tensor(stream("pe"))
            block.scalar(stream("act"))
            block.vector(stream("dve"))
            block.gpsimd(stream("pool"))
            block.sync(stream("sp"))


CST_W = 128 * 4 + 512 + 8


def make_consts():
    c = np.zeros((128, CST_W), np.float32)
    i = np.arange(128)
    c[:, 0:128] = np.eye(128, dtype=np.float32)
    c[:, 128:256] = (i[:, None] <= i[None, :]).astype(np.float32)
    c[:, 256:384] = (i[:, None] > i[None, :]).astype(np.float32)
    c[:, 384:512] = ((i[:, None] <= i[None, :]) & ((i[:, None] // 64) == (i[None, :] // 64))).astype(np.float32)
    sm = np.zeros(512, np.float32)
    sm[::64] = 1.0
    c[:, 512:1024] = sm[None, :]
    invf = (np.float32(500000.0) ** (-(np.arange(8, dtype=np.float32) * np.float32(2.0) / np.float32(16.0)))).astype(np.float32)
    c[:, 1024:1032] = invf[None, :]
    return c


def build_program(NSEQ, SEQ):
    NT = NSEQ * SEQ
    NB = NT // 128
    NMT = NT // 512
    MT_PER_SEQ = SEQ // 512
    nc = bass.Bass("TRN2", target_bir_lowering=False)

    def din(name, shape, dt=F32):
        return nc.dram_tensor(name, list(shape), dt, kind="ExternalInput").ap()

    x_d = din("x", [NT, D])
    pos_d = din("pos", [128, NB], I32)
    cst_d = din("cst", [128, CST_W])
    prew_d = din("pre_norm_w", [2, D])
    postw_d = din("post_norm_w", [2, D])
    w0_d = din("attn_w_in", [D, 2304])
    b0_d = din("attn_b_in", [1, 2304])
    sink_d = din("attn_sinks", [1, 16])
    wo0_d = din("attn_w_out", [D, D])
    bo0_d = din("attn_b_out", [1, D])
    w1_d = din("rec_w_in", [D, 4096])
    lb_d = din("rec_lb_logits", [2, D])
    gnw_d = din("rec_gnorm_w", [1, 128])
    wo1_d = din("rec_w_out", [D, D])
    out_d = nc.dram_tensor("out", [NT, D], F32, kind="ExternalOutput").ap()
    w1s_d = nc.dram_tensor("w1s", [8, 128, 8, 512], BF16, kind="Internal").ap()

    with ExitStack() as st:
        def sb(name, shape, dt):
            return st.enter_context(nc.sbuf_tensor(name, list(shape), dt))

        def ps(name, shape, dt):
            return st.enter_context(nc.psum_tensor(name, list(shape), dt))

        W0 = sb("W0", [128, 8, 2304], BF16)
        Wo0 = sb("Wo0", [128, 8, 1024], BF16)
        Wo1 = sb("Wo1", [128, 8, 1024], BF16)
        W1h = [sb("W1h%d" % i, [128, 8, 512], BF16) for i in range(2)]
        xb = [sb("xb%d" % i, [128, 4, 1024], F32) for i in range(2)]
        FF = sb("FF", [128, 4096], F32)
        hT = sb("hT", [128, 8, 512], BF16)
        og_all = sb("og_all", [128, 4, 1024], BF16)
        ident = sb("ident", [128, 128], BF16)
        mask_cur = sb("mask_cur", [128, 128], BF16)
        mask_prev = sb("mask_prev", [128, 128], BF16)
        maskbd = sb("maskbd", [128, 128], BF16)
        startmask = sb("startmask", [128, 512], F32)
        invf = sb("invf", [128, 8], F32)
        cosT = sb("cosT", [128, NB, 8], F32)
        sinT = sb("sinT", [128, NB, 8], F32)
        wpost = sb("wpost", [128, 2, 1024], F32)
        browA = sb("browA", [65, 1024], BF16)
        posi = sb("posi", [128, NB], I32)
        browB = sb("browB", [1, 1024], BF16)
        ones = sb("ones", [65, 128], BF16)
        small = sb("small", [128, 144], F32)
        S32 = sb("S32", [128, 8, 128], F32)
        Sbf = sb("Sbf", [128, 8, 128], BF16)
        hb = sb("hb", [128, 1024], BF16)
        qkb = sb("qkb", [128, 18, 64], BF16)
        qkr = sb("qkr", [128, 18, 16], F32)
        rt = sb("rt", [128, 4, 18, 8], F32)
        qT = sb("qT", [64, 16, 128], BF16)
        kT = sb("kT", [64, 2, 2, 128], BF16)
        vaug = sb("vaug", [128, 2, 2, 65], BF16)
        PT = [sb("PT%d" % i, [128, 512], BF16) for i in range(4)]
        ogT = sb("ogT", [128, 8, 128], BF16)
        qeT = [sb("qeT%d" % i, [128, 512], BF16) for i in range(2)]
        keT = [sb("keT%d" % i, [128, 512], BF16) for i in range(2)]
        kdT = [sb("kdT%d" % i, [128, 512], BF16) for i in range(2)]
        kd_tm = sb("kd_tm", [128, 4, 128], BF16)
        vb = [sb("vb%d" % i, [128, 4, 128], BF16) for i in range(2)]
        smk = sb("smk", [128, 4, 128], BF16)

        PREW0, PREW1 = 0, 8
        SCQ0, SCH0, SCH1 = 16, 24, 32
        LBA, LBB, LBNB = 40, 48, 56
        GNW = 64
        ESINK = 65
        SS = 81
        RSTD = 85
        DEN = 89
        RDEN = 105
        NHALF = 121
        SS2, RSTD2 = 125, 127
        SS3, RSTD3 = 128, 132

        def sv(c, n=1):
            return small[:, c:c + n]

        pab = [ps("pab%d" % i, [128, 512], F32) for i in range(8)]
        GZ = [sb("gz0", [128, 512], F32)]
        glast = sb("glast", [128, 2, 8], F32)

        def ptv(i):
            return pab[i][:].bitcast(BF16)

        S = Sched(nc)
        op = S.op
        FK = ["FF0", "FF1", "FF2", "FF3"]

        def F(i):
            return FF[:, i * 1024:(i + 1) * 1024]

        def Fh(i):
            return FF[:, i * 512:(i + 1) * 512]

        def FhK(i):
            return "FH%d" % i

        ALLF = FK + [FhK(i) for i in range(8)]

        op("sp", lambda q: q.dma_start(out=FF[:, 0:CST_W], in_=cst_d), writes=ALLF, dma="cst")
        op("dve", lambda q: q.tensor_copy(out=ident[:], in_=FF[:, 0:128]), reads=ALLF, writes=["ident"])
        op("dve", lambda q: q.tensor_copy(out=mask_cur[:], in_=FF[:, 128:256]), reads=ALLF, writes=["mask_cur"])
        op("dve", lambda q: q.tensor_copy(out=mask_prev[:], in_=FF[:, 256:384]), reads=ALLF, writes=["mask_prev"])
        op("dve", lambda q: q.tensor_copy(out=maskbd[:], in_=FF[:, 384:512]), reads=ALLF, writes=["maskbd"])
        op("dve", lambda q: q.tensor_copy(out=startmask[:], in_=FF[:, 512:1024]), reads=ALLF, writes=["startmask"])
        op("dve", lambda q: q.tensor_copy(out=invf[:], in_=FF[:, 1024:1032]), reads=ALLF, writes=["invf"])
        op("pool", lambda q: q.memset(ones[:], 1.0), writes=["ones"])
        op("pool", lambda q: q.memset(small[:, NHALF:NHALF + 4], -0.5), writes=["nhalf"])
        op("pool", lambda q: q.memset(vaug[:], 1.0), writes=["vaug0", "vaug1"])
        op("sp", lambda q: q.dma_start(out=small[:, PREW0:PREW0 + 8], in_=prew_d[0:1, :].rearrange("o (k p) -> p (o k)", p=128),
                                       allow_slow_non_contiguous=True), writes=["prew", "smq"], dma="sm")
        op("sp", lambda q: q.dma_start(out=small[:, PREW1:PREW1 + 8], in_=prew_d[1:2, :].rearrange("o (k p) -> p (o k)", p=128),
                                       allow_slow_non_contiguous=True), writes=["prew", "smq"], dma="sm")
        op("sp", lambda q: q.dma_start(out=small[:, LBA:LBA + 8], in_=lb_d[0:1, :].rearrange("o (k p) -> p (o k)", p=128),
                                       allow_slow_non_contiguous=True), writes=["lb0", "smq"], dma="sm")
        op("sp", lambda q: q.dma_start(out=small[:, LBB:LBB + 8], in_=lb_d[1:2, :].rearrange("o (k p) -> p (o k)", p=128),
                                       allow_slow_non_contiguous=True), writes=["lb1", "smq"], dma="sm")
        op("sp", lambda q: q.dma_start(out=small[:, GNW:GNW + 1], in_=gnw_d.rearrange("o p -> p o"),
                                       allow_slow_non_contiguous=True), writes=["gnw", "smq"], dma="sm")
        op("sp", lambda q: q.dma_start(out=small[:, ESINK:ESINK + 16], in_=sink_d.partition_broadcast(128)), writes=["esink", "smq"], dma="sm")
        op("sp", lambda q: q.dma_start(out=wpost[:, 0, :], in_=postw_d[0:1, :].partition_broadcast(128)), writes=["wpost", "smq"], dma="sm")
        op("sp", lambda q: q.dma_start(out=wpost[:, 1, :], in_=postw_d[1:2, :].partition_broadcast(128)), writes=["wpost", "smq"], dma="sm")
        op("sp", lambda q: q.dma_start(out=posi[:], in_=pos_d), writes=["posi", "smq"], dma="sm")
        op("dve", lambda q: q.tensor_scalar(out=sv(SCQ0, 8), in0=sv(PREW0, 8), scalar1=0.125, scalar2=None, op0=ALU.mult), reads=["prew"], writes=["scq0"])
        op("dve", lambda q: q.tensor_scalar(out=sv(SCH0, 8), in0=sv(PREW0, 8), scalar1=0.5, scalar2=None, op0=ALU.mult), reads=["prew"], writes=["sch0"])
        op("dve", lambda q: q.tensor_scalar(out=sv(SCH1, 8), in0=sv(PREW1, 8), scalar1=0.5, scalar2=None, op0=ALU.mult), reads=["prew"], writes=["sch1"])
        op("act", lambda q: q.activation(out=sv(ESINK, 16), in_=sv(ESINK, 16), func=AF.Exp), reads=["esink"], writes=["esink"])
        op("dve", lambda q: q.tensor_tensor(out=sv(LBNB, 8), in0=sv(LBB, 8), in1=sv(LBA, 8), op=ALU.subtract), reads=["lb0", "lb1"], writes=["lbt"])
        op("act", lambda q: q.activation(out=sv(LBNB, 8), in_=sv(LBNB, 8), func=AF.Tanh, scale=0.5), reads=["lbt"], writes=["lbt"])
        op("dve", lambda q: q.tensor_scalar(out=sv(LBA, 8), in0=sv(LBNB, 8), scalar1=0.25, scalar2=0.75, op0=ALU.mult, op1=ALU.add), reads=["lbt"], writes=["lb0"])
        op("dve", lambda q: q.tensor_scalar(out=sv(LBB, 8), in0=sv(LBNB, 8), scalar1=-0.25, scalar2=0.25, op0=ALU.mult, op1=ALU.add), reads=["lbt"], writes=["lb1"])
        op("dve", lambda q: q.tensor_scalar(out=sv(LBNB, 8), in0=sv(LBB, 8), scalar1=-1.0, scalar2=None, op0=ALU.mult), reads=["lb1", "lbt"], writes=["lbt"])
        LBK = ["lb0", "lb1", "lbt"]

        o0 = 1040
        posf = FF[:, o0:o0 + NB]
        ang = FF[:, o0 + NB:o0 + NB + NB * 8]
        tmpu = FF[:, o0 + 9 * NB:o0 + 17 * NB]
        tmpk = FF[:, o0 + 17 * NB:o0 + 25 * NB]
        assert o0 + 25 * NB <= 4096
        tmpi = hb[:].bitcast(I32)[:, 0:NB * 8]
        op("dve", lambda q: q.tensor_copy(out=posf, in_=posi[:]), reads=["posi"] + ALLF, writes=ALLF)
        op("dve", lambda q: q.tensor_tensor(out=ang.rearrange("p (b i) -> p b i", i=8),
                                            in0=posf.unsqueeze(2).to_broadcast([128, NB, 8]),
                                            in1=invf[:].unsqueeze(1).to_broadcast([128, NB, 8]), op=ALU.mult),
           reads=ALLF + ["invf"], writes=ALLF)
        for which, tab in ((0, sinT), (1, cosT)):
            if which == 1:
                op("dve", lambda q: q.tensor_scalar(out=ang, in0=ang, scalar1=math.pi / 2, scalar2=None, op0=ALU.add), reads=ALLF, writes=ALLF)
            op("dve", lambda q: q.tensor_scalar(out=tmpu, in0=ang, scalar1=1.0 / TWO_PI, scalar2=None, op0=ALU.mult), reads=ALLF, writes=ALLF)
            op("dve", lambda q: q.tensor_copy(out=tmpi, in_=tmpu), reads=ALLF, writes=["hb"])
            op("dve", lambda q: q.tensor_copy(out=tmpk, in_=tmpi), reads=["hb"], writes=ALLF)
            op("dve", lambda q: q.scalar_tensor_tensor(out=tmpu, in0=tmpk, scalar=-C1, in1=ang, op0=ALU.mult, op1=ALU.add), reads=ALLF, writes=ALLF)
            op("dve", lambda q: q.scalar_tensor_tensor(out=tmpu, in0=tmpk, scalar=-C2, in1=tmpu, op0=ALU.mult, op1=ALU.add), reads=ALLF, writes=ALLF)
            op("dve", lambda q: q.tensor_scalar(out=tmpu, in0=tmpu, scalar1=-PI_LO, scalar2=PI_LO, op0=ALU.max, op1=ALU.min), reads=ALLF, writes=ALLF)
            op("act", lambda q, tab=tab: q.activation(out=tab[:].rearrange("p b i -> p (b i)"), in_=tmpu, func=AF.Sin),
               reads=ALLF, writes=["cosT" if which == 1 else "sinT"])

        BROWS = [(0, 1024, 0), (1024, 1792, 32), (1792, 2304, 64)]
        for (c0, c1, p) in BROWS:
            op("sp", lambda q, c0=c0, c1=c1, p=p: q.dma_start(out=FF[p:p + 1, 0:c1 - c0], in_=b0_d[0:1, c0:c1]),
               reads=["cosT", "sinT"], writes=ALLF + ["smq"], dma="sm")
        op("sp", lambda q: q.dma_start(out=FF[0:1, 1024:2048], in_=bo0_d), writes=ALLF + ["smq"], dma="sm")
        op("dve", lambda q: q.tensor_scalar(out=browA[0:1, 0:1024], in0=FF[0:1, 0:1024], scalar1=0.125, scalar2=None, op0=ALU.mult), reads=ALLF, writes=["browA"])
        op("dve", lambda q: q.tensor_copy(out=browA[32:33, 0:256], in_=FF[32:33, 0:256]), reads=ALLF, writes=["browA"])
        op("dve", lambda q: q.tensor_scalar(out=browA[32:33, 256:768], in0=FF[32:33, 256:768], scalar1=0.5, scalar2=None, op0=ALU.mult), reads=ALLF, writes=["browA"])
        op("dve", lambda q: q.tensor_scalar(out=browA[64:65, 0:512], in0=FF[64:65, 0:512], scalar1=0.5, scalar2=None, op0=ALU.mult), reads=ALLF, writes=["browA"])
        op("dve", lambda q: q.tensor_copy(out=browB[:], in_=FF[0:1, 1024:2048]), reads=ALLF, writes=["browB"])

        def brow(c0, n):
            for (r0, r1, p) in BROWS:
                if r0 <= c0 and c0 + n <= r1:
                    return ones[p:p + 1, :], browA[p:p + 1, c0 - r0:c0 - r0 + n]
            raise AssertionError((c0, n))

        stageA = FF
        stageB = xb[1][:].rearrange("p j d -> p (j d)")
        XB1K = [("x", 1, j) for j in range(4)]
        stg = [(stageA, ALLF), (stageB, XB1K)]
        stb = [(og_all[:].rearrange("p j d -> p (j d)"), [("og", j) for j in range(4)]), (hT[:].rearrange("p k t -> p (k t)"), [("hT", j) for j in range(4)])]
        cnt = [0]

        def conv(eng, out, in_, scal, rd, wr):
            if eng == "dve":
                op("dve", lambda q: q.tensor_scalar(out=out, in0=in_, scalar1=scal, scalar2=None, op0=ALU.mult), reads=rd, writes=wr)
            else:
                op("act", lambda q: q.activation(out=out, in_=in_, func=AF.Copy, scale=scal), reads=rd, writes=wr)

        for kc in range(8):
            sg, sk = stg[cnt[0] % 2]
            cnt[0] += 1
            op("sp", lambda q, sg=sg, kc=kc: q.dma_start(out=sg[:, 0:2304], in_=w0_d[kc * 128:(kc + 1) * 128, :]),
               reads=["browA", "browB"], writes=sk, dma="wl%d" % (cnt[0] % 2))
            conv("dve", W0[:, kc, 0:1024], sg[:, 0:1024], sv(SCQ0 + kc), sk + ["scq0"], ["W0"])
            conv("act", W0[:, kc, 1024:1280], sg[:, 1024:1280], sv(PREW0 + kc), sk + ["prew"], ["W0"])
            conv("act", W0[:, kc, 1280:2304], sg[:, 1280:2304], sv(SCH0 + kc), sk + ["sch0"], ["W0"])
        for kc in range(8):
            sg, sk = stg[cnt[0] % 2]
            cnt[0] += 1
            op("sp", lambda q, sg=sg, kc=kc: q.dma_start(out=sg[:, 0:1024], in_=wo0_d[kc * 128:(kc + 1) * 128, :]),
               writes=sk, dma="wl%d" % (cnt[0] % 2))
            op("sp", lambda q, sg=sg, kc=kc: q.dma_start(out=sg[:, 1024:2048], in_=wo1_d[kc * 128:(kc + 1) * 128, :]),
               writes=sk, dma="wl%d" % (cnt[0] % 2))
            op("act", lambda q, sg=sg, kc=kc: q.activation(out=Wo0[:, kc, :], in_=sg[:, 0:1024], func=AF.Copy), reads=sk, writes=["Wo0"])
            conv("dve", Wo1[:, kc, :], sg[:, 1024:2048], sv(GNW), sk + ["gnw"], ["Wo1"])
        for kc in range(8):
            sg, sk = stg[cnt[0] % 2]
            sbt, sbk = stb[cnt[0] % 2]
            cnt[0] += 1
            op("sp", lambda q, sg=sg, kc=kc: q.dma_start(out=sg[:, 0:4096], in_=w1_d[kc * 128:(kc + 1) * 128, :]),
               writes=sk, dma="wl%d" % (cnt[0] % 2))
            for t in range(4):
                o_ap = sbt.rearrange("p (h t c) -> p t h c", h=8, t=4, c=128)[:, t, :, :]
                i_ap = sg[:, t * 1024:(t + 1) * 1024].rearrange("p (h c) -> p h c", h=8)
                scal = sv(PREW1 + kc) if t == 2 else sv(SCH1 + kc)
                conv("dve" if t % 2 == 0 else "act", o_ap, i_ap, scal, sk + ["prew", "sch1"], sbk)
            op("sp", lambda q, sbt=sbt, kc=kc: q.dma_start(out=w1s_d[:, :, kc, :].rearrange("h p c -> p h c"),
                                                          in_=sbt.rearrange("p (h c) -> p h c", h=8)),
               reads=sbk, writes=["w1s"], dma="ws%d" % (cnt[0] % 2))

        def xk(buf, j):
            return ("x", buf, j)

        def load_x(m):
            buf = m % 2
            op("sp", lambda q: q.dma_start(out=xb[buf][:], in_=x_d[m * 512:(m + 1) * 512, :].rearrange("(j p) d -> p j d", p=128)),
               writes=[xk(buf, j) for j in range(4)], dma="xl%d" % buf)

        def load_w1h(h, slot):
            op("sp", lambda q: q.dma_start(out=W1h[slot][:], in_=w1s_d[h]), reads=["w1s"], writes=["W1h%d" % slot], dma="w1l%d" % slot)

        def rms_rstd(src_ap, src_keys, col):
            op("act", lambda q: q.activation(out=hb[:], in_=src_ap, func=AF.Square, accum_out=sv(SS + col)),
               reads=src_keys, writes=["hb", ("ss", col)])
            op("dve", lambda q: q.tensor_scalar(out=sv(SS + col), in0=sv(SS + col), scalar1=1.0 / 1024, scalar2=EPS, op0=ALU.mult, op1=ALU.add),
               reads=[("ss", col)], writes=[("ss", col)])
            op("pool", lambda q: q.tensor_tensor(out=sv(RSTD + col), in0=sv(SS + col), in1=sv(NHALF), op=ALU.pow),
               reads=[("ss", col), "nhalf"], writes=[("rstd", col)])

        TB = L1_BANKS[7]

        def make_hT(buf, j, col, tb=TB):
            xs = xb[buf][:, j, :]
            rms_rstd(xs, [xk(buf, j)], col)
            op("act", lambda q: q.activation(out=hb[:], in_=xs, func=AF.Copy, scale=sv(RSTD + col)),
               reads=[xk(buf, j), ("rstd", col)], writes=["hb"])
            for kc in range(8):
                op("pe", lambda q, kc=kc: q.transpose(out=ptv(tb)[:, kc * 128:(kc + 1) * 128], in_=hb[:, kc * 128:(kc + 1) * 128], identity=ident[:]),
                   reads=["hb", "ident"], writes=[("pa", tb)])
            op("act", lambda q: q.activation(out=hT[:, :, j * 128:(j + 1) * 128], in_=ptv(tb).rearrange("p (k t) -> p k t", k=8), func=AF.Copy),
               reads=[("pa", tb)], writes=[("hT", j)])

        def post_norm_residual(buf, j, layer, pbanks):
            c0 = 4
            for hf in range(2):
                b = pbanks[hf]
                op("act", lambda q, b=b, hf=hf: q.activation(out=Fh(6 + hf), in_=pab[b][:], func=AF.Square, accum_out=sv(SS2 + hf)),
                   reads=[("pa", b)], writes=[FhK(6 + hf), ("ss2", hf)])
            op("dve", lambda q: q.tensor_scalar(out=sv(SS2, 2), in0=sv(SS2, 2), scalar1=1.0 / 1024, scalar2=EPS / 2, op0=ALU.mult, op1=ALU.add),
               reads=[("ss2", 0), ("ss2", 1)], writes=[("ss2", 0), ("ss2", 1)])
            op("dve", lambda q: q.tensor_tensor(out=sv(SS2), in0=sv(SS2), in1=sv(SS2 + 1), op=ALU.add),
               reads=[("ss2", 0), ("ss2", 1)], writes=[("ss2", 0)])
            op("pool", lambda q: q.tensor_tensor(out=sv(RSTD2), in0=sv(SS2), in1=sv(NHALF), op=ALU.pow),
               reads=[("ss2", 0), "nhalf"], writes=["rstd2"])
            for hf in range(2):
                b = pbanks[hf]
                op("dve", lambda q, b=b, hf=hf: q.scalar_tensor_tensor(out=Fh(6 + hf), in0=pab[b][:], scalar=sv(RSTD2),
                                                                        in1=wpost[:, layer, hf * 512:(hf + 1) * 512], op0=ALU.mult, op1=ALU.mult),
                   reads=[("pa", b), "rstd2", "wpost"], writes=[FhK(6 + hf)])
            op("pool", lambda q: q.tensor_tensor(out=xb[buf][:, j, :], in0=xb[buf][:, j, :], in1=F(3), op=ALU.add),
               reads=[xk(buf, j), FhK(6), FhK(7)], writes=[xk(buf, j)])

        def out_proj(src_bf16_ap, src_keys, Wo, wo_key, bias, ybanks, tb=TB):
            for kc in range(8):
                op("pe", lambda q, kc=kc: q.transpose(out=ptv(tb)[:, kc * 128:(kc + 1) * 128], in_=src_bf16_ap[:, kc * 128:(kc + 1) * 128], identity=ident[:]),
                   reads=src_keys + ["ident"], writes=[("pa", tb)])
            op("act", lambda q: q.activation(out=ogT[:].rearrange("p k t -> p (k t)"), in_=ptv(tb), func=AF.Copy),
               reads=[("pa", tb)], writes=["ogT"])
            banks = list(ybanks)
            for hf in range(2):
                b = banks[hf]
                for kc in range(8):
                    op("pe", lambda q, b=b, kc=kc, hf=hf: q.matmul(pab[b][:], lhsT=ogT[:, kc, :], rhs=Wo[:, kc, hf * 512:(hf + 1) * 512],
                                                                   start=(kc == 0), stop=(kc == 7 and not bias)),
                       reads=["ogT", wo_key], writes=[("pa", b)])
                if bias:
                    op("pe", lambda q, b=b, hf=hf: q.matmul(pab[b][:], lhsT=ones[0:1, :], rhs=browB[0:1, hf * 512:(hf + 1) * 512], start=False, stop=True),
                       reads=["ones", "browB"], writes=[("pa", b)])
            return banks

        def proj_tm(b, j, c0, n, Wt, wkey, bias=True):
            for kc in range(8):
                op("pe", lambda q, kc=kc: q.matmul(pab[b][:, 0:n], lhsT=hT[:, kc, j * 128:(j + 1) * 128], rhs=Wt[:, kc, c0:c0 + n],
                                                   start=(kc == 0), stop=(kc == 7 and not bias)),
                   reads=[("hT", j), wkey], writes=[("pa", b)])
            if bias:
                o1, br = brow(c0, n)
                op("pe", lambda q: q.matmul(pab[b][:, 0:n], lhsT=o1, rhs=br, start=False, stop=True),
                   reads=["ones", "browA"], writes=[("pa", b)])
            return b

        def l0_s1(m, j):
            buf = m % 2
            blk = m * 4 + j
            slot = blk % 2
            S.tag = "m%d.L0.%d.s1" % (m, j)
            make_hT(buf, j, j, 2)
            bq = [proj_tm(0, j, 0, 512, W0, "W0"), proj_tm(1, j, 512, 512, W0, "W0")]
            for a in range(2):
                op("act", lambda q, a=a: q.activation(out=qkb[:, 8 * a:8 * a + 8, :], in_=pab[bq[a]][:].rearrange("p (h d) -> p h d", h=8), func=AF.Copy),
                   reads=[("pa", bq[a])], writes=["qkb"])
                op("act", lambda q, a=a: q.activation(out=qkr[:, 8 * a:8 * a + 8, :], in_=pab[bq[a]][:].rearrange("p (h d) -> p h d", h=8)[:, :, 0:16], func=AF.Copy),
                   reads=[("pa", bq[a])], writes=["qkr"])
            bkv = proj_tm(0, j, 1024, 256, W0, "W0")
            op("act", lambda q: q.activation(out=qkb[:, 16:18, :], in_=pab[bkv][:, 0:128].rearrange("p (h d) -> p h d", h=2), func=AF.Copy),
               reads=[("pa", bkv)], writes=["qkb"])
            op("act", lambda q: q.activation(out=qkr[:, 16:18, :], in_=pab[bkv][:, 0:128].rearrange("p (h d) -> p h d", h=2)[:, :, 0:16], func=AF.Copy),
               reads=[("pa", bkv)], writes=["qkr"])
            op("act", lambda q: q.activation(out=vaug[:, slot, :, 0:64], in_=pab[bkv][:, 128:256].rearrange("p (g d) -> p g d", g=2), func=AF.Copy),
               reads=[("pa", bkv)], writes=["vaug%d" % slot])
            cb = cosT[:, blk, :].unsqueeze(1).to_broadcast([128, 18, 8])
            sbb = sinT[:, blk, :].unsqueeze(1).to_broadcast([128, 18, 8])
            x1 = qkr[:, :, 0:8]
            x2 = qkr[:, :, 8:16]
            op("pool", lambda q: q.tensor_tensor(out=rt[:, 0], in0=x1, in1=cb, op=ALU.mult), reads=["qkr", "cosT"], writes=["rt0"])
            op("pool", lambda q: q.tensor_tensor(out=rt[:, 1], in0=x2, in1=sbb, op=ALU.mult), reads=["qkr", "sinT"], writes=["rt1"])
            op("pool", lambda q: q.tensor_tensor(out=rt[:, 2], in0=x2, in1=cb, op=ALU.mult), reads=["qkr", "cosT"], writes=["rt2"])
            op("pool", lambda q: q.tensor_tensor(out=rt[:, 3], in0=x1, in1=sbb, op=ALU.mult), reads=["qkr", "sinT"], writes=["rt3"])
            op("dve", lambda q: q.tensor_tensor(out=qkb[:, :, 0:8], in0=rt[:, 0], in1=rt[:, 1], op=ALU.subtract), reads=["rt0", "rt1"], writes=["qkb"])
            op("dve", lambda q: q.tensor_tensor(out=qkb[:, :, 8:16], in0=rt[:, 2], in1=rt[:, 3], op=ALU.add), reads=["rt2", "rt3"], writes=["qkb"])
            tbk = (2, 2)
            for a in range(2):
                for hh in range(8):
                    op("pe", lambda q, a=a, hh=hh: q.transpose(out=ptv(tbk[a])[0:64, hh * 128:(hh + 1) * 128], in_=qkb[:, 8 * a + hh, :], identity=ident[:]),
                       reads=["qkb", "ident"], writes=[("pa", tbk[a])])
                op("act", lambda q, a=a: q.activation(out=qT[:, 8 * a:8 * a + 8, :], in_=ptv(tbk[a])[0:64, :].rearrange("p (h t) -> p h t", h=8), func=AF.Copy),
                   reads=[("pa", tbk[a])], writes=["qT"])
            for g in range(2):
                op("pe", lambda q, g=g: q.transpose(out=ptv(2)[0:64, g * 128:(g + 1) * 128], in_=qkb[:, 16 + g, :], identity=ident[:]),
                   reads=["qkb", "ident"], writes=[("pa", 2)])
            op("dve", lambda q: q.tensor_copy(out=kT[:, slot, :, :], in_=ptv(2)[0:64, 0:256].rearrange("p (g t) -> p g t", g=2)),
               reads=[("pa", 2)], writes=["kT%d" % slot])

        def l0_s2(m, j):
            blk = m * 4 + j
            first = (blk % (SEQ // 128) == 0)
            slot = blk % 2
            S.tag = "m%d.L0.%d.s2" % (m, j)
            oa = [3, 4, 5]
            pti = 0
            sb_i = 0
            for g in range(2):
                for a in range(2):
                    kbs = ([] if first else [(1 - slot, mask_prev, "mask_prev")]) + [(slot, mask_cur, "mask_cur")]
                    pts = []
                    for (ks, msk, mkey) in kbs:
                        b = 6 + sb_i % 2
                        sb_i += 1
                        op("pe", lambda q, b=b, ks=ks, g=g, a=a: q.matmul(pab[b][:], lhsT=kT[:, ks, g, :],
                                                                          rhs=qT[:, 8 * g + 4 * a:8 * g + 4 * a + 4, :].rearrange("p h t -> p (h t)"),
                                                                          start=True, stop=True),
                           reads=["kT%d" % ks, "qT"], writes=[("pa", b)])
                        p = pti % 4
                        pti += 1
                        op("act", lambda q, b=b, p=p: q.activation(out=PT[p][:], in_=pab[b][:], func=AF.Exp), reads=[("pa", b)], writes=[("PT", p)])
                        op("dve", lambda q, p=p, msk=msk: q.tensor_tensor(out=PT[p][:].rearrange("p (h t) -> p h t", h=4),
                                                                         in0=PT[p][:].rearrange("p (h t) -> p h t", h=4),
                                                                         in1=msk[:].unsqueeze(1).to_broadcast([128, 4, 128]), op=ALU.mult),
                           reads=[("PT", p), mkey], writes=[("PT", p)])
                        pts.append((p, ks))
                    for hh in range(4):
                        h = 8 * g + 4 * a + hh
                        ob, oo = oa[h // 7], (h % 7) * 65
                        for i, (p, ks) in enumerate(pts):
                            op("pe", lambda q, p=p, ks=ks, hh=hh, ob=ob, oo=oo, i=i, n=len(pts), g=g: q.matmul(
                                pab[ob][:, oo:oo + 65], lhsT=PT[p][:, hh * 128:(hh + 1) * 128], rhs=vaug[:, ks, g, :],
                                start=(i == 0), stop=(i == n - 1)),
                               reads=[("PT", p), "vaug%d" % ks], writes=[("pa", ob)])
            for bi in range(3):
                h0 = 7 * bi
                nh = min(7, 16 - h0)
                ov = pab[oa[bi]][:, 0:nh * 65].rearrange("p (h d) -> p h d", d=65)
                op("dve", lambda q, ov=ov, h0=h0, nh=nh: q.tensor_tensor(out=sv(DEN + h0, nh), in0=ov[:, :, 64], in1=sv(ESINK + h0, nh), op=ALU.add),
                   reads=[("pa", oa[bi]), "esink"], writes=[("den", bi)])
            op("dve", lambda q: q.reciprocal(out=sv(RDEN, 16), in_=sv(DEN, 16)), reads=[("den", 0), ("den", 1), ("den", 2)], writes=["rden"])
            for bi in range(3):
                h0 = 7 * bi
                nh = min(7, 16 - h0)
                ov = pab[oa[bi]][:, 0:nh * 65].rearrange("p (h d) -> p h d", d=65)
                op("dve", lambda q, ov=ov, h0=h0, nh=nh: q.tensor_tensor(out=F(0).rearrange("p (h d) -> p h d", d=64)[:, h0:h0 + nh, :], in0=ov[:, :, 0:64],
                                                                         in1=sv(RDEN + h0, nh).unsqueeze(2).to_broadcast([128, nh, 64]), op=ALU.mult),
                   reads=[("pa", oa[bi]), "rden"], writes=[FhK(0), FhK(1)])

        def l0_s3(m, j):
            buf = m % 2
            S.tag = "m%d.L0.%d.s3" % (m, j)
            bz = [proj_tm(3, j, 1280, 512, W0, "W0"), proj_tm(4, j, 1792, 512, W0, "W0")]
            for hf in range(2):
                op("act", lambda q, hf=hf: q.activation(out=Fh(2 + hf), in_=pab[bz[hf]][:], func=AF.Tanh), reads=[("pa", bz[hf])], writes=[FhK(2 + hf)])
                op("dve", lambda q, hf=hf: q.scalar_tensor_tensor(out=Fh(2 + hf), in0=Fh(2 + hf), scalar=1.0, in1=pab[bz[hf]][:], op0=ALU.add, op1=ALU.mult),
                   reads=[FhK(2 + hf), ("pa", bz[hf])], writes=[FhK(2 + hf)])
            op("dve", lambda q: q.tensor_tensor(out=og_all[:, 0, :], in0=F(0), in1=F(1), op=ALU.mult),
               reads=[FhK(0), FhK(1), FhK(2), FhK(3)], writes=[("og", 0)])
            yb = out_proj(og_all[:, 0, :], [("og", 0)], Wo0, "Wo0", True, (5, 3), 2)
            post_norm_residual(buf, j, 0, yb)

        def l1_A(m, h):
            ws = h % 2
            Wt = W1h[ws]
            wk = "W1h%d" % ws
            db = h % 2
            S.tag = "m%d.L1.h%d.A" % (m, h)
            bq, bf, bv, bzz = L1_BANKS[0:4]
            hTk = [("hT", j) for j in range(4)]
            for (b, c0) in ((bq, 0), (bf, 128)):
                for kc in range(8):
                    op("pe", lambda q, b=b, c0=c0, kc=kc: q.matmul(pab[b][:], lhsT=Wt[:, kc, c0:c0 + 128], rhs=hT[:, kc, :], start=(kc == 0), stop=(kc == 7)),
                       reads=hTk + [wk], writes=[("pa", b)])
            for (b, c0) in ((bv, 256), (bzz, 384)):
                for j in range(4):
                    for kc in range(8):
                        op("pe", lambda q, b=b, c0=c0, kc=kc, j=j: q.matmul(pab[b][:, j * 128:(j + 1) * 128], lhsT=hT[:, kc, j * 128:(j + 1) * 128],
                                                                            rhs=Wt[:, kc, c0:c0 + 128], start=(kc == 0), stop=(kc == 7)),
                           reads=[("hT", j), wk], writes=[("pa", b)])
            if h + 2 < 8:
                load_w1h(h + 2, ws)
            A0, A1, A2, A3, A4, A5, A6 = [Fh(i) for i in range(7)]
            K0, K1, K2, K3, K4, K5, K6 = [FhK(i) for i in range(7)]
            gz = Fh(7) if db == 0 else GZ[0][:]
            gzk = FhK(7) if db == 0 else "gz0"
            op("act", lambda q: q.activation(out=A0, in_=pab[bq][:], func=AF.Tanh), reads=[("pa", bq)], writes=[K0])
            op("dve", lambda q: q.scalar_tensor_tensor(out=A0, in0=A0, scalar=1.0, in1=pab[bq][:], op0=ALU.add, op1=ALU.mult),
               reads=[K0, ("pa", bq)], writes=[K0])
            op("act", lambda q: q.activation(out=A1, in_=pab[bf][:], func=AF.Tanh), reads=[("pa", bf)], writes=[K1])
            op("act", lambda q: q.activation(out=A2, in_=A1, func=AF.Identity, scale=sv(LBB + h), bias=sv(LBA + h)), reads=[K1] + LBK, writes=[K2])
            op("act", lambda q: q.activation(out=A3, in_=A1, func=AF.Identity, scale=sv(LBNB + h), bias=sv(LBB + h)), reads=[K1] + LBK, writes=[K3])
            op("dve", lambda q: q.tensor_tensor_scan(out=A4, data0=startmask[:], data1=A2, initial=0.0, op0=ALU.max, op1=ALU.mult),
               reads=["startmask", K2], writes=[K4])
            op("dve", lambda q: q.tensor_tensor(out=qeT[db][:], in0=A0, in1=A4, op=ALU.mult), reads=[K0, K4], writes=[("qeT", db)])
            op("dve", lambda q: q.reciprocal(out=A5, in_=A4), reads=[K4], writes=[K5], dur=3.3)
            op("dve", lambda q: q.tensor_tensor(out=A6, in0=A3, in1=A5, op=ALU.mult), reads=[K3, K5], writes=[K6])
            op("act", lambda q: q.activation(out=keT[db][:], in_=A6, func=AF.Copy), reads=[K6], writes=[("keT", db)])
            op("dve", lambda q: q.tensor_tensor(out=kdT[db][:].rearrange("p (c t) -> p c t", t=64), in0=A6.rearrange("p (c t) -> p c t", t=64),
                                                in1=A4.rearrange("p (c t) -> p c t", t=64)[:, :, 63:64].to_broadcast([128, 8, 64]), op=ALU.mult),
               reads=[K6, K4], writes=[("kdT", db)])
            op("act", lambda q: q.activation(out=glast[:, db, :], in_=A4.rearrange("p (c t) -> p c t", t=64)[:, :, 63], func=AF.Copy),
               reads=[K4], writes=[("glast", db)])
            op("act", lambda q: q.activation(out=vb[db][:].rearrange("p j v -> p (j v)"), in_=pab[bv][:], func=AF.Copy), reads=[("pa", bv)], writes=[("vb", db)])
            op("act", lambda q: q.activation(out=gz, in_=pab[bzz][:], func=AF.Tanh), reads=[("pa", bzz)], writes=[gzk])
            op("dve", lambda q: q.scalar_tensor_tensor(out=gz, in0=gz, scalar=1.0, in1=pab[bzz][:], op0=ALU.add, op1=ALU.mult),
               reads=[gzk, ("pa", bzz)], writes=[gzk])

        def l1_B(m, h):
            db = h % 2
            S.tag = "m%d.L1.h%d.B" % (m, h)
            gz = Fh(7) if db == 0 else GZ[0][:]
            gzk = FhK(7) if db == 0 else "gz0"
            ub = [L1_BANKS[4], L1_BANKS[5]]
            bso = L1_BANKS[6]
            for j in range(4):
                op("pe", lambda q, j=j: q.transpose(out=ptv(TB)[:, j * 128:(j + 1) * 128], in_=kdT[db][:, j * 128:(j + 1) * 128], identity=ident[:]),
                   reads=[("kdT", db), "ident"], writes=[("pa", TB)])
            op("act", lambda q: q.activation(out=kd_tm[:].rearrange("p j k -> p (j k)"), in_=ptv(TB)[:, 0:512], func=AF.Copy), reads=[("pa", TB)], writes=["kd_tm"])
            for j in range(4):
                for c in range(2):
                    op("pe", lambda q, j=j, c=c: q.matmul(pab[ub[c]][:, j * 128:(j + 1) * 128], lhsT=kd_tm[64 * c:64 * c + 64, j, :],
                                                          rhs=vb[db][64 * c:64 * c + 64, j, :], start=True, stop=True),
                       reads=["kd_tm", ("vb", db)], writes=[("pa", ub[c])])
            for j in range(4):
                op("pe", lambda q, j=j: q.matmul(pab[bso][:, j * 128:(j + 1) * 128], lhsT=keT[db][:, j * 128:(j + 1) * 128], rhs=qeT[db][:, j * 128:(j + 1) * 128],
                                                 start=True, stop=True),
                   reads=[("keT", db), ("qeT", db)], writes=[("pa", bso)])
            op("dve", lambda q: q.tensor_tensor(out=smk[:], in0=pab[bso][:].rearrange("p (j t) -> p j t", j=4),
                                                in1=maskbd[:].unsqueeze(1).to_broadcast([128, 4, 128]), op=ALU.mult),
               reads=[("pa", bso), "maskbd"], writes=["smk"])
            for k in range(8):
                j, c = k // 2, k % 2
                op("act", lambda q, k=k: q.activation(out=Sbf[:, k, :], in_=S32[:, h, :], func=AF.Copy), reads=[("S32", h)], writes=[("Sbf", k)])
                op("dve", lambda q, k=k, j=j, c=c: q.scalar_tensor_tensor(out=S32[:, h, :], in0=S32[:, h, :], scalar=glast[:, db, k:k + 1],
                                                                             in1=pab[ub[c]][:, j * 128:(j + 1) * 128], op0=ALU.mult, op1=ALU.add),
                   reads=[("S32", h), ("glast", db), ("pa", ub[c])], writes=[("S32", h)])
            for j in range(4):
                op("pe", lambda q, j=j: q.matmul(pab[bso][:, j * 128:(j + 1) * 128], lhsT=smk[:, j, :], rhs=vb[db][:, j, :], start=True, stop=True),
                   reads=["smk", ("vb", db)], writes=[("pa", bso)])
                for c in range(2):
                    k = 2 * j + c
                    op("pe", lambda q, j=j, c=c, k=k: q.matmul(pab[bso][64 * c:64 * c + 64, j * 128:(j + 1) * 128],
                                                               lhsT=qeT[db][:, j * 128 + 64 * c:j * 128 + 64 * c + 64], rhs=Sbf[:, k, :],
                                                               start=False, stop=True, skip_group_check=True),
                       reads=[("qeT", db), ("Sbf", k)], writes=[("pa", bso)])
            for j in range(4):
                op("act", lambda q, j=j: q.activation(out=og_all[:, j, h * 128:(h + 1) * 128], in_=pab[bso][:, j * 128:(j + 1) * 128], func=AF.Square, accum_out=sv(SS3 + j)),
                   reads=[("pa", bso)], writes=[("og", j), ("ss3", j)])
            op("dve", lambda q: q.tensor_scalar(out=sv(SS3, 4), in0=sv(SS3, 4), scalar1=1.0 / 128, scalar2=EPS, op0=ALU.mult, op1=ALU.add),
               reads=[("ss3", j) for j in range(4)], writes=[("ss3", j) for j in range(4)])
            op("pool", lambda q: q.tensor_tensor(out=sv(RSTD3, 4), in0=sv(SS3, 4), in1=sv(NHALF, 4), op=ALU.pow),
               reads=[("ss3", j) for j in range(4)] + ["nhalf"], writes=[("rstd3", j) for j in range(4)])
            for j in range(4):
                op("dve", lambda q, j=j: q.scalar_tensor_tensor(out=og_all[:, j, h * 128:(h + 1) * 128], in0=pab[bso][:, j * 128:(j + 1) * 128],
                                                                 scalar=sv(RSTD3 + j), in1=gz[:, j * 128:(j + 1) * 128], op0=ALU.mult, op1=ALU.mult),
                   reads=[("pa", bso), ("rstd3", j), gzk], writes=[("og", j)])

        load_x(0)
        for m in range(NMT):
            S.new_epoch()
            buf = m % 2
            if m + 1 < NMT:
                load_x(m + 1)
            load_w1h(0, 0)
            load_w1h(1, 1)
            l0_s1(m, 0)
            for j in range(4):
                l0_s2(m, j)
                if j + 1 < 4:
                    l0_s1(m, j + 1)
                l0_s3(m, j)
            S.tag = "m%d.L1.pre" % m
            for j in range(4):
                make_hT(buf, j, j)
            if m % MT_PER_SEQ == 0:
                op("pool", lambda q: q.memset(S32[:], 0.0), writes=[("S32", h) for h in range(8)])
            l1_A(m, 0)
            for h in range(8):
                if h + 1 < 8:
                    l1_A(m, h + 1)
                l1_B(m, h)
            S.tag = "m%d.L1.out" % m
            for j in range(4):
                yb = out_proj(og_all[:, j, :], [("og", j)], Wo1, "Wo1", False, ((L1_BANKS[0], L1_BANKS[1]), (L1_BANKS[2], L1_BANKS[3]))[j % 2])
                post_norm_residual(buf, j, 1, yb)
            op("sp", lambda q, m=m, buf=buf: q.dma_start(out=out_d[m * 512:(m + 1) * 512, :].rearrange("(j p) d -> p j d", p=128), in_=xb[buf][:]),
               reads=[xk(buf, j) for j in range(4)], dma="xs%d" % buf)
        S.emit(st)
    build_program.last_sched = S
    return nc


_PROG_CACHE = {}


def _get_prog(nseq, seq):
    key = (nseq, seq)
    if key not in _PROG_CACHE:
        _PROG_CACHE[key] = build_program(nseq, seq)
    return _PROG_CACHE[key]


def make_in_maps(inputs, n_cores, nseq, seq):
    x = np.ascontiguousarray(inputs["x"], dtype=np.float32)
    pos = np.ascontiguousarray(inputs["positions"], dtype=np.int32)
    cst = make_consts()
    maps = []
    for c in range(n_cores):
        xs = x[c * nseq:(c + 1) * nseq, :seq].reshape(nseq * seq, D)
        ps = pos[c * nseq:(c + 1) * nseq, :seq].reshape(nseq * seq // 128, 128).T
        maps.append({
            "x": np.ascontiguousarray(xs),
            "pos": np.ascontiguousarray(ps),
            "cst": cst,
            "pre_norm_w": np.ascontiguousarray(inputs["pre_norm_w"], dtype=np.float32),
            "post_norm_w": np.ascontiguousarray(inputs["post_norm_w"], dtype=np.float32),
            "attn_w_in": np.ascontiguousarray(inputs["attn_w_in"][0], dtype=np.float32),
            "attn_b_in": np.ascontiguousarray(inputs["attn_b_in"], dtype=np.float32).reshape(1, 2304),
            "attn_sinks": np.ascontiguousarray(inputs["attn_sinks"], dtype=np.float32).reshape(1, 16),
            "attn_w_out": np.ascontiguousarray(inputs["attn_w_out"][0], dtype=np.float32),
            "attn_b_out": np.ascontiguousarray(inputs["attn_b_out"], dtype=np.float32).reshape(1, D),
            "rec_w_in": np.ascontiguousarray(inputs["rec_w_in"][0], dtype=np.float32),
            "rec_lb_logits": np.ascontiguousarray(inputs["rec_lb_logits"], dtype=np.float32),
            "rec_gnorm_w": np.ascontiguousarray(inputs["rec_gnorm_w"], dtype=np.float32).reshape(1, 128),
            "rec_w_out": np.ascontiguousarray(inputs["rec_w_out"][0], dtype=np.float32),
        })
    return maps


def kernel(**inputs):
    B, T, _ = inputs["x"].shape
    nseq = B // N_CORES
    nc = _get_prog(nseq, T)
    maps = make_in_maps(inputs, N_CORES, nseq, T)
    res = run_bass_kernel_spmd(nc, maps, core_ids=list(range(N_CORES)))
    outs = [np.asarray(r["out"], dtype=np.float32).reshape(nseq, T, D) for r in res.results]
    return np.concatenate(outs, axis=0)
```

```python
import math
from contextlib import ExitStack

import numpy as np
import concourse.bass as bass
import concourse.mybir as mybir
from concourse.bass_utils import run_bass_kernel_spmd

F32 = mybir.dt.float32
BF16 = mybir.dt.bfloat16
I32 = mybir.dt.int32
AF = mybir.ActivationFunctionType
ALU = mybir.AluOpType

N_CORES = 8
OA_BANKS = (3,)
SC_BANKS = (6, 7)
Z_BANKS = (6, 7)
Y_BANKS = (4, 5)
L1_BANKS = (4, 5, 6, 7, 2, 3, 0, 1)
D = 1024
EPS = 1e-6
TWO_PI = 2.0 * math.pi
C1 = 6.28125
C2 = TWO_PI - C1
PI_LO = 3.1415925


class _Op:
    __slots__ = ("eng", "fn", "idx", "deps", "signal", "sem", "count", "dma", "epoch", "tag", "dur", "start")


class _Rec:
    def __init__(self):
        self.call = None

    def __getattr__(self, name):
        def f(*a, **k):
            self.call = (name, a, k)
            return self
        return f


def _nelem(ap):
    n = 1
    for d in ap.shape[1:]:
        n *= int(d)
    return n


def _estimate_us(eng, fn, is_dma):
    r = _Rec()
    try:
        fn(r)
    except Exception:
        return 0.5
    if r.call is None:
        return 0.3
    name, a, k = r.call
    out = k.get("out", a[0] if a else None)
    try:
        if is_dma:
            esz = 2 if out.dtype == BF16 else 4
            nbytes = _nelem(out) * int(out.shape[0]) * esz
            return 2.0 + nbytes / 200e3
        if eng == "pe":
            if name == "transpose":
                return 0.10
            rhs = k.get("rhs")
            n = _nelem(rhs)
            return 0.02 + n * 0.00058
        n = _nelem(out) if out is not None else 64
        if eng == "act":
            return 0.26 + n * 0.00085 + (0.1 if k.get("accum_out") is not None else 0.0)
        if eng == "dve":
            f = 1.0
            if name == "reciprocal":
                f = 4.2
            elif name == "tensor_tensor_scan":
                f = 2.0
            elif name == "tensor_tensor":
                f = 1.6
            return 0.12 + n * 0.00104 * f
        if eng == "pool":
            if name == "tensor_tensor" and k.get("op") == ALU.pow:
                return 0.65
            return 0.3 + n * 0.0021
    except Exception:
        pass
    return 0.4


class Sched:
    EPOCH = 1500

    def __init__(self, nc):
        self.nc = nc
        self.ops = []
        self.last_w = {}
        self.readers = {}
        self.epoch = 0
        self.tag = ""

    def new_epoch(self):
        pass

    def op(self, eng, fn, reads=(), writes=(), dma=None, dur=None):
        o = _Op()
        o.eng, o.fn, o.idx, o.dma, o.epoch = eng, fn, len(self.ops), dma, 0
        o.signal = False
        o.tag = self.tag
        o.dur = dur if dur is not None else _estimate_us(eng, fn, dma is not None)
        deps = set()
        for k in reads:
            w = self.last_w.get(k)
            if w is not None:
                deps.add(w)
        for k in writes:
            w = self.last_w.get(k)
            if w is not None:
                deps.add(w)
            for r in self.readers.get(k, ()):
                deps.add(r)
        deps.discard(o.idx)
        o.deps = deps
        for k in writes:
            self.last_w[k] = o.idx
            self.readers[k] = []
        for k in reads:
            if k not in writes:
                self.readers.setdefault(k, []).append(o.idx)
        self.ops.append(o)
        return o

    def list_schedule(self, reorder=True):
        import heapq
        ops = self.ops
        n = len(ops)
        if not reorder:
            return {e: [o for o in ops if o.eng == e] for e in ("pe", "act", "dve", "pool", "sp")}, 0.0
        succ = [[] for _ in range(n)]
        indeg = [0] * n
        for o in ops:
            indeg[o.idx] = len(o.deps)
            for d in o.deps:
                succ[d].append(o.idx)
        ready_t = [0.0] * n
        free = {e: 0.0 for e in ("pe", "act", "dve", "pool", "sp")}
        heaps = {e: [] for e in free}
        for o in ops:
            if indeg[o.idx] == 0:
                heapq.heappush(heaps[o.eng], (0.0, o.idx))
        order = {e: [] for e in free}
        done = 0
        SEM_LAT = 0.15
        while done < n:
            best = None
            for e, h in heaps.items():
                if not h:
                    continue
                t0 = max(free[e], h[0][0])
                cand = None
                tmp = []
                while h and h[0][0] <= t0 and len(tmp) < 24:
                    tmp.append(heapq.heappop(h))
                pick = min(tmp, key=lambda x: x[1])
                for x in tmp:
                    if x is not pick:
                        heapq.heappush(h, x)
                heapq.heappush(h, pick)
                cand = (t0, pick[1], e, pick)
                if best is None or cand[:2] < best[:2]:
                    best = cand
            t0, idx, e, pick = best
            h = heaps[e]
            h.remove(pick)
            heapq.heapify(h)
            o = ops[idx]
            o.start = t0
            if o.dma is not None:
                free[e] = t0 + 0.06
                fin = t0 + o.dur
            else:
                free[e] = t0 + o.dur
                fin = free[e]
            order[e].append(o)
            done += 1
            for sidx in succ[idx]:
                so = ops[sidx]
                lat = 0.0 if (so.eng == e and e == "pe" and o.dma is None) else SEM_LAT
                ready_t[sidx] = max(ready_t[sidx], fin + lat)
                indeg[sidx] -= 1
                if indeg[sidx] == 0:
                    heapq.heappush(heaps[so.eng], (ready_t[sidx], sidx))
        return order, max(free.values())

    def emit(self, stack, reorder=True):
        nc = self.nc
        ops = self.ops
        order, makespan = self.list_schedule(reorder)
        self.makespan = makespan
        pos = {}
        for e, lst in order.items():
            for i, o in enumerate(lst):
                pos[o.idx] = i
        for o in ops:
            for d in o.deps:
                p = ops[d]
                if p.dma is None and p.eng == "pe" and o.eng == "pe" and o.dma is None:
                    assert pos[p.idx] < pos[o.idx]
                    continue
                p.signal = True
        counts = {}
        nsig = {}
        for e, lst in order.items():
            for o in lst:
                if o.dma is not None:
                    key = ("dma", o.dma)
                    counts[key] = counts.get(key, 0) + 16
                    o.sem, o.count = key, counts[key]
                elif o.signal:
                    k = nsig.get(e, 0)
                    nsig[e] = k + 1
                    o.epoch = k // self.EPOCH
                    key = (e, o.epoch)
                    counts[key] = counts.get(key, 0) + 1
                    o.sem, o.count = key, counts[key]
        sems = {}
        for key in counts:
            sems[key] = stack.enter_context(nc.semaphore("s_%s_%s" % key))
        self.n_sems = len(sems)
        final = dict(counts)

        def stream(eng_name):
            def body(eng):
                seen = {}
                for o in order[eng_name]:
                    need = {}
                    for d in o.deps:
                        p = ops[d]
                        if p.dma is None and p.eng == "pe" and eng_name == "pe" and o.dma is None:
                            continue
                        if p.dma is not None:
                            skey, val = ("dma", p.dma), (0, p.count)
                        else:
                            skey, val = ("eng", p.eng), (p.epoch, p.count)
                        if val > need.get(skey, (-1, -1)):
                            need[skey] = val
                    for skey, val in need.items():
                        if val <= seen.get(skey, (-1, -1)):
                            continue
                        seen[skey] = val
                        if skey[0] == "dma":
                            eng.wait_ge(sems[("dma", skey[1])], val[1])
                        else:
                            eng.wait_ge(sems[(skey[1], val[0])], val[1])
                    ins = o.fn(eng)
                    if o.dma is not None:
                        ins.then_inc(sems[o.sem], 16)
                    elif o.signal:
                        ins.then_inc(sems[o.sem], 1)
                if eng_name == "sp":
                    for key, c in final.items():
                        if key[0] == "dma":
                            eng.wait_ge(sems[key], c)
            return body

        with nc.Block() as block:
            block.tensor(stream("pe"))
            block.scalar(stream("act"))
            block.vector(stream("dve"))
            block.gpsimd(stream("pool"))
            block.sync(stream("sp"))


CST_W = 128 * 4 + 512 + 8


def make_consts():
    c = np.zeros((128, CST_W), np.float32)
    i = np.arange(128)
    c[:, 0:128] = np.eye(128, dtype=np.float32)
    c[:, 128:256] = (i[:, None] <= i[None, :]).astype(np.float32)
    c[:, 256:384] = (i[:, None] > i[None, :]).astype(np.float32)
    c[:, 384:512] = ((i[:, None] <= i[None, :]) & ((i[:, None] // 64) == (i[None, :] // 64))).astype(np.float32)
    sm = np.zeros(512, np.float32)
    sm[::64] = 1.0
    c[:, 512:1024] = sm[None, :]
    invf = (np.float32(500000.0) ** (-(np.arange(8, dtype=np.float32) * np.float32(2.0) / np.float32(16.0)))).astype(np.float32)
    c[:, 1024:1032] = invf[None, :]
    return c


def build_program(NSEQ, SEQ):
    NT = NSEQ * SEQ
    NB = NT // 128
    NMT = NT // 512
    MT_PER_SEQ = SEQ // 512
    nc = bass.Bass("TRN2", target_bir_lowering=False)

    def din(name, shape, dt=F32):
        return nc.dram_tensor(name, list(shape), dt, kind="ExternalInput").ap()

    x_d = din("x", [NT, D])
    pos_d = din("pos", [128, NB], I32)
    cst_d = din("cst", [128, CST_W])
    prew_d = din("pre_norm_w", [2, D])
    postw_d = din("post_norm_w", [2, D])
    w0_d = din("attn_w_in", [D, 2304])
    b0_d = din("attn_b_in", [1, 2304])
    sink_d = din("attn_sinks", [1, 16])
    wo0_d = din("attn_w_out", [D, D])
    bo0_d = din("attn_b_out", [1, D])
    w1_d = din("rec_w_in", [D, 4096])
    lb_d = din("rec_lb_logits", [2, D])
    gnw_d = din("rec_gnorm_w", [1, 128])
    wo1_d = din("rec_w_out", [D, D])
    out_d = nc.dram_tensor("out", [NT, D], F32, kind="ExternalOutput").ap()
    w1s_d = nc.dram_tensor("w1s", [8, 128, 8, 512], BF16, kind="Internal").ap()

    with ExitStack() as st:
        def sb(name, shape, dt):
            return st.enter_context(nc.sbuf_tensor(name, list(shape), dt))

        def ps(name, shape, dt):
            return st.enter_context(nc.psum_tensor(name, list(shape), dt))

        W0 = sb("W0", [128, 8, 2304], BF16)
        Wo0 = sb("Wo0", [128, 8, 1024], BF16)
        Wo1 = sb("Wo1", [128, 8, 1024], BF16)
        W1h = [sb("W1h%d" % i, [128, 8, 512], BF16) for i in range(2)]
        xb = [sb("xb%d" % i, [128, 4, 1024], F32) for i in range(2)]
        FF = sb("FF", [128, 4096], F32)
        hT = sb("hT", [128, 8, 512], BF16)
        og_all = sb("og_all", [128, 4, 1024], BF16)
        ident = sb("ident", [128, 128], BF16)
        mask_cur = sb("mask_cur", [128, 128], BF16)
        mask_prev = sb("mask_prev", [128, 128], BF16)
        maskbd = sb("maskbd", [128, 128], BF16)
        startmask = sb("startmask", [128, 512], F32)
        invf = sb("invf", [128, 8], F32)
        cosT = sb("cosT", [128, NB, 8], F32)
        sinT = sb("sinT", [128, NB, 8], F32)
        wpost = sb("wpost", [128, 2, 1024], F32)
        browA = sb("browA", [65, 1024], BF16)
        posi = sb("posi", [128, NB], I32)
        browB = sb("browB", [1, 1024], BF16)
        ones = sb("ones", [65, 128], BF16)
        small = sb("small", [128, 144], F32)
        S32 = sb("S32", [128, 8, 128], F32)
        Sbf = sb("Sbf", [128, 8, 128], BF16)
        hb = sb("hb", [128, 1024], BF16)
        qkb = sb("qkb", [128, 18, 64], BF16)
        qkr = sb("qkr", [128, 18, 16], F32)
        rt = sb("rt", [128, 4, 18, 8], F32)
        qT = sb("qT", [64, 16, 128], BF16)
        kT = sb("kT", [64, 2, 2, 128], BF16)
        vaug = sb("vaug", [128, 2, 2, 65], BF16)
        PT = [sb("PT%d" % i, [128, 512], BF16) for i in range(4)]
        ogT = sb("ogT", [128, 8, 128], BF16)
        qeT = [sb("qeT%d" % i, [128, 512], BF16) for i in range(2)]
        keT = [sb("keT%d" % i, [128, 512], BF16) for i in range(2)]
        kdT = [sb("kdT%d" % i, [128, 512], BF16) for i in range(2)]
        kd_tm = sb("kd_tm", [128, 4, 128], BF16)
        vb = [sb("vb%d" % i, [128, 4, 128], BF16) for i in range(2)]
        smk = sb("smk", [128, 4, 128], BF16)

        PREW0, PREW1 = 0, 8
        SCQ0, SCH0, SCH1 = 16, 24, 32
        LBA, LBB, LBNB = 40, 48, 56
        GNW = 64
        ESINK = 65
        SS = 81
        RSTD = 85
        DEN = 89
        RDEN = 105
        NHALF = 121
        SS2, RSTD2 = 125, 127
        SS3, RSTD3 = 128, 132

        def sv(c, n=1):
            return small[:, c:c + n]

        pab = [ps("pab%d" % i, [128, 512], F32) for i in range(8)]
        GZ = [sb("gz0", [128, 512], F32)]
        glast = sb("glast", [128, 2, 8], F32)

        def ptv(i):
            return pab[i][:].bitcast(BF16)

        S = Sched(nc)
        op = S.op
        FK = ["FF0", "FF1", "FF2", "FF3"]

        def F(i):
            return FF[:, i * 1024:(i + 1) * 1024]

        def Fh(i):
            return FF[:, i * 512:(i + 1) * 512]

        def FhK(i):
            return "FH%d" % i

        ALLF = FK + [FhK(i) for i in range(8)]

        op("sp", lambda q: q.dma_start(out=FF[:, 0:CST_W], in_=cst_d), writes=ALLF, dma="cst")
        op("dve", lambda q: q.tensor_copy(out=ident[:], in_=FF[:, 0:128]), reads=ALLF, writes=["ident"])
        op("dve", lambda q: q.tensor_copy(out=mask_cur[:], in_=FF[:, 128:256]), reads=ALLF, writes=["mask_cur"])
        op("dve", lambda q: q.tensor_copy(out=mask_prev[:], in_=FF[:, 256:384]), reads=ALLF, writes=["mask_prev"])
        op("dve", lambda q: q.tensor_copy(out=maskbd[:], in_=FF[:, 384:512]), reads=ALLF, writes=["maskbd"])
        op("dve", lambda q: q.tensor_copy(out=startmask[:], in_=FF[:, 512:1024]), reads=ALLF, writes=["startmask"])
        op("dve", lambda q: q.tensor_copy(out=invf[:], in_=FF[:, 1024:1032]), reads=ALLF, writes=["invf"])
        op("pool", lambda q: q.memset(ones[:], 1.0), writes=["ones"])
        op("pool", lambda q: q.memset(small[:, NHALF:NHALF + 4], -0.5), writes=["nhalf"])
        op("pool", lambda q: q.memset(vaug[:], 1.0), writes=["vaug0", "vaug1"])
        op("sp", lambda q: q.dma_start(out=small[:, PREW0:PREW0 + 8], in_=prew_d[0:1, :].rearrange("o (k p) -> p (o k)", p=128),
                                       allow_slow_non_contiguous=True), writes=["prew", "smq"], dma="sm")
        op("sp", lambda q: q.dma_start(out=small[:, PREW1:PREW1 + 8], in_=prew_d[1:2, :].rearrange("o (k p) -> p (o k)", p=128),
                                       allow_slow_non_contiguous=True), writes=["prew", "smq"], dma="sm")
        op("sp", lambda q: q.dma_start(out=small[:, LBA:LBA + 8], in_=lb_d[0:1, :].rearrange("o (k p) -> p (o k)", p=128),
                                       allow_slow_non_contiguous=True), writes=["lb0", "smq"], dma="sm")
        op("sp", lambda q: q.dma_start(out=small[:, LBB:LBB + 8], in_=lb_d[1:2, :].rearrange("o (k p) -> p (o k)", p=128),
                                       allow_slow_non_contiguous=True), writes=["lb1", "smq"], dma="sm")
        op("sp", lambda q: q.dma_start(out=small[:, GNW:GNW + 1], in_=gnw_d.rearrange("o p -> p o"),
                                       allow_slow_non_contiguous=True), writes=["gnw", "smq"], dma="sm")
        op("sp", lambda q: q.dma_start(out=small[:, ESINK:ESINK + 16], in_=sink_d.partition_broadcast(128)), writes=["esink", "smq"], dma="sm")
        op("sp", lambda q: q.dma_start(out=wpost[:, 0, :], in_=postw_d[0:1, :].partition_broadcast(128)), writes=["wpost", "smq"], dma="sm")
        op("sp", lambda q: q.dma_start(out=wpost[:, 1, :], in_=postw_d[1:2, :].partition_broadcast(128)), writes=["wpost", "smq"], dma="sm")
        op("sp", lambda q: q.dma_start(out=posi[:], in_=pos_d), writes=["posi", "smq"], dma="sm")
        op("dve", lambda q: q.tensor_scalar(out=sv(SCQ0, 8), in0=sv(PREW0, 8), scalar1=0.125, scalar2=None, op0=ALU.mult), reads=["prew"], writes=["scq0"])
        op("dve", lambda q: q.tensor_scalar(out=sv(SCH0, 8), in0=sv(PREW0, 8), scalar1=0.5, scalar2=None, op0=ALU.mult), reads=["prew"], writes=["sch0"])
        op("dve", lambda q: q.tensor_scalar(out=sv(SCH1, 8), in0=sv(PREW1, 8), scalar1=0.5, scalar2=None, op0=ALU.mult), reads=["prew"], writes=["sch1"])
        op("act", lambda q: q.activation(out=sv(ESINK, 16), in_=sv(ESINK, 16), func=AF.Exp), reads=["esink"], writes=["esink"])
        op("dve", lambda q: q.tensor_tensor(out=sv(LBNB, 8), in0=sv(LBB, 8), in1=sv(LBA, 8), op=ALU.subtract), reads=["lb0", "lb1"], writes=["lbt"])
        op("act", lambda q: q.activation(out=sv(LBNB, 8), in_=sv(LBNB, 8), func=AF.Tanh, scale=0.5), reads=["lbt"], writes=["lbt"])
        op("dve", lambda q: q.tensor_scalar(out=sv(LBA, 8), in0=sv(LBNB, 8), scalar1=0.25, scalar2=0.75, op0=ALU.mult, op1=ALU.add), reads=["lbt"], writes=["lb0"])
        op("dve", lambda q: q.tensor_scalar(out=sv(LBB, 8), in0=sv(LBNB, 8), scalar1=-0.25, scalar2=0.25, op0=ALU.mult, op1=ALU.add), reads=["lbt"], writes=["lb1"])
        op("dve", lambda q: q.tensor_scalar(out=sv(LBNB, 8), in0=sv(LBB, 8), scalar1=-1.0, scalar2=None, op0=ALU.mult), reads=["lb1", "lbt"], writes=["lbt"])
        LBK = ["lb0", "lb1", "lbt"]

        o0 = 1040
        posf = FF[:, o0:o0 + NB]
        ang = FF[:, o0 + NB:o0 + NB + NB * 8]
        tmpu = FF[:, o0 + 9 * NB:o0 + 17 * NB]
        tmpk = FF[:, o0 + 17 * NB:o0 + 25 * NB]
        assert o0 + 25 * NB <= 4096
        tmpi = hb[:].bitcast(I32)[:, 0:NB * 8]
        op("dve", lambda q: q.tensor_copy(out=posf, in_=posi[:]), reads=["posi"] + ALLF, writes=ALLF)
        op("dve", lambda q: q.tensor_tensor(out=ang.rearrange("p (b i) -> p b i", i=8),
                                            in0=posf.unsqueeze(2).to_broadcast([128, NB, 8]),
                                            in1=invf[:].unsqueeze(1).to_broadcast([128, NB, 8]), op=ALU.mult),
           reads=ALLF + ["invf"], writes=ALLF)
        for which, tab in ((0, sinT), (1, cosT)):
            if which == 1:
                op("dve", lambda q: q.tensor_scalar(out=ang, in0=ang, scalar1=math.pi / 2, scalar2=None, op0=ALU.add), reads=ALLF, writes=ALLF)
            op("dve", lambda q: q.tensor_scalar(out=tmpu, in0=ang, scalar1=1.0 / TWO_PI, scalar2=None, op0=ALU.mult), reads=ALLF, writes=ALLF)
            op("dve", lambda q: q.tensor_copy(out=tmpi, in_=tmpu), reads=ALLF, writes=["hb"])
            op("dve", lambda q: q.tensor_copy(out=tmpk, in_=tmpi), reads=["hb"], writes=ALLF)
            op("dve", lambda q: q.scalar_tensor_tensor(out=tmpu, in0=tmpk, scalar=-C1, in1=ang, op0=ALU.mult, op1=ALU.add), reads=ALLF, writes=ALLF)
            op("dve", lambda q: q.scalar_tensor_tensor(out=tmpu, in0=tmpk, scalar=-C2, in1=tmpu, op0=ALU.mult, op1=ALU.add), reads=ALLF, writes=ALLF)
            op("dve", lambda q: q.tensor_scalar(out=tmpu, in0=tmpu, scalar1=-PI_LO, scalar2=PI_LO, op0=ALU.max, op1=ALU.min), reads=ALLF, writes=ALLF)
            op("act", lambda q, tab=tab: q.activation(out=tab[:].rearrange("p b i -> p (b i)"), in_=tmpu, func=AF.Sin),
               reads=ALLF, writes=["cosT" if which == 1 else "sinT"])

        BROWS = [(0, 1024, 0), (1024, 1792, 32), (1792, 2304, 64)]
        for (c0, c1, p) in BROWS:
            op("sp", lambda q, c0=c0, c1=c1, p=p: q.dma_start(out=FF[p:p + 1, 0:c1 - c0], in_=b0_d[0:1, c0:c1]),
               reads=["cosT", "sinT"], writes=ALLF + ["smq"], dma="sm")
        op("sp", lambda q: q.dma_start(out=FF[0:1, 1024:2048], in_=bo0_d), writes=ALLF + ["smq"], dma="sm")
        op("dve", lambda q: q.tensor_scalar(out=browA[0:1, 0:1024], in0=FF[0:1, 0:1024], scalar1=0.125, scalar2=None, op0=ALU.mult), reads=ALLF, writes=["browA"])
        op("dve", lambda q: q.tensor_copy(out=browA[32:33, 0:256], in_=FF[32:33, 0:256]), reads=ALLF, writes=["browA"])
        op("dve", lambda q: q.tensor_scalar(out=browA[32:33, 256:768], in0=FF[32:33, 256:768], scalar1=0.5, scalar2=None, op0=ALU.mult), reads=ALLF, writes=["browA"])
        op("dve", lambda q: q.tensor_scalar(out=browA[64:65, 0:512], in0=FF[64:65, 0:512], scalar1=0.5, scalar2=None, op0=ALU.mult), reads=ALLF, writes=["browA"])
        op("dve", lambda q: q.tensor_copy(out=browB[:], in_=FF[0:1, 1024:2048]), reads=ALLF, writes=["browB"])

        def brow(c0, n):
            for (r0, r1, p) in BROWS:
                if r0 <= c0 and c0 + n <= r1:
                    return ones[p:p + 1, :], browA[p:p + 1, c0 - r0:c0 - r0 + n]
            raise AssertionError((c0, n))

        stageA = FF
        stageB = xb[1][:].rearrange("p j d -> p (j d)")
        XB1K = [("x", 1, j) for j in range(4)]
        stg = [(stageA, ALLF), (stageB, XB1K)]
        stb = [(og_all[:].rearrange("p j d -> p (j d)"), [("og", j) for j in range(4)]), (hT[:].rearrange("p k t -> p (k t)"), [("hT", j) for j in range(4)])]
        cnt = [0]

        def conv(eng, out, in_, scal, rd, wr):
            if eng == "dve":
                op("dve", lambda q: q.tensor_scalar(out=out, in0=in_, scalar1=scal, scalar2=None, op0=ALU.mult), reads=rd, writes=wr)
            else:
                op("act", lambda q: q.activation(out=out, in_=in_, func=AF.Copy, scale=scal), reads=rd, writes=wr)

        for kc in range(8):
            sg, sk = stg[cnt[0] % 2]
            cnt[0] += 1
            op("sp", lambda q, sg=sg, kc=kc: q.dma_start(out=sg[:, 0:2304], in_=w0_d[kc * 128:(kc + 1) * 128, :]),
               reads=["browA", "browB"], writes=sk, dma="wl%d" % (cnt[0] % 2))
            conv("dve", W0[:, kc, 0:1024], sg[:, 0:1024], sv(SCQ0 + kc), sk + ["scq0"], ["W0"])
            conv("act", W0[:, kc, 1024:1280], sg[:, 1024:1280], sv(PREW0 + kc), sk + ["prew"], ["W0"])
            conv("act", W0[:, kc, 1280:2304], sg[:, 1280:2304], sv(SCH0 + kc), sk + ["sch0"], ["W0"])
        for kc in range(8):
            sg, sk = stg[cnt[0] % 2]
            cnt[0] += 1
            op("sp", lambda q, sg=sg, kc=kc: q.dma_start(out=sg[:, 0:1024], in_=wo0_d[kc * 128:(kc + 1) * 128, :]),
               writes=sk, dma="wl%d" % (cnt[0] % 2))
            op("sp", lambda q, sg=sg, kc=kc: q.dma_start(out=sg[:, 1024:2048], in_=wo1_d[kc * 128:(kc + 1) * 128, :]),
               writes=sk, dma="wl%d" % (cnt[0] % 2))
            op("act", lambda q, sg=sg, kc=kc: q.activation(out=Wo0[:, kc, :], in_=sg[:, 0:1024], func=AF.Copy), reads=sk, writes=["Wo0"])
            conv("dve", Wo1[:, kc, :], sg[:, 1024:2048], sv(GNW), sk + ["gnw"], ["Wo1"])
        for kc in range(8):
            sg, sk = stg[cnt[0] % 2]
            sbt, sbk = stb[cnt[0] % 2]
            cnt[0] += 1
            op("sp", lambda q, sg=sg, kc=kc: q.dma_start(out=sg[:, 0:4096], in_=w1_d[kc * 128:(kc + 1) * 128, :]),
               writes=sk, dma="wl%d" % (cnt[0] % 2))
            for t in range(4):
                o_ap = sbt.rearrange("p (h t c) -> p t h c", h=8, t=4, c=128)[:, t, :, :]
                i_ap = sg[:, t * 1024:(t + 1) * 1024].rearrange("p (h c) -> p h c", h=8)
                scal = sv(PREW1 + kc) if t == 2 else sv(SCH1 + kc)
                conv("dve" if t % 2 == 0 else "act", o_ap, i_ap, scal, sk + ["prew", "sch1"], sbk)
            op("sp", lambda q, sbt=sbt, kc=kc: q.dma_start(out=w1s_d[:, :, kc, :].rearrange("h p c -> p h c"),
                                                          in_=sbt.rearrange("p (h c) -> p h c", h=8)),
               reads=sbk, writes=["w1s"], dma="ws%d" % (cnt[0] % 2))

        def xk(buf, j):
            return ("x", buf, j)

        def load_x(m):
            buf = m % 2
            op("sp", lambda q: q.dma_start(out=xb[buf][:], in_=x_d[m * 512:(m + 1) * 512, :].rearrange("(j p) d -> p j d", p=128)),
               writes=[xk(buf, j) for j in range(4)], dma="xl%d" % buf)

        def load_w1h(h, slot):
            op("sp", lambda q: q.dma_start(out=W1h[slot][:], in_=w1s_d[h]), reads=["w1s"], writes=["W1h%d" % slot], dma="w1l%d" % slot)

        def rms_rstd(src_ap, src_keys, col):
            op("act", lambda q: q.activation(out=hb[:], in_=src_ap, func=AF.Square, accum_out=sv(SS + col)),
               reads=src_keys, writes=["hb", ("ss", col)])
            op("dve", lambda q: q.tensor_scalar(out=sv(SS + col), in0=sv(SS + col), scalar1=1.0 / 1024, scalar2=EPS, op0=ALU.mult, op1=ALU.add),
               reads=[("ss", col)], writes=[("ss", col)])
            op("pool", lambda q: q.tensor_tensor(out=sv(RSTD + col), in0=sv(SS + col), in1=sv(NHALF), op=ALU.pow),
               reads=[("ss", col), "nhalf"], writes=[("rstd", col)])

        TB = L1_BANKS[7]

        def make_hT(buf, j, col, tb=TB):
            xs = xb[buf][:, j, :]
            rms_rstd(xs, [xk(buf, j)], col)
            op("act", lambda q: q.activation(out=hb[:], in_=xs, func=AF.Copy, scale=sv(RSTD + col)),
               reads=[xk(buf, j), ("rstd", col)], writes=["hb"])
            for kc in range(8):
                op("pe", lambda q, kc=kc: q.transpose(out=ptv(tb)[:, kc * 128:(kc + 1) * 128], in_=hb[:, kc * 128:(kc + 1) * 128], identity=ident[:]),
                   reads=["hb", "ident"], writes=[("pa", tb)])
            op("act", lambda q: q.activation(out=hT[:, :, j * 128:(j + 1) * 128], in_=ptv(tb).rearrange("p (k t) -> p k t", k=8), func=AF.Copy),
               reads=[("pa", tb)], writes=[("hT", j)])

        def post_norm_residual(buf, j, layer, pbanks):
            c0 = 4
            for hf in range(2):
                b = pbanks[hf]
                op("act", lambda q, b=b, hf=hf: q.activation(out=Fh(6 + hf), in_=pab[b][:], func=AF.Square, accum_out=sv(SS2 + hf)),
                   reads=[("pa", b)], writes=[FhK(6 + hf), ("ss2", hf)])
            op("dve", lambda q: q.tensor_scalar(out=sv(SS2, 2), in0=sv(SS2, 2), scalar1=1.0 / 1024, scalar2=EPS / 2, op0=ALU.mult, op1=ALU.add),
               reads=[("ss2", 0), ("ss2", 1)], writes=[("ss2", 0), ("ss2", 1)])
            op("dve", lambda q: q.tensor_tensor(out=sv(SS2), in0=sv(SS2), in1=sv(SS2 + 1), op=ALU.add),
               reads=[("ss2", 0), ("ss2", 1)], writes=[("ss2", 0)])
            op("pool", lambda q: q.tensor_tensor(out=sv(RSTD2), in0=sv(SS2), in1=sv(NHALF), op=ALU.pow),
               reads=[("ss2", 0), "nhalf"], writes=["rstd2"])
            for hf in range(2):
                b = pbanks[hf]
                op("dve", lambda q, b=b, hf=hf: q.scalar_tensor_tensor(out=Fh(6 + hf), in0=pab[b][:], scalar=sv(RSTD2),
                                                                        in1=wpost[:, layer, hf * 512:(hf + 1) * 512], op0=ALU.mult, op1=ALU.mult),
                   reads=[("pa", b), "rstd2", "wpost"], writes=[FhK(6 + hf)])
            op("pool", lambda q: q.tensor_tensor(out=xb[buf][:, j, :], in0=xb[buf][:, j, :], in1=F(3), op=ALU.add),
               reads=[xk(buf, j), FhK(6), FhK(7)], writes=[xk(buf, j)])

        def out_proj(src_bf16_ap, src_keys, Wo, wo_key, bias, ybanks, tb=TB):
            for kc in range(8):
                op("pe", lambda q, kc=kc: q.transpose(out=ptv(tb)[:, kc * 128:(kc + 1) * 128], in_=src_bf16_ap[:, kc * 128:(kc + 1) * 128], identity=ident[:]),
                   reads=src_keys + ["ident"], writes=[("pa", tb)])
            op("act", lambda q: q.activation(out=ogT[:].rearrange("p k t -> p (k t)"), in_=ptv(tb), func=AF.Copy),
               reads=[("pa", tb)], writes=["ogT"])
            banks = list(ybanks)
            for hf in range(2):
                b = banks[hf]
                for kc in range(8):
                    op("pe", lambda q, b=b, kc=kc, hf=hf: q.matmul(pab[b][:], lhsT=ogT[:, kc, :], rhs=Wo[:, kc, hf * 512:(hf + 1) * 512],
                                                                   start=(kc == 0), stop=(kc == 7 and not bias)),
                       reads=["ogT", wo_key], writes=[("pa", b)])
                if bias:
                    op("pe", lambda q, b=b, hf=hf: q.matmul(pab[b][:], lhsT=ones[0:1, :], rhs=browB[0:1, hf * 512:(hf + 1) * 512], start=False, stop=True),
                       reads=["ones", "browB"], writes=[("pa", b)])
            return banks

        def proj_tm(b, j, c0, n, Wt, wkey, bias=True):
            for kc in range(8):
                op("pe", lambda q, kc=kc: q.matmul(pab[b][:, 0:n], lhsT=hT[:, kc, j * 128:(j + 1) * 128], rhs=Wt[:, kc, c0:c0 + n],
                                                   start=(kc == 0), stop=(kc == 7 and not bias)),
                   reads=[("hT", j), wkey], writes=[("pa", b)])
            if bias:
                o1, br = brow(c0, n)
                op("pe", lambda q: q.matmul(pab[b][:, 0:n], lhsT=o1, rhs=br, start=False, stop=True),
                   reads=["ones", "browA"], writes=[("pa", b)])
            return b

        def l0_s1(m, j):
            buf = m % 2
            blk = m * 4 + j
            slot = blk % 2
            S.tag = "m%d.L0.%d.s1" % (m, j)
            make_hT(buf, j, j, 2)
            bq = [proj_tm(0, j, 0, 512, W0, "W0"), proj_tm(1, j, 512, 512, W0, "W0")]
            for a in range(2):
                op("act", lambda q, a=a: q.activation(out=qkb[:, 8 * a:8 * a + 8, :], in_=pab[bq[a]][:].rearrange("p (h d) -> p h d", h=8), func=AF.Copy),
                   reads=[("pa", bq[a])], writes=["qkb"])
                op("act", lambda q, a=a: q.activation(out=qkr[:, 8 * a:8 * a + 8, :], in_=pab[bq[a]][:].rearrange("p (h d) -> p h d", h=8)[:, :, 0:16], func=AF.Copy),
                   reads=[("pa", bq[a])], writes=["qkr"])
            bkv = proj_tm(0, j, 1024, 256, W0, "W0")
            op("act", lambda q: q.activation(out=qkb[:, 16:18, :], in_=pab[bkv][:, 0:128].rearrange("p (h d) -> p h d", h=2), func=AF.Copy),
               reads=[("pa", bkv)], writes=["qkb"])
            op("act", lambda q: q.activation(out=qkr[:, 16:18, :], in_=pab[bkv][:, 0:128].rearrange("p (h d) -> p h d", h=2)[:, :, 0:16], func=AF.Copy),
               reads=[("pa", bkv)], writes=["qkr"])
            op("act", lambda q: q.activation(out=vaug[:, slot, :, 0:64], in_=pab[bkv][:, 128:256].rearrange("p (g d) -> p g d", g=2), func=AF.Copy),
               reads=[("pa", bkv)], writes=["vaug%d" % slot])
            cb = cosT[:, blk, :].unsqueeze(1).to_broadcast([128, 18, 8])
            sbb = sinT[:, blk, :].unsqueeze(1).to_broadcast([128, 18, 8])
            x1 = qkr[:, :, 0:8]
            x2 = qkr[:, :, 8:16]
            op("pool", lambda q: q.tensor_tensor(out=rt[:, 0], in0=x1, in1=cb, op=ALU.mult), reads=["qkr", "cosT"], writes=["rt0"])
            op("pool", lambda q: q.tensor_tensor(out=rt[:, 1], in0=x2, in1=sbb, op=ALU.mult), reads=["qkr", "sinT"], writes=["rt1"])
            op("pool", lambda q: q.tensor_tensor(out=rt[:, 2], in0=x2, in1=cb, op=ALU.mult), reads=["qkr", "cosT"], writes=["rt2"])
            op("pool", lambda q: q.tensor_tensor(out=rt[:, 3], in0=x1, in1=sbb, op=ALU.mult), reads=["qkr", "sinT"], writes=["rt3"])
            op("dve", lambda q: q.tensor_tensor(out=qkb[:, :, 0:8], in0=rt[:, 0], in1=rt[:, 1], op=ALU.subtract), reads=["rt0", "rt1"], writes=["qkb"])
            op("dve", lambda q: q.tensor_tensor(out=qkb[:, :, 8:16], in0=rt[:, 2], in1=rt[:, 3], op=ALU.add), reads=["rt2", "rt3"], writes=["qkb"])
            tbk = (2, 2)
            for a in range(2):
                for hh in range(8):
                    op("pe", lambda q, a=a, hh=hh: q.transpose(out=ptv(tbk[a])[0:64, hh * 128:(hh + 1) * 128], in_=qkb[:, 8 * a + hh, :], identity=ident[:]),
                       reads=["qkb", "ident"], writes=[("pa", tbk[a])])
                op("act", lambda q, a=a: q.activation(out=qT[:, 8 * a:8 * a + 8, :], in_=ptv(tbk[a])[0:64, :].rearrange("p (h t) -> p h t", h=8), func=AF.Copy),
                   reads=[("pa", tbk[a])], writes=["qT"])
            for g in range(2):
                op("pe", lambda q, g=g: q.transpose(out=ptv(2)[0:64, g * 128:(g + 1) * 128], in_=qkb[:, 16 + g, :], identity=ident[:]),
                   reads=["qkb", "ident"], writes=[("pa", 2)])
            op("dve", lambda q: q.tensor_copy(out=kT[:, slot, :, :], in_=ptv(2)[0:64, 0:256].rearrange("p (g t) -> p g t", g=2)),
               reads=[("pa", 2)], writes=["kT%d" % slot])

        def l0_s2(m, j):
            blk = m * 4 + j
            first = (blk % (SEQ // 128) == 0)
            slot = blk % 2
            S.tag = "m%d.L0.%d.s2" % (m, j)
            pti = 0
            sb_i = 0
            for g in range(2):
                for a in range(2):
                    grp = 2 * g + a
                    ob = OA_BANKS[grp % len(OA_BANKS)]
                    kbs = ([] if first else [(1 - slot, mask_prev, "mask_prev")]) + [(slot, mask_cur, "mask_cur")]
                    pts = []
                    for (ks, msk, mkey) in kbs:
                        b = SC_BANKS[sb_i % len(SC_BANKS)]
                        sb_i += 1
                        op("pe", lambda q, b=b, ks=ks, g=g, a=a: q.matmul(pab[b][:], lhsT=kT[:, ks, g, :],
                                                                          rhs=qT[:, 8 * g + 4 * a:8 * g + 4 * a + 4, :].rearrange("p h t -> p (h t)"),
                                                                          start=True, stop=True),
                           reads=["kT%d" % ks, "qT"], writes=[("pa", b)])
                        p = pti % 4
                        pti += 1
                        op("act", lambda q, b=b, p=p: q.activation(out=PT[p][:], in_=pab[b][:], func=AF.Exp), reads=[("pa", b)], writes=[("PT", p)])
                        op("dve", lambda q, p=p, msk=msk: q.tensor_tensor(out=PT[p][:].rearrange("p (h t) -> p h t", h=4),
                                                                         in0=PT[p][:].rearrange("p (h t) -> p h t", h=4),
                                                                         in1=msk[:].unsqueeze(1).to_broadcast([128, 4, 128]), op=ALU.mult),
                           reads=[("PT", p), mkey], writes=[("PT", p)])
                        pts.append((p, ks))
                    for hh in range(4):
                        oo = hh * 65
                        for i, (p, ks) in enumerate(pts):
                            op("pe", lambda q, p=p, ks=ks, hh=hh, ob=ob, oo=oo, i=i, n=len(pts), g=g: q.matmul(
                                pab[ob][:, oo:oo + 65], lhsT=PT[p][:, hh * 128:(hh + 1) * 128], rhs=vaug[:, ks, g, :],
                                start=(i == 0), stop=(i == n - 1)),
                               reads=[("PT", p), "vaug%d" % ks], writes=[("pa", ob)])
                    h0 = 4 * grp
                    ov = pab[ob][:, 0:260].rearrange("p (h d) -> p h d", d=65)
                    op("dve", lambda q, ov=ov, h0=h0: q.tensor_tensor(out=sv(DEN + h0, 4), in0=ov[:, :, 64], in1=sv(ESINK + h0, 4), op=ALU.add),
                       reads=[("pa", ob), "esink"], writes=[("den", grp)])
                    op("dve", lambda q, h0=h0: q.reciprocal(out=sv(RDEN + h0, 4), in_=sv(DEN + h0, 4)), reads=[("den", grp)], writes=[("rden", grp)])
                    op("dve", lambda q, ov=ov, h0=h0: q.tensor_tensor(out=F(0).rearrange("p (h d) -> p h d", d=64)[:, h0:h0 + 4, :], in0=ov[:, :, 0:64],
                                                                     in1=sv(RDEN + h0, 4).unsqueeze(2).to_broadcast([128, 4, 64]), op=ALU.mult),
                       reads=[("pa", ob), ("rden", grp)], writes=[("of", grp)])

        def l0_s3(m, j):
            buf = m % 2
            S.tag = "m%d.L0.%d.s3" % (m, j)
            bz = [proj_tm(Z_BANKS[0], j, 1280, 512, W0, "W0"), proj_tm(Z_BANKS[1], j, 1792, 512, W0, "W0")]
            for hf in range(2):
                op("act", lambda q, hf=hf: q.activation(out=Fh(2 + hf), in_=pab[bz[hf]][:], func=AF.Tanh), reads=[("pa", bz[hf])], writes=[FhK(2 + hf)])
                op("dve", lambda q, hf=hf: q.scalar_tensor_tensor(out=Fh(2 + hf), in0=Fh(2 + hf), scalar=1.0, in1=pab[bz[hf]][:], op0=ALU.add, op1=ALU.mult),
                   reads=[FhK(2 + hf), ("pa", bz[hf])], writes=[FhK(2 + hf)])
            op("dve", lambda q: q.tensor_tensor(out=og_all[:, 0, :], in0=F(0), in1=F(1), op=ALU.mult),
               reads=[("of", 0), ("of", 1), ("of", 2), ("of", 3), FhK(0), FhK(1), FhK(2), FhK(3)], writes=[("og", 0), FhK(0), FhK(1)])
            yb = out_proj(og_all[:, 0, :], [("og", 0)], Wo0, "Wo0", True, Y_BANKS, 2)
            post_norm_residual(buf, j, 0, yb)

        def l1_A(m, h):
            ws = h % 2
            Wt = W1h[ws]
            wk = "W1h%d" % ws
            db = h % 2
            S.tag = "m%d.L1.h%d.A" % (m, h)
            bq, bf, bv, bzz = L1_BANKS[0:4]
            hTk = [("hT", j) for j in range(4)]
            for (b, c0) in ((bq, 0), (bf, 128)):
                for kc in range(8):
                    op("pe", lambda q, b=b, c0=c0, kc=kc: q.matmul(pab[b][:], lhsT=Wt[:, kc, c0:c0 + 128], rhs=hT[:, kc, :], start=(kc == 0), stop=(kc == 7)),
                       reads=hTk + [wk], writes=[("pa", b)])
            for (b, c0) in ((bv, 256), (bzz, 384)):
                for j in range(4):
                    for kc in range(8):
                        op("pe", lambda q, b=b, c0=c0, kc=kc, j=j: q.matmul(pab[b][:, j * 128:(j + 1) * 128], lhsT=hT[:, kc, j * 128:(j + 1) * 128],
                                                                            rhs=Wt[:, kc, c0:c0 + 128], start=(kc == 0), stop=(kc == 7)),
                           reads=[("hT", j), wk], writes=[("pa", b)])
            if h + 2 < 8:
                load_w1h(h + 2, ws)
            A0, A1, A2, A3, A4, A5, A6 = [Fh(i) for i in range(7)]
            K0, K1, K2, K3, K4, K5, K6 = [FhK(i) for i in range(7)]
            gz = Fh(7) if db == 0 else GZ[0][:]
            gzk = FhK(7) if db == 0 else "gz0"
            op("act", lambda q: q.activation(out=A0, in_=pab[bq][:], func=AF.Tanh), reads=[("pa", bq)], writes=[K0])
            op("dve", lambda q: q.scalar_tensor_tensor(out=A0, in0=A0, scalar=1.0, in1=pab[bq][:], op0=ALU.add, op1=ALU.mult),
               reads=[K0, ("pa", bq)], writes=[K0])
            op("act", lambda q: q.activation(out=A1, in_=pab[bf][:], func=AF.Tanh), reads=[("pa", bf)], writes=[K1])
            op("act", lambda q: q.activation(out=A2, in_=A1, func=AF.Identity, scale=sv(LBB + h), bias=sv(LBA + h)), reads=[K1] + LBK, writes=[K2])
            op("act", lambda q: q.activation(out=A3, in_=A1, func=AF.Identity, scale=sv(LBNB + h), bias=sv(LBB + h)), reads=[K1] + LBK, writes=[K3])
            op("dve", lambda q: q.tensor_tensor_scan(out=A4, data0=startmask[:], data1=A2, initial=0.0, op0=ALU.max, op1=ALU.mult),
               reads=["startmask", K2], writes=[K4])
            op("dve", lambda q: q.tensor_tensor(out=qeT[db][:], in0=A0, in1=A4, op=ALU.mult), reads=[K0, K4], writes=[("qeT", db)])
            op("dve", lambda q: q.reciprocal(out=A5, in_=A4), reads=[K4], writes=[K5], dur=3.3)
            op("dve", lambda q: q.tensor_tensor(out=A6, in0=A3, in1=A5, op=ALU.mult), reads=[K3, K5], writes=[K6])
            op("act", lambda q: q.activation(out=keT[db][:], in_=A6, func=AF.Copy), reads=[K6], writes=[("keT", db)])
            op("dve", lambda q: q.tensor_tensor(out=kdT[db][:].rearrange("p (c t) -> p c t", t=64), in0=A6.rearrange("p (c t) -> p c t", t=64),
                                                in1=A4.rearrange("p (c t) -> p c t", t=64)[:, :, 63:64].to_broadcast([128, 8, 64]), op=ALU.mult),
               reads=[K6, K4], writes=[("kdT", db)])
            op("act", lambda q: q.activation(out=glast[:, db, :], in_=A4.rearrange("p (c t) -> p c t", t=64)[:, :, 63], func=AF.Copy),
               reads=[K4], writes=[("glast", db)])
            op("act", lambda q: q.activation(out=vb[db][:].rearrange("p j v -> p (j v)"), in_=pab[bv][:], func=AF.Copy), reads=[("pa", bv)], writes=[("vb", db)])
            op("act", lambda q: q.activation(out=gz, in_=pab[bzz][:], func=AF.Tanh), reads=[("pa", bzz)], writes=[gzk])
            op("dve", lambda q: q.scalar_tensor_tensor(out=gz, in0=gz, scalar=1.0, in1=pab[bzz][:], op0=ALU.add, op1=ALU.mult),
               reads=[gzk, ("pa", bzz)], writes=[gzk])

        def l1_B(m, h):
            db = h % 2
            S.tag = "m%d.L1.h%d.B" % (m, h)
            gz = Fh(7) if db == 0 else GZ[0][:]
            gzk = FhK(7) if db == 0 else "gz0"
            ub = [L1_BANKS[4], L1_BANKS[5]]
            bso = L1_BANKS[6]
            for j in range(4):
                op("pe", lambda q, j=j: q.transpose(out=ptv(TB)[:, j * 128:(j + 1) * 128], in_=kdT[db][:, j * 128:(j + 1) * 128], identity=ident[:]),
                   reads=[("kdT", db), "ident"], writes=[("pa", TB)])
            op("act", lambda q: q.activation(out=kd_tm[:].rearrange("p j k -> p (j k)"), in_=ptv(TB)[:, 0:512], func=AF.Copy), reads=[("pa", TB)], writes=["kd_tm"])
            for j in range(4):
                for c in range(2):
                    op("pe", lambda q, j=j, c=c: q.matmul(pab[ub[c]][:, j * 128:(j + 1) * 128], lhsT=kd_tm[64 * c:64 * c + 64, j, :],
                                                          rhs=vb[db][64 * c:64 * c + 64, j, :], start=True, stop=True),
                       reads=["kd_tm", ("vb", db)], writes=[("pa", ub[c])])
            for j in range(4):
                op("pe", lambda q, j=j: q.matmul(pab[bso][:, j * 128:(j + 1) * 128], lhsT=keT[db][:, j * 128:(j + 1) * 128], rhs=qeT[db][:, j * 128:(j + 1) * 128],
                                                 start=True, stop=True),
                   reads=[("keT", db), ("qeT", db)], writes=[("pa", bso)])
            op("dve", lambda q: q.tensor_tensor(out=smk[:], in0=pab[bso][:].rearrange("p (j t) -> p j t", j=4),
                                                in1=maskbd[:].unsqueeze(1).to_broadcast([128, 4, 128]), op=ALU.mult),
               reads=[("pa", bso), "maskbd"], writes=["smk"])
            for k in range(8):
                j, c = k // 2, k % 2
                op("act", lambda q, k=k: q.activation(out=Sbf[:, k, :], in_=S32[:, h, :], func=AF.Copy), reads=[("S32", h)], writes=[("Sbf", k)])
                op("dve", lambda q, k=k, j=j, c=c: q.scalar_tensor_tensor(out=S32[:, h, :], in0=S32[:, h, :], scalar=glast[:, db, k:k + 1],
                                                                             in1=pab[ub[c]][:, j * 128:(j + 1) * 128], op0=ALU.mult, op1=ALU.add),
                   reads=[("S32", h), ("glast", db), ("pa", ub[c])], writes=[("S32", h)])
            for j in range(4):
                op("pe", lambda q, j=j: q.matmul(pab[bso][:, j * 128:(j + 1) * 128], lhsT=smk[:, j, :], rhs=vb[db][:, j, :], start=True, stop=True),
                   reads=["smk", ("vb", db)], writes=[("pa", bso)])
                for c in range(2):
                    k = 2 * j + c
                    op("pe", lambda q, j=j, c=c, k=k: q.matmul(pab[bso][64 * c:64 * c + 64, j * 128:(j + 1) * 128],
                                                               lhsT=qeT[db][:, j * 128 + 64 * c:j * 128 + 64 * c + 64], rhs=Sbf[:, k, :],
                                                               start=False, stop=True, skip_group_check=True),
                       reads=[("qeT", db), ("Sbf", k)], writes=[("pa", bso)])
            for j in range(4):
                op("act", lambda q, j=j: q.activation(out=og_all[:, j, h * 128:(h + 1) * 128], in_=pab[bso][:, j * 128:(j + 1) * 128], func=AF.Square, accum_out=sv(SS3 + j)),
                   reads=[("pa", bso)], writes=[("og", j), ("ss3", j)])
            op("dve", lambda q: q.tensor_scalar(out=sv(SS3, 4), in0=sv(SS3, 4), scalar1=1.0 / 128, scalar2=EPS, op0=ALU.mult, op1=ALU.add),
               reads=[("ss3", j) for j in range(4)], writes=[("ss3", j) for j in range(4)])
            op("pool", lambda q: q.tensor_tensor(out=sv(RSTD3, 4), in0=sv(SS3, 4), in1=sv(NHALF, 4), op=ALU.pow),
               reads=[("ss3", j) for j in range(4)] + ["nhalf"], writes=[("rstd3", j) for j in range(4)])
            for j in range(4):
                op("dve", lambda q, j=j: q.scalar_tensor_tensor(out=og_all[:, j, h * 128:(h + 1) * 128], in0=pab[bso][:, j * 128:(j + 1) * 128],
                                                                 scalar=sv(RSTD3 + j), in1=gz[:, j * 128:(j + 1) * 128], op0=ALU.mult, op1=ALU.mult),
                   reads=[("pa", bso), ("rstd3", j), gzk], writes=[("og", j)])

        load_x(0)
        for m in range(NMT):
            S.new_epoch()
            buf = m % 2
            if m + 1 < NMT:
                load_x(m + 1)
            load_w1h(0, 0)
            load_w1h(1, 1)
            l0_s1(m, 0)
            for j in range(4):
                l0_s2(m, j)
                if j + 1 < 4:
                    l0_s1(m, j + 1)
                l0_s3(m, j)
            S.tag = "m%d.L1.pre" % m
            for j in range(4):
                make_hT(buf, j, j)
            if m % MT_PER_SEQ == 0:
                op("pool", lambda q: q.memset(S32[:], 0.0), writes=[("S32", h) for h in range(8)])
            l1_A(m, 0)
            for h in range(8):
                if h + 1 < 8:
                    l1_A(m, h + 1)
                l1_B(m, h)
            S.tag = "m%d.L1.out" % m
            for j in range(4):
                yb = out_proj(og_all[:, j, :], [("og", j)], Wo1, "Wo1", False, ((L1_BANKS[0], L1_BANKS[1]), (L1_BANKS[2], L1_BANKS[3]))[j % 2])
                post_norm_residual(buf, j, 1, yb)
            op("sp", lambda q, m=m, buf=buf: q.dma_start(out=out_d[m * 512:(m + 1) * 512, :].rearrange("(j p) d -> p j d", p=128), in_=xb[buf][:]),
               reads=[xk(buf, j) for j in range(4)], dma="xs%d" % buf)
        S.emit(st)
    build_program.last_sched = S
    return nc


_PROG_CACHE = {}


def _get_prog(nseq, seq):
    key = (nseq, seq)
    if key not in _PROG_CACHE:
        _PROG_CACHE[key] = build_program(nseq, seq)
    return _PROG_CACHE[key]


def make_in_maps(inputs, n_cores, nseq, seq):
    x = np.ascontiguousarray(inputs["x"], dtype=np.float32)
    pos = np.ascontiguousarray(inputs["positions"], dtype=np.int32)
    cst = make_consts()
    maps = []
    for c in range(n_cores):
        xs = x[c * nseq:(c + 1) * nseq, :seq].reshape(nseq * seq, D)
        ps = pos[c * nseq:(c + 1) * nseq, :seq].reshape(nseq * seq // 128, 128).T
        maps.append({
            "x": np.ascontiguousarray(xs),
            "pos": np.ascontiguousarray(ps),
            "cst": cst,
            "pre_norm_w": np.ascontiguousarray(inputs["pre_norm_w"], dtype=np.float32),
            "post_norm_w": np.ascontiguousarray(inputs["post_norm_w"], dtype=np.float32),
            "attn_w_in": np.ascontiguousarray(inputs["attn_w_in"][0], dtype=np.float32),
            "attn_b_in": np.ascontiguousarray(inputs["attn_b_in"], dtype=np.float32).reshape(1, 2304),
            "attn_sinks": np.ascontiguousarray(inputs["attn_sinks"], dtype=np.float32).reshape(1, 16),
            "attn_w_out": np.ascontiguousarray(inputs["attn_w_out"][0], dtype=np.float32),
            "attn_b_out": np.ascontiguousarray(inputs["attn_b_out"], dtype=np.float32).reshape(1, D),
            "rec_w_in": np.ascontiguousarray(inputs["rec_w_in"][0], dtype=np.float32),
            "rec_lb_logits": np.ascontiguousarray(inputs["rec_lb_logits"], dtype=np.float32),
            "rec_gnorm_w": np.ascontiguousarray(inputs["rec_gnorm_w"], dtype=np.float32).reshape(1, 128),
            "rec_w_out": np.ascontiguousarray(inputs["rec_w_out"][0], dtype=np.float32),
        })
    return maps


def kernel(**inputs):
    B, T, _ = inputs["x"].shape
    nseq = B // N_CORES
    nc = _get_prog(nseq, T)
    maps = make_in_maps(inputs, N_CORES, nseq, T)
    res = run_bass_kernel_spmd(nc, maps, core_ids=list(range(N_CORES)))
    outs = [np.asarray(r["out"], dtype=np.float32).reshape(nseq, T, D) for r in res.results]
    return np.concatenate(outs, axis=0)
```

```python
import math
from contextlib import ExitStack

import numpy as np
import concourse.bass as bass
import concourse.mybir as mybir
from concourse.bass_utils import run_bass_kernel_spmd

F32 = mybir.dt.float32
BF16 = mybir.dt.bfloat16
I32 = mybir.dt.int32
AF = mybir.ActivationFunctionType
ALU = mybir.AluOpType

N_CORES = 8
OA_BANKS = (3,)
SC_BANKS = (6, 7)
Z_BANKS = (6, 7)
Y_BANKS = (4, 5)
L1_BANKS = (4, 5, 6, 7, 2, 3, 0, 1)
D = 1024
EPS = 1e-6
TWO_PI = 2.0 * math.pi
C1 = 6.28125
C2 = TWO_PI - C1
PI_LO = 3.1415925


class _Op:
    __slots__ = ("eng", "fn", "idx", "deps", "signal", "sem", "count", "dma", "epoch", "tag", "dur", "start")


class _Rec:
    def __init__(self):
        self.call = None

    def __getattr__(self, name):
        def f(*a, **k):
            self.call = (name, a, k)
            return self
        return f


def _nelem(ap):
    n = 1
    for d in ap.shape[1:]:
        n *= int(d)
    return n


def _estimate_us(eng, fn, is_dma):
    r = _Rec()
    try:
        fn(r)
    except Exception:
        return 0.5
    if r.call is None:
        return 0.3
    name, a, k = r.call
    out = k.get("out", a[0] if a else None)
    try:
        if is_dma:
            esz = 2 if out.dtype == BF16 else 4
            nbytes = _nelem(out) * int(out.shape[0]) * esz
            return 2.0 + nbytes / 200e3
        if eng == "pe":
            if name == "transpose":
                return 0.10
            rhs = k.get("rhs")
            n = _nelem(rhs)
            return 0.02 + n * 0.00058
        n = _nelem(out) if out is not None else 64
        if eng == "act":
            return 0.26 + n * 0.00085 + (0.1 if k.get("accum_out") is not None else 0.0)
        if eng == "dve":
            f = 1.0
            if name == "reciprocal":
                f = 4.2
            elif name == "tensor_tensor_scan":
                f = 2.0
            elif name == "tensor_tensor":
                f = 1.6
            return 0.12 + n * 0.00104 * f
        if eng == "pool":
            if name == "tensor_tensor" and k.get("op") == ALU.pow:
                return 0.65
            return 0.3 + n * 0.0021
    except Exception:
        pass
    return 0.4


class Sched:
    EPOCH = 1500

    def __init__(self, nc):
        self.nc = nc
        self.ops = []
        self.last_w = {}
        self.readers = {}
        self.epoch = 0
        self.tag = ""

    def new_epoch(self):
        pass

    def op(self, eng, fn, reads=(), writes=(), dma=None, dur=None):
        o = _Op()
        o.eng, o.fn, o.idx, o.dma, o.epoch = eng, fn, len(self.ops), dma, 0
        o.signal = False
        o.tag = self.tag
        o.dur = dur if dur is not None else _estimate_us(eng, fn, dma is not None)
        deps = set()
        for k in reads:
            w = self.last_w.get(k)
            if w is not None:
                deps.add(w)
        for k in writes:
            w = self.last_w.get(k)
            if w is not None:
                deps.add(w)
            for r in self.readers.get(k, ()):
                deps.add(r)
        deps.discard(o.idx)
        o.deps = deps
        for k in writes:
            self.last_w[k] = o.idx
            self.readers[k] = []
        for k in reads:
            if k not in writes:
                self.readers.setdefault(k, []).append(o.idx)
        self.ops.append(o)
        return o

    def list_schedule(self, reorder=True):
        import heapq
        ops = self.ops
        n = len(ops)
        if not reorder:
            return {e: [o for o in ops if o.eng == e] for e in ("pe", "act", "dve", "pool", "sp")}, 0.0
        succ = [[] for _ in range(n)]
        indeg = [0] * n
        for o in ops:
            indeg[o.idx] = len(o.deps)
            for d in o.deps:
                succ[d].append(o.idx)
        ready_t = [0.0] * n
        free = {e: 0.0 for e in ("pe", "act", "dve", "pool", "sp")}
        heaps = {e: [] for e in free}
        for o in ops:
            if indeg[o.idx] == 0:
                heapq.heappush(heaps[o.eng], (0.0, o.idx))
        order = {e: [] for e in free}
        done = 0
        SEM_LAT = 0.15
        while done < n:
            best = None
            for e, h in heaps.items():
                if not h:
                    continue
                t0 = max(free[e], h[0][0])
                cand = None
                tmp = []
                while h and h[0][0] <= t0 and len(tmp) < 24:
                    tmp.append(heapq.heappop(h))
                pick = min(tmp, key=lambda x: x[1])
                for x in tmp:
                    if x is not pick:
                        heapq.heappush(h, x)
                heapq.heappush(h, pick)
                cand = (t0, pick[1], e, pick)
                if best is None or cand[:2] < best[:2]:
                    best = cand
            t0, idx, e, pick = best
            h = heaps[e]
            h.remove(pick)
            heapq.heapify(h)
            o = ops[idx]
            o.start = t0
            if o.dma is not None:
                free[e] = t0 + 0.06
                fin = t0 + o.dur
            else:
                free[e] = t0 + o.dur
                fin = free[e]
            order[e].append(o)
            done += 1
            for sidx in succ[idx]:
                so = ops[sidx]
                lat = 0.0 if (so.eng == e and e == "pe" and o.dma is None) else SEM_LAT
                ready_t[sidx] = max(ready_t[sidx], fin + lat)
                indeg[sidx] -= 1
                if indeg[sidx] == 0:
                    heapq.heappush(heaps[so.eng], (ready_t[sidx], sidx))
        return order, max(free.values())

    def emit(self, stack, reorder=True):
        nc = self.nc
        ops = self.ops
        order, makespan = self.list_schedule(reorder)
        self.makespan = makespan
        pos = {}
        for e, lst in order.items():
            for i, o in enumerate(lst):
                pos[o.idx] = i
        for o in ops:
            for d in o.deps:
                p = ops[d]
                if p.dma is None and p.eng == "pe" and o.eng == "pe" and o.dma is None:
                    assert pos[p.idx] < pos[o.idx]
                    continue
                p.signal = True
        counts = {}
        nsig = {}
        for e, lst in order.items():
            for o in lst:
                if o.dma is not None:
                    key = ("dma", o.dma)
                    counts[key] = counts.get(key, 0) + 16
                    o.sem, o.count = key, counts[key]
                elif o.signal:
                    k = nsig.get(e, 0)
                    nsig[e] = k + 1
                    o.epoch = k // self.EPOCH
                    key = (e, o.epoch)
                    counts[key] = counts.get(key, 0) + 1
                    o.sem, o.count = key, counts[key]
        sems = {}
        for key in counts:
            sems[key] = stack.enter_context(nc.semaphore("s_%s_%s" % key))
        self.n_sems = len(sems)
        final = dict(counts)

        def stream(eng_name):
            def body(eng):
                seen = {}
                for o in order[eng_name]:
                    need = {}
                    for d in o.deps:
                        p = ops[d]
                        if p.dma is None and p.eng == "pe" and eng_name == "pe" and o.dma is None:
                            continue
                        if p.dma is not None:
                            skey, val = ("dma", p.dma), (0, p.count)
                        else:
                            skey, val = ("eng", p.eng), (p.epoch, p.count)
                        if val > need.get(skey, (-1, -1)):
                            need[skey] = val
                    for skey, val in need.items():
                        if val <= seen.get(skey, (-1, -1)):
                            continue
                        seen[skey] = val
                        if skey[0] == "dma":
                            eng.wait_ge(sems[("dma", skey[1])], val[1])
                        else:
                            eng.wait_ge(sems[(skey[1], val[0])], val[1])
                    ins = o.fn(eng)
                    if o.dma is not None:
                        ins.then_inc(sems[o.sem], 16)
                    elif o.signal:
                        ins.then_inc(sems[o.sem], 1)
                if eng_name == "sp":
                    for key, c in final.items():
                        if key[0] == "dma":
                            eng.wait_ge(sems[key], c)
            return body

        with nc.Block() as block:
            block.tensor(stream("pe"))
            block.scalar(stream("act"))
            block.vector(stream("dve"))
            block.gpsimd(stream("pool"))
            block.sync(stream("sp"))


CST_W = 128 * 4 + 512 + 8


def make_consts():
    c = np.zeros((128, CST_W), np.float32)
    i = np.arange(128)
    c[:, 0:128] = np.eye(128, dtype=np.float32)
    c[:, 128:256] = (i[:, None] <= i[None, :]).astype(np.float32)
    c[:, 256:384] = (i[:, None] > i[None, :]).astype(np.float32)
    c[:, 384:512] = ((i[:, None] <= i[None, :]) & ((i[:, None] // 64) == (i[None, :] // 64))).astype(np.float32)
    sm = np.zeros(512, np.float32)
    sm[::64] = 1.0
    c[:, 512:1024] = sm[None, :]
    invf = (np.float32(500000.0) ** (-(np.arange(8, dtype=np.float32) * np.float32(2.0) / np.float32(16.0)))).astype(np.float32)
    c[:, 1024:1032] = invf[None, :]
    return c


def build_program(NSEQ, SEQ):
    NT = NSEQ * SEQ
    NB = NT // 128
    NMT = NT // 512
    MT_PER_SEQ = SEQ // 512
    nc = bass.Bass("TRN2", target_bir_lowering=False)

    def din(name, shape, dt=F32):
        return nc.dram_tensor(name, list(shape), dt, kind="ExternalInput").ap()

    x_d = din("x", [NT, D])
    pos_d = din("pos", [128, NB], I32)
    cst_d = din("cst", [128, CST_W])
    prew_d = din("pre_norm_w", [2, D])
    postw_d = din("post_norm_w", [2, D])
    w0_d = din("attn_w_in", [D, 2304])
    b0_d = din("attn_b_in", [1, 2304])
    sink_d = din("attn_sinks", [1, 16])
    wo0_d = din("attn_w_out", [D, D])
    bo0_d = din("attn_b_out", [1, D])
    w1_d = din("rec_w_in", [D, 4096])
    lb_d = din("rec_lb_logits", [2, D])
    gnw_d = din("rec_gnorm_w", [1, 128])
    wo1_d = din("rec_w_out", [D, D])
    out_d = nc.dram_tensor("out", [NT, D], F32, kind="ExternalOutput").ap()
    w1s_d = nc.dram_tensor("w1s", [8, 128, 8, 512], BF16, kind="Internal").ap()

    with ExitStack() as st:
        def sb(name, shape, dt):
            return st.enter_context(nc.sbuf_tensor(name, list(shape), dt))

        def ps(name, shape, dt):
            return st.enter_context(nc.psum_tensor(name, list(shape), dt))

        W0 = sb("W0", [128, 8, 2304], BF16)
        Wo0 = sb("Wo0", [128, 8, 1024], BF16)
        Wo1 = sb("Wo1", [128, 8, 1024], BF16)
        W1h = [sb("W1h%d" % i, [128, 8, 512], BF16) for i in range(2)]
        xb = [sb("xb%d" % i, [128, 4, 1024], F32) for i in range(2)]
        FF = sb("FF", [128, 4096], F32)
        hT = sb("hT", [128, 8, 512], BF16)
        og_all = sb("og_all", [128, 4, 1024], BF16)
        ident = sb("ident", [128, 128], BF16)
        mask_cur = sb("mask_cur", [128, 128], BF16)
        mask_prev = sb("mask_prev", [128, 128], BF16)
        maskbd = sb("maskbd", [128, 128], BF16)
        startmask = sb("startmask", [128, 512], F32)
        invf = sb("invf", [128, 8], F32)
        cosT = sb("cosT", [128, NB, 8], F32)
        sinT = sb("sinT", [128, NB, 8], F32)
        wpost = sb("wpost", [128, 2, 1024], F32)
        browA = sb("browA", [65, 1024], BF16)
        posi = sb("posi", [128, NB], I32)
        browB = sb("browB", [1, 1024], BF16)
        ones = sb("ones", [65, 128], BF16)
        small = sb("small", [128, 144], F32)
        S32 = sb("S32", [128, 8, 128], F32)
        Sbf = sb("Sbf", [128, 8, 128], BF16)
        hb = sb("hb", [128, 1024], BF16)
        qkb = sb("qkb", [128, 18, 64], BF16)
        Cbuf = sb("Cbuf", [128, 7, 128], F32)
        qT = sb("qT", [64, 16, 128], BF16)
        kT = sb("kT", [64, 2, 2, 128], BF16)
        vaug = sb("vaug", [128, 2, 2, 65], BF16)
        PT = [sb("PT%d" % i, [128, 512], BF16) for i in range(4)]
        ogT = sb("ogT", [128, 8, 128], BF16)
        qeT = [sb("qeT%d" % i, [128, 512], BF16) for i in range(2)]
        keT = [sb("keT%d" % i, [128, 512], BF16) for i in range(2)]
        kdT = [sb("kdT%d" % i, [128, 512], BF16) for i in range(2)]
        kd_tm = sb("kd_tm", [128, 4, 128], BF16)
        vb = [sb("vb%d" % i, [128, 4, 128], BF16) for i in range(2)]
        smk = sb("smk", [128, 4, 128], BF16)

        PREW0, PREW1 = 0, 8
        SCQ0, SCH0, SCH1 = 16, 24, 32
        LBA, LBB, LBNB = 40, 48, 56
        GNW = 64
        ESINK = 65
        SS = 81
        RSTD = 85
        DEN = 89
        RDEN = 105
        NHALF = 121
        SS2, RSTD2 = 125, 127
        SS3, RSTD3 = 128, 132

        def sv(c, n=1):
            return small[:, c:c + n]

        pab = [ps("pab%d" % i, [128, 512], F32) for i in range(8)]
        GZ = [sb("gz0", [128, 512], F32)]
        glast = sb("glast", [128, 2, 8], F32)

        def ptv(i):
            return pab[i][:].bitcast(BF16)

        S = Sched(nc)
        op = S.op
        FK = ["FF0", "FF1", "FF2", "FF3"]

        def F(i):
            return FF[:, i * 1024:(i + 1) * 1024]

        def Fh(i):
            return FF[:, i * 512:(i + 1) * 512]

        def FhK(i):
            return "FH%d" % i

        ALLF = FK + [FhK(i) for i in range(8)]

        op("sp", lambda q: q.dma_start(out=FF[:, 0:CST_W], in_=cst_d), writes=ALLF, dma="cst")
        op("dve", lambda q: q.tensor_copy(out=ident[:], in_=FF[:, 0:128]), reads=ALLF, writes=["ident"])
        op("dve", lambda q: q.tensor_copy(out=mask_cur[:], in_=FF[:, 128:256]), reads=ALLF, writes=["mask_cur"])
        op("dve", lambda q: q.tensor_copy(out=mask_prev[:], in_=FF[:, 256:384]), reads=ALLF, writes=["mask_prev"])
        op("dve", lambda q: q.tensor_copy(out=maskbd[:], in_=FF[:, 384:512]), reads=ALLF, writes=["maskbd"])
        op("dve", lambda q: q.tensor_copy(out=startmask[:], in_=FF[:, 512:1024]), reads=ALLF, writes=["startmask"])
        op("dve", lambda q: q.tensor_copy(out=invf[:], in_=FF[:, 1024:1032]), reads=ALLF, writes=["invf"])
        op("pool", lambda q: q.memset(ones[:], 1.0), writes=["ones"])
        op("pool", lambda q: q.memset(small[:, NHALF:NHALF + 4], -0.5), writes=["nhalf"])
        op("pool", lambda q: q.memset(vaug[:], 1.0), writes=["vaug0", "vaug1"])
        op("sp", lambda q: q.dma_start(out=small[:, PREW0:PREW0 + 8], in_=prew_d[0:1, :].rearrange("o (k p) -> p (o k)", p=128),
                                       allow_slow_non_contiguous=True), writes=["prew", "smq"], dma="sm")
        op("sp", lambda q: q.dma_start(out=small[:, PREW1:PREW1 + 8], in_=prew_d[1:2, :].rearrange("o (k p) -> p (o k)", p=128),
                                       allow_slow_non_contiguous=True), writes=["prew", "smq"], dma="sm")
        op("sp", lambda q: q.dma_start(out=small[:, LBA:LBA + 8], in_=lb_d[0:1, :].rearrange("o (k p) -> p (o k)", p=128),
                                       allow_slow_non_contiguous=True), writes=["lb0", "smq"], dma="sm")
        op("sp", lambda q: q.dma_start(out=small[:, LBB:LBB + 8], in_=lb_d[1:2, :].rearrange("o (k p) -> p (o k)", p=128),
                                       allow_slow_non_contiguous=True), writes=["lb1", "smq"], dma="sm")
        op("sp", lambda q: q.dma_start(out=small[:, GNW:GNW + 1], in_=gnw_d.rearrange("o p -> p o"),
                                       allow_slow_non_contiguous=True), writes=["gnw", "smq"], dma="sm")
        op("sp", lambda q: q.dma_start(out=small[:, ESINK:ESINK + 16], in_=sink_d.partition_broadcast(128)), writes=["esink", "smq"], dma="sm")
        op("sp", lambda q: q.dma_start(out=wpost[:, 0, :], in_=postw_d[0:1, :].partition_broadcast(128)), writes=["wpost", "smq"], dma="sm")
        op("sp", lambda q: q.dma_start(out=wpost[:, 1, :], in_=postw_d[1:2, :].partition_broadcast(128)), writes=["wpost", "smq"], dma="sm")
        op("sp", lambda q: q.dma_start(out=posi[:], in_=pos_d), writes=["posi", "smq"], dma="sm")
        op("dve", lambda q: q.tensor_scalar(out=sv(SCQ0, 8), in0=sv(PREW0, 8), scalar1=0.125, scalar2=None, op0=ALU.mult), reads=["prew"], writes=["scq0"])
        op("dve", lambda q: q.tensor_scalar(out=sv(SCH0, 8), in0=sv(PREW0, 8), scalar1=0.5, scalar2=None, op0=ALU.mult), reads=["prew"], writes=["sch0"])
        op("dve", lambda q: q.tensor_scalar(out=sv(SCH1, 8), in0=sv(PREW1, 8), scalar1=0.5, scalar2=None, op0=ALU.mult), reads=["prew"], writes=["sch1"])
        op("act", lambda q: q.activation(out=sv(ESINK, 16), in_=sv(ESINK, 16), func=AF.Exp), reads=["esink"], writes=["esink"])
        op("dve", lambda q: q.tensor_tensor(out=sv(LBNB, 8), in0=sv(LBB, 8), in1=sv(LBA, 8), op=ALU.subtract), reads=["lb0", "lb1"], writes=["lbt"])
        op("act", lambda q: q.activation(out=sv(LBNB, 8), in_=sv(LBNB, 8), func=AF.Tanh, scale=0.5), reads=["lbt"], writes=["lbt"])
        op("dve", lambda q: q.tensor_scalar(out=sv(LBA, 8), in0=sv(LBNB, 8), scalar1=0.25, scalar2=0.75, op0=ALU.mult, op1=ALU.add), reads=["lbt"], writes=["lb0"])
        op("dve", lambda q: q.tensor_scalar(out=sv(LBB, 8), in0=sv(LBNB, 8), scalar1=-0.25, scalar2=0.25, op0=ALU.mult, op1=ALU.add), reads=["lbt"], writes=["lb1"])
        op("dve", lambda q: q.tensor_scalar(out=sv(LBNB, 8), in0=sv(LBB, 8), scalar1=-1.0, scalar2=None, op0=ALU.mult), reads=["lb1", "lbt"], writes=["lbt"])
        LBK = ["lb0", "lb1", "lbt"]

        o0 = 1040
        posf = FF[:, o0:o0 + NB]
        ang = FF[:, o0 + NB:o0 + NB + NB * 8]
        tmpu = FF[:, o0 + 9 * NB:o0 + 17 * NB]
        tmpk = FF[:, o0 + 17 * NB:o0 + 25 * NB]
        assert o0 + 25 * NB <= 4096
        tmpi = hb[:].bitcast(I32)[:, 0:NB * 8]
        op("dve", lambda q: q.tensor_copy(out=posf, in_=posi[:]), reads=["posi"] + ALLF, writes=ALLF)
        op("dve", lambda q: q.tensor_tensor(out=ang.rearrange("p (b i) -> p b i", i=8),
                                            in0=posf.unsqueeze(2).to_broadcast([128, NB, 8]),
                                            in1=invf[:].unsqueeze(1).to_broadcast([128, NB, 8]), op=ALU.mult),
           reads=ALLF + ["invf"], writes=ALLF)
        for which, tab in ((0, sinT), (1, cosT)):
            if which == 1:
                op("dve", lambda q: q.tensor_scalar(out=ang, in0=ang, scalar1=math.pi / 2, scalar2=None, op0=ALU.add), reads=ALLF, writes=ALLF)
            op("dve", lambda q: q.tensor_scalar(out=tmpu, in0=ang, scalar1=1.0 / TWO_PI, scalar2=None, op0=ALU.mult), reads=ALLF, writes=ALLF)
            op("dve", lambda q: q.tensor_copy(out=tmpi, in_=tmpu), reads=ALLF, writes=["hb"])
            op("dve", lambda q: q.tensor_copy(out=tmpk, in_=tmpi), reads=["hb"], writes=ALLF)
            op("dve", lambda q: q.scalar_tensor_tensor(out=tmpu, in0=tmpk, scalar=-C1, in1=ang, op0=ALU.mult, op1=ALU.add), reads=ALLF, writes=ALLF)
            op("dve", lambda q: q.scalar_tensor_tensor(out=tmpu, in0=tmpk, scalar=-C2, in1=tmpu, op0=ALU.mult, op1=ALU.add), reads=ALLF, writes=ALLF)
            op("dve", lambda q: q.tensor_scalar(out=tmpu, in0=tmpu, scalar1=-PI_LO, scalar2=PI_LO, op0=ALU.max, op1=ALU.min), reads=ALLF, writes=ALLF)
            op("act", lambda q, tab=tab: q.activation(out=tab[:].rearrange("p b i -> p (b i)"), in_=tmpu, func=AF.Sin),
               reads=ALLF, writes=["cosT" if which == 1 else "sinT"])

        BROWS = [(0, 1024, 0), (1024, 1792, 32), (1792, 2304, 64)]
        for (c0, c1, p) in BROWS:
            op("sp", lambda q, c0=c0, c1=c1, p=p: q.dma_start(out=FF[p:p + 1, 0:c1 - c0], in_=b0_d[0:1, c0:c1]),
               reads=["cosT", "sinT"], writes=ALLF + ["smq"], dma="sm")
        op("sp", lambda q: q.dma_start(out=FF[0:1, 1024:2048], in_=bo0_d), writes=ALLF + ["smq"], dma="sm")
        op("dve", lambda q: q.tensor_scalar(out=browA[0:1, 0:1024], in0=FF[0:1, 0:1024], scalar1=0.125, scalar2=None, op0=ALU.mult), reads=ALLF, writes=["browA"])
        op("dve", lambda q: q.tensor_copy(out=browA[32:33, 0:256], in_=FF[32:33, 0:256]), reads=ALLF, writes=["browA"])
        op("dve", lambda q: q.tensor_scalar(out=browA[32:33, 256:768], in0=FF[32:33, 256:768], scalar1=0.5, scalar2=None, op0=ALU.mult), reads=ALLF, writes=["browA"])
        op("dve", lambda q: q.tensor_scalar(out=browA[64:65, 0:512], in0=FF[64:65, 0:512], scalar1=0.5, scalar2=None, op0=ALU.mult), reads=ALLF, writes=["browA"])
        op("dve", lambda q: q.tensor_copy(out=browB[:], in_=FF[0:1, 1024:2048]), reads=ALLF, writes=["browB"])

        def brow(c0, n):
            for (r0, r1, p) in BROWS:
                if r0 <= c0 and c0 + n <= r1:
                    return ones[p:p + 1, :], browA[p:p + 1, c0 - r0:c0 - r0 + n]
            raise AssertionError((c0, n))

        stageA = FF
        stageB = xb[1][:].rearrange("p j d -> p (j d)")
        XB1K = [("x", 1, j) for j in range(4)]
        stg = [(stageA, ALLF), (stageB, XB1K)]
        stb = [(og_all[:].rearrange("p j d -> p (j d)"), [("og", j) for j in range(4)]), (hT[:].rearrange("p k t -> p (k t)"), [("hT", j) for j in range(4)])]
        cnt = [0]

        def conv(eng, out, in_, scal, rd, wr):
            if eng == "dve":
                op("dve", lambda q: q.tensor_scalar(out=out, in0=in_, scalar1=scal, scalar2=None, op0=ALU.mult), reads=rd, writes=wr)
            else:
                op("act", lambda q: q.activation(out=out, in_=in_, func=AF.Copy, scale=scal), reads=rd, writes=wr)

        for kc in range(8):
            sg, sk = stg[cnt[0] % 2]
            cnt[0] += 1
            op("sp", lambda q, sg=sg, kc=kc: q.dma_start(out=sg[:, 0:2304], in_=w0_d[kc * 128:(kc + 1) * 128, :]),
               reads=["browA", "browB"], writes=sk, dma="wl%d" % (cnt[0] % 2))
            conv("dve", W0[:, kc, 0:1024], sg[:, 0:1024], sv(SCQ0 + kc), sk + ["scq0"], ["W0"])
            conv("act", W0[:, kc, 1024:1280], sg[:, 1024:1280], sv(PREW0 + kc), sk + ["prew"], ["W0"])
            conv("act", W0[:, kc, 1280:2304], sg[:, 1280:2304], sv(SCH0 + kc), sk + ["sch0"], ["W0"])
        for kc in range(8):
            sg, sk = stg[cnt[0] % 2]
            cnt[0] += 1
            op("sp", lambda q, sg=sg, kc=kc: q.dma_start(out=sg[:, 0:1024], in_=wo0_d[kc * 128:(kc + 1) * 128, :]),
               writes=sk, dma="wl%d" % (cnt[0] % 2))
            op("sp", lambda q, sg=sg, kc=kc: q.dma_start(out=sg[:, 1024:2048], in_=wo1_d[kc * 128:(kc + 1) * 128, :]),
               writes=sk, dma="wl%d" % (cnt[0] % 2))
            op("act", lambda q, sg=sg, kc=kc: q.activation(out=Wo0[:, kc, :], in_=sg[:, 0:1024], func=AF.Copy), reads=sk, writes=["Wo0"])
            conv("dve", Wo1[:, kc, :], sg[:, 1024:2048], sv(GNW), sk + ["gnw"], ["Wo1"])
        for kc in range(8):
            sg, sk = stg[cnt[0] % 2]
            sbt, sbk = stb[cnt[0] % 2]
            cnt[0] += 1
            op("sp", lambda q, sg=sg, kc=kc: q.dma_start(out=sg[:, 0:4096], in_=w1_d[kc * 128:(kc + 1) * 128, :]),
               writes=sk, dma="wl%d" % (cnt[0] % 2))
            for t in range(4):
                o_ap = sbt.rearrange("p (h t c) -> p t h c", h=8, t=4, c=128)[:, t, :, :]
                i_ap = sg[:, t * 1024:(t + 1) * 1024].rearrange("p (h c) -> p h c", h=8)
                scal = sv(PREW1 + kc) if t == 2 else sv(SCH1 + kc)
                conv("dve" if t % 2 == 0 else "act", o_ap, i_ap, scal, sk + ["prew", "sch1"], sbk)
            op("sp", lambda q, sbt=sbt, kc=kc: q.dma_start(out=w1s_d[:, :, kc, :].rearrange("h p c -> p h c"),
                                                          in_=sbt.rearrange("p (h c) -> p h c", h=8)),
               reads=sbk, writes=["w1s"], dma="ws%d" % (cnt[0] % 2))

        def xk(buf, j):
            return ("x", buf, j)

        def load_x(m):
            buf = m % 2
            op("sp", lambda q: q.dma_start(out=xb[buf][:], in_=x_d[m * 512:(m + 1) * 512, :].rearrange("(j p) d -> p j d", p=128)),
               writes=[xk(buf, j) for j in range(4)], dma="xl%d" % buf)

        def load_w1h(h, slot):
            op("sp", lambda q: q.dma_start(out=W1h[slot][:], in_=w1s_d[h]), reads=["w1s"], writes=["W1h%d" % slot], dma="w1l%d" % slot)

        def rms_rstd(src_ap, src_keys, col):
            op("act", lambda q: q.activation(out=hb[:], in_=src_ap, func=AF.Square, accum_out=sv(SS + col)),
               reads=src_keys, writes=["hb", ("ss", col)])
            op("dve", lambda q: q.tensor_scalar(out=sv(SS + col), in0=sv(SS + col), scalar1=1.0 / 1024, scalar2=EPS, op0=ALU.mult, op1=ALU.add),
               reads=[("ss", col)], writes=[("ss", col)])
            op("pool", lambda q: q.tensor_tensor(out=sv(RSTD + col), in0=sv(SS + col), in1=sv(NHALF), op=ALU.pow),
               reads=[("ss", col), "nhalf"], writes=[("rstd", col)])

        TB = L1_BANKS[7]

        def make_hT(buf, j, col, tb=TB):
            xs = xb[buf][:, j, :]
            rms_rstd(xs, [xk(buf, j)], col)
            op("act", lambda q: q.activation(out=hb[:], in_=xs, func=AF.Copy, scale=sv(RSTD + col)),
               reads=[xk(buf, j), ("rstd", col)], writes=["hb"])
            for kc in range(8):
                op("pe", lambda q, kc=kc: q.transpose(out=ptv(tb)[:, kc * 128:(kc + 1) * 128], in_=hb[:, kc * 128:(kc + 1) * 128], identity=ident[:]),
                   reads=["hb", "ident"], writes=[("pa", tb)])
            op("act", lambda q: q.activation(out=hT[:, :, j * 128:(j + 1) * 128], in_=ptv(tb).rearrange("p (k t) -> p k t", k=8), func=AF.Copy),
               reads=[("pa", tb)], writes=[("hT", j)])

        def post_norm_residual(buf, j, layer, pbanks):
            c0 = 4
            for hf in range(2):
                b = pbanks[hf]
                op("act", lambda q, b=b, hf=hf: q.activation(out=Fh(6 + hf), in_=pab[b][:], func=AF.Square, accum_out=sv(SS2 + hf)),
                   reads=[("pa", b)], writes=[FhK(6 + hf), ("ss2", hf)])
            op("dve", lambda q: q.tensor_scalar(out=sv(SS2, 2), in0=sv(SS2, 2), scalar1=1.0 / 1024, scalar2=EPS / 2, op0=ALU.mult, op1=ALU.add),
               reads=[("ss2", 0), ("ss2", 1)], writes=[("ss2", 0), ("ss2", 1)])
            op("dve", lambda q: q.tensor_tensor(out=sv(SS2), in0=sv(SS2), in1=sv(SS2 + 1), op=ALU.add),
               reads=[("ss2", 0), ("ss2", 1)], writes=[("ss2", 0)])
            op("pool", lambda q: q.tensor_tensor(out=sv(RSTD2), in0=sv(SS2), in1=sv(NHALF), op=ALU.pow),
               reads=[("ss2", 0), "nhalf"], writes=["rstd2"])
            for hf in range(2):
                b = pbanks[hf]
                op("dve", lambda q, b=b, hf=hf: q.scalar_tensor_tensor(out=Fh(6 + hf), in0=pab[b][:], scalar=sv(RSTD2),
                                                                        in1=wpost[:, layer, hf * 512:(hf + 1) * 512], op0=ALU.mult, op1=ALU.mult),
                   reads=[("pa", b), "rstd2", "wpost"], writes=[FhK(6 + hf)])
            op("pool", lambda q: q.tensor_tensor(out=xb[buf][:, j, :], in0=xb[buf][:, j, :], in1=F(3), op=ALU.add),
               reads=[xk(buf, j), FhK(6), FhK(7)], writes=[xk(buf, j)])

        def out_proj(src_bf16_ap, src_keys, Wo, wo_key, bias, ybanks, tb=TB):
            for kc in range(8):
                op("pe", lambda q, kc=kc: q.transpose(out=ptv(tb)[:, kc * 128:(kc + 1) * 128], in_=src_bf16_ap[:, kc * 128:(kc + 1) * 128], identity=ident[:]),
                   reads=src_keys + ["ident"], writes=[("pa", tb)])
            op("act", lambda q: q.activation(out=ogT[:].rearrange("p k t -> p (k t)"), in_=ptv(tb), func=AF.Copy),
               reads=[("pa", tb)], writes=["ogT"])
            banks = list(ybanks)
            for hf in range(2):
                b = banks[hf]
                for kc in range(8):
                    op("pe", lambda q, b=b, kc=kc, hf=hf: q.matmul(pab[b][:], lhsT=ogT[:, kc, :], rhs=Wo[:, kc, hf * 512:(hf + 1) * 512],
                                                                   start=(kc == 0), stop=(kc == 7 and not bias)),
                       reads=["ogT", wo_key], writes=[("pa", b)])
                if bias:
                    op("pe", lambda q, b=b, hf=hf: q.matmul(pab[b][:], lhsT=ones[0:1, :], rhs=browB[0:1, hf * 512:(hf + 1) * 512], start=False, stop=True),
                       reads=["ones", "browB"], writes=[("pa", b)])
            return banks

        def proj_tm(b, j, c0, n, Wt, wkey, bias=True):
            for kc in range(8):
                op("pe", lambda q, kc=kc: q.matmul(pab[b][:, 0:n], lhsT=hT[:, kc, j * 128:(j + 1) * 128], rhs=Wt[:, kc, c0:c0 + n],
                                                   start=(kc == 0), stop=(kc == 7 and not bias)),
                   reads=[("hT", j), wkey], writes=[("pa", b)])
            if bias:
                o1, br = brow(c0, n)
                op("pe", lambda q: q.matmul(pab[b][:, 0:n], lhsT=o1, rhs=br, start=False, stop=True),
                   reads=["ones", "browA"], writes=[("pa", b)])
            return b

        def l0_s1(m, j):
            buf = m % 2
            blk = m * 4 + j
            slot = blk % 2
            S.tag = "m%d.L0.%d.s1" % (m, j)
            make_hT(buf, j, j, 2)
            qkr = F(2)[:, 576:864].rearrange("p (h d) -> p h d", d=16)
            rt = F(2)[:, 0:576].rearrange("p (f h d) -> p f h d", f=4, d=8)
            RK = [FhK(4), FhK(5)]
            bq = [proj_tm(0, j, 0, 512, W0, "W0"), proj_tm(1, j, 512, 512, W0, "W0")]
            for a in range(2):
                op("act", lambda q, a=a: q.activation(out=qkb[:, 8 * a:8 * a + 8, :], in_=pab[bq[a]][:].rearrange("p (h d) -> p h d", h=8), func=AF.Copy),
                   reads=[("pa", bq[a])], writes=["qkb"])
                op("act", lambda q, a=a: q.activation(out=qkr[:, 8 * a:8 * a + 8, :], in_=pab[bq[a]][:].rearrange("p (h d) -> p h d", h=8)[:, :, 0:16], func=AF.Copy),
                   reads=[("pa", bq[a])], writes=RK)
            bkv = proj_tm(0, j, 1024, 256, W0, "W0")
            op("act", lambda q: q.activation(out=qkb[:, 16:18, :], in_=pab[bkv][:, 0:128].rearrange("p (h d) -> p h d", h=2), func=AF.Copy),
               reads=[("pa", bkv)], writes=["qkb"])
            op("act", lambda q: q.activation(out=qkr[:, 16:18, :], in_=pab[bkv][:, 0:128].rearrange("p (h d) -> p h d", h=2)[:, :, 0:16], func=AF.Copy),
               reads=[("pa", bkv)], writes=RK)
            op("act", lambda q: q.activation(out=vaug[:, slot, :, 0:64], in_=pab[bkv][:, 128:256].rearrange("p (g d) -> p g d", g=2), func=AF.Copy),
               reads=[("pa", bkv)], writes=["vaug%d" % slot])
            cb = cosT[:, blk, :].unsqueeze(1).to_broadcast([128, 18, 8])
            sbb = sinT[:, blk, :].unsqueeze(1).to_broadcast([128, 18, 8])
            x1 = qkr[:, :, 0:8]
            x2 = qkr[:, :, 8:16]
            op("pool", lambda q: q.tensor_tensor(out=rt[:, 0], in0=x1, in1=cb, op=ALU.mult), reads=RK + ["cosT"], writes=RK)
            op("pool", lambda q: q.tensor_tensor(out=rt[:, 1], in0=x2, in1=sbb, op=ALU.mult), reads=RK + ["sinT"], writes=RK)
            op("pool", lambda q: q.tensor_tensor(out=rt[:, 2], in0=x2, in1=cb, op=ALU.mult), reads=RK + ["cosT"], writes=RK)
            op("pool", lambda q: q.tensor_tensor(out=rt[:, 3], in0=x1, in1=sbb, op=ALU.mult), reads=RK + ["sinT"], writes=RK)
            op("dve", lambda q: q.tensor_tensor(out=qkb[:, :, 0:8], in0=rt[:, 0], in1=rt[:, 1], op=ALU.subtract), reads=RK, writes=["qkb"])
            op("dve", lambda q: q.tensor_tensor(out=qkb[:, :, 8:16], in0=rt[:, 2], in1=rt[:, 3], op=ALU.add), reads=RK, writes=["qkb"])
            tbk = (2, 2)
            for a in range(2):
                for hh in range(8):
                    op("pe", lambda q, a=a, hh=hh: q.transpose(out=ptv(tbk[a])[0:64, hh * 128:(hh + 1) * 128], in_=qkb[:, 8 * a + hh, :], identity=ident[:]),
                       reads=["qkb", "ident"], writes=[("pa", tbk[a])])
                op("act", lambda q, a=a: q.activation(out=qT[:, 8 * a:8 * a + 8, :], in_=ptv(tbk[a])[0:64, :].rearrange("p (h t) -> p h t", h=8), func=AF.Copy),
                   reads=[("pa", tbk[a])], writes=["qT"])
            for g in range(2):
                op("pe", lambda q, g=g: q.transpose(out=ptv(2)[0:64, g * 128:(g + 1) * 128], in_=qkb[:, 16 + g, :], identity=ident[:]),
                   reads=["qkb", "ident"], writes=[("pa", 2)])
            op("dve", lambda q: q.tensor_copy(out=kT[:, slot, :, :], in_=ptv(2)[0:64, 0:256].rearrange("p (g t) -> p g t", g=2)),
               reads=[("pa", 2)], writes=["kT%d" % slot])

        def l0_s2(m, j):
            blk = m * 4 + j
            first = (blk % (SEQ // 128) == 0)
            slot = blk % 2
            S.tag = "m%d.L0.%d.s2" % (m, j)
            pti = 0
            sb_i = 0
            for g in range(2):
                for a in range(2):
                    grp = 2 * g + a
                    ob = OA_BANKS[grp % len(OA_BANKS)]
                    kbs = ([] if first else [(1 - slot, mask_prev, "mask_prev")]) + [(slot, mask_cur, "mask_cur")]
                    pts = []
                    for (ks, msk, mkey) in kbs:
                        b = SC_BANKS[sb_i % len(SC_BANKS)]
                        sb_i += 1
                        op("pe", lambda q, b=b, ks=ks, g=g, a=a: q.matmul(pab[b][:], lhsT=kT[:, ks, g, :],
                                                                          rhs=qT[:, 8 * g + 4 * a:8 * g + 4 * a + 4, :].rearrange("p h t -> p (h t)"),
                                                                          start=True, stop=True),
                           reads=["kT%d" % ks, "qT"], writes=[("pa", b)])
                        p = pti % 4
                        pti += 1
                        op("act", lambda q, b=b, p=p: q.activation(out=PT[p][:], in_=pab[b][:], func=AF.Exp), reads=[("pa", b)], writes=[("PT", p)])
                        op("dve", lambda q, p=p, msk=msk: q.tensor_tensor(out=PT[p][:].rearrange("p (h t) -> p h t", h=4),
                                                                         in0=PT[p][:].rearrange("p (h t) -> p h t", h=4),
                                                                         in1=msk[:].unsqueeze(1).to_broadcast([128, 4, 128]), op=ALU.mult),
                           reads=[("PT", p), mkey], writes=[("PT", p)])
                        pts.append((p, ks))
                    for hh in range(4):
                        oo = hh * 65
                        for i, (p, ks) in enumerate(pts):
                            op("pe", lambda q, p=p, ks=ks, hh=hh, ob=ob, oo=oo, i=i, n=len(pts), g=g: q.matmul(
                                pab[ob][:, oo:oo + 65], lhsT=PT[p][:, hh * 128:(hh + 1) * 128], rhs=vaug[:, ks, g, :],
                                start=(i == 0), stop=(i == n - 1)),
                               reads=[("PT", p), "vaug%d" % ks], writes=[("pa", ob)])
                    h0 = 4 * grp
                    ov = pab[ob][:, 0:260].rearrange("p (h d) -> p h d", d=65)
                    op("dve", lambda q, ov=ov, h0=h0: q.tensor_tensor(out=sv(DEN + h0, 4), in0=ov[:, :, 64], in1=sv(ESINK + h0, 4), op=ALU.add),
                       reads=[("pa", ob), "esink"], writes=[("den", grp)])
                    op("dve", lambda q, h0=h0: q.reciprocal(out=sv(RDEN + h0, 4), in_=sv(DEN + h0, 4)), reads=[("den", grp)], writes=[("rden", grp)])
                    op("dve", lambda q, ov=ov, h0=h0: q.tensor_tensor(out=F(0).rearrange("p (h d) -> p h d", d=64)[:, h0:h0 + 4, :], in0=ov[:, :, 0:64],
                                                                     in1=sv(RDEN + h0, 4).unsqueeze(2).to_broadcast([128, 4, 64]), op=ALU.mult),
                       reads=[("pa", ob), ("rden", grp)], writes=[("of", grp)])

        def l0_s3(m, j):
            buf = m % 2
            S.tag = "m%d.L0.%d.s3" % (m, j)
            bz = [proj_tm(Z_BANKS[0], j, 1280, 512, W0, "W0"), proj_tm(Z_BANKS[1], j, 1792, 512, W0, "W0")]
            for hf in range(2):
                op("act", lambda q, hf=hf: q.activation(out=Fh(2 + hf), in_=pab[bz[hf]][:], func=AF.Tanh), reads=[("pa", bz[hf])], writes=[FhK(2 + hf)])
                op("dve", lambda q, hf=hf: q.scalar_tensor_tensor(out=Fh(2 + hf), in0=Fh(2 + hf), scalar=1.0, in1=pab[bz[hf]][:], op0=ALU.add, op1=ALU.mult),
                   reads=[FhK(2 + hf), ("pa", bz[hf])], writes=[FhK(2 + hf)])
            op("dve", lambda q: q.tensor_tensor(out=og_all[:, 0, :], in0=F(0), in1=F(1), op=ALU.mult),
               reads=[("of", 0), ("of", 1), ("of", 2), ("of", 3), FhK(0), FhK(1), FhK(2), FhK(3)], writes=[("og", 0), FhK(0), FhK(1)])
            yb = out_proj(og_all[:, 0, :], [("og", 0)], Wo0, "Wo0", True, Y_BANKS, 2)
            post_norm_residual(buf, j, 0, yb)

        def l1_A(m, h):
            ws = h % 2
            Wt = W1h[ws]
            wk = "W1h%d" % ws
            db = h % 2
            S.tag = "m%d.L1.h%d.A" % (m, h)
            bq, bf, bv, bzz = L1_BANKS[0:4]
            hTk = [("hT", j) for j in range(4)]
            for (b, c0) in ((bq, 0), (bf, 128)):
                for kc in range(8):
                    op("pe", lambda q, b=b, c0=c0, kc=kc: q.matmul(pab[b][:], lhsT=Wt[:, kc, c0:c0 + 128], rhs=hT[:, kc, :], start=(kc == 0), stop=(kc == 7)),
                       reads=hTk + [wk], writes=[("pa", b)])
            for (b, c0) in ((bv, 256), (bzz, 384)):
                for j in range(4):
                    for kc in range(8):
                        op("pe", lambda q, b=b, c0=c0, kc=kc, j=j: q.matmul(pab[b][:, j * 128:(j + 1) * 128], lhsT=hT[:, kc, j * 128:(j + 1) * 128],
                                                                            rhs=Wt[:, kc, c0:c0 + 128], start=(kc == 0), stop=(kc == 7)),
                           reads=[("hT", j), wk], writes=[("pa", b)])
            if h + 2 < 8:
                load_w1h(h + 2, ws)
            A0, A1, A2, A3, A4, A5, A6 = [Fh(i) for i in range(7)]
            K0, K1, K2, K3, K4, K5, K6 = [FhK(i) for i in range(7)]
            gz = Fh(7) if db == 0 else GZ[0][:]
            gzk = FhK(7) if db == 0 else "gz0"
            op("act", lambda q: q.activation(out=A0, in_=pab[bq][:], func=AF.Tanh), reads=[("pa", bq)], writes=[K0])
            op("dve", lambda q: q.scalar_tensor_tensor(out=A0, in0=A0, scalar=1.0, in1=pab[bq][:], op0=ALU.add, op1=ALU.mult),
               reads=[K0, ("pa", bq)], writes=[K0])
            op("act", lambda q: q.activation(out=A1, in_=pab[bf][:], func=AF.Tanh), reads=[("pa", bf)], writes=[K1])
            op("act", lambda q: q.activation(out=A2, in_=A1, func=AF.Identity, scale=sv(LBB + h), bias=sv(LBA + h)), reads=[K1] + LBK, writes=[K2])
            op("act", lambda q: q.activation(out=A3, in_=A1, func=AF.Identity, scale=sv(LBNB + h), bias=sv(LBB + h)), reads=[K1] + LBK, writes=[K3])
            op("dve", lambda q: q.tensor_tensor_scan(out=A4, data0=startmask[:], data1=A2, initial=0.0, op0=ALU.max, op1=ALU.mult),
               reads=["startmask", K2], writes=[K4])
            op("dve", lambda q: q.tensor_tensor(out=qeT[db][:], in0=A0, in1=A4, op=ALU.mult), reads=[K0, K4], writes=[("qeT", db)])
            op("dve", lambda q: q.reciprocal(out=A5, in_=A4), reads=[K4], writes=[K5], dur=3.3)
            op("dve", lambda q: q.tensor_tensor(out=A6, in0=A3, in1=A5, op=ALU.mult), reads=[K3, K5], writes=[K6])
            op("act", lambda q: q.activation(out=keT[db][:], in_=A6, func=AF.Copy), reads=[K6], writes=[("keT", db)])
            op("dve", lambda q: q.tensor_tensor(out=kdT[db][:].rearrange("p (c t) -> p c t", t=64), in0=A6.rearrange("p (c t) -> p c t", t=64),
                                                in1=A4.rearrange("p (c t) -> p c t", t=64)[:, :, 63:64].to_broadcast([128, 8, 64]), op=ALU.mult),
               reads=[K6, K4], writes=[("kdT", db)])
            op("act", lambda q: q.activation(out=glast[:, db, :], in_=A4.rearrange("p (c t) -> p c t", t=64)[:, :, 63], func=AF.Copy),
               reads=[K4], writes=[("glast", db)])
            op("act", lambda q: q.activation(out=vb[db][:].rearrange("p j v -> p (j v)"), in_=pab[bv][:], func=AF.Copy), reads=[("pa", bv)], writes=[("vb", db)])
            op("act", lambda q: q.activation(out=gz, in_=pab[bzz][:], func=AF.Tanh), reads=[("pa", bzz)], writes=[gzk])
            op("dve", lambda q: q.scalar_tensor_tensor(out=gz, in0=gz, scalar=1.0, in1=pab[bzz][:], op0=ALU.add, op1=ALU.mult),
               reads=[gzk, ("pa", bzz)], writes=[gzk])

        def l1_B(m, h):
            db = h % 2
            S.tag = "m%d.L1.h%d.B" % (m, h)
            gz = Fh(7) if db == 0 else GZ[0][:]
            gzk = FhK(7) if db == 0 else "gz0"
            ub = [L1_BANKS[4], L1_BANKS[5]]
            bso = L1_BANKS[6]
            for j in range(4):
                op("pe", lambda q, j=j: q.transpose(out=ptv(TB)[:, j * 128:(j + 1) * 128], in_=kdT[db][:, j * 128:(j + 1) * 128], identity=ident[:]),
                   reads=[("kdT", db), "ident"], writes=[("pa", TB)])
            op("act", lambda q: q.activation(out=kd_tm[:].rearrange("p j k -> p (j k)"), in_=ptv(TB)[:, 0:512], func=AF.Copy), reads=[("pa", TB)], writes=["kd_tm"])
            for j in range(4):
                for c in range(2):
                    op("pe", lambda q, j=j, c=c: q.matmul(pab[ub[c]][:, j * 128:(j + 1) * 128], lhsT=kd_tm[64 * c:64 * c + 64, j, :],
                                                          rhs=vb[db][64 * c:64 * c + 64, j, :], start=True, stop=True),
                       reads=["kd_tm", ("vb", db)], writes=[("pa", ub[c])])
            for j in range(4):
                op("pe", lambda q, j=j: q.matmul(pab[bso][:, j * 128:(j + 1) * 128], lhsT=keT[db][:, j * 128:(j + 1) * 128], rhs=qeT[db][:, j * 128:(j + 1) * 128],
                                                 start=True, stop=True),
                   reads=[("keT", db), ("qeT", db)], writes=[("pa", bso)])
            op("dve", lambda q: q.tensor_tensor(out=smk[:], in0=pab[bso][:].rearrange("p (j t) -> p j t", j=4),
                                                in1=maskbd[:].unsqueeze(1).to_broadcast([128, 4, 128]), op=ALU.mult),
               reads=[("pa", bso), "maskbd"], writes=["smk"])
            op("act", lambda q: q.activation(out=Sbf[:, 0, :], in_=S32[:, h, :], func=AF.Copy), reads=[("S32", h)], writes=[("Sbf", 0)])
            for k in range(8):
                j, c = k // 2, k % 2
                src = S32[:, h, :] if k == 0 else Cbuf[:, k - 1, :]
                skey = ("S32", h) if k == 0 else ("Cbuf", k - 1)
                dst = S32[:, h, :] if k == 7 else Cbuf[:, k, :]
                dkey = ("S32", h) if k == 7 else ("Cbuf", k)
                op("dve", lambda q, k=k, j=j, c=c, src=src, dst=dst: q.scalar_tensor_tensor(out=dst, in0=src, scalar=glast[:, db, k:k + 1],
                                                                                              in1=pab[ub[c]][:, j * 128:(j + 1) * 128], op0=ALU.mult, op1=ALU.add),
                   reads=[skey, ("glast", db), ("pa", ub[c])], writes=[dkey])
            op("act", lambda q: q.activation(out=Sbf[:, 1:8, :], in_=Cbuf[:], func=AF.Copy),
               reads=[("Cbuf", k) for k in range(7)], writes=[("Sbf", k) for k in range(1, 8)])
            for j in range(4):
                op("pe", lambda q, j=j: q.matmul(pab[bso][:, j * 128:(j + 1) * 128], lhsT=smk[:, j, :], rhs=vb[db][:, j, :], start=True, stop=True),
                   reads=["smk", ("vb", db)], writes=[("pa", bso)])
                for c in range(2):
                    k = 2 * j + c
                    op("pe", lambda q, j=j, c=c, k=k: q.matmul(pab[bso][64 * c:64 * c + 64, j * 128:(j + 1) * 128],
                                                               lhsT=qeT[db][:, j * 128 + 64 * c:j * 128 + 64 * c + 64], rhs=Sbf[:, k, :],
                                                               start=False, stop=True, skip_group_check=True),
                       reads=[("qeT", db), ("Sbf", k)], writes=[("pa", bso)])
            for j in range(4):
                op("act", lambda q, j=j: q.activation(out=og_all[:, j, h * 128:(h + 1) * 128], in_=pab[bso][:, j * 128:(j + 1) * 128], func=AF.Square, accum_out=sv(SS3 + j)),
                   reads=[("pa", bso)], writes=[("og", j), ("ss3", j)])
            op("dve", lambda q: q.tensor_scalar(out=sv(SS3, 4), in0=sv(SS3, 4), scalar1=1.0 / 128, scalar2=EPS, op0=ALU.mult, op1=ALU.add),
               reads=[("ss3", j) for j in range(4)], writes=[("ss3", j) for j in range(4)])
            op("pool", lambda q: q.tensor_tensor(out=sv(RSTD3, 4), in0=sv(SS3, 4), in1=sv(NHALF, 4), op=ALU.pow),
               reads=[("ss3", j) for j in range(4)] + ["nhalf"], writes=[("rstd3", j) for j in range(4)])
            for j in range(4):
                op("dve", lambda q, j=j: q.scalar_tensor_tensor(out=og_all[:, j, h * 128:(h + 1) * 128], in0=pab[bso][:, j * 128:(j + 1) * 128],
                                                                 scalar=sv(RSTD3 + j), in1=gz[:, j * 128:(j + 1) * 128], op0=ALU.mult, op1=ALU.mult),
                   reads=[("pa", bso), ("rstd3", j), gzk], writes=[("og", j)])

        load_x(0)
        for m in range(NMT):
            S.new_epoch()
            buf = m % 2
            if m + 1 < NMT:
                load_x(m + 1)
            load_w1h(0, 0)
            load_w1h(1, 1)
            l0_s1(m, 0)
            for j in range(4):
                l0_s2(m, j)
                if j + 1 < 4:
                    l0_s1(m, j + 1)
                l0_s3(m, j)
            S.tag = "m%d.L1.pre" % m
            for j in range(4):
                make_hT(buf, j, j)
            if m % MT_PER_SEQ == 0:
                op("pool", lambda q: q.memset(S32[:], 0.0), writes=[("S32", h) for h in range(8)])
            l1_A(m, 0)
            for h in range(8):
                if h + 1 < 8:
                    l1_A(m, h + 1)
                l1_B(m, h)
            S.tag = "m%d.L1.out" % m
            for j in range(4):
                yb = out_proj(og_all[:, j, :], [("og", j)], Wo1, "Wo1", False, ((L1_BANKS[0], L1_BANKS[1]), (L1_BANKS[2], L1_BANKS[3]))[j % 2])
                post_norm_residual(buf, j, 1, yb)
            op("sp", lambda q, m=m, buf=buf: q.dma_start(out=out_d[m * 512:(m + 1) * 512, :].rearrange("(j p) d -> p j d", p=128), in_=xb[buf][:]),
               reads=[xk(buf, j) for j in range(4)], dma="xs%d" % buf)
        S.emit(st)
    build_program.last_sched = S
    return nc


_PROG_CACHE = {}


def _get_prog(nseq, seq):
    key = (nseq, seq)
    if key not in _PROG_CACHE:
        _PROG_CACHE[key] = build_program(nseq, seq)
    return _PROG_CACHE[key]


def make_in_maps(inputs, n_cores, nseq, seq):
    x = np.ascontiguousarray(inputs["x"], dtype=np.float32)
    pos = np.ascontiguousarray(inputs["positions"], dtype=np.int32)
    cst = make_consts()
    maps = []
    for c in range(n_cores):
        xs = x[c * nseq:(c + 1) * nseq, :seq].reshape(nseq * seq, D)
        ps = pos[c * nseq:(c + 1) * nseq, :seq].reshape(nseq * seq // 128, 128).T
        maps.append({
            "x": np.ascontiguousarray(xs),
            "pos": np.ascontiguousarray(ps),
            "cst": cst,
            "pre_norm_w": np.ascontiguousarray(inputs["pre_norm_w"], dtype=np.float32),
            "post_norm_w": np.ascontiguousarray(inputs["post_norm_w"], dtype=np.float32),
            "attn_w_in": np.ascontiguousarray(inputs["attn_w_in"][0], dtype=np.float32),
            "attn_b_in": np.ascontiguousarray(inputs["attn_b_in"], dtype=np.float32).reshape(1, 2304),
            "attn_sinks": np.ascontiguousarray(inputs["attn_sinks"], dtype=np.float32).reshape(1, 16),
            "attn_w_out": np.ascontiguousarray(inputs["attn_w_out"][0], dtype=np.float32),
            "attn_b_out": np.ascontiguousarray(inputs["attn_b_out"], dtype=np.float32).reshape(1, D),
            "rec_w_in": np.ascontiguousarray(inputs["rec_w_in"][0], dtype=np.float32),
            "rec_lb_logits": np.ascontiguousarray(inputs["rec_lb_logits"], dtype=np.float32),
            "rec_gnorm_w": np.ascontiguousarray(inputs["rec_gnorm_w"], dtype=np.float32).reshape(1, 128),
            "rec_w_out": np.ascontiguousarray(inputs["rec_w_out"][0], dtype=np.float32),
        })
    return maps


def kernel(**inputs):
    B, T, _ = inputs["x"].shape
    nseq = B // N_CORES
    nc = _get_prog(nseq, T)
    maps = make_in_maps(inputs, N_CORES, nseq, T)
    res = run_bass_kernel_spmd(nc, maps, core_ids=list(range(N_CORES)))
    outs = [np.asarray(r["out"], dtype=np.float32).reshape(nseq, T, D) for r in res.results]
    return np.concatenate(outs, axis=0)
```

```python
import math
from contextlib import ExitStack

import numpy as np
import concourse.bass as bass
import concourse.mybir as mybir
from concourse.bass_utils import run_bass_kernel_spmd

F32 = mybir.dt.float32
BF16 = mybir.dt.bfloat16
I32 = mybir.dt.int32
AF = mybir.ActivationFunctionType
ALU = mybir.AluOpType

N_CORES = 8
S1B = (0, 1, 0, 2, 4, 5, 4)
OA_BANKS = (3,)
SC_BANKS = (6, 7)
Z_BANKS = (6, 7)
Y_BANKS = (4, 5)
L1_BANKS = (4, 5, 6, 7, 2, 3, 0, 1)
D = 1024
EPS = 1e-6
TWO_PI = 2.0 * math.pi
C1 = 6.28125
C2 = TWO_PI - C1
PI_LO = 3.1415925


class _Op:
    __slots__ = ("eng", "fn", "idx", "deps", "signal", "sem", "count", "dma", "epoch", "tag", "dur", "start")


class _Rec:
    def __init__(self):
        self.call = None

    def __getattr__(self, name):
        def f(*a, **k):
            self.call = (name, a, k)
            return self
        return f


def _nelem(ap):
    n = 1
    for d in ap.shape[1:]:
        n *= int(d)
    return n


def _estimate_us(eng, fn, is_dma):
    r = _Rec()
    try:
        fn(r)
    except Exception:
        return 0.5
    if r.call is None:
        return 0.3
    name, a, k = r.call
    out = k.get("out", a[0] if a else None)
    try:
        if is_dma:
            esz = 2 if out.dtype == BF16 else 4
            nbytes = _nelem(out) * int(out.shape[0]) * esz
            return 2.0 + nbytes / 200e3
        if eng == "pe":
            if name == "transpose":
                return 0.10
            rhs = k.get("rhs")
            n = _nelem(rhs)
            return 0.02 + n * 0.00058
        n = _nelem(out) if out is not None else 64
        if eng == "act":
            return 0.26 + n * 0.00085 + (0.1 if k.get("accum_out") is not None else 0.0)
        if eng == "dve":
            f = 1.0
            if name == "reciprocal":
                f = 4.2
            elif name == "tensor_tensor_scan":
                f = 2.0
            elif name == "tensor_tensor":
                f = 1.6
            return 0.12 + n * 0.00104 * f
        if eng == "pool":
            if name == "tensor_tensor" and k.get("op") == ALU.pow:
                return 0.65
            return 0.3 + n * 0.0021
    except Exception:
        pass
    return 0.4


class Sched:
    EPOCH = 1500

    def __init__(self, nc):
        self.nc = nc
        self.ops = []
        self.last_w = {}
        self.readers = {}
        self.epoch = 0
        self.tag = ""

    def new_epoch(self):
        pass

    def op(self, eng, fn, reads=(), writes=(), dma=None, dur=None):
        o = _Op()
        o.eng, o.fn, o.idx, o.dma, o.epoch = eng, fn, len(self.ops), dma, 0
        o.signal = False
        o.tag = self.tag
        o.dur = dur if dur is not None else _estimate_us(eng, fn, dma is not None)
        deps = set()
        for k in reads:
            w = self.last_w.get(k)
            if w is not None:
                deps.add(w)
        for k in writes:
            w = self.last_w.get(k)
            if w is not None:
                deps.add(w)
            for r in self.readers.get(k, ()):
                deps.add(r)
        deps.discard(o.idx)
        o.deps = deps
        for k in writes:
            self.last_w[k] = o.idx
            self.readers[k] = []
        for k in reads:
            if k not in writes:
                self.readers.setdefault(k, []).append(o.idx)
        self.ops.append(o)
        return o

    def list_schedule(self, reorder=True):
        import heapq
        ops = self.ops
        n = len(ops)
        if not reorder:
            return {e: [o for o in ops if o.eng == e] for e in ("pe", "act", "dve", "pool", "sp")}, 0.0
        succ = [[] for _ in range(n)]
        indeg = [0] * n
        for o in ops:
            indeg[o.idx] = len(o.deps)
            for d in o.deps:
                succ[d].append(o.idx)
        ready_t = [0.0] * n
        free = {e: 0.0 for e in ("pe", "act", "dve", "pool", "sp")}
        heaps = {e: [] for e in free}
        for o in ops:
            if indeg[o.idx] == 0:
                heapq.heappush(heaps[o.eng], (0.0, o.idx))
        order = {e: [] for e in free}
        done = 0
        SEM_LAT = 0.15
        while done < n:
            best = None
            for e, h in heaps.items():
                if not h:
                    continue
                t0 = max(free[e], h[0][0])
                cand = None
                tmp = []
                while h and h[0][0] <= t0 and len(tmp) < 24:
                    tmp.append(heapq.heappop(h))
                pick = min(tmp, key=lambda x: x[1])
                for x in tmp:
                    if x is not pick:
                        heapq.heappush(h, x)
                heapq.heappush(h, pick)
                cand = (t0, pick[1], e, pick)
                if best is None or cand[:2] < best[:2]:
                    best = cand
            t0, idx, e, pick = best
            h = heaps[e]
            h.remove(pick)
            heapq.heapify(h)
            o = ops[idx]
            o.start = t0
            if o.dma is not None:
                free[e] = t0 + 0.06
                fin = t0 + o.dur
            else:
                free[e] = t0 + o.dur
                fin = free[e]
            order[e].append(o)
            done += 1
            for sidx in succ[idx]:
                so = ops[sidx]
                lat = 0.0 if (so.eng == e and e == "pe" and o.dma is None) else SEM_LAT
                ready_t[sidx] = max(ready_t[sidx], fin + lat)
                indeg[sidx] -= 1
                if indeg[sidx] == 0:
                    heapq.heappush(heaps[so.eng], (ready_t[sidx], sidx))
        return order, max(free.values())

    def emit(self, stack, reorder=True):
        nc = self.nc
        ops = self.ops
        order, makespan = self.list_schedule(reorder)
        self.makespan = makespan
        pos = {}
        for e, lst in order.items():
            for i, o in enumerate(lst):
                pos[o.idx] = i
        for o in ops:
            for d in o.deps:
                p = ops[d]
                if p.dma is None and p.eng == "pe" and o.eng == "pe" and o.dma is None:
                    assert pos[p.idx] < pos[o.idx]
                    continue
                p.signal = True
        counts = {}
        nsig = {}
        for e, lst in order.items():
            for o in lst:
                if o.dma is not None:
                    key = ("dma", o.dma)
                    counts[key] = counts.get(key, 0) + 16
                    o.sem, o.count = key, counts[key]
                elif o.signal:
                    k = nsig.get(e, 0)
                    nsig[e] = k + 1
                    o.epoch = k // self.EPOCH
                    key = (e, o.epoch)
                    counts[key] = counts.get(key, 0) + 1
                    o.sem, o.count = key, counts[key]
        sems = {}
        for key in counts:
            sems[key] = stack.enter_context(nc.semaphore("s_%s_%s" % key))
        self.n_sems = len(sems)
        final = dict(counts)

        def stream(eng_name):
            def body(eng):
                seen = {}
                for o in order[eng_name]:
                    need = {}
                    for d in o.deps:
                        p = ops[d]
                        if p.dma is None and p.eng == "pe" and eng_name == "pe" and o.dma is None:
                            continue
                        if p.dma is not None:
                            skey, val = ("dma", p.dma), (0, p.count)
                        else:
                            skey, val = ("eng", p.eng), (p.epoch, p.count)
                        if val > need.get(skey, (-1, -1)):
                            need[skey] = val
                    for skey, val in need.items():
                        if val <= seen.get(skey, (-1, -1)):
                            continue
                        seen[skey] = val
                        if skey[0] == "dma":
                            eng.wait_ge(sems[("dma", skey[1])], val[1])
                        else:
                            eng.wait_ge(sems[(skey[1], val[0])], val[1])
                    ins = o.fn(eng)
                    if o.dma is not None:
                        ins.then_inc(sems[o.sem], 16)
                    elif o.signal:
                        ins.then_inc(sems[o.sem], 1)
                if eng_name == "sp":
                    for key, c in final.items():
                        if key[0] == "dma":
                            eng.wait_ge(sems[key], c)
            return body

        with nc.Block() as block:
            block.tensor(stream("pe"))
            block.scalar(stream("act"))
            block.vector(stream("dve"))
            block.gpsimd(stream("pool"))
            block.sync(stream("sp"))


CST_W = 128 * 4 + 512 + 8


def make_consts():
    c = np.zeros((128, CST_W), np.float32)
    i = np.arange(128)
    c[:, 0:128] = np.eye(128, dtype=np.float32)
    c[:, 128:256] = (i[:, None] <= i[None, :]).astype(np.float32)
    c[:, 256:384] = (i[:, None] > i[None, :]).astype(np.float32)
    c[:, 384:512] = ((i[:, None] <= i[None, :]) & ((i[:, None] // 64) == (i[None, :] // 64))).astype(np.float32)
    sm = np.zeros(512, np.float32)
    sm[::64] = 1.0
    c[:, 512:1024] = sm[None, :]
    invf = (np.float32(500000.0) ** (-(np.arange(8, dtype=np.float32) * np.float32(2.0) / np.float32(16.0)))).astype(np.float32)
    c[:, 1024:1032] = invf[None, :]
    return c


def build_program(NSEQ, SEQ):
    NT = NSEQ * SEQ
    NB = NT // 128
    NMT = NT // 512
    MT_PER_SEQ = SEQ // 512
    nc = bass.Bass("TRN2", target_bir_lowering=False)

    def din(name, shape, dt=F32):
        return nc.dram_tensor(name, list(shape), dt, kind="ExternalInput").ap()

    x_d = din("x", [NT, D])
    pos_d = din("pos", [128, NB], I32)
    cst_d = din("cst", [128, CST_W])
    prew_d = din("pre_norm_w", [2, D])
    postw_d = din("post_norm_w", [2, D])
    w0_d = din("attn_w_in", [D, 2304])
    b0_d = din("attn_b_in", [1, 2304])
    sink_d = din("attn_sinks", [1, 16])
    wo0_d = din("attn_w_out", [D, D])
    bo0_d = din("attn_b_out", [1, D])
    w1_d = din("rec_w_in", [D, 4096])
    lb_d = din("rec_lb_logits", [2, D])
    gnw_d = din("rec_gnorm_w", [1, 128])
    wo1_d = din("rec_w_out", [D, D])
    out_d = nc.dram_tensor("out", [NT, D], F32, kind="ExternalOutput").ap()
    w1s_d = nc.dram_tensor("w1s", [8, 128, 8, 512], BF16, kind="Internal").ap()

    with ExitStack() as st:
        def sb(name, shape, dt):
            return st.enter_context(nc.sbuf_tensor(name, list(shape), dt))

        def ps(name, shape, dt):
            return st.enter_context(nc.psum_tensor(name, list(shape), dt))

        W0 = sb("W0", [128, 8, 2304], BF16)
        Wo0 = sb("Wo0", [128, 8, 1024], BF16)
        Wo1 = sb("Wo1", [128, 8, 1024], BF16)
        W1h = [sb("W1h%d" % i, [128, 8, 512], BF16) for i in range(2)]
        xb = [sb("xb%d" % i, [128, 4, 1024], F32) for i in range(2)]
        FF = sb("FF", [128, 4096], F32)
        hT = sb("hT", [128, 8, 512], BF16)
        og_all = sb("og_all", [128, 4, 1024], BF16)
        ident = sb("ident", [128, 128], BF16)
        mask_cur = sb("mask_cur", [128, 128], BF16)
        mask_prev = sb("mask_prev", [128, 128], BF16)
        maskbd = sb("maskbd", [128, 128], BF16)
        startmask = sb("startmask", [128, 512], F32)
        invf = sb("invf", [128, 8], F32)
        cosT = sb("cosT", [128, NB, 8], F32)
        sinT = sb("sinT", [128, NB, 8], F32)
        wpost = sb("wpost", [128, 2, 1024], F32)
        browA = sb("browA", [65, 1024], BF16)
        posi = sb("posi", [128, NB], I32)
        browB = sb("browB", [1, 1024], BF16)
        ones = sb("ones", [65, 128], BF16)
        small = sb("small", [128, 144], F32)
        S32 = sb("S32", [128, 8, 128], F32)
        Sbf = sb("Sbf", [128, 8, 128], BF16)
        hb = sb("hb", [128, 1024], BF16)
        qkb = sb("qkb", [128, 18, 64], BF16)
        qkr = sb("qkr", [128, 18, 16], F32)
        rt = sb("rt", [128, 4, 18, 8], F32)
        qT = sb("qT", [64, 16, 128], BF16)
        kT = sb("kT", [64, 2, 2, 128], BF16)
        vaug = sb("vaug", [128, 2, 2, 65], BF16)
        PT = [sb("PT%d" % i, [128, 512], BF16) for i in range(4)]
        ogT = sb("ogT", [128, 8, 128], BF16)
        qeT = [sb("qeT%d" % i, [128, 512], BF16) for i in range(2)]
        keT = [sb("keT%d" % i, [128, 512], BF16) for i in range(2)]
        kdT = [sb("kdT%d" % i, [128, 512], BF16) for i in range(2)]
        kd_tm = sb("kd_tm", [128, 4, 128], BF16)
        vb = [sb("vb%d" % i, [128, 4, 128], BF16) for i in range(2)]
        smk = sb("smk", [128, 4, 128], BF16)

        PREW0, PREW1 = 0, 8
        SCQ0, SCH0, SCH1 = 16, 24, 32
        LBA, LBB, LBNB = 40, 48, 56
        GNW = 64
        ESINK = 65
        SS = 81
        RSTD = 85
        DEN = 89
        RDEN = 105
        NHALF = 121
        SS2, RSTD2 = 125, 127
        SS3, RSTD3 = 128, 132

        def sv(c, n=1):
            return small[:, c:c + n]

        pab = [ps("pab%d" % i, [128, 512], F32) for i in range(8)]
        GZ = [sb("gz0", [128, 512], F32)]
        glast = sb("glast", [128, 2, 8], F32)

        def ptv(i):
            return pab[i][:].bitcast(BF16)

        S = Sched(nc)
        op = S.op
        FK = ["FF0", "FF1", "FF2", "FF3"]

        def F(i):
            return FF[:, i * 1024:(i + 1) * 1024]

        def Fh(i):
            return FF[:, i * 512:(i + 1) * 512]

        def FhK(i):
            return "FH%d" % i

        ALLF = FK + [FhK(i) for i in range(8)]

        op("sp", lambda q: q.dma_start(out=FF[:, 0:CST_W], in_=cst_d), writes=ALLF, dma="cst")
        op("dve", lambda q: q.tensor_copy(out=ident[:], in_=FF[:, 0:128]), reads=ALLF, writes=["ident"])
        op("dve", lambda q: q.tensor_copy(out=mask_cur[:], in_=FF[:, 128:256]), reads=ALLF, writes=["mask_cur"])
        op("dve", lambda q: q.tensor_copy(out=mask_prev[:], in_=FF[:, 256:384]), reads=ALLF, writes=["mask_prev"])
        op("dve", lambda q: q.tensor_copy(out=maskbd[:], in_=FF[:, 384:512]), reads=ALLF, writes=["maskbd"])
        op("dve", lambda q: q.tensor_copy(out=startmask[:], in_=FF[:, 512:1024]), reads=ALLF, writes=["startmask"])
        op("dve", lambda q: q.tensor_copy(out=invf[:], in_=FF[:, 1024:1032]), reads=ALLF, writes=["invf"])
        op("pool", lambda q: q.memset(ones[:], 1.0), writes=["ones"])
        op("pool", lambda q: q.memset(small[:, NHALF:NHALF + 4], -0.5), writes=["nhalf"])
        op("pool", lambda q: q.memset(vaug[:], 1.0), writes=["vaug0", "vaug1"])
        op("sp", lambda q: q.dma_start(out=small[:, PREW0:PREW0 + 8], in_=prew_d[0:1, :].rearrange("o (k p) -> p (o k)", p=128),
                                       allow_slow_non_contiguous=True), writes=["prew", "smq"], dma="sm")
        op("sp", lambda q: q.dma_start(out=small[:, PREW1:PREW1 + 8], in_=prew_d[1:2, :].rearrange("o (k p) -> p (o k)", p=128),
                                       allow_slow_non_contiguous=True), writes=["prew", "smq"], dma="sm")
        op("sp", lambda q: q.dma_start(out=small[:, LBA:LBA + 8], in_=lb_d[0:1, :].rearrange("o (k p) -> p (o k)", p=128),
                                       allow_slow_non_contiguous=True), writes=["lb0", "smq"], dma="sm")
        op("sp", lambda q: q.dma_start(out=small[:, LBB:LBB + 8], in_=lb_d[1:2, :].rearrange("o (k p) -> p (o k)", p=128),
                                       allow_slow_non_contiguous=True), writes=["lb1", "smq"], dma="sm")
        op("sp", lambda q: q.dma_start(out=small[:, GNW:GNW + 1], in_=gnw_d.rearrange("o p -> p o"),
                                       allow_slow_non_contiguous=True), writes=["gnw", "smq"], dma="sm")
        op("sp", lambda q: q.dma_start(out=small[:, ESINK:ESINK + 16], in_=sink_d.partition_broadcast(128)), writes=["esink", "smq"], dma="sm")
        op("sp", lambda q: q.dma_start(out=wpost[:, 0, :], in_=postw_d[0:1, :].partition_broadcast(128)), writes=["wpost", "smq"], dma="sm")
        op("sp", lambda q: q.dma_start(out=wpost[:, 1, :], in_=postw_d[1:2, :].partition_broadcast(128)), writes=["wpost", "smq"], dma="sm")
        op("sp", lambda q: q.dma_start(out=posi[:], in_=pos_d), writes=["posi", "smq"], dma="sm")
        op("dve", lambda q: q.tensor_scalar(out=sv(SCQ0, 8), in0=sv(PREW0, 8), scalar1=0.125, scalar2=None, op0=ALU.mult), reads=["prew"], writes=["scq0"])
        op("dve", lambda q: q.tensor_scalar(out=sv(SCH0, 8), in0=sv(PREW0, 8), scalar1=0.5, scalar2=None, op0=ALU.mult), reads=["prew"], writes=["sch0"])
        op("dve", lambda q: q.tensor_scalar(out=sv(SCH1, 8), in0=sv(PREW1, 8), scalar1=0.5, scalar2=None, op0=ALU.mult), reads=["prew"], writes=["sch1"])
        op("act", lambda q: q.activation(out=sv(ESINK, 16), in_=sv(ESINK, 16), func=AF.Exp), reads=["esink"], writes=["esink"])
        op("dve", lambda q: q.tensor_tensor(out=sv(LBNB, 8), in0=sv(LBB, 8), in1=sv(LBA, 8), op=ALU.subtract), reads=["lb0", "lb1"], writes=["lbt"])
        op("act", lambda q: q.activation(out=sv(LBNB, 8), in_=sv(LBNB, 8), func=AF.Tanh, scale=0.5), reads=["lbt"], writes=["lbt"])
        op("dve", lambda q: q.tensor_scalar(out=sv(LBA, 8), in0=sv(LBNB, 8), scalar1=0.25, scalar2=0.75, op0=ALU.mult, op1=ALU.add), reads=["lbt"], writes=["lb0"])
        op("dve", lambda q: q.tensor_scalar(out=sv(LBB, 8), in0=sv(LBNB, 8), scalar1=-0.25, scalar2=0.25, op0=ALU.mult, op1=ALU.add), reads=["lbt"], writes=["lb1"])
        op("dve", lambda q: q.tensor_scalar(out=sv(LBNB, 8), in0=sv(LBB, 8), scalar1=-1.0, scalar2=None, op0=ALU.mult), reads=["lb1", "lbt"], writes=["lbt"])
        LBK = ["lb0", "lb1", "lbt"]

        o0 = 1040
        posf = FF[:, o0:o0 + NB]
        ang = FF[:, o0 + NB:o0 + NB + NB * 8]
        tmpu = FF[:, o0 + 9 * NB:o0 + 17 * NB]
        tmpk = FF[:, o0 + 17 * NB:o0 + 25 * NB]
        assert o0 + 25 * NB <= 4096
        tmpi = hb[:].bitcast(I32)[:, 0:NB * 8]
        op("dve", lambda q: q.tensor_copy(out=posf, in_=posi[:]), reads=["posi"] + ALLF, writes=ALLF)
        op("dve", lambda q: q.tensor_tensor(out=ang.rearrange("p (b i) -> p b i", i=8),
                                            in0=posf.unsqueeze(2).to_broadcast([128, NB, 8]),
                                            in1=invf[:].unsqueeze(1).to_broadcast([128, NB, 8]), op=ALU.mult),
           reads=ALLF + ["invf"], writes=ALLF)
        for which, tab in ((0, sinT), (1, cosT)):
            if which == 1:
                op("dve", lambda q: q.tensor_scalar(out=ang, in0=ang, scalar1=math.pi / 2, scalar2=None, op0=ALU.add), reads=ALLF, writes=ALLF)
            op("dve", lambda q: q.tensor_scalar(out=tmpu, in0=ang, scalar1=1.0 / TWO_PI, scalar2=None, op0=ALU.mult), reads=ALLF, writes=ALLF)
            op("dve", lambda q: q.tensor_copy(out=tmpi, in_=tmpu), reads=ALLF, writes=["hb"])
            op("dve", lambda q: q.tensor_copy(out=tmpk, in_=tmpi), reads=["hb"], writes=ALLF)
            op("dve", lambda q: q.scalar_tensor_tensor(out=tmpu, in0=tmpk, scalar=-C1, in1=ang, op0=ALU.mult, op1=ALU.add), reads=ALLF, writes=ALLF)
            op("dve", lambda q: q.scalar_tensor_tensor(out=tmpu, in0=tmpk, scalar=-C2, in1=tmpu, op0=ALU.mult, op1=ALU.add), reads=ALLF, writes=ALLF)
            op("dve", lambda q: q.tensor_scalar(out=tmpu, in0=tmpu, scalar1=-PI_LO, scalar2=PI_LO, op0=ALU.max, op1=ALU.min), reads=ALLF, writes=ALLF)
            op("act", lambda q, tab=tab: q.activation(out=tab[:].rearrange("p b i -> p (b i)"), in_=tmpu, func=AF.Sin),
               reads=ALLF, writes=["cosT" if which == 1 else "sinT"])

        BROWS = [(0, 1024, 0), (1024, 1792, 32), (1792, 2304, 64)]
        for (c0, c1, p) in BROWS:
            op("sp", lambda q, c0=c0, c1=c1, p=p: q.dma_start(out=FF[p:p + 1, 0:c1 - c0], in_=b0_d[0:1, c0:c1]),
               reads=["cosT", "sinT"], writes=ALLF + ["smq"], dma="sm")
        op("sp", lambda q: q.dma_start(out=FF[0:1, 1024:2048], in_=bo0_d), writes=ALLF + ["smq"], dma="sm")
        op("dve", lambda q: q.tensor_scalar(out=browA[0:1, 0:1024], in0=FF[0:1, 0:1024], scalar1=0.125, scalar2=None, op0=ALU.mult), reads=ALLF, writes=["browA"])
        op("dve", lambda q: q.tensor_copy(out=browA[32:33, 0:256], in_=FF[32:33, 0:256]), reads=ALLF, writes=["browA"])
        op("dve", lambda q: q.tensor_scalar(out=browA[32:33, 256:768], in0=FF[32:33, 256:768], scalar1=0.5, scalar2=None, op0=ALU.mult), reads=ALLF, writes=["browA"])
        op("dve", lambda q: q.tensor_scalar(out=browA[64:65, 0:512], in0=FF[64:65, 0:512], scalar1=0.5, scalar2=None, op0=ALU.mult), reads=ALLF, writes=["browA"])
        op("dve", lambda q: q.tensor_copy(out=browB[:], in_=FF[0:1, 1024:2048]), reads=ALLF, writes=["browB"])

        def brow(c0, n):
            for (r0, r1, p) in BROWS:
                if r0 <= c0 and c0 + n <= r1:
                    return ones[p:p + 1, :], browA[p:p + 1, c0 - r0:c0 - r0 + n]
            raise AssertionError((c0, n))

        stageA = FF
        stageB = xb[1][:].rearrange("p j d -> p (j d)")
        XB1K = [("x", 1, j) for j in range(4)]
        stg = [(stageA, ALLF), (stageB, XB1K)]
        stb = [(og_all[:].rearrange("p j d -> p (j d)"), [("og", j) for j in range(4)]), (hT[:].rearrange("p k t -> p (k t)"), [("hT", j) for j in range(4)])]
        cnt = [0]

        def conv(eng, out, in_, scal, rd, wr):
            if eng == "dve":
                op("dve", lambda q: q.tensor_scalar(out=out, in0=in_, scalar1=scal, scalar2=None, op0=ALU.mult), reads=rd, writes=wr)
            else:
                op("act", lambda q: q.activation(out=out, in_=in_, func=AF.Copy, scale=scal), reads=rd, writes=wr)

        for kc in range(8):
            sg, sk = stg[cnt[0] % 2]
            cnt[0] += 1
            op("sp", lambda q, sg=sg, kc=kc: q.dma_start(out=sg[:, 0:2304], in_=w0_d[kc * 128:(kc + 1) * 128, :]),
               reads=["browA", "browB"], writes=sk, dma="wl%d" % (cnt[0] % 2))
            conv("dve", W0[:, kc, 0:1024], sg[:, 0:1024], sv(SCQ0 + kc), sk + ["scq0"], ["W0"])
            conv("act", W0[:, kc, 1024:1280], sg[:, 1024:1280], sv(PREW0 + kc), sk + ["prew"], ["W0"])
            conv("act", W0[:, kc, 1280:2304], sg[:, 1280:2304], sv(SCH0 + kc), sk + ["sch0"], ["W0"])
        for kc in range(8):
            sg, sk = stg[cnt[0] % 2]
            cnt[0] += 1
            op("sp", lambda q, sg=sg, kc=kc: q.dma_start(out=sg[:, 0:1024], in_=wo0_d[kc * 128:(kc + 1) * 128, :]),
               writes=sk, dma="wl%d" % (cnt[0] % 2))
            op("sp", lambda q, sg=sg, kc=kc: q.dma_start(out=sg[:, 1024:2048], in_=wo1_d[kc * 128:(kc + 1) * 128, :]),
               writes=sk, dma="wl%d" % (cnt[0] % 2))
            op("act", lambda q, sg=sg, kc=kc: q.activation(out=Wo0[:, kc, :], in_=sg[:, 0:1024], func=AF.Copy), reads=sk, writes=["Wo0"])
            conv("dve", Wo1[:, kc, :], sg[:, 1024:2048], sv(GNW), sk + ["gnw"], ["Wo1"])
        for kc in range(8):
            sg, sk = stg[cnt[0] % 2]
            sbt, sbk = stb[cnt[0] % 2]
            cnt[0] += 1
            op("sp", lambda q, sg=sg, kc=kc: q.dma_start(out=sg[:, 0:4096], in_=w1_d[kc * 128:(kc + 1) * 128, :]),
               writes=sk, dma="wl%d" % (cnt[0] % 2))
            for t in range(4):
                o_ap = sbt.rearrange("p (h t c) -> p t h c", h=8, t=4, c=128)[:, t, :, :]
                i_ap = sg[:, t * 1024:(t + 1) * 1024].rearrange("p (h c) -> p h c", h=8)
                scal = sv(PREW1 + kc) if t == 2 else sv(SCH1 + kc)
                conv("dve" if t % 2 == 0 else "act", o_ap, i_ap, scal, sk + ["prew", "sch1"], sbk)
            op("sp", lambda q, sbt=sbt, kc=kc: q.dma_start(out=w1s_d[:, :, kc, :].rearrange("h p c -> p h c"),
                                                          in_=sbt.rearrange("p (h c) -> p h c", h=8)),
               reads=sbk, writes=["w1s"], dma="ws%d" % (cnt[0] % 2))

        def xk(buf, j):
            return ("x", buf, j)

        def load_x(m):
            buf = m % 2
            op("sp", lambda q: q.dma_start(out=xb[buf][:], in_=x_d[m * 512:(m + 1) * 512, :].rearrange("(j p) d -> p j d", p=128)),
               writes=[xk(buf, j) for j in range(4)], dma="xl%d" % buf)

        def load_w1h(h, slot):
            op("sp", lambda q: q.dma_start(out=W1h[slot][:], in_=w1s_d[h]), reads=["w1s"], writes=["W1h%d" % slot], dma="w1l%d" % slot)

        def rms_rstd(src_ap, src_keys, col):
            op("act", lambda q: q.activation(out=hb[:], in_=src_ap, func=AF.Square, accum_out=sv(SS + col)),
               reads=src_keys, writes=["hb", ("ss", col)])
            op("dve", lambda q: q.tensor_scalar(out=sv(SS + col), in0=sv(SS + col), scalar1=1.0 / 1024, scalar2=EPS, op0=ALU.mult, op1=ALU.add),
               reads=[("ss", col)], writes=[("ss", col)])
            op("pool", lambda q: q.tensor_tensor(out=sv(RSTD + col), in0=sv(SS + col), in1=sv(NHALF), op=ALU.pow),
               reads=[("ss", col), "nhalf"], writes=[("rstd", col)])

        TB = L1_BANKS[7]

        def make_hT(buf, j, col, tb=TB):
            xs = xb[buf][:, j, :]
            rms_rstd(xs, [xk(buf, j)], col)
            op("act", lambda q: q.activation(out=hb[:], in_=xs, func=AF.Copy, scale=sv(RSTD + col)),
               reads=[xk(buf, j), ("rstd", col)], writes=["hb"])
            for kc in range(8):
                op("pe", lambda q, kc=kc: q.transpose(out=ptv(tb)[:, kc * 128:(kc + 1) * 128], in_=hb[:, kc * 128:(kc + 1) * 128], identity=ident[:]),
                   reads=["hb", "ident"], writes=[("pa", tb)])
            op("act", lambda q: q.activation(out=hT[:, :, j * 128:(j + 1) * 128], in_=ptv(tb).rearrange("p (k t) -> p k t", k=8), func=AF.Copy),
               reads=[("pa", tb)], writes=[("hT", j)])

        def post_norm_residual(buf, j, layer, pbanks):
            c0 = 4
            for hf in range(2):
                b = pbanks[hf]
                op("act", lambda q, b=b, hf=hf: q.activation(out=Fh(6 + hf), in_=pab[b][:], func=AF.Square, accum_out=sv(SS2 + hf)),
                   reads=[("pa", b)], writes=[FhK(6 + hf), ("ss2", hf)])
            op("dve", lambda q: q.tensor_scalar(out=sv(SS2, 2), in0=sv(SS2, 2), scalar1=1.0 / 1024, scalar2=EPS / 2, op0=ALU.mult, op1=ALU.add),
               reads=[("ss2", 0), ("ss2", 1)], writes=[("ss2", 0), ("ss2", 1)])
            op("dve", lambda q: q.tensor_tensor(out=sv(SS2), in0=sv(SS2), in1=sv(SS2 + 1), op=ALU.add),
               reads=[("ss2", 0), ("ss2", 1)], writes=[("ss2", 0)])
            op("pool", lambda q: q.tensor_tensor(out=sv(RSTD2), in0=sv(SS2), in1=sv(NHALF), op=ALU.pow),
               reads=[("ss2", 0), "nhalf"], writes=["rstd2"])
            for hf in range(2):
                b = pbanks[hf]
                op("dve", lambda q, b=b, hf=hf: q.scalar_tensor_tensor(out=Fh(6 + hf), in0=pab[b][:], scalar=sv(RSTD2),
                                                                        in1=wpost[:, layer, hf * 512:(hf + 1) * 512], op0=ALU.mult, op1=ALU.mult),
                   reads=[("pa", b), "rstd2", "wpost"], writes=[FhK(6 + hf)])
            op("pool", lambda q: q.tensor_tensor(out=xb[buf][:, j, :], in0=xb[buf][:, j, :], in1=F(3), op=ALU.add),
               reads=[xk(buf, j), FhK(6), FhK(7)], writes=[xk(buf, j)])

        def out_proj(src_bf16_ap, src_keys, Wo, wo_key, bias, ybanks, tb=TB):
            for kc in range(8):
                op("pe", lambda q, kc=kc: q.transpose(out=ptv(tb)[:, kc * 128:(kc + 1) * 128], in_=src_bf16_ap[:, kc * 128:(kc + 1) * 128], identity=ident[:]),
                   reads=src_keys + ["ident"], writes=[("pa", tb)])
            op("act", lambda q: q.activation(out=ogT[:].rearrange("p k t -> p (k t)"), in_=ptv(tb), func=AF.Copy),
               reads=[("pa", tb)], writes=["ogT"])
            banks = list(ybanks)
            for hf in range(2):
                b = banks[hf]
                for kc in range(8):
                    op("pe", lambda q, b=b, kc=kc, hf=hf: q.matmul(pab[b][:], lhsT=ogT[:, kc, :], rhs=Wo[:, kc, hf * 512:(hf + 1) * 512],
                                                                   start=(kc == 0), stop=(kc == 7 and not bias)),
                       reads=["ogT", wo_key], writes=[("pa", b)])
                if bias:
                    op("pe", lambda q, b=b, hf=hf: q.matmul(pab[b][:], lhsT=ones[0:1, :], rhs=browB[0:1, hf * 512:(hf + 1) * 512], start=False, stop=True),
                       reads=["ones", "browB"], writes=[("pa", b)])
            return banks

        def proj_tm(b, j, c0, n, Wt, wkey, bias=True):
            for kc in range(8):
                op("pe", lambda q, kc=kc: q.matmul(pab[b][:, 0:n], lhsT=hT[:, kc, j * 128:(j + 1) * 128], rhs=Wt[:, kc, c0:c0 + n],
                                                   start=(kc == 0), stop=(kc == 7 and not bias)),
                   reads=[("hT", j), wkey], writes=[("pa", b)])
            if bias:
                o1, br = brow(c0, n)
                op("pe", lambda q: q.matmul(pab[b][:, 0:n], lhsT=o1, rhs=br, start=False, stop=True),
                   reads=["ones", "browA"], writes=[("pa", b)])
            return b

        def l0_s1(m, j):
            buf = m % 2
            blk = m * 4 + j
            slot = blk % 2
            S.tag = "m%d.L0.%d.s1" % (m, j)
            make_hT(buf, j, j, S1B[3])
            bq = [proj_tm(S1B[0], j, 0, 512, W0, "W0"), proj_tm(S1B[1], j, 512, 512, W0, "W0")]
            for a in range(2):
                op("act", lambda q, a=a: q.activation(out=qkb[:, 8 * a:8 * a + 8, :], in_=pab[bq[a]][:].rearrange("p (h d) -> p h d", h=8), func=AF.Copy),
                   reads=[("pa", bq[a])], writes=["qkb"])
                op("act", lambda q, a=a: q.activation(out=qkr[:, 8 * a:8 * a + 8, :], in_=pab[bq[a]][:].rearrange("p (h d) -> p h d", h=8)[:, :, 0:16], func=AF.Copy),
                   reads=[("pa", bq[a])], writes=["qkr"])
            bkv = proj_tm(S1B[2], j, 1024, 256, W0, "W0")
            op("act", lambda q: q.activation(out=qkb[:, 16:18, :], in_=pab[bkv][:, 0:128].rearrange("p (h d) -> p h d", h=2), func=AF.Copy),
               reads=[("pa", bkv)], writes=["qkb"])
            op("act", lambda q: q.activation(out=qkr[:, 16:18, :], in_=pab[bkv][:, 0:128].rearrange("p (h d) -> p h d", h=2)[:, :, 0:16], func=AF.Copy),
               reads=[("pa", bkv)], writes=["qkr"])
            op("act", lambda q: q.activation(out=vaug[:, slot, :, 0:64], in_=pab[bkv][:, 128:256].rearrange("p (g d) -> p g d", g=2), func=AF.Copy),
               reads=[("pa", bkv)], writes=["vaug%d" % slot])
            cb = cosT[:, blk, :].unsqueeze(1).to_broadcast([128, 18, 8])
            sbb = sinT[:, blk, :].unsqueeze(1).to_broadcast([128, 18, 8])
            x1 = qkr[:, :, 0:8]
            x2 = qkr[:, :, 8:16]
            op("pool", lambda q: q.tensor_tensor(out=rt[:, 0], in0=x1, in1=cb, op=ALU.mult), reads=["qkr", "cosT"], writes=["rt0"])
            op("pool", lambda q: q.tensor_tensor(out=rt[:, 1], in0=x2, in1=sbb, op=ALU.mult), reads=["qkr", "sinT"], writes=["rt1"])
            op("pool", lambda q: q.tensor_tensor(out=rt[:, 2], in0=x2, in1=cb, op=ALU.mult), reads=["qkr", "cosT"], writes=["rt2"])
            op("pool", lambda q: q.tensor_tensor(out=rt[:, 3], in0=x1, in1=sbb, op=ALU.mult), reads=["qkr", "sinT"], writes=["rt3"])
            op("dve", lambda q: q.tensor_tensor(out=qkb[:, :, 0:8], in0=rt[:, 0], in1=rt[:, 1], op=ALU.subtract), reads=["rt0", "rt1"], writes=["qkb"])
            op("dve", lambda q: q.tensor_tensor(out=qkb[:, :, 8:16], in0=rt[:, 2], in1=rt[:, 3], op=ALU.add), reads=["rt2", "rt3"], writes=["qkb"])
            tbk = (S1B[4], S1B[5])
            for a in range(2):
                for hh in range(8):
                    op("pe", lambda q, a=a, hh=hh: q.transpose(out=ptv(tbk[a])[0:64, hh * 128:(hh + 1) * 128], in_=qkb[:, 8 * a + hh, :], identity=ident[:]),
                       reads=["qkb", "ident"], writes=[("pa", tbk[a])])
                op("act", lambda q, a=a: q.activation(out=qT[:, 8 * a:8 * a + 8, :], in_=ptv(tbk[a])[0:64, :].rearrange("p (h t) -> p h t", h=8), func=AF.Copy),
                   reads=[("pa", tbk[a])], writes=["qT"])
            for g in range(2):
                op("pe", lambda q, g=g: q.transpose(out=ptv(S1B[6])[0:64, g * 128:(g + 1) * 128], in_=qkb[:, 16 + g, :], identity=ident[:]),
                   reads=["qkb", "ident"], writes=[("pa", S1B[6])])
            op("dve", lambda q: q.tensor_copy(out=kT[:, slot, :, :], in_=ptv(S1B[6])[0:64, 0:256].rearrange("p (g t) -> p g t", g=2)),
               reads=[("pa", S1B[6])], writes=["kT%d" % slot])

        def l0_s2(m, j):
            blk = m * 4 + j
            first = (blk % (SEQ // 128) == 0)
            slot = blk % 2
            S.tag = "m%d.L0.%d.s2" % (m, j)
            pti = 0
            sb_i = 0
            for g in range(2):
                for a in range(2):
                    grp = 2 * g + a
                    ob = OA_BANKS[grp % len(OA_BANKS)]
                    kbs = ([] if first else [(1 - slot, mask_prev, "mask_prev")]) + [(slot, mask_cur, "mask_cur")]
                    pts = []
                    for (ks, msk, mkey) in kbs:
                        b = SC_BANKS[sb_i % len(SC_BANKS)]
                        sb_i += 1
                        op("pe", lambda q, b=b, ks=ks, g=g, a=a: q.matmul(pab[b][:], lhsT=kT[:, ks, g, :],
                                                                          rhs=qT[:, 8 * g + 4 * a:8 * g + 4 * a + 4, :].rearrange("p h t -> p (h t)"),
                                                                          start=True, stop=True),
                           reads=["kT%d" % ks, "qT"], writes=[("pa", b)])
                        p = pti % 4
                        pti += 1
                        op("act", lambda q, b=b, p=p: q.activation(out=PT[p][:], in_=pab[b][:], func=AF.Exp), reads=[("pa", b)], writes=[("PT", p)])
                        op("dve", lambda q, p=p, msk=msk: q.tensor_tensor(out=PT[p][:].rearrange("p (h t) -> p h t", h=4),
                                                                         in0=PT[p][:].rearrange("p (h t) -> p h t", h=4),
                                                                         in1=msk[:].unsqueeze(1).to_broadcast([128, 4, 128]), op=ALU.mult),
                           reads=[("PT", p), mkey], writes=[("PT", p)])
                        pts.append((p, ks))
                    for hh in range(4):
                        oo = hh * 65
                        for i, (p, ks) in enumerate(pts):
                            op("pe", lambda q, p=p, ks=ks, hh=hh, ob=ob, oo=oo, i=i, n=len(pts), g=g: q.matmul(
                                pab[ob][:, oo:oo + 65], lhsT=PT[p][:, hh * 128:(hh + 1) * 128], rhs=vaug[:, ks, g, :],
                                start=(i == 0), stop=(i == n - 1)),
                               reads=[("PT", p), "vaug%d" % ks], writes=[("pa", ob)])
                    h0 = 4 * grp
                    ov = pab[ob][:, 0:260].rearrange("p (h d) -> p h d", d=65)
                    op("dve", lambda q, ov=ov, h0=h0: q.tensor_tensor(out=sv(DEN + h0, 4), in0=ov[:, :, 64], in1=sv(ESINK + h0, 4), op=ALU.add),
                       reads=[("pa", ob), "esink"], writes=[("den", grp)])
                    op("dve", lambda q, h0=h0: q.reciprocal(out=sv(RDEN + h0, 4), in_=sv(DEN + h0, 4)), reads=[("den", grp)], writes=[("rden", grp)])
                    op("dve", lambda q, ov=ov, h0=h0: q.tensor_tensor(out=F(0).rearrange("p (h d) -> p h d", d=64)[:, h0:h0 + 4, :], in0=ov[:, :, 0:64],
                                                                     in1=sv(RDEN + h0, 4).unsqueeze(2).to_broadcast([128, 4, 64]), op=ALU.mult),
                       reads=[("pa", ob), ("rden", grp)], writes=[("of", grp)])

        def l0_s3(m, j):
            buf = m % 2
            S.tag = "m%d.L0.%d.s3" % (m, j)
            bz = [proj_tm(Z_BANKS[0], j, 1280, 512, W0, "W0"), proj_tm(Z_BANKS[1], j, 1792, 512, W0, "W0")]
            for hf in range(2):
                op("act", lambda q, hf=hf: q.activation(out=Fh(2 + hf), in_=pab[bz[hf]][:], func=AF.Tanh), reads=[("pa", bz[hf])], writes=[FhK(2 + hf)])
                op("dve", lambda q, hf=hf: q.scalar_tensor_tensor(out=Fh(2 + hf), in0=Fh(2 + hf), scalar=1.0, in1=pab[bz[hf]][:], op0=ALU.add, op1=ALU.mult),
                   reads=[FhK(2 + hf), ("pa", bz[hf])], writes=[FhK(2 + hf)])
            op("dve", lambda q: q.tensor_tensor(out=og_all[:, 0, :], in0=F(0), in1=F(1), op=ALU.mult),
               reads=[("of", 0), ("of", 1), ("of", 2), ("of", 3), FhK(0), FhK(1), FhK(2), FhK(3)], writes=[("og", 0), FhK(0), FhK(1)])
            yb = out_proj(og_all[:, 0, :], [("og", 0)], Wo0, "Wo0", True, Y_BANKS, 2)
            post_norm_residual(buf, j, 0, yb)

        def l1_A(m, h):
            ws = h % 2
            Wt = W1h[ws]
            wk = "W1h%d" % ws
            db = h % 2
            S.tag = "m%d.L1.h%d.A" % (m, h)
            bq, bf, bv, bzz = L1_BANKS[0:4]
            hTk = [("hT", j) for j in range(4)]
            for (b, c0) in ((bq, 0), (bf, 128)):
                for kc in range(8):
                    op("pe", lambda q, b=b, c0=c0, kc=kc: q.matmul(pab[b][:], lhsT=Wt[:, kc, c0:c0 + 128], rhs=hT[:, kc, :], start=(kc == 0), stop=(kc == 7)),
                       reads=hTk + [wk], writes=[("pa", b)])
            for (b, c0) in ((bv, 256), (bzz, 384)):
                for j in range(4):
                    for kc in range(8):
                        op("pe", lambda q, b=b, c0=c0, kc=kc, j=j: q.matmul(pab[b][:, j * 128:(j + 1) * 128], lhsT=hT[:, kc, j * 128:(j + 1) * 128],
                                                                            rhs=Wt[:, kc, c0:c0 + 128], start=(kc == 0), stop=(kc == 7)),
                           reads=[("hT", j), wk], writes=[("pa", b)])
            if h + 2 < 8:
                load_w1h(h + 2, ws)
            A0, A1, A2, A3, A4, A5, A6 = [Fh(i) for i in range(7)]
            K0, K1, K2, K3, K4, K5, K6 = [FhK(i) for i in range(7)]
            gz = Fh(7) if db == 0 else GZ[0][:]
            gzk = FhK(7) if db == 0 else "gz0"
            op("act", lambda q: q.activation(out=A0, in_=pab[bq][:], func=AF.Tanh), reads=[("pa", bq)], writes=[K0])
            op("dve", lambda q: q.scalar_tensor_tensor(out=A0, in0=A0, scalar=1.0, in1=pab[bq][:], op0=ALU.add, op1=ALU.mult),
               reads=[K0, ("pa", bq)], writes=[K0])
            op("act", lambda q: q.activation(out=A1, in_=pab[bf][:], func=AF.Tanh), reads=[("pa", bf)], writes=[K1])
            op("act", lambda q: q.activation(out=A2, in_=A1, func=AF.Identity, scale=sv(LBB + h), bias=sv(LBA + h)), reads=[K1] + LBK, writes=[K2])
            op("act", lambda q: q.activation(out=A3, in_=A1, func=AF.Identity, scale=sv(LBNB + h), bias=sv(LBB + h)), reads=[K1] + LBK, writes=[K3])
            op("dve", lambda q: q.tensor_tensor_scan(out=A4, data0=startmask[:], data1=A2, initial=0.0, op0=ALU.max, op1=ALU.mult),
               reads=["startmask", K2], writes=[K4])
            op("dve", lambda q: q.tensor_tensor(out=qeT[db][:], in0=A0, in1=A4, op=ALU.mult), reads=[K0, K4], writes=[("qeT", db)])
            op("dve", lambda q: q.reciprocal(out=A5, in_=A4), reads=[K4], writes=[K5], dur=3.3)
            op("dve", lambda q: q.tensor_tensor(out=A6, in0=A3, in1=A5, op=ALU.mult), reads=[K3, K5], writes=[K6])
            op("act", lambda q: q.activation(out=keT[db][:], in_=A6, func=AF.Copy), reads=[K6], writes=[("keT", db)])
            op("dve", lambda q: q.tensor_tensor(out=kdT[db][:].rearrange("p (c t) -> p c t", t=64), in0=A6.rearrange("p (c t) -> p c t", t=64),
                                                in1=A4.rearrange("p (c t) -> p c t", t=64)[:, :, 63:64].to_broadcast([128, 8, 64]), op=ALU.mult),
               reads=[K6, K4], writes=[("kdT", db)])
            op("act", lambda q: q.activation(out=glast[:, db, :], in_=A4.rearrange("p (c t) -> p c t", t=64)[:, :, 63], func=AF.Copy),
               reads=[K4], writes=[("glast", db)])
            op("act", lambda q: q.activation(out=vb[db][:].rearrange("p j v -> p (j v)"), in_=pab[bv][:], func=AF.Copy), reads=[("pa", bv)], writes=[("vb", db)])
            op("act", lambda q: q.activation(out=gz, in_=pab[bzz][:], func=AF.Tanh), reads=[("pa", bzz)], writes=[gzk])
            op("dve", lambda q: q.scalar_tensor_tensor(out=gz, in0=gz, scalar=1.0, in1=pab[bzz][:], op0=ALU.add, op1=ALU.mult),
               reads=[gzk, ("pa", bzz)], writes=[gzk])

        def l1_B(m, h):
            db = h % 2
            S.tag = "m%d.L1.h%d.B" % (m, h)
            gz = Fh(7) if db == 0 else GZ[0][:]
            gzk = FhK(7) if db == 0 else "gz0"
            ub = [L1_BANKS[4], L1_BANKS[5]]
            bso = L1_BANKS[6]
            for j in range(4):
                op("pe", lambda q, j=j: q.transpose(out=ptv(TB)[:, j * 128:(j + 1) * 128], in_=kdT[db][:, j * 128:(j + 1) * 128], identity=ident[:]),
                   reads=[("kdT", db), "ident"], writes=[("pa", TB)])
            op("act", lambda q: q.activation(out=kd_tm[:].rearrange("p j k -> p (j k)"), in_=ptv(TB)[:, 0:512], func=AF.Copy), reads=[("pa", TB)], writes=["kd_tm"])
            for j in range(4):
                for c in range(2):
                    op("pe", lambda q, j=j, c=c: q.matmul(pab[ub[c]][:, j * 128:(j + 1) * 128], lhsT=kd_tm[64 * c:64 * c + 64, j, :],
                                                          rhs=vb[db][64 * c:64 * c + 64, j, :], start=True, stop=True),
                       reads=["kd_tm", ("vb", db)], writes=[("pa", ub[c])])
            for j in range(4):
                op("pe", lambda q, j=j: q.matmul(pab[bso][:, j * 128:(j + 1) * 128], lhsT=keT[db][:, j * 128:(j + 1) * 128], rhs=qeT[db][:, j * 128:(j + 1) * 128],
                                                 start=True, stop=True),
                   reads=[("keT", db), ("qeT", db)], writes=[("pa", bso)])
            op("dve", lambda q: q.tensor_tensor(out=smk[:], in0=pab[bso][:].rearrange("p (j t) -> p j t", j=4),
                                                in1=maskbd[:].unsqueeze(1).to_broadcast([128, 4, 128]), op=ALU.mult),
               reads=[("pa", bso), "maskbd"], writes=["smk"])
            for k in range(8):
                j, c = k // 2, k % 2
                op("act", lambda q, k=k: q.activation(out=Sbf[:, k, :], in_=S32[:, h, :], func=AF.Copy), reads=[("S32", h)], writes=[("Sbf", k)])
                op("dve", lambda q, k=k, j=j, c=c: q.scalar_tensor_tensor(out=S32[:, h, :], in0=S32[:, h, :], scalar=glast[:, db, k:k + 1],
                                                                             in1=pab[ub[c]][:, j * 128:(j + 1) * 128], op0=ALU.mult, op1=ALU.add),
                   reads=[("S32", h), ("glast", db), ("pa", ub[c])], writes=[("S32", h)])
            for j in range(4):
                op("pe", lambda q, j=j: q.matmul(pab[bso][:, j * 128:(j + 1) * 128], lhsT=smk[:, j, :], rhs=vb[db][:, j, :], start=True, stop=True),
                   reads=["smk", ("vb", db)], writes=[("pa", bso)])
                for c in range(2):
                    k = 2 * j + c
                    op("pe", lambda q, j=j, c=c, k=k: q.matmul(pab[bso][64 * c:64 * c + 64, j * 128:(j + 1) * 128],
                                                               lhsT=qeT[db][:, j * 128 + 64 * c:j * 128 + 64 * c + 64], rhs=Sbf[:, k, :],
                                                               start=False, stop=True, skip_group_check=True),
                       reads=[("qeT", db), ("Sbf", k)], writes=[("pa", bso)])
            for j in range(4):
                op("act", lambda q, j=j: q.activation(out=og_all[:, j, h * 128:(h + 1) * 128], in_=pab[bso][:, j * 128:(j + 1) * 128], func=AF.Square, accum_out=sv(SS3 + j)),
                   reads=[("pa", bso)], writes=[("og", j), ("ss3", j)])
            op("dve", lambda q: q.tensor_scalar(out=sv(SS3, 4), in0=sv(SS3, 4), scalar1=1.0 / 128, scalar2=EPS, op0=ALU.mult, op1=ALU.add),
               reads=[("ss3", j) for j in range(4)], writes=[("ss3", j) for j in range(4)])
            op("pool", lambda q: q.tensor_tensor(out=sv(RSTD3, 4), in0=sv(SS3, 4), in1=sv(NHALF, 4), op=ALU.pow),
               reads=[("ss3", j) for j in range(4)] + ["nhalf"], writes=[("rstd3", j) for j in range(4)])
            for j in range(4):
                op("dve", lambda q, j=j: q.scalar_tensor_tensor(out=og_all[:, j, h * 128:(h + 1) * 128], in0=pab[bso][:, j * 128:(j + 1) * 128],
                                                                 scalar=sv(RSTD3 + j), in1=gz[:, j * 128:(j + 1) * 128], op0=ALU.mult, op1=ALU.mult),
                   reads=[("pa", bso), ("rstd3", j), gzk], writes=[("og", j)])

        load_x(0)
        for m in range(NMT):
            S.new_epoch()
            buf = m % 2
            if m + 1 < NMT:
                load_x(m + 1)
            load_w1h(0, 0)
            load_w1h(1, 1)
            l0_s1(m, 0)
            for j in range(4):
                l0_s2(m, j)
                if j + 1 < 4:
                    l0_s1(m, j + 1)
                l0_s3(m, j)
            S.tag = "m%d.L1.pre" % m
            for j in range(4):
                make_hT(buf, j, j)
            if m % MT_PER_SEQ == 0:
                op("pool", lambda q: q.memset(S32[:], 0.0), writes=[("S32", h) for h in range(8)])
            l1_A(m, 0)
            for h in range(8):
                if h + 1 < 8:
                    l1_A(m, h + 1)
                l1_B(m, h)
            S.tag = "m%d.L1.out" % m
            for j in range(4):
                yb = out_proj(og_all[:, j, :], [("og", j)], Wo1, "Wo1", False, ((L1_BANKS[0], L1_BANKS[1]), (L1_BANKS[2], L1_BANKS[3]))[j % 2])
                post_norm_residual(buf, j, 1, yb)
            op("sp", lambda q, m=m, buf=buf: q.dma_start(out=out_d[m * 512:(m + 1) * 512, :].rearrange("(j p) d -> p j d", p=128), in_=xb[buf][:]),
               reads=[xk(buf, j) for j in range(4)], dma="xs%d" % buf)
        S.emit(st)
    build_program.last_sched = S
    return nc


_PROG_CACHE = {}


def _get_prog(nseq, seq):
    key = (nseq, seq)
    if key not in _PROG_CACHE:
        _PROG_CACHE[key] = build_program(nseq, seq)
    return _PROG_CACHE[key]


def make_in_maps(inputs, n_cores, nseq, seq):
    x = np.ascontiguousarray(inputs["x"], dtype=np.float32)
    pos = np.ascontiguousarray(inputs["positions"], dtype=np.int32)
    cst = make_consts()
    maps = []
    for c in range(n_cores):
        xs = x[c * nseq:(c + 1) * nseq, :seq].reshape(nseq * seq, D)
        ps = pos[c * nseq:(c + 1) * nseq, :seq].reshape(nseq * seq // 128, 128).T
        maps.append({
            "x": np.ascontiguousarray(xs),
            "pos": np.ascontiguousarray(ps),
            "cst": cst,
            "pre_norm_w": np.ascontiguousarray(inputs["pre_norm_w"], dtype=np.float32),
            "post_norm_w": np.ascontiguousarray(inputs["post_norm_w"], dtype=np.float32),
            "attn_w_in": np.ascontiguousarray(inputs["attn_w_in"][0], dtype=np.float32),
            "attn_b_in": np.ascontiguousarray(inputs["attn_b_in"], dtype=np.float32).reshape(1, 2304),
            "attn_sinks": np.ascontiguousarray(inputs["attn_sinks"], dtype=np.float32).reshape(1, 16),
            "attn_w_out": np.ascontiguousarray(inputs["attn_w_out"][0], dtype=np.float32),
            "attn_b_out": np.ascontiguousarray(inputs["attn_b_out"], dtype=np.float32).reshape(1, D),
            "rec_w_in": np.ascontiguousarray(inputs["rec_w_in"][0], dtype=np.float32),
            "rec_lb_logits": np.ascontiguousarray(inputs["rec_lb_logits"], dtype=np.float32),
            "rec_gnorm_w": np.ascontiguousarray(inputs["rec_gnorm_w"], dtype=np.float32).reshape(1, 128),
            "rec_w_out": np.ascontiguousarray(inputs["rec_w_out"][0], dtype=np.float32),
        })
    return maps


def kernel(**inputs):
    B, T, _ = inputs["x"].shape
    nseq = B // N_CORES
    nc = _get_prog(nseq, T)
    maps = make_in_maps(inputs, N_CORES, nseq, T)
    res = run_bass_kernel_spmd(nc, maps, core_ids=list(range(N_CORES)))
    outs = [np.asarray(r["out"], dtype=np.float32).reshape(nseq, T, D) for r in res.results]
    return np.concatenate(outs, axis=0)
```

```python
import math
from contextlib import ExitStack

import numpy as np
import concourse.bass as bass
import concourse.mybir as mybir
from concourse.bass_utils import run_bass_kernel_spmd

F32 = mybir.dt.float32
BF16 = mybir.dt.bfloat16
I32 = mybir.dt.int32
AF = mybir.ActivationFunctionType
ALU = mybir.AluOpType

N_CORES = 8
S1B = (2, 1, 0, 2, 0, 5, 5)
OA_BANKS = (5,)
SC_BANKS = (4, 3)
Z_BANKS = (4, 6)
Y_BANKS = (1, 7)
L1_BANKS = (4, 5, 6, 7, 2, 3, 0, 6)
D = 1024
EPS = 1e-6
TWO_PI = 2.0 * math.pi
C1 = 6.28125
C2 = TWO_PI - C1
PI_LO = 3.1415925


class _Op:
    __slots__ = ("eng", "fn", "idx", "deps", "signal", "sem", "count", "dma", "epoch", "tag", "dur", "start")


class _Rec:
    def __init__(self):
        self.call = None

    def __getattr__(self, name):
        def f(*a, **k):
            self.call = (name, a, k)
            return self
        return f


def _nelem(ap):
    n = 1
    for d in ap.shape[1:]:
        n *= int(d)
    return n


def _estimate_us(eng, fn, is_dma):
    r = _Rec()
    try:
        fn(r)
    except Exception:
        return 0.5
    if r.call is None:
        return 0.3
    name, a, k = r.call
    out = k.get("out", a[0] if a else None)
    try:
        if is_dma:
            esz = 2 if out.dtype == BF16 else 4
            nbytes = _nelem(out) * int(out.shape[0]) * esz
            return 2.0 + nbytes / 200e3
        if eng == "pe":
            if name == "transpose":
                return 0.10
            rhs = k.get("rhs")
            n = _nelem(rhs)
            return 0.02 + n * 0.00058
        n = _nelem(out) if out is not None else 64
        if eng == "act":
            return 0.26 + n * 0.00085 + (0.1 if k.get("accum_out") is not None else 0.0)
        if eng == "dve":
            f = 1.0
            if name == "reciprocal":
                f = 4.2
            elif name == "tensor_tensor_scan":
                f = 2.0
            elif name == "tensor_tensor":
                f = 1.6
            return 0.12 + n * 0.00104 * f
        if eng == "pool":
            if name == "tensor_tensor" and k.get("op") == ALU.pow:
                return 0.65
            return 0.3 + n * 0.0021
    except Exception:
        pass
    return 0.4


class Sched:
    EPOCH = 1500

    def __init__(self, nc):
        self.nc = nc
        self.ops = []
        self.last_w = {}
        self.readers = {}
        self.epoch = 0
        self.tag = ""

    def new_epoch(self):
        pass

    def op(self, eng, fn, reads=(), writes=(), dma=None, dur=None):
        o = _Op()
        o.eng, o.fn, o.idx, o.dma, o.epoch = eng, fn, len(self.ops), dma, 0
        o.signal = False
        o.tag = self.tag
        o.dur = dur if dur is not None else _estimate_us(eng, fn, dma is not None)
        deps = set()
        for k in reads:
            w = self.last_w.get(k)
            if w is not None:
                deps.add(w)
        for k in writes:
            w = self.last_w.get(k)
            if w is not None:
                deps.add(w)
            for r in self.readers.get(k, ()):
                deps.add(r)
        deps.discard(o.idx)
        o.deps = deps
        for k in writes:
            self.last_w[k] = o.idx
            self.readers[k] = []
        for k in reads:
            if k not in writes:
                self.readers.setdefault(k, []).append(o.idx)
        self.ops.append(o)
        return o

    def list_schedule(self, reorder=True):
        import heapq
        ops = self.ops
        n = len(ops)
        if not reorder:
            return {e: [o for o in ops if o.eng == e] for e in ("pe", "act", "dve", "pool", "sp")}, 0.0
        succ = [[] for _ in range(n)]
        indeg = [0] * n
        for o in ops:
            indeg[o.idx] = len(o.deps)
            for d in o.deps:
                succ[d].append(o.idx)
        ready_t = [0.0] * n
        free = {e: 0.0 for e in ("pe", "act", "dve", "pool", "sp")}
        heaps = {e: [] for e in free}
        for o in ops:
            if indeg[o.idx] == 0:
                heapq.heappush(heaps[o.eng], (0.0, o.idx))
        order = {e: [] for e in free}
        done = 0
        SEM_LAT = 0.15
        while done < n:
            best = None
            for e, h in heaps.items():
                if not h:
                    continue
                t0 = max(free[e], h[0][0])
                cand = None
                tmp = []
                while h and h[0][0] <= t0 and len(tmp) < 24:
                    tmp.append(heapq.heappop(h))
                pick = min(tmp, key=lambda x: x[1])
                for x in tmp:
                    if x is not pick:
                        heapq.heappush(h, x)
                heapq.heappush(h, pick)
                cand = (t0, pick[1], e, pick)
                if best is None or cand[:2] < best[:2]:
                    best = cand
            t0, idx, e, pick = best
            h = heaps[e]
            h.remove(pick)
            heapq.heapify(h)
            o = ops[idx]
            o.start = t0
            if o.dma is not None:
                free[e] = t0 + 0.06
                fin = t0 + o.dur
            else:
                free[e] = t0 + o.dur
                fin = free[e]
            order[e].append(o)
            done += 1
            for sidx in succ[idx]:
                so = ops[sidx]
                lat = 0.0 if (so.eng == e and e == "pe" and o.dma is None) else SEM_LAT
                ready_t[sidx] = max(ready_t[sidx], fin + lat)
                indeg[sidx] -= 1
                if indeg[sidx] == 0:
                    heapq.heappush(heaps[so.eng], (ready_t[sidx], sidx))
        return order, max(free.values())

    def emit(self, stack, reorder=True):
        nc = self.nc
        ops = self.ops
        order, makespan = self.list_schedule(reorder)
        self.makespan = makespan
        pos = {}
        for e, lst in order.items():
            for i, o in enumerate(lst):
                pos[o.idx] = i
        for o in ops:
            for d in o.deps:
                p = ops[d]
                if p.dma is None and p.eng == "pe" and o.eng == "pe" and o.dma is None:
                    assert pos[p.idx] < pos[o.idx]
                    continue
                p.signal = True
        counts = {}
        nsig = {}
        for e, lst in order.items():
            for o in lst:
                if o.dma is not None:
                    key = ("dma", o.dma)
                    counts[key] = counts.get(key, 0) + 16
                    o.sem, o.count = key, counts[key]
                elif o.signal:
                    k = nsig.get(e, 0)
                    nsig[e] = k + 1
                    o.epoch = k // self.EPOCH
                    key = (e, o.epoch)
                    counts[key] = counts.get(key, 0) + 1
                    o.sem, o.count = key, counts[key]
        sems = {}
        for key in counts:
            sems[key] = stack.enter_context(nc.semaphore("s_%s_%s" % key))
        self.n_sems = len(sems)
        final = dict(counts)

        def stream(eng_name):
            def body(eng):
                seen = {}
                for o in order[eng_name]:
                    need = {}
                    for d in o.deps:
                        p = ops[d]
                        if p.dma is None and p.eng == "pe" and eng_name == "pe" and o.dma is None:
                            continue
                        if p.dma is not None:
                            skey, val = ("dma", p.dma), (0, p.count)
                        else:
                            skey, val = ("eng", p.eng), (p.epoch, p.count)
                        if val > need.get(skey, (-1, -1)):
                            need[skey] = val
                    for skey, val in need.items():
                        if val <= seen.get(skey, (-1, -1)):
                            continue
                        seen[skey] = val
                        if skey[0] == "dma":
                            eng.wait_ge(sems[("dma", skey[1])], val[1])
                        else:
                            eng.wait_ge(sems[(skey[1], val[0])], val[1])
                    ins = o.fn(eng)
                    if o.dma is not None:
                        ins.then_inc(sems[o.sem], 16)
                    elif o.signal:
                        ins.then_inc(sems[o.sem], 1)
                if eng_name == "sp":
                    for key, c in final.items():
                        if key[0] == "dma":
                            eng.wait_ge(sems[key], c)
            return body

        with nc.Block() as block:
            block.tensor(stream("pe"))
            block.scalar(stream("act"))
            block.vector(stream("dve"))
            block.gpsimd(stream("pool"))
            block.sync(stream("sp"))


CST_W = 128 * 4 + 512 + 8


def make_consts():
    c = np.zeros((128, CST_W), np.float32)
    i = np.arange(128)
    c[:, 0:128] = np.eye(128, dtype=np.float32)
    c[:, 128:256] = (i[:, None] <= i[None, :]).astype(np.float32)
    c[:, 256:384] = (i[:, None] > i[None, :]).astype(np.float32)
    c[:, 384:512] = ((i[:, None] <= i[None, :]) & ((i[:, None] // 64) == (i[None, :] // 64))).astype(np.float32)
    sm = np.zeros(512, np.float32)
    sm[::64] = 1.0
    c[:, 512:1024] = sm[None, :]
    invf = (np.float32(500000.0) ** (-(np.arange(8, dtype=np.float32) * np.float32(2.0) / np.float32(16.0)))).astype(np.float32)
    c[:, 1024:1032] = invf[None, :]
    return c


def build_program(NSEQ, SEQ):
    NT = NSEQ * SEQ
    NB = NT // 128
    NMT = NT // 512
    MT_PER_SEQ = SEQ // 512
    nc = bass.Bass("TRN2", target_bir_lowering=False)

    def din(name, shape, dt=F32):
        return nc.dram_tensor(name, list(shape), dt, kind="ExternalInput").ap()

    x_d = din("x", [NT, D])
    pos_d = din("pos", [128, NB], I32)
    cst_d = din("cst", [128, CST_W])
    prew_d = din("pre_norm_w", [2, D])
    postw_d = din("post_norm_w", [2, D])
    w0_d = din("attn_w_in", [D, 2304])
    b0_d = din("attn_b_in", [1, 2304])
    sink_d = din("attn_sinks", [1, 16])
    wo0_d = din("attn_w_out", [D, D])
    bo0_d = din("attn_b_out", [1, D])
    w1_d = din("rec_w_in", [D, 4096])
    lb_d = din("rec_lb_logits", [2, D])
    gnw_d = din("rec_gnorm_w", [1, 128])
    wo1_d = din("rec_w_out", [D, D])
    out_d = nc.dram_tensor("out", [NT, D], F32, kind="ExternalOutput").ap()
    w1s_d = nc.dram_tensor("w1s", [8, 128, 8, 512], BF16, kind="Internal").ap()

    with ExitStack() as st:
        def sb(name, shape, dt):
            return st.enter_context(nc.sbuf_tensor(name, list(shape), dt))

        def ps(name, shape, dt):
            return st.enter_context(nc.psum_tensor(name, list(shape), dt))

        W0 = sb("W0", [128, 8, 2304], BF16)
        Wo0 = sb("Wo0", [128, 8, 1024], BF16)
        Wo1 = sb("Wo1", [128, 8, 1024], BF16)
        W1h = [sb("W1h%d" % i, [128, 8, 512], BF16) for i in range(2)]
        xb = [sb("xb%d" % i, [128, 4, 1024], F32) for i in range(2)]
        FF = sb("FF", [128, 4096], F32)
        hT = sb("hT", [128, 8, 512], BF16)
        og_all = sb("og_all", [128, 4, 1024], BF16)
        ident = sb("ident", [128, 128], BF16)
        mask_cur = sb("mask_cur", [128, 128], BF16)
        mask_prev = sb("mask_prev", [128, 128], BF16)
        maskbd = sb("maskbd", [128, 128], BF16)
        startmask = sb("startmask", [128, 512], F32)
        invf = sb("invf", [128, 8], F32)
        cosT = sb("cosT", [128, NB, 8], F32)
        sinT = sb("sinT", [128, NB, 8], F32)
        wpost = sb("wpost", [128, 2, 1024], F32)
        browA = sb("browA", [65, 1024], BF16)
        posi = sb("posi", [128, NB], I32)
        browB = sb("browB", [1, 1024], BF16)
        ones = sb("ones", [65, 128], BF16)
        small = sb("small", [128, 144], F32)
        S32 = sb("S32", [128, 8, 128], F32)
        Sbf = sb("Sbf", [128, 8, 128], BF16)
        hb = sb("hb", [128, 1024], BF16)
        qkb = sb("qkb", [128, 18, 64], BF16)
        qkr = sb("qkr", [128, 18, 16], F32)
        rt = sb("rt", [128, 4, 18, 8], F32)
        qT = sb("qT", [64, 16, 128], BF16)
        kT = sb("kT", [64, 2, 2, 128], BF16)
        vaug = sb("vaug", [128, 2, 2, 65], BF16)
        PT = [sb("PT%d" % i, [128, 512], BF16) for i in range(4)]
        ogT = sb("ogT", [128, 8, 128], BF16)
        qeT = [sb("qeT%d" % i, [128, 512], BF16) for i in range(2)]
        keT = [sb("keT%d" % i, [128, 512], BF16) for i in range(2)]
        kdT = [sb("kdT%d" % i, [128, 512], BF16) for i in range(2)]
        kd_tm = sb("kd_tm", [128, 4, 128], BF16)
        vb = [sb("vb%d" % i, [128, 4, 128], BF16) for i in range(2)]
        smk = sb("smk", [128, 4, 128], BF16)

        PREW0, PREW1 = 0, 8
        SCQ0, SCH0, SCH1 = 16, 24, 32
        LBA, LBB, LBNB = 40, 48, 56
        GNW = 64
        ESINK = 65
        SS = 81
        RSTD = 85
        DEN = 89
        RDEN = 105
        NHALF = 121
        SS2, RSTD2 = 125, 127
        SS3, RSTD3 = 128, 132

        def sv(c, n=1):
            return small[:, c:c + n]

        pab = [ps("pab%d" % i, [128, 512], F32) for i in range(8)]
        GZ = [sb("gz0", [128, 512], F32)]
        glast = sb("glast", [128, 2, 8], F32)

        def ptv(i):
            return pab[i][:].bitcast(BF16)

        S = Sched(nc)
        op = S.op
        FK = ["FF0", "FF1", "FF2", "FF3"]

        def F(i):
            return FF[:, i * 1024:(i + 1) * 1024]

        def Fh(i):
            return FF[:, i * 512:(i + 1) * 512]

        def FhK(i):
            return "FH%d" % i

        ALLF = FK + [FhK(i) for i in range(8)]

        op("sp", lambda q: q.dma_start(out=FF[:, 0:CST_W], in_=cst_d), writes=ALLF, dma="cst")
        op("dve", lambda q: q.tensor_copy(out=ident[:], in_=FF[:, 0:128]), reads=ALLF, writes=["ident"])
        op("dve", lambda q: q.tensor_copy(out=mask_cur[:], in_=FF[:, 128:256]), reads=ALLF, writes=["mask_cur"])
        op("dve", lambda q: q.tensor_copy(out=mask_prev[:], in_=FF[:, 256:384]), reads=ALLF, writes=["mask_prev"])
        op("dve", lambda q: q.tensor_copy(out=maskbd[:], in_=FF[:, 384:512]), reads=ALLF, writes=["maskbd"])
        op("dve", lambda q: q.tensor_copy(out=startmask[:], in_=FF[:, 512:1024]), reads=ALLF, writes=["startmask"])
        op("dve", lambda q: q.tensor_copy(out=invf[:], in_=FF[:, 1024:1032]), reads=ALLF, writes=["invf"])
        op("pool", lambda q: q.memset(ones[:], 1.0), writes=["ones"])
        op("pool", lambda q: q.memset(small[:, NHALF:NHALF + 4], -0.5), writes=["nhalf"])
        op("pool", lambda q: q.memset(vaug[:], 1.0), writes=["vaug0", "vaug1"])
        op("sp", lambda q: q.dma_start(out=small[:, PREW0:PREW0 + 8], in_=prew_d[0:1, :].rearrange("o (k p) -> p (o k)", p=128),
                                       allow_slow_non_contiguous=True), writes=["prew", "smq"], dma="sm")
        op("sp", lambda q: q.dma_start(out=small[:, PREW1:PREW1 + 8], in_=prew_d[1:2, :].rearrange("o (k p) -> p (o k)", p=128),
                                       allow_slow_non_contiguous=True), writes=["prew", "smq"], dma="sm")
        op("sp", lambda q: q.dma_start(out=small[:, LBA:LBA + 8], in_=lb_d[0:1, :].rearrange("o (k p) -> p (o k)", p=128),
                                       allow_slow_non_contiguous=True), writes=["lb0", "smq"], dma="sm")
        op("sp", lambda q: q.dma_start(out=small[:, LBB:LBB + 8], in_=lb_d[1:2, :].rearrange("o (k p) -> p (o k)", p=128),
                                       allow_slow_non_contiguous=True), writes=["lb1", "smq"], dma="sm")
        op("sp", lambda q: q.dma_start(out=small[:, GNW:GNW + 1], in_=gnw_d.rearrange("o p -> p o"),
                                       allow_slow_non_contiguous=True), writes=["gnw", "smq"], dma="sm")
        op("sp", lambda q: q.dma_start(out=small[:, ESINK:ESINK + 16], in_=sink_d.partition_broadcast(128)), writes=["esink", "smq"], dma="sm")
        op("sp", lambda q: q.dma_start(out=wpost[:, 0, :], in_=postw_d[0:1, :].partition_broadcast(128)), writes=["wpost", "smq"], dma="sm")
        op("sp", lambda q: q.dma_start(out=wpost[:, 1, :], in_=postw_d[1:2, :].partition_broadcast(128)), writes=["wpost", "smq"], dma="sm")
        op("sp", lambda q: q.dma_start(out=posi[:], in_=pos_d), writes=["posi", "smq"], dma="sm")
        op("dve", lambda q: q.tensor_scalar(out=sv(SCQ0, 8), in0=sv(PREW0, 8), scalar1=0.125, scalar2=None, op0=ALU.mult), reads=["prew"], writes=["scq0"])
        op("dve", lambda q: q.tensor_scalar(out=sv(SCH0, 8), in0=sv(PREW0, 8), scalar1=0.5, scalar2=None, op0=ALU.mult), reads=["prew"], writes=["sch0"])
        op("dve", lambda q: q.tensor_scalar(out=sv(SCH1, 8), in0=sv(PREW1, 8), scalar1=0.5, scalar2=None, op0=ALU.mult), reads=["prew"], writes=["sch1"])
        op("act", lambda q: q.activation(out=sv(ESINK, 16), in_=sv(ESINK, 16), func=AF.Exp), reads=["esink"], writes=["esink"])
        op("dve", lambda q: q.tensor_tensor(out=sv(LBNB, 8), in0=sv(LBB, 8), in1=sv(LBA, 8), op=ALU.subtract), reads=["lb0", "lb1"], writes=["lbt"])
        op("act", lambda q: q.activation(out=sv(LBNB, 8), in_=sv(LBNB, 8), func=AF.Tanh, scale=0.5), reads=["lbt"], writes=["lbt"])
        op("dve", lambda q: q.tensor_scalar(out=sv(LBA, 8), in0=sv(LBNB, 8), scalar1=0.25, scalar2=0.75, op0=ALU.mult, op1=ALU.add), reads=["lbt"], writes=["lb0"])
        op("dve", lambda q: q.tensor_scalar(out=sv(LBB, 8), in0=sv(LBNB, 8), scalar1=-0.25, scalar2=0.25, op0=ALU.mult, op1=ALU.add), reads=["lbt"], writes=["lb1"])
        op("dve", lambda q: q.tensor_scalar(out=sv(LBNB, 8), in0=sv(LBB, 8), scalar1=-1.0, scalar2=None, op0=ALU.mult), reads=["lb1", "lbt"], writes=["lbt"])
        LBK = ["lb0", "lb1", "lbt"]

        o0 = 1040
        posf = FF[:, o0:o0 + NB]
        ang = FF[:, o0 + NB:o0 + NB + NB * 8]
        tmpu = FF[:, o0 + 9 * NB:o0 + 17 * NB]
        tmpk = FF[:, o0 + 17 * NB:o0 + 25 * NB]
        assert o0 + 25 * NB <= 4096
        tmpi = hb[:].bitcast(I32)[:, 0:NB * 8]
        op("dve", lambda q: q.tensor_copy(out=posf, in_=posi[:]), reads=["posi"] + ALLF, writes=ALLF)
        op("dve", lambda q: q.tensor_tensor(out=ang.rearrange("p (b i) -> p b i", i=8),
                                            in0=posf.unsqueeze(2).to_broadcast([128, NB, 8]),
                                            in1=invf[:].unsqueeze(1).to_broadcast([128, NB, 8]), op=ALU.mult),
           reads=ALLF + ["invf"], writes=ALLF)
        for which, tab in ((0, sinT), (1, cosT)):
            if which == 1:
                op("dve", lambda q: q.tensor_scalar(out=ang, in0=ang, scalar1=math.pi / 2, scalar2=None, op0=ALU.add), reads=ALLF, writes=ALLF)
            op("dve", lambda q: q.tensor_scalar(out=tmpu, in0=ang, scalar1=1.0 / TWO_PI, scalar2=None, op0=ALU.mult), reads=ALLF, writes=ALLF)
            op("dve", lambda q: q.tensor_copy(out=tmpi, in_=tmpu), reads=ALLF, writes=["hb"])
            op("dve", lambda q: q.tensor_copy(out=tmpk, in_=tmpi), reads=["hb"], writes=ALLF)
            op("dve", lambda q: q.scalar_tensor_tensor(out=tmpu, in0=tmpk, scalar=-C1, in1=ang, op0=ALU.mult, op1=ALU.add), reads=ALLF, writes=ALLF)
            op("dve", lambda q: q.scalar_tensor_tensor(out=tmpu, in0=tmpk, scalar=-C2, in1=tmpu, op0=ALU.mult, op1=ALU.add), reads=ALLF, writes=ALLF)
            op("dve", lambda q: q.tensor_scalar(out=tmpu, in0=tmpu, scalar1=-PI_LO, scalar2=PI_LO, op0=ALU.max, op1=ALU.min), reads=ALLF, writes=ALLF)
            op("act", lambda q, tab=tab: q.activation(out=tab[:].rearrange("p b i -> p (b i)"), in_=tmpu, func=AF.Sin),
               reads=ALLF, writes=["cosT" if which == 1 else "sinT"])

        BROWS = [(0, 1024, 0), (1024, 1792, 32), (1792, 2304, 64)]
        for (c0, c1, p) in BROWS:
            op("sp", lambda q, c0=c0, c1=c1, p=p: q.dma_start(out=FF[p:p + 1, 0:c1 - c0], in_=b0_d[0:1, c0:c1]),
               reads=["cosT", "sinT"], writes=ALLF + ["smq"], dma="sm")
        op("sp", lambda q: q.dma_start(out=FF[0:1, 1024:2048], in_=bo0_d), writes=ALLF + ["smq"], dma="sm")
        op("dve", lambda q: q.tensor_scalar(out=browA[0:1, 0:1024], in0=FF[0:1, 0:1024], scalar1=0.125, scalar2=None, op0=ALU.mult), reads=ALLF, writes=["browA"])
        op("dve", lambda q: q.tensor_copy(out=browA[32:33, 0:256], in_=FF[32:33, 0:256]), reads=ALLF, writes=["browA"])
        op("dve", lambda q: q.tensor_scalar(out=browA[32:33, 256:768], in0=FF[32:33, 256:768], scalar1=0.5, scalar2=None, op0=ALU.mult), reads=ALLF, writes=["browA"])
        op("dve", lambda q: q.tensor_scalar(out=browA[64:65, 0:512], in0=FF[64:65, 0:512], scalar1=0.5, scalar2=None, op0=ALU.mult), reads=ALLF, writes=["browA"])
        op("dve", lambda q: q.tensor_copy(out=browB[:], in_=FF[0:1, 1024:2048]), reads=ALLF, writes=["browB"])

        def brow(c0, n):
            for (r0, r1, p) in BROWS:
                if r0 <= c0 and c0 + n <= r1:
                    return ones[p:p + 1, :], browA[p:p + 1, c0 - r0:c0 - r0 + n]
            raise AssertionError((c0, n))

        stageA = FF
        stageB = xb[1][:].rearrange("p j d -> p (j d)")
        XB1K = [("x", 1, j) for j in range(4)]
        stg = [(stageA, ALLF), (stageB, XB1K)]
        stb = [(og_all[:].rearrange("p j d -> p (j d)"), [("og", j) for j in range(4)]), (hT[:].rearrange("p k t -> p (k t)"), [("hT", j) for j in range(4)])]
        cnt = [0]

        def conv(eng, out, in_, scal, rd, wr):
            if eng == "dve":
                op("dve", lambda q: q.tensor_scalar(out=out, in0=in_, scalar1=scal, scalar2=None, op0=ALU.mult), reads=rd, writes=wr)
            else:
                op("act", lambda q: q.activation(out=out, in_=in_, func=AF.Copy, scale=scal), reads=rd, writes=wr)

        for kc in range(8):
            sg, sk = stg[cnt[0] % 2]
            cnt[0] += 1
            op("sp", lambda q, sg=sg, kc=kc: q.dma_start(out=sg[:, 0:2304], in_=w0_d[kc * 128:(kc + 1) * 128, :]),
               reads=["browA", "browB"], writes=sk, dma="wl%d" % (cnt[0] % 2))
            conv("dve", W0[:, kc, 0:1024], sg[:, 0:1024], sv(SCQ0 + kc), sk + ["scq0"], ["W0"])
            conv("act", W0[:, kc, 1024:1280], sg[:, 1024:1280], sv(PREW0 + kc), sk + ["prew"], ["W0"])
            conv("act", W0[:, kc, 1280:2304], sg[:, 1280:2304], sv(SCH0 + kc), sk + ["sch0"], ["W0"])
        for kc in range(8):
            sg, sk = stg[cnt[0] % 2]
            cnt[0] += 1
            op("sp", lambda q, sg=sg, kc=kc: q.dma_start(out=sg[:, 0:1024], in_=wo0_d[kc * 128:(kc + 1) * 128, :]),
               writes=sk, dma="wl%d" % (cnt[0] % 2))
            op("sp", lambda q, sg=sg, kc=kc: q.dma_start(out=sg[:, 1024:2048], in_=wo1_d[kc * 128:(kc + 1) * 128, :]),
               writes=sk, dma="wl%d" % (cnt[0] % 2))
            op("act", lambda q, sg=sg, kc=kc: q.activation(out=Wo0[:, kc, :], in_=sg[:, 0:1024], func=AF.Copy), reads=sk, writes=["Wo0"])
            conv("dve", Wo1[:, kc, :], sg[:, 1024:2048], sv(GNW), sk + ["gnw"], ["Wo1"])
        for kc in range(8):
            sg, sk = stg[cnt[0] % 2]
            sbt, sbk = stb[cnt[0] % 2]
            cnt[0] += 1
            op("sp", lambda q, sg=sg, kc=kc: q.dma_start(out=sg[:, 0:4096], in_=w1_d[kc * 128:(kc + 1) * 128, :]),
               writes=sk, dma="wl%d" % (cnt[0] % 2))
            for t in range(4):
                o_ap = sbt.rearrange("p (h t c) -> p t h c", h=8, t=4, c=128)[:, t, :, :]
                i_ap = sg[:, t * 1024:(t + 1) * 1024].rearrange("p (h c) -> p h c", h=8)
                scal = sv(PREW1 + kc) if t == 2 else sv(SCH1 + kc)
                conv("dve" if t % 2 == 0 else "act", o_ap, i_ap, scal, sk + ["prew", "sch1"], sbk)
            op("sp", lambda q, sbt=sbt, kc=kc: q.dma_start(out=w1s_d[:, :, kc, :].rearrange("h p c -> p h c"),
                                                          in_=sbt.rearrange("p (h c) -> p h c", h=8)),
               reads=sbk, writes=["w1s"], dma="ws%d" % (cnt[0] % 2))

        def xk(buf, j):
            return ("x", buf, j)

        def load_x(m):
            buf = m % 2
            op("sp", lambda q: q.dma_start(out=xb[buf][:], in_=x_d[m * 512:(m + 1) * 512, :].rearrange("(j p) d -> p j d", p=128)),
               writes=[xk(buf, j) for j in range(4)], dma="xl%d" % buf)

        def load_w1h(h, slot):
            op("sp", lambda q: q.dma_start(out=W1h[slot][:], in_=w1s_d[h]), reads=["w1s"], writes=["W1h%d" % slot], dma="w1l%d" % slot)

        def rms_rstd(src_ap, src_keys, col):
            op("act", lambda q: q.activation(out=hb[:], in_=src_ap, func=AF.Square, accum_out=sv(SS + col)),
               reads=src_keys, writes=["hb", ("ss", col)])
            op("dve", lambda q: q.tensor_scalar(out=sv(SS + col), in0=sv(SS + col), scalar1=1.0 / 1024, scalar2=EPS, op0=ALU.mult, op1=ALU.add),
               reads=[("ss", col)], writes=[("ss", col)])
            op("pool", lambda q: q.tensor_tensor(out=sv(RSTD + col), in0=sv(SS + col), in1=sv(NHALF), op=ALU.pow),
               reads=[("ss", col), "nhalf"], writes=[("rstd", col)])

        TB = L1_BANKS[7]

        def make_hT(buf, j, col, tb=TB):
            xs = xb[buf][:, j, :]
            rms_rstd(xs, [xk(buf, j)], col)
            op("act", lambda q: q.activation(out=hb[:], in_=xs, func=AF.Copy, scale=sv(RSTD + col)),
               reads=[xk(buf, j), ("rstd", col)], writes=["hb"])
            for kc in range(8):
                op("pe", lambda q, kc=kc: q.transpose(out=ptv(tb)[:, kc * 128:(kc + 1) * 128], in_=hb[:, kc * 128:(kc + 1) * 128], identity=ident[:]),
                   reads=["hb", "ident"], writes=[("pa", tb)])
            op("act", lambda q: q.activation(out=hT[:, :, j * 128:(j + 1) * 128], in_=ptv(tb).rearrange("p (k t) -> p k t", k=8), func=AF.Copy),
               reads=[("pa", tb)], writes=[("hT", j)])

        def post_norm_residual(buf, j, layer, pbanks):
            c0 = 4
            for hf in range(2):
                b = pbanks[hf]
                op("act", lambda q, b=b, hf=hf: q.activation(out=Fh(6 + hf), in_=pab[b][:], func=AF.Square, accum_out=sv(SS2 + hf)),
                   reads=[("pa", b)], writes=[FhK(6 + hf), ("ss2", hf)])
            op("dve", lambda q: q.tensor_scalar(out=sv(SS2, 2), in0=sv(SS2, 2), scalar1=1.0 / 1024, scalar2=EPS / 2, op0=ALU.mult, op1=ALU.add),
               reads=[("ss2", 0), ("ss2", 1)], writes=[("ss2", 0), ("ss2", 1)])
            op("dve", lambda q: q.tensor_tensor(out=sv(SS2), in0=sv(SS2), in1=sv(SS2 + 1), op=ALU.add),
               reads=[("ss2", 0), ("ss2", 1)], writes=[("ss2", 0)])
            op("pool", lambda q: q.tensor_tensor(out=sv(RSTD2), in0=sv(SS2), in1=sv(NHALF), op=ALU.pow),
               reads=[("ss2", 0), "nhalf"], writes=["rstd2"])
            for hf in range(2):
                b = pbanks[hf]
                op("dve", lambda q, b=b, hf=hf: q.scalar_tensor_tensor(out=Fh(6 + hf), in0=pab[b][:], scalar=sv(RSTD2),
                                                                        in1=wpost[:, layer, hf * 512:(hf + 1) * 512], op0=ALU.mult, op1=ALU.mult),
                   reads=[("pa", b), "rstd2", "wpost"], writes=[FhK(6 + hf)])
            op("pool", lambda q: q.tensor_tensor(out=xb[buf][:, j, :], in0=xb[buf][:, j, :], in1=F(3), op=ALU.add),
               reads=[xk(buf, j), FhK(6), FhK(7)], writes=[xk(buf, j)])

        def out_proj(src_bf16_ap, src_keys, Wo, wo_key, bias, ybanks, tb=TB):
            for kc in range(8):
                op("pe", lambda q, kc=kc: q.transpose(out=ptv(tb)[:, kc * 128:(kc + 1) * 128], in_=src_bf16_ap[:, kc * 128:(kc + 1) * 128], identity=ident[:]),
                   reads=src_keys + ["ident"], writes=[("pa", tb)])
            op("act", lambda q: q.activation(out=ogT[:].rearrange("p k t -> p (k t)"), in_=ptv(tb), func=AF.Copy),
               reads=[("pa", tb)], writes=["ogT"])
            banks = list(ybanks)
            for hf in range(2):
                b = banks[hf]
                for kc in range(8):
                    op("pe", lambda q, b=b, kc=kc, hf=hf: q.matmul(pab[b][:], lhsT=ogT[:, kc, :], rhs=Wo[:, kc, hf * 512:(hf + 1) * 512],
                                                                   start=(kc == 0), stop=(kc == 7 and not bias)),
                       reads=["ogT", wo_key], writes=[("pa", b)])
                if bias:
                    op("pe", lambda q, b=b, hf=hf: q.matmul(pab[b][:], lhsT=ones[0:1, :], rhs=browB[0:1, hf * 512:(hf + 1) * 512], start=False, stop=True),
                       reads=["ones", "browB"], writes=[("pa", b)])
            return banks

        def proj_tm(b, j, c0, n, Wt, wkey, bias=True):
            for kc in range(8):
                op("pe", lambda q, kc=kc: q.matmul(pab[b][:, 0:n], lhsT=hT[:, kc, j * 128:(j + 1) * 128], rhs=Wt[:, kc, c0:c0 + n],
                                                   start=(kc == 0), stop=(kc == 7 and not bias)),
                   reads=[("hT", j), wkey], writes=[("pa", b)])
            if bias:
                o1, br = brow(c0, n)
                op("pe", lambda q: q.matmul(pab[b][:, 0:n], lhsT=o1, rhs=br, start=False, stop=True),
                   reads=["ones", "browA"], writes=[("pa", b)])
            return b

        def l0_s1(m, j):
            buf = m % 2
            blk = m * 4 + j
            slot = blk % 2
            S.tag = "m%d.L0.%d.s1" % (m, j)
            make_hT(buf, j, j, S1B[3])
            bq = [proj_tm(S1B[0], j, 0, 512, W0, "W0"), proj_tm(S1B[1], j, 512, 512, W0, "W0")]
            for a in range(2):
                op("act", lambda q, a=a: q.activation(out=qkb[:, 8 * a:8 * a + 8, :], in_=pab[bq[a]][:].rearrange("p (h d) -> p h d", h=8), func=AF.Copy),
                   reads=[("pa", bq[a])], writes=["qkb"])
                op("act", lambda q, a=a: q.activation(out=qkr[:, 8 * a:8 * a + 8, :], in_=pab[bq[a]][:].rearrange("p (h d) -> p h d", h=8)[:, :, 0:16], func=AF.Copy),
                   reads=[("pa", bq[a])], writes=["qkr"])
            bkv = proj_tm(S1B[2], j, 1024, 256, W0, "W0")
            op("act", lambda q: q.activation(out=qkb[:, 16:18, :], in_=pab[bkv][:, 0:128].rearrange("p (h d) -> p h d", h=2), func=AF.Copy),
               reads=[("pa", bkv)], writes=["qkb"])
            op("act", lambda q: q.activation(out=qkr[:, 16:18, :], in_=pab[bkv][:, 0:128].rearrange("p (h d) -> p h d", h=2)[:, :, 0:16], func=AF.Copy),
               reads=[("pa", bkv)], writes=["qkr"])
            op("act", lambda q: q.activation(out=vaug[:, slot, :, 0:64], in_=pab[bkv][:, 128:256].rearrange("p (g d) -> p g d", g=2), func=AF.Copy),
               reads=[("pa", bkv)], writes=["vaug%d" % slot])
            cb = cosT[:, blk, :].unsqueeze(1).to_broadcast([128, 18, 8])
            sbb = sinT[:, blk, :].unsqueeze(1).to_broadcast([128, 18, 8])
            x1 = qkr[:, :, 0:8]
            x2 = qkr[:, :, 8:16]
            op("pool", lambda q: q.tensor_tensor(out=rt[:, 0], in0=x1, in1=cb, op=ALU.mult), reads=["qkr", "cosT"], writes=["rt0"])
            op("pool", lambda q: q.tensor_tensor(out=rt[:, 1], in0=x2, in1=sbb, op=ALU.mult), reads=["qkr", "sinT"], writes=["rt1"])
            op("pool", lambda q: q.tensor_tensor(out=rt[:, 2], in0=x2, in1=cb, op=ALU.mult), reads=["qkr", "cosT"], writes=["rt2"])
            op("pool", lambda q: q.tensor_tensor(out=rt[:, 3], in0=x1, in1=sbb, op=ALU.mult), reads=["qkr", "sinT"], writes=["rt3"])
            op("dve", lambda q: q.tensor_tensor(out=qkb[:, :, 0:8], in0=rt[:, 0], in1=rt[:, 1], op=ALU.subtract), reads=["rt0", "rt1"], writes=["qkb"])
            op("dve", lambda q: q.tensor_tensor(out=qkb[:, :, 8:16], in0=rt[:, 2], in1=rt[:, 3], op=ALU.add), reads=["rt2", "rt3"], writes=["qkb"])
            tbk = (S1B[4], S1B[5])
            for a in range(2):
                for hh in range(8):
                    op("pe", lambda q, a=a, hh=hh: q.transpose(out=ptv(tbk[a])[0:64, hh * 128:(hh + 1) * 128], in_=qkb[:, 8 * a + hh, :], identity=ident[:]),
                       reads=["qkb", "ident"], writes=[("pa", tbk[a])])
                op("act", lambda q, a=a: q.activation(out=qT[:, 8 * a:8 * a + 8, :], in_=ptv(tbk[a])[0:64, :].rearrange("p (h t) -> p h t", h=8), func=AF.Copy),
                   reads=[("pa", tbk[a])], writes=["qT"])
            for g in range(2):
                op("pe", lambda q, g=g: q.transpose(out=ptv(S1B[6])[0:64, g * 128:(g + 1) * 128], in_=qkb[:, 16 + g, :], identity=ident[:]),
                   reads=["qkb", "ident"], writes=[("pa", S1B[6])])
            op("dve", lambda q: q.tensor_copy(out=kT[:, slot, :, :], in_=ptv(S1B[6])[0:64, 0:256].rearrange("p (g t) -> p g t", g=2)),
               reads=[("pa", S1B[6])], writes=["kT%d" % slot])

        def l0_s2(m, j):
            blk = m * 4 + j
            first = (blk % (SEQ // 128) == 0)
            slot = blk % 2
            S.tag = "m%d.L0.%d.s2" % (m, j)
            pti = 0
            sb_i = 0
            for g in range(2):
                for a in range(2):
                    grp = 2 * g + a
                    ob = OA_BANKS[grp % len(OA_BANKS)]
                    kbs = ([] if first else [(1 - slot, mask_prev, "mask_prev")]) + [(slot, mask_cur, "mask_cur")]
                    pts = []
                    for (ks, msk, mkey) in kbs:
                        b = SC_BANKS[sb_i % len(SC_BANKS)]
                        sb_i += 1
                        op("pe", lambda q, b=b, ks=ks, g=g, a=a: q.matmul(pab[b][:], lhsT=kT[:, ks, g, :],
                                                                          rhs=qT[:, 8 * g + 4 * a:8 * g + 4 * a + 4, :].rearrange("p h t -> p (h t)"),
                                                                          start=True, stop=True),
                           reads=["kT%d" % ks, "qT"], writes=[("pa", b)])
                        p = pti % 4
                        pti += 1
                        op("act", lambda q, b=b, p=p: q.activation(out=PT[p][:], in_=pab[b][:], func=AF.Exp), reads=[("pa", b)], writes=[("PT", p)])
                        op("dve", lambda q, p=p, msk=msk: q.tensor_tensor(out=PT[p][:].rearrange("p (h t) -> p h t", h=4),
                                                                         in0=PT[p][:].rearrange("p (h t) -> p h t", h=4),
                                                                         in1=msk[:].unsqueeze(1).to_broadcast([128, 4, 128]), op=ALU.mult),
                           reads=[("PT", p), mkey], writes=[("PT", p)])
                        pts.append((p, ks))
                    for hh in range(4):
                        oo = hh * 65
                        for i, (p, ks) in enumerate(pts):
                            op("pe", lambda q, p=p, ks=ks, hh=hh, ob=ob, oo=oo, i=i, n=len(pts), g=g: q.matmul(
                                pab[ob][:, oo:oo + 65], lhsT=PT[p][:, hh * 128:(hh + 1) * 128], rhs=vaug[:, ks, g, :],
                                start=(i == 0), stop=(i == n - 1)),
                               reads=[("PT", p), "vaug%d" % ks], writes=[("pa", ob)])
                    h0 = 4 * grp
                    ov = pab[ob][:, 0:260].rearrange("p (h d) -> p h d", d=65)
                    op("dve", lambda q, ov=ov, h0=h0: q.tensor_tensor(out=sv(DEN + h0, 4), in0=ov[:, :, 64], in1=sv(ESINK + h0, 4), op=ALU.add),
                       reads=[("pa", ob), "esink"], writes=[("den", grp)])
                    op("dve", lambda q, h0=h0: q.reciprocal(out=sv(RDEN + h0, 4), in_=sv(DEN + h0, 4)), reads=[("den", grp)], writes=[("rden", grp)])
                    op("dve", lambda q, ov=ov, h0=h0: q.tensor_tensor(out=F(0).rearrange("p (h d) -> p h d", d=64)[:, h0:h0 + 4, :], in0=ov[:, :, 0:64],
                                                                     in1=sv(RDEN + h0, 4).unsqueeze(2).to_broadcast([128, 4, 64]), op=ALU.mult),
                       reads=[("pa", ob), ("rden", grp)], writes=[("of", grp)])

        def l0_s3(m, j):
            buf = m % 2
            S.tag = "m%d.L0.%d.s3" % (m, j)
            bz = [proj_tm(Z_BANKS[0], j, 1280, 512, W0, "W0"), proj_tm(Z_BANKS[1], j, 1792, 512, W0, "W0")]
            for hf in range(2):
                op("act", lambda q, hf=hf: q.activation(out=Fh(2 + hf), in_=pab[bz[hf]][:], func=AF.Tanh), reads=[("pa", bz[hf])], writes=[FhK(2 + hf)])
                op("dve", lambda q, hf=hf: q.scalar_tensor_tensor(out=Fh(2 + hf), in0=Fh(2 + hf), scalar=1.0, in1=pab[bz[hf]][:], op0=ALU.add, op1=ALU.mult),
                   reads=[FhK(2 + hf), ("pa", bz[hf])], writes=[FhK(2 + hf)])
            op("dve", lambda q: q.tensor_tensor(out=og_all[:, 0, :], in0=F(0), in1=F(1), op=ALU.mult),
               reads=[("of", 0), ("of", 1), ("of", 2), ("of", 3), FhK(0), FhK(1), FhK(2), FhK(3)], writes=[("og", 0), FhK(0), FhK(1)])
            yb = out_proj(og_all[:, 0, :], [("og", 0)], Wo0, "Wo0", True, Y_BANKS, 2)
            post_norm_residual(buf, j, 0, yb)

        def l1_A(m, h):
            ws = h % 2
            Wt = W1h[ws]
            wk = "W1h%d" % ws
            db = h % 2
            S.tag = "m%d.L1.h%d.A" % (m, h)
            bq, bf, bv, bzz = L1_BANKS[0:4]
            hTk = [("hT", j) for j in range(4)]
            for (b, c0) in ((bq, 0), (bf, 128)):
                for kc in range(8):
                    op("pe", lambda q, b=b, c0=c0, kc=kc: q.matmul(pab[b][:], lhsT=Wt[:, kc, c0:c0 + 128], rhs=hT[:, kc, :], start=(kc == 0), stop=(kc == 7)),
                       reads=hTk + [wk], writes=[("pa", b)])
            for (b, c0) in ((bv, 256), (bzz, 384)):
                for j in range(4):
                    for kc in range(8):
                        op("pe", lambda q, b=b, c0=c0, kc=kc, j=j: q.matmul(pab[b][:, j * 128:(j + 1) * 128], lhsT=hT[:, kc, j * 128:(j + 1) * 128],
                                                                            rhs=Wt[:, kc, c0:c0 + 128], start=(kc == 0), stop=(kc == 7)),
                           reads=[("hT", j), wk], writes=[("pa", b)])
            if h + 2 < 8:
                load_w1h(h + 2, ws)
            A0, A1, A2, A3, A4, A5, A6 = [Fh(i) for i in range(7)]
            K0, K1, K2, K3, K4, K5, K6 = [FhK(i) for i in range(7)]
            gz = Fh(7) if db == 0 else GZ[0][:]
            gzk = FhK(7) if db == 0 else "gz0"
            op("act", lambda q: q.activation(out=A0, in_=pab[bq][:], func=AF.Tanh), reads=[("pa", bq)], writes=[K0])
            op("dve", lambda q: q.scalar_tensor_tensor(out=A0, in0=A0, scalar=1.0, in1=pab[bq][:], op0=ALU.add, op1=ALU.mult),
               reads=[K0, ("pa", bq)], writes=[K0])
            op("act", lambda q: q.activation(out=A1, in_=pab[bf][:], func=AF.Tanh), reads=[("pa", bf)], writes=[K1])
            op("act", lambda q: q.activation(out=A2, in_=A1, func=AF.Identity, scale=sv(LBB + h), bias=sv(LBA + h)), reads=[K1] + LBK, writes=[K2])
            op("act", lambda q: q.activation(out=A3, in_=A1, func=AF.Identity, scale=sv(LBNB + h), bias=sv(LBB + h)), reads=[K1] + LBK, writes=[K3])
            op("dve", lambda q: q.tensor_tensor_scan(out=A4, data0=startmask[:], data1=A2, initial=0.0, op0=ALU.max, op1=ALU.mult),
               reads=["startmask", K2], writes=[K4])
            op("dve", lambda q: q.tensor_tensor(out=qeT[db][:], in0=A0, in1=A4, op=ALU.mult), reads=[K0, K4], writes=[("qeT", db)])
            op("dve", lambda q: q.reciprocal(out=A5, in_=A4), reads=[K4], writes=[K5], dur=3.3)
            op("dve", lambda q: q.tensor_tensor(out=A6, in0=A3, in1=A5, op=ALU.mult), reads=[K3, K5], writes=[K6])
            op("act", lambda q: q.activation(out=keT[db][:], in_=A6, func=AF.Copy), reads=[K6], writes=[("keT", db)])
            op("dve", lambda q: q.tensor_tensor(out=kdT[db][:].rearrange("p (c t) -> p c t", t=64), in0=A6.rearrange("p (c t) -> p c t", t=64),
                                                in1=A4.rearrange("p (c t) -> p c t", t=64)[:, :, 63:64].to_broadcast([128, 8, 64]), op=ALU.mult),
               reads=[K6, K4], writes=[("kdT", db)])
            op("act", lambda q: q.activation(out=glast[:, db, :], in_=A4.rearrange("p (c t) -> p c t", t=64)[:, :, 63], func=AF.Copy),
               reads=[K4], writes=[("glast", db)])
            op("act", lambda q: q.activation(out=vb[db][:].rearrange("p j v -> p (j v)"), in_=pab[bv][:], func=AF.Copy), reads=[("pa", bv)], writes=[("vb", db)])
            op("act", lambda q: q.activation(out=gz, in_=pab[bzz][:], func=AF.Tanh), reads=[("pa", bzz)], writes=[gzk])
            op("dve", lambda q: q.scalar_tensor_tensor(out=gz, in0=gz, scalar=1.0, in1=pab[bzz][:], op0=ALU.add, op1=ALU.mult),
               reads=[gzk, ("pa", bzz)], writes=[gzk])

        def l1_B(m, h):
            db = h % 2
            S.tag = "m%d.L1.h%d.B" % (m, h)
            gz = Fh(7) if db == 0 else GZ[0][:]
            gzk = FhK(7) if db == 0 else "gz0"
            ub = [L1_BANKS[4], L1_BANKS[5]]
            bso = L1_BANKS[6]
            for j in range(4):
                op("pe", lambda q, j=j: q.transpose(out=ptv(TB)[:, j * 128:(j + 1) * 128], in_=kdT[db][:, j * 128:(j + 1) * 128], identity=ident[:]),
                   reads=[("kdT", db), "ident"], writes=[("pa", TB)])
            op("act", lambda q: q.activation(out=kd_tm[:].rearrange("p j k -> p (j k)"), in_=ptv(TB)[:, 0:512], func=AF.Copy), reads=[("pa", TB)], writes=["kd_tm"])
            for j in range(4):
                for c in range(2):
                    op("pe", lambda q, j=j, c=c: q.matmul(pab[ub[c]][:, j * 128:(j + 1) * 128], lhsT=kd_tm[64 * c:64 * c + 64, j, :],
                                                          rhs=vb[db][64 * c:64 * c + 64, j, :], start=True, stop=True),
                       reads=["kd_tm", ("vb", db)], writes=[("pa", ub[c])])
            for j in range(4):
                op("pe", lambda q, j=j: q.matmul(pab[bso][:, j * 128:(j + 1) * 128], lhsT=keT[db][:, j * 128:(j + 1) * 128], rhs=qeT[db][:, j * 128:(j + 1) * 128],
                                                 start=True, stop=True),
                   reads=[("keT", db), ("qeT", db)], writes=[("pa", bso)])
            op("dve", lambda q: q.tensor_tensor(out=smk[:], in0=pab[bso][:].rearrange("p (j t) -> p j t", j=4),
                                                in1=maskbd[:].unsqueeze(1).to_broadcast([128, 4, 128]), op=ALU.mult),
               reads=[("pa", bso), "maskbd"], writes=["smk"])
            for k in range(8):
                j, c = k // 2, k % 2
                op("act", lambda q, k=k: q.activation(out=Sbf[:, k, :], in_=S32[:, h, :], func=AF.Copy), reads=[("S32", h)], writes=[("Sbf", k)])
                op("dve", lambda q, k=k, j=j, c=c: q.scalar_tensor_tensor(out=S32[:, h, :], in0=S32[:, h, :], scalar=glast[:, db, k:k + 1],
                                                                             in1=pab[ub[c]][:, j * 128:(j + 1) * 128], op0=ALU.mult, op1=ALU.add),
                   reads=[("S32", h), ("glast", db), ("pa", ub[c])], writes=[("S32", h)])
            for j in range(4):
                op("pe", lambda q, j=j: q.matmul(pab[bso][:, j * 128:(j + 1) * 128], lhsT=smk[:, j, :], rhs=vb[db][:, j, :], start=True, stop=True),
                   reads=["smk", ("vb", db)], writes=[("pa", bso)])
                for c in range(2):
                    k = 2 * j + c
                    op("pe", lambda q, j=j, c=c, k=k: q.matmul(pab[bso][64 * c:64 * c + 64, j * 128:(j + 1) * 128],
                                                               lhsT=qeT[db][:, j * 128 + 64 * c:j * 128 + 64 * c + 64], rhs=Sbf[:, k, :],
                                                               start=False, stop=True, skip_group_check=True),
                       reads=[("qeT", db), ("Sbf", k)], writes=[("pa", bso)])
            for j in range(4):
                op("act", lambda q, j=j: q.activation(out=og_all[:, j, h * 128:(h + 1) * 128], in_=pab[bso][:, j * 128:(j + 1) * 128], func=AF.Square, accum_out=sv(SS3 + j)),
                   reads=[("pa", bso)], writes=[("og", j), ("ss3", j)])
            op("dve", lambda q: q.tensor_scalar(out=sv(SS3, 4), in0=sv(SS3, 4), scalar1=1.0 / 128, scalar2=EPS, op0=ALU.mult, op1=ALU.add),
               reads=[("ss3", j) for j in range(4)], writes=[("ss3", j) for j in range(4)])
            op("pool", lambda q: q.tensor_tensor(out=sv(RSTD3, 4), in0=sv(SS3, 4), in1=sv(NHALF, 4), op=ALU.pow),
               reads=[("ss3", j) for j in range(4)] + ["nhalf"], writes=[("rstd3", j) for j in range(4)])
            for j in range(4):
                op("dve", lambda q, j=j: q.scalar_tensor_tensor(out=og_all[:, j, h * 128:(h + 1) * 128], in0=pab[bso][:, j * 128:(j + 1) * 128],
                                                                 scalar=sv(RSTD3 + j), in1=gz[:, j * 128:(j + 1) * 128], op0=ALU.mult, op1=ALU.mult),
                   reads=[("pa", bso), ("rstd3", j), gzk], writes=[("og", j)])

        load_x(0)
        for m in range(NMT):
            S.new_epoch()
            buf = m % 2
            if m + 1 < NMT:
                load_x(m + 1)
            load_w1h(0, 0)
            load_w1h(1, 1)
            l0_s1(m, 0)
            for j in range(4):
                l0_s2(m, j)
                if j + 1 < 4:
                    l0_s1(m, j + 1)
                l0_s3(m, j)
            S.tag = "m%d.L1.pre" % m
            for j in range(4):
                make_hT(buf, j, j)
            if m % MT_PER_SEQ == 0:
                op("pool", lambda q: q.memset(S32[:], 0.0), writes=[("S32", h) for h in range(8)])
            l1_A(m, 0)
            for h in range(8):
                if h + 1 < 8:
                    l1_A(m, h + 1)
                l1_B(m, h)
            S.tag = "m%d.L1.out" % m
            for j in range(4):
                yb = out_proj(og_all[:, j, :], [("og", j)], Wo1, "Wo1", False, ((L1_BANKS[0], L1_BANKS[1]), (L1_BANKS[2], L1_BANKS[3]))[j % 2])
                post_norm_residual(buf, j, 1, yb)
            op("sp", lambda q, m=m, buf=buf: q.dma_start(out=out_d[m * 512:(m + 1) * 512, :].rearrange("(j p) d -> p j d", p=128), in_=xb[buf][:]),
               reads=[xk(buf, j) for j in range(4)], dma="xs%d" % buf)
        S.emit(st)
    build_program.last_sched = S
    return nc


_PROG_CACHE = {}


def _get_prog(nseq, seq):
    key = (nseq, seq)
    if key not in _PROG_CACHE:
        _PROG_CACHE[key] = build_program(nseq, seq)
    return _PROG_CACHE[key]


def make_in_maps(inputs, n_cores, nseq, seq):
    x = np.ascontiguousarray(inputs["x"], dtype=np.float32)
    pos = np.ascontiguousarray(inputs["positions"], dtype=np.int32)
    cst = make_consts()
    maps = []
    for c in range(n_cores):
        xs = x[c * nseq:(c + 1) * nseq, :seq].reshape(nseq * seq, D)
        ps = pos[c * nseq:(c + 1) * nseq, :seq].reshape(nseq * seq // 128, 128).T
        maps.append({
            "x": np.ascontiguousarray(xs),
            "pos": np.ascontiguousarray(ps),
            "cst": cst,
            "pre_norm_w": np.ascontiguousarray(inputs["pre_norm_w"], dtype=np.float32),
            "post_norm_w": np.ascontiguousarray(inputs["post_norm_w"], dtype=np.float32),
            "attn_w_in": np.ascontiguousarray(inputs["attn_w_in"][0], dtype=np.float32),
            "attn_b_in": np.ascontiguousarray(inputs["attn_b_in"], dtype=np.float32).reshape(1, 2304),
            "attn_sinks": np.ascontiguousarray(inputs["attn_sinks"], dtype=np.float32).reshape(1, 16),
            "attn_w_out": np.ascontiguousarray(inputs["attn_w_out"][0], dtype=np.float32),
            "attn_b_out": np.ascontiguousarray(inputs["attn_b_out"], dtype=np.float32).reshape(1, D),
            "rec_w_in": np.ascontiguousarray(inputs["rec_w_in"][0], dtype=np.float32),
            "rec_lb_logits": np.ascontiguousarray(inputs["rec_lb_logits"], dtype=np.float32),
            "rec_gnorm_w": np.ascontiguousarray(inputs["rec_gnorm_w"], dtype=np.float32).reshape(1, 128),
            "rec_w_out": np.ascontiguousarray(inputs["rec_w_out"][0], dtype=np.float32),
        })
    return maps


def kernel(**inputs):
    B, T, _ = inputs["x"].shape
    nseq = B // N_CORES
    nc = _get_prog(nseq, T)
    maps = make_in_maps(inputs, N_CORES, nseq, T)
    res = run_bass_kernel_spmd(nc, maps, core_ids=list(range(N_CORES)))
    outs = [np.asarray(r["out"], dtype=np.float32).reshape(nseq, T, D) for r in res.results]
    return np.concatenate(outs, axis=0)
```

```python
import math
from contextlib import ExitStack

import numpy as np
import concourse.bass as bass
import concourse.mybir as mybir
from concourse.bass_utils import run_bass_kernel_spmd

F32 = mybir.dt.float32
BF16 = mybir.dt.bfloat16
I32 = mybir.dt.int32
AF = mybir.ActivationFunctionType
ALU = mybir.AluOpType

N_CORES = 8
S1B = (7, 1, 0, 7, 0, 2, 2)
OA_BANKS = (2,)
SC_BANKS = (4, 3)
Z_BANKS = (4, 6)
Y_BANKS = (1, 5)
L1_BANKS = (4, 2, 6, 5, 7, 3, 0, 6)
D = 1024
EPS = 1e-6
TWO_PI = 2.0 * math.pi
C1 = 6.28125
C2 = TWO_PI - C1
PI_LO = 3.1415925


class _Op:
    __slots__ = ("eng", "fn", "idx", "deps", "signal", "sem", "count", "dma", "epoch", "tag", "dur", "start")


class _Rec:
    def __init__(self):
        self.call = None

    def __getattr__(self, name):
        def f(*a, **k):
            self.call = (name, a, k)
            return self
        return f


def _nelem(ap):
    n = 1
    for d in ap.shape[1:]:
        n *= int(d)
    return n


def _estimate_us(eng, fn, is_dma):
    r = _Rec()
    try:
        fn(r)
    except Exception:
        return 0.5
    if r.call is None:
        return 0.3
    name, a, k = r.call
    out = k.get("out", a[0] if a else None)
    try:
        if is_dma:
            esz = 2 if out.dtype == BF16 else 4
            nbytes = _nelem(out) * int(out.shape[0]) * esz
            return 2.0 + nbytes / 200e3
        if eng == "pe":
            if name == "transpose":
                return 0.10
            rhs = k.get("rhs")
            n = _nelem(rhs)
            return 0.02 + n * 0.00058
        n = _nelem(out) if out is not None else 64
        if eng == "act":
            return 0.26 + n * 0.00085 + (0.1 if k.get("accum_out") is not None else 0.0)
        if eng == "dve":
            f = 1.0
            if name == "reciprocal":
                f = 4.2
            elif name == "tensor_tensor_scan":
                f = 2.0
            elif name == "tensor_tensor":
                f = 1.6
            return 0.12 + n * 0.00104 * f
        if eng == "pool":
            if name == "tensor_tensor" and k.get("op") == ALU.pow:
                return 0.65
            return 0.3 + n * 0.0021
    except Exception:
        pass
    return 0.4


class Sched:
    EPOCH = 1500

    def __init__(self, nc):
        self.nc = nc
        self.ops = []
        self.last_w = {}
        self.readers = {}
        self.epoch = 0
        self.tag = ""

    def new_epoch(self):
        pass

    def op(self, eng, fn, reads=(), writes=(), dma=None, dur=None):
        o = _Op()
        o.eng, o.fn, o.idx, o.dma, o.epoch = eng, fn, len(self.ops), dma, 0
        o.signal = False
        o.tag = self.tag
        o.dur = dur if dur is not None else _estimate_us(eng, fn, dma is not None)
        deps = set()
        for k in reads:
            w = self.last_w.get(k)
            if w is not None:
                deps.add(w)
        for k in writes:
            w = self.last_w.get(k)
            if w is not None:
                deps.add(w)
            for r in self.readers.get(k, ()):
                deps.add(r)
        deps.discard(o.idx)
        o.deps = deps
        for k in writes:
            self.last_w[k] = o.idx
            self.readers[k] = []
        for k in reads:
            if k not in writes:
                self.readers.setdefault(k, []).append(o.idx)
        self.ops.append(o)
        return o

    def list_schedule(self, reorder=True):
        import heapq
        ops = self.ops
        n = len(ops)
        if not reorder:
            return {e: [o for o in ops if o.eng == e] for e in ("pe", "act", "dve", "pool", "sp")}, 0.0
        succ = [[] for _ in range(n)]
        indeg = [0] * n
        for o in ops:
            indeg[o.idx] = len(o.deps)
            for d in o.deps:
                succ[d].append(o.idx)
        ready_t = [0.0] * n
        free = {e: 0.0 for e in ("pe", "act", "dve", "pool", "sp")}
        heaps = {e: [] for e in free}
        for o in ops:
            if indeg[o.idx] == 0:
                heapq.heappush(heaps[o.eng], (0.0, o.idx))
        order = {e: [] for e in free}
        done = 0
        SEM_LAT = 0.15
        while done < n:
            best = None
            for e, h in heaps.items():
                if not h:
                    continue
                t0 = max(free[e], h[0][0])
                cand = None
                tmp = []
                while h and h[0][0] <= t0 and len(tmp) < 24:
                    tmp.append(heapq.heappop(h))
                pick = min(tmp, key=lambda x: x[1])
                for x in tmp:
                    if x is not pick:
                        heapq.heappush(h, x)
                heapq.heappush(h, pick)
                cand = (t0, pick[1], e, pick)
                if best is None or cand[:2] < best[:2]:
                    best = cand
            t0, idx, e, pick = best
            h = heaps[e]
            h.remove(pick)
            heapq.heapify(h)
            o = ops[idx]
            o.start = t0
            if o.dma is not None:
                free[e] = t0 + 0.06
                fin = t0 + o.dur
            else:
                free[e] = t0 + o.dur
                fin = free[e]
            order[e].append(o)
            done += 1
            for sidx in succ[idx]:
                so = ops[sidx]
                lat = 0.0 if (so.eng == e and e == "pe" and o.dma is None) else SEM_LAT
                ready_t[sidx] = max(ready_t[sidx], fin + lat)
                indeg[sidx] -= 1
                if indeg[sidx] == 0:
                    heapq.heappush(heaps[so.eng], (ready_t[sidx], sidx))
        return order, max(free.values())

    def emit(self, stack, reorder=True):
        nc = self.nc
        ops = self.ops
        order, makespan = self.list_schedule(reorder)
        self.makespan = makespan
        pos = {}
        for e, lst in order.items():
            for i, o in enumerate(lst):
                pos[o.idx] = i
        for o in ops:
            for d in o.deps:
                p = ops[d]
                if p.dma is None and p.eng == "pe" and o.eng == "pe" and o.dma is None:
                    assert pos[p.idx] < pos[o.idx]
                    continue
                p.signal = True
        counts = {}
        nsig = {}
        for e, lst in order.items():
            for o in lst:
                if o.dma is not None:
                    key = ("dma", o.dma)
                    counts[key] = counts.get(key, 0) + 16
                    o.sem, o.count = key, counts[key]
                elif o.signal:
                    k = nsig.get(e, 0)
                    nsig[e] = k + 1
                    o.epoch = k // self.EPOCH
                    key = (e, o.epoch)
                    counts[key] = counts.get(key, 0) + 1
                    o.sem, o.count = key, counts[key]
        sems = {}
        for key in counts:
            sems[key] = stack.enter_context(nc.semaphore("s_%s_%s" % key))
        self.n_sems = len(sems)
        final = dict(counts)

        def stream(eng_name):
            def body(eng):
                seen = {}
                for o in order[eng_name]:
                    need = {}
                    for d in o.deps:
                        p = ops[d]
                        if p.dma is None and p.eng == "pe" and eng_name == "pe" and o.dma is None:
                            continue
                        if p.dma is not None:
                            skey, val = ("dma", p.dma), (0, p.count)
                        else:
                            skey, val = ("eng", p.eng), (p.epoch, p.count)
                        if val > need.get(skey, (-1, -1)):
                            need[skey] = val
                    for skey, val in need.items():
                        if val <= seen.get(skey, (-1, -1)):
                            continue
                        seen[skey] = val
                        if skey[0] == "dma":
                            eng.wait_ge(sems[("dma", skey[1])], val[1])
                        else:
                            eng.wait_ge(sems[(skey[1], val[0])], val[1])
                    ins = o.fn(eng)
                    if o.dma is not None:
                        ins.then_inc(sems[o.sem], 16)
                    elif o.signal:
                        ins.then_inc(sems[o.sem], 1)
                if eng_name == "sp":
                    for key, c in final.items():
                        if key[0] == "dma":
                            eng.wait_ge(sems[key], c)
            return body

        with nc.Block() as block:
            block.tensor(stream("pe"))
            block.scalar(stream("act"))
            block.vector(stream("dve"))
            block.gpsimd(stream("pool"))
            block.sync(stream("sp"))


CST_W = 128 * 4 + 512 + 8


def make_consts():
    c = np.zeros((128, CST_W), np.float32)
    i = np.arange(128)
    c[:, 0:128] = np.eye(128, dtype=np.float32)
    c[:, 128:256] = (i[:, None] <= i[None, :]).astype(np.float32)
    c[:, 256:384] = (i[:, None] > i[None, :]).astype(np.float32)
    c[:, 384:512] = ((i[:, None] <= i[None, :]) & ((i[:, None] // 64) == (i[None, :] // 64))).astype(np.float32)
    sm = np.zeros(512, np.float32)
    sm[::64] = 1.0
    c[:, 512:1024] = sm[None, :]
    invf = (np.float32(500000.0) ** (-(np.arange(8, dtype=np.float32) * np.float32(2.0) / np.float32(16.0)))).astype(np.float32)
    c[:, 1024:1032] = invf[None, :]
    return c


def build_program(NSEQ, SEQ):
    NT = NSEQ * SEQ
    NB = NT // 128
    NMT = NT // 512
    MT_PER_SEQ = SEQ // 512
    nc = bass.Bass("TRN2", target_bir_lowering=False)

    def din(name, shape, dt=F32):
        return nc.dram_tensor(name, list(shape), dt, kind="ExternalInput").ap()

    x_d = din("x", [NT, D])
    pos_d = din("pos", [128, NB], I32)
    cst_d = din("cst", [128, CST_W])
    prew_d = din("pre_norm_w", [2, D])
    postw_d = din("post_norm_w", [2, D])
    w0_d = din("attn_w_in", [D, 2304])
    b0_d = din("attn_b_in", [1, 2304])
    sink_d = din("attn_sinks", [1, 16])
    wo0_d = din("attn_w_out", [D, D])
    bo0_d = din("attn_b_out", [1, D])
    w1_d = din("rec_w_in", [D, 4096])
    lb_d = din("rec_lb_logits", [2, D])
    gnw_d = din("rec_gnorm_w", [1, 128])
    wo1_d = din("rec_w_out", [D, D])
    out_d = nc.dram_tensor("out", [NT, D], F32, kind="ExternalOutput").ap()
    w1s_d = nc.dram_tensor("w1s", [8, 128, 8, 512], BF16, kind="Internal").ap()

    with ExitStack() as st:
        def sb(name, shape, dt):
            return st.enter_context(nc.sbuf_tensor(name, list(shape), dt))

        def ps(name, shape, dt):
            return st.enter_context(nc.psum_tensor(name, list(shape), dt))

        W0 = sb("W0", [128, 8, 2304], BF16)
        Wo0 = sb("Wo0", [128, 8, 1024], BF16)
        Wo1 = sb("Wo1", [128, 8, 1024], BF16)
        W1h = [sb("W1h%d" % i, [128, 8, 512], BF16) for i in range(2)]
        xb = [sb("xb%d" % i, [128, 4, 1024], F32) for i in range(2)]
        FF = sb("FF", [128, 4096], F32)
        hT = sb("hT", [128, 8, 512], BF16)
        og_all = sb("og_all", [128, 4, 1024], BF16)
        ident = sb("ident", [128, 128], BF16)
        mask_cur = sb("mask_cur", [128, 128], BF16)
        mask_prev = sb("mask_prev", [128, 128], BF16)
        maskbd = sb("maskbd", [128, 128], BF16)
        startmask = sb("startmask", [128, 512], F32)
        invf = sb("invf", [128, 8], F32)
        cosT = sb("cosT", [128, NB, 8], F32)
        sinT = sb("sinT", [128, NB, 8], F32)
        wpost = sb("wpost", [128, 2, 1024], F32)
        browA = sb("browA", [65, 1024], BF16)
        posi = sb("posi", [128, NB], I32)
        browB = sb("browB", [1, 1024], BF16)
        ones = sb("ones", [65, 128], BF16)
        small = sb("small", [128, 144], F32)
        S32 = sb("S32", [128, 8, 128], F32)
        Sbf = sb("Sbf", [128, 8, 128], BF16)
        hb = sb("hb", [128, 1024], BF16)
        qkb = sb("qkb", [128, 18, 64], BF16)
        qkr = sb("qkr", [128, 18, 16], F32)
        rt = sb("rt", [128, 4, 18, 8], F32)
        qT = sb("qT", [64, 16, 128], BF16)
        kT = sb("kT", [64, 2, 2, 128], BF16)
        vaug = sb("vaug", [128, 2, 2, 65], BF16)
        PT = [sb("PT%d" % i, [128, 512], BF16) for i in range(4)]
        ogT = sb("ogT", [128, 8, 128], BF16)
        qeT = [sb("qeT%d" % i, [128, 512], BF16) for i in range(2)]
        keT = [sb("keT%d" % i, [128, 512], BF16) for i in range(2)]
        kdT = [sb("kdT%d" % i, [128, 512], BF16) for i in range(2)]
        kd_tm = sb("kd_tm", [128, 4, 128], BF16)
        vb = [sb("vb%d" % i, [128, 4, 128], BF16) for i in range(2)]
        smk = sb("smk", [128, 4, 128], BF16)

        PREW0, PREW1 = 0, 8
        SCQ0, SCH0, SCH1 = 16, 24, 32
        LBA, LBB, LBNB = 40, 48, 56
        GNW = 64
        ESINK = 65
        SS = 81
        RSTD = 85
        DEN = 89
        RDEN = 105
        NHALF = 121
        SS2, RSTD2 = 125, 127
        SS3, RSTD3 = 128, 132

        def sv(c, n=1):
            return small[:, c:c + n]

        pab = [ps("pab%d" % i, [128, 512], F32) for i in range(8)]
        GZ = [sb("gz0", [128, 512], F32)]
        glast = sb("glast", [128, 2, 8], F32)

        def ptv(i):
            return pab[i][:].bitcast(BF16)

        S = Sched(nc)
        op = S.op
        FK = ["FF0", "FF1", "FF2", "FF3"]

        def F(i):
            return FF[:, i * 1024:(i + 1) * 1024]

        def Fh(i):
            return FF[:, i * 512:(i + 1) * 512]

        def FhK(i):
            return "FH%d" % i

        ALLF = FK + [FhK(i) for i in range(8)]

        op("sp", lambda q: q.dma_start(out=FF[:, 0:CST_W], in_=cst_d), writes=ALLF, dma="cst")
        op("dve", lambda q: q.tensor_copy(out=ident[:], in_=FF[:, 0:128]), reads=ALLF, writes=["ident"])
        op("dve", lambda q: q.tensor_copy(out=mask_cur[:], in_=FF[:, 128:256]), reads=ALLF, writes=["mask_cur"])
        op("dve", lambda q: q.tensor_copy(out=mask_prev[:], in_=FF[:, 256:384]), reads=ALLF, writes=["mask_prev"])
        op("dve", lambda q: q.tensor_copy(out=maskbd[:], in_=FF[:, 384:512]), reads=ALLF, writes=["maskbd"])
        op("dve", lambda q: q.tensor_copy(out=startmask[:], in_=FF[:, 512:1024]), reads=ALLF, writes=["startmask"])
        op("dve", lambda q: q.tensor_copy(out=invf[:], in_=FF[:, 1024:1032]), reads=ALLF, writes=["invf"])
        op("pool", lambda q: q.memset(ones[:], 1.0), writes=["ones"])
        op("pool", lambda q: q.memset(small[:, NHALF:NHALF + 4], -0.5), writes=["nhalf"])
        op("pool", lambda q: q.memset(vaug[:], 1.0), writes=["vaug0", "vaug1"])
        op("sp", lambda q: q.dma_start(out=small[:, PREW0:PREW0 + 8], in_=prew_d[0:1, :].rearrange("o (k p) -> p (o k)", p=128),
                                       allow_slow_non_contiguous=True), writes=["prew", "smq"], dma="sm")
        op("sp", lambda q: q.dma_start(out=small[:, PREW1:PREW1 + 8], in_=prew_d[1:2, :].rearrange("o (k p) -> p (o k)", p=128),
                                       allow_slow_non_contiguous=True), writes=["prew", "smq"], dma="sm")
        op("sp", lambda q: q.dma_start(out=small[:, LBA:LBA + 8], in_=lb_d[0:1, :].rearrange("o (k p) -> p (o k)", p=128),
                                       allow_slow_non_contiguous=True), writes=["lb0", "smq"], dma="sm")
        op("sp", lambda q: q.dma_start(out=small[:, LBB:LBB + 8], in_=lb_d[1:2, :].rearrange("o (k p) -> p (o k)", p=128),
                                       allow_slow_non_contiguous=True), writes=["lb1", "smq"], dma="sm")
        op("sp", lambda q: q.dma_start(out=small[:, GNW:GNW + 1], in_=gnw_d.rearrange("o p -> p o"),
                                       allow_slow_non_contiguous=True), writes=["gnw", "smq"], dma="sm")
        op("sp", lambda q: q.dma_start(out=small[:, ESINK:ESINK + 16], in_=sink_d.partition_broadcast(128)), writes=["esink", "smq"], dma="sm")
        op("sp", lambda q: q.dma_start(out=wpost[:, 0, :], in_=postw_d[0:1, :].partition_broadcast(128)), writes=["wpost", "smq"], dma="sm")
        op("sp", lambda q: q.dma_start(out=wpost[:, 1, :], in_=postw_d[1:2, :].partition_broadcast(128)), writes=["wpost", "smq"], dma="sm")
        op("sp", lambda q: q.dma_start(out=posi[:], in_=pos_d), writes=["posi", "smq"], dma="sm")
        op("dve", lambda q: q.tensor_scalar(out=sv(SCQ0, 8), in0=sv(PREW0, 8), scalar1=0.125, scalar2=None, op0=ALU.mult), reads=["prew"], writes=["scq0"])
        op("dve", lambda q: q.tensor_scalar(out=sv(SCH0, 8), in0=sv(PREW0, 8), scalar1=0.5, scalar2=None, op0=ALU.mult), reads=["prew"], writes=["sch0"])
        op("dve", lambda q: q.tensor_scalar(out=sv(SCH1, 8), in0=sv(PREW1, 8), scalar1=0.5, scalar2=None, op0=ALU.mult), reads=["prew"], writes=["sch1"])
        op("act", lambda q: q.activation(out=sv(ESINK, 16), in_=sv(ESINK, 16), func=AF.Exp), reads=["esink"], writes=["esink"])
        op("dve", lambda q: q.tensor_tensor(out=sv(LBNB, 8), in0=sv(LBB, 8), in1=sv(LBA, 8), op=ALU.subtract), reads=["lb0", "lb1"], writes=["lbt"])
        op("act", lambda q: q.activation(out=sv(LBNB, 8), in_=sv(LBNB, 8), func=AF.Tanh, scale=0.5), reads=["lbt"], writes=["lbt"])
        op("dve", lambda q: q.tensor_scalar(out=sv(LBA, 8), in0=sv(LBNB, 8), scalar1=0.25, scalar2=0.75, op0=ALU.mult, op1=ALU.add), reads=["lbt"], writes=["lb0"])
        op("dve", lambda q: q.tensor_scalar(out=sv(LBB, 8), in0=sv(LBNB, 8), scalar1=-0.25, scalar2=0.25, op0=ALU.mult, op1=ALU.add), reads=["lbt"], writes=["lb1"])
        op("dve", lambda q: q.tensor_scalar(out=sv(LBNB, 8), in0=sv(LBB, 8), scalar1=-1.0, scalar2=None, op0=ALU.mult), reads=["lb1", "lbt"], writes=["lbt"])
        LBK = ["lb0", "lb1", "lbt"]

        o0 = 1040
        posf = FF[:, o0:o0 + NB]
        ang = FF[:, o0 + NB:o0 + NB + NB * 8]
        tmpu = FF[:, o0 + 9 * NB:o0 + 17 * NB]
        tmpk = FF[:, o0 + 17 * NB:o0 + 25 * NB]
        assert o0 + 25 * NB <= 4096
        tmpi = hb[:].bitcast(I32)[:, 0:NB * 8]
        op("dve", lambda q: q.tensor_copy(out=posf, in_=posi[:]), reads=["posi"] + ALLF, writes=ALLF)
        op("dve", lambda q: q.tensor_tensor(out=ang.rearrange("p (b i) -> p b i", i=8),
                                            in0=posf.unsqueeze(2).to_broadcast([128, NB, 8]),
                                            in1=invf[:].unsqueeze(1).to_broadcast([128, NB, 8]), op=ALU.mult),
           reads=ALLF + ["invf"], writes=ALLF)
        for which, tab in ((0, sinT), (1, cosT)):
            if which == 1:
                op("dve", lambda q: q.tensor_scalar(out=ang, in0=ang, scalar1=math.pi / 2, scalar2=None, op0=ALU.add), reads=ALLF, writes=ALLF)
            op("dve", lambda q: q.tensor_scalar(out=tmpu, in0=ang, scalar1=1.0 / TWO_PI, scalar2=None, op0=ALU.mult), reads=ALLF, writes=ALLF)
            op("dve", lambda q: q.tensor_copy(out=tmpi, in_=tmpu), reads=ALLF, writes=["hb"])
            op("dve", lambda q: q.tensor_copy(out=tmpk, in_=tmpi), reads=["hb"], writes=ALLF)
            op("dve", lambda q: q.scalar_tensor_tensor(out=tmpu, in0=tmpk, scalar=-C1, in1=ang, op0=ALU.mult, op1=ALU.add), reads=ALLF, writes=ALLF)
            op("dve", lambda q: q.scalar_tensor_tensor(out=tmpu, in0=tmpk, scalar=-C2, in1=tmpu, op0=ALU.mult, op1=ALU.add), reads=ALLF, writes=ALLF)
            op("dve", lambda q: q.tensor_scalar(out=tmpu, in0=tmpu, scalar1=-PI_LO, scalar2=PI_LO, op0=ALU.max, op1=ALU.min), reads=ALLF, writes=ALLF)
            op("act", lambda q, tab=tab: q.activation(out=tab[:].rearrange("p b i -> p (b i)"), in_=tmpu, func=AF.Sin),
               reads=ALLF, writes=["cosT" if which == 1 else "sinT"])

        BROWS = [(0, 1024, 0), (1024, 1792, 32), (1792, 2304, 64)]
        for (c0, c1, p) in BROWS:
            op("sp", lambda q, c0=c0, c1=c1, p=p: q.dma_start(out=FF[p:p + 1, 0:c1 - c0], in_=b0_d[0:1, c0:c1]),
               reads=["cosT", "sinT"], writes=ALLF + ["smq"], dma="sm")
        op("sp", lambda q: q.dma_start(out=FF[0:1, 1024:2048], in_=bo0_d), writes=ALLF + ["smq"], dma="sm")
        op("dve", lambda q: q.tensor_scalar(out=browA[0:1, 0:1024], in0=FF[0:1, 0:1024], scalar1=0.125, scalar2=None, op0=ALU.mult), reads=ALLF, writes=["browA"])
        op("dve", lambda q: q.tensor_copy(out=browA[32:33, 0:256], in_=FF[32:33, 0:256]), reads=ALLF, writes=["browA"])
        op("dve", lambda q: q.tensor_scalar(out=browA[32:33, 256:768], in0=FF[32:33, 256:768], scalar1=0.5, scalar2=None, op0=ALU.mult), reads=ALLF, writes=["browA"])
        op("dve", lambda q: q.tensor_scalar(out=browA[64:65, 0:512], in0=FF[64:65, 0:512], scalar1=0.5, scalar2=None, op0=ALU.mult), reads=ALLF, writes=["browA"])
        op("dve", lambda q: q.tensor_copy(out=browB[:], in_=FF[0:1, 1024:2048]), reads=ALLF, writes=["browB"])

        def brow(c0, n):
            for (r0, r1, p) in BROWS:
                if r0 <= c0 and c0 + n <= r1:
                    return ones[p:p + 1, :], browA[p:p + 1, c0 - r0:c0 - r0 + n]
            raise AssertionError((c0, n))

        stageA = FF
        stageB = xb[1][:].rearrange("p j d -> p (j d)")
        XB1K = [("x", 1, j) for j in range(4)]
        stg = [(stageA, ALLF), (stageB, XB1K)]
        stb = [(og_all[:].rearrange("p j d -> p (j d)"), [("og", j) for j in range(4)]), (hT[:].rearrange("p k t -> p (k t)"), [("hT", j) for j in range(4)])]
        cnt = [0]

        def conv(eng, out, in_, scal, rd, wr):
            if eng == "dve":
                op("dve", lambda q: q.tensor_scalar(out=out, in0=in_, scalar1=scal, scalar2=None, op0=ALU.mult), reads=rd, writes=wr)
            else:
                op("act", lambda q: q.activation(out=out, in_=in_, func=AF.Copy, scale=scal), reads=rd, writes=wr)

        for kc in range(8):
            sg, sk = stg[cnt[0] % 2]
            cnt[0] += 1
            op("sp", lambda q, sg=sg, kc=kc: q.dma_start(out=sg[:, 0:2304], in_=w0_d[kc * 128:(kc + 1) * 128, :]),
               reads=["browA", "browB"], writes=sk, dma="wl%d" % (cnt[0] % 2))
            conv("dve", W0[:, kc, 0:1024], sg[:, 0:1024], sv(SCQ0 + kc), sk + ["scq0"], ["W0"])
            conv("act", W0[:, kc, 1024:1280], sg[:, 1024:1280], sv(PREW0 + kc), sk + ["prew"], ["W0"])
            conv("act", W0[:, kc, 1280:2304], sg[:, 1280:2304], sv(SCH0 + kc), sk + ["sch0"], ["W0"])
        for kc in range(8):
            sg, sk = stg[cnt[0] % 2]
            cnt[0] += 1
            op("sp", lambda q, sg=sg, kc=kc: q.dma_start(out=sg[:, 0:1024], in_=wo0_d[kc * 128:(kc + 1) * 128, :]),
               writes=sk, dma="wl%d" % (cnt[0] % 2))
            op("sp", lambda q, sg=sg, kc=kc: q.dma_start(out=sg[:, 1024:2048], in_=wo1_d[kc * 128:(kc + 1) * 128, :]),
               writes=sk, dma="wl%d" % (cnt[0] % 2))
            op("act", lambda q, sg=sg, kc=kc: q.activation(out=Wo0[:, kc, :], in_=sg[:, 0:1024], func=AF.Copy), reads=sk, writes=["Wo0"])
            conv("dve", Wo1[:, kc, :], sg[:, 1024:2048], sv(GNW), sk + ["gnw"], ["Wo1"])
        for kc in range(8):
            sg, sk = stg[cnt[0] % 2]
            sbt, sbk = stb[cnt[0] % 2]
            cnt[0] += 1
            op("sp", lambda q, sg=sg, kc=kc: q.dma_start(out=sg[:, 0:4096], in_=w1_d[kc * 128:(kc + 1) * 128, :]),
               writes=sk, dma="wl%d" % (cnt[0] % 2))
            for t in range(4):
                o_ap = sbt.rearrange("p (h t c) -> p t h c", h=8, t=4, c=128)[:, t, :, :]
                i_ap = sg[:, t * 1024:(t + 1) * 1024].rearrange("p (h c) -> p h c", h=8)
                scal = sv(PREW1 + kc) if t == 2 else sv(SCH1 + kc)
                conv("dve" if t % 2 == 0 else "act", o_ap, i_ap, scal, sk + ["prew", "sch1"], sbk)
            op("sp", lambda q, sbt=sbt, kc=kc: q.dma_start(out=w1s_d[:, :, kc, :].rearrange("h p c -> p h c"),
                                                          in_=sbt.rearrange("p (h c) -> p h c", h=8)),
               reads=sbk, writes=["w1s"], dma="ws%d" % (cnt[0] % 2))

        def xk(buf, j):
            return ("x", buf, j)

        def load_x(m):
            buf = m % 2
            op("sp", lambda q: q.dma_start(out=xb[buf][:], in_=x_d[m * 512:(m + 1) * 512, :].rearrange("(j p) d -> p j d", p=128)),
               writes=[xk(buf, j) for j in range(4)], dma="xl%d" % buf)

        def load_w1h(h, slot):
            op("sp", lambda q: q.dma_start(out=W1h[slot][:], in_=w1s_d[h]), reads=["w1s"], writes=["W1h%d" % slot], dma="w1l%d" % slot)

        def rms_rstd(src_ap, src_keys, col):
            op("act", lambda q: q.activation(out=hb[:], in_=src_ap, func=AF.Square, accum_out=sv(SS + col)),
               reads=src_keys, writes=["hb", ("ss", col)])
            op("dve", lambda q: q.tensor_scalar(out=sv(SS + col), in0=sv(SS + col), scalar1=1.0 / 1024, scalar2=EPS, op0=ALU.mult, op1=ALU.add),
               reads=[("ss", col)], writes=[("ss", col)])
            op("pool", lambda q: q.tensor_tensor(out=sv(RSTD + col), in0=sv(SS + col), in1=sv(NHALF), op=ALU.pow),
               reads=[("ss", col), "nhalf"], writes=[("rstd", col)])

        TB = L1_BANKS[7]

        def make_hT(buf, j, col, tb=TB):
            xs = xb[buf][:, j, :]
            rms_rstd(xs, [xk(buf, j)], col)
            op("act", lambda q: q.activation(out=hb[:], in_=xs, func=AF.Copy, scale=sv(RSTD + col)),
               reads=[xk(buf, j), ("rstd", col)], writes=["hb"])
            for kc in range(8):
                op("pe", lambda q, kc=kc: q.transpose(out=ptv(tb)[:, kc * 128:(kc + 1) * 128], in_=hb[:, kc * 128:(kc + 1) * 128], identity=ident[:]),
                   reads=["hb", "ident"], writes=[("pa", tb)])
            op("act", lambda q: q.activation(out=hT[:, :, j * 128:(j + 1) * 128], in_=ptv(tb).rearrange("p (k t) -> p k t", k=8), func=AF.Copy),
               reads=[("pa", tb)], writes=[("hT", j)])

        def post_norm_residual(buf, j, layer, pbanks):
            c0 = 4
            for hf in range(2):
                b = pbanks[hf]
                op("act", lambda q, b=b, hf=hf: q.activation(out=Fh(6 + hf), in_=pab[b][:], func=AF.Square, accum_out=sv(SS2 + hf)),
                   reads=[("pa", b)], writes=[FhK(6 + hf), ("ss2", hf)])
            op("dve", lambda q: q.tensor_scalar(out=sv(SS2, 2), in0=sv(SS2, 2), scalar1=1.0 / 1024, scalar2=EPS / 2, op0=ALU.mult, op1=ALU.add),
               reads=[("ss2", 0), ("ss2", 1)], writes=[("ss2", 0), ("ss2", 1)])
            op("dve", lambda q: q.tensor_tensor(out=sv(SS2), in0=sv(SS2), in1=sv(SS2 + 1), op=ALU.add),
               reads=[("ss2", 0), ("ss2", 1)], writes=[("ss2", 0)])
            op("pool", lambda q: q.tensor_tensor(out=sv(RSTD2), in0=sv(SS2), in1=sv(NHALF), op=ALU.pow),
               reads=[("ss2", 0), "nhalf"], writes=["rstd2"])
            for hf in range(2):
                b = pbanks[hf]
                op("dve", lambda q, b=b, hf=hf: q.scalar_tensor_tensor(out=Fh(6 + hf), in0=pab[b][:], scalar=sv(RSTD2),
                                                                        in1=wpost[:, layer, hf * 512:(hf + 1) * 512], op0=ALU.mult, op1=ALU.mult),
                   reads=[("pa", b), "rstd2", "wpost"], writes=[FhK(6 + hf)])
            op("pool", lambda q: q.tensor_tensor(out=xb[buf][:, j, :], in0=xb[buf][:, j, :], in1=F(3), op=ALU.add),
               reads=[xk(buf, j), FhK(6), FhK(7)], writes=[xk(buf, j)])

        def out_proj(src_bf16_ap, src_keys, Wo, wo_key, bias, ybanks, tb=TB):
            for kc in range(8):
                op("pe", lambda q, kc=kc: q.transpose(out=ptv(tb)[:, kc * 128:(kc + 1) * 128], in_=src_bf16_ap[:, kc * 128:(kc + 1) * 128], identity=ident[:]),
                   reads=src_keys + ["ident"], writes=[("pa", tb)])
            op("act", lambda q: q.activation(out=ogT[:].rearrange("p k t -> p (k t)"), in_=ptv(tb), func=AF.Copy),
               reads=[("pa", tb)], writes=["ogT"])
            banks = list(ybanks)
            for hf in range(2):
                b = banks[hf]
                for kc in range(8):
                    op("pe", lambda q, b=b, kc=kc, hf=hf: q.matmul(pab[b][:], lhsT=ogT[:, kc, :], rhs=Wo[:, kc, hf * 512:(hf + 1) * 512],
                                                                   start=(kc == 0), stop=(kc == 7 and not bias)),
                       reads=["ogT", wo_key], writes=[("pa", b)])
                if bias:
                    op("pe", lambda q, b=b, hf=hf: q.matmul(pab[b][:], lhsT=ones[0:1, :], rhs=browB[0:1, hf * 512:(hf + 1) * 512], start=False, stop=True),
                       reads=["ones", "browB"], writes=[("pa", b)])
            return banks

        def proj_tm(b, j, c0, n, Wt, wkey, bias=True):
            for kc in range(8):
                op("pe", lambda q, kc=kc: q.matmul(pab[b][:, 0:n], lhsT=hT[:, kc, j * 128:(j + 1) * 128], rhs=Wt[:, kc, c0:c0 + n],
                                                   start=(kc == 0), stop=(kc == 7 and not bias)),
                   reads=[("hT", j), wkey], writes=[("pa", b)])
            if bias:
                o1, br = brow(c0, n)
                op("pe", lambda q: q.matmul(pab[b][:, 0:n], lhsT=o1, rhs=br, start=False, stop=True),
                   reads=["ones", "browA"], writes=[("pa", b)])
            return b

        def l0_s1(m, j):
            buf = m % 2
            blk = m * 4 + j
            slot = blk % 2
            S.tag = "m%d.L0.%d.s1" % (m, j)
            make_hT(buf, j, j, S1B[3])
            bq = [proj_tm(S1B[0], j, 0, 512, W0, "W0"), proj_tm(S1B[1], j, 512, 512, W0, "W0")]
            for a in range(2):
                op("act", lambda q, a=a: q.activation(out=qkb[:, 8 * a:8 * a + 8, :], in_=pab[bq[a]][:].rearrange("p (h d) -> p h d", h=8), func=AF.Copy),
                   reads=[("pa", bq[a])], writes=["qkb"])
                op("act", lambda q, a=a: q.activation(out=qkr[:, 8 * a:8 * a + 8, :], in_=pab[bq[a]][:].rearrange("p (h d) -> p h d", h=8)[:, :, 0:16], func=AF.Copy),
                   reads=[("pa", bq[a])], writes=["qkr"])
            bkv = proj_tm(S1B[2], j, 1024, 256, W0, "W0")
            op("act", lambda q: q.activation(out=qkb[:, 16:18, :], in_=pab[bkv][:, 0:128].rearrange("p (h d) -> p h d", h=2), func=AF.Copy),
               reads=[("pa", bkv)], writes=["qkb"])
            op("act", lambda q: q.activation(out=qkr[:, 16:18, :], in_=pab[bkv][:, 0:128].rearrange("p (h d) -> p h d", h=2)[:, :, 0:16], func=AF.Copy),
               reads=[("pa", bkv)], writes=["qkr"])
            op("act", lambda q: q.activation(out=vaug[:, slot, :, 0:64], in_=pab[bkv][:, 128:256].rearrange("p (g d) -> p g d", g=2), func=AF.Copy),
               reads=[("pa", bkv)], writes=["vaug%d" % slot])
            cb = cosT[:, blk, :].unsqueeze(1).to_broadcast([128, 18, 8])
            sbb = sinT[:, blk, :].unsqueeze(1).to_broadcast([128, 18, 8])
            x1 = qkr[:, :, 0:8]
            x2 = qkr[:, :, 8:16]
            op("pool", lambda q: q.tensor_tensor(out=rt[:, 0], in0=x1, in1=cb, op=ALU.mult), reads=["qkr", "cosT"], writes=["rt0"])
            op("pool", lambda q: q.tensor_tensor(out=rt[:, 1], in0=x2, in1=sbb, op=ALU.mult), reads=["qkr", "sinT"], writes=["rt1"])
            op("pool", lambda q: q.tensor_tensor(out=rt[:, 2], in0=x2, in1=cb, op=ALU.mult), reads=["qkr", "cosT"], writes=["rt2"])
            op("pool", lambda q: q.tensor_tensor(out=rt[:, 3], in0=x1, in1=sbb, op=ALU.mult), reads=["qkr", "sinT"], writes=["rt3"])
            op("dve", lambda q: q.tensor_tensor(out=qkb[:, :, 0:8], in0=rt[:, 0], in1=rt[:, 1], op=ALU.subtract), reads=["rt0", "rt1"], writes=["qkb"])
            op("dve", lambda q: q.tensor_tensor(out=qkb[:, :, 8:16], in0=rt[:, 2], in1=rt[:, 3], op=ALU.add), reads=["rt2", "rt3"], writes=["qkb"])
            tbk = (S1B[4], S1B[5])
            for a in range(2):
                for hh in range(8):
                    op("pe", lambda q, a=a, hh=hh: q.transpose(out=ptv(tbk[a])[0:64, hh * 128:(hh + 1) * 128], in_=qkb[:, 8 * a + hh, :], identity=ident[:]),
                       reads=["qkb", "ident"], writes=[("pa", tbk[a])])
                op("act", lambda q, a=a: q.activation(out=qT[:, 8 * a:8 * a + 8, :], in_=ptv(tbk[a])[0:64, :].rearrange("p (h t) -> p h t", h=8), func=AF.Copy),
                   reads=[("pa", tbk[a])], writes=["qT"])
            for g in range(2):
                op("pe", lambda q, g=g: q.transpose(out=ptv(S1B[6])[0:64, g * 128:(g + 1) * 128], in_=qkb[:, 16 + g, :], identity=ident[:]),
                   reads=["qkb", "ident"], writes=[("pa", S1B[6])])
            op("dve", lambda q: q.tensor_copy(out=kT[:, slot, :, :], in_=ptv(S1B[6])[0:64, 0:256].rearrange("p (g t) -> p g t", g=2)),
               reads=[("pa", S1B[6])], writes=["kT%d" % slot])

        def l0_s2(m, j):
            blk = m * 4 + j
            first = (blk % (SEQ // 128) == 0)
            slot = blk % 2
            S.tag = "m%d.L0.%d.s2" % (m, j)
            pti = 0
            sb_i = 0
            for g in range(2):
                for a in range(2):
                    grp = 2 * g + a
                    ob = OA_BANKS[grp % len(OA_BANKS)]
                    kbs = ([] if first else [(1 - slot, mask_prev, "mask_prev")]) + [(slot, mask_cur, "mask_cur")]
                    pts = []
                    for (ks, msk, mkey) in kbs:
                        b = SC_BANKS[sb_i % len(SC_BANKS)]
                        sb_i += 1
                        op("pe", lambda q, b=b, ks=ks, g=g, a=a: q.matmul(pab[b][:], lhsT=kT[:, ks, g, :],
                                                                          rhs=qT[:, 8 * g + 4 * a:8 * g + 4 * a + 4, :].rearrange("p h t -> p (h t)"),
                                                                          start=True, stop=True),
                           reads=["kT%d" % ks, "qT"], writes=[("pa", b)])
                        p = pti % 4
                        pti += 1
                        op("act", lambda q, b=b, p=p: q.activation(out=PT[p][:], in_=pab[b][:], func=AF.Exp), reads=[("pa", b)], writes=[("PT", p)])
                        op("dve", lambda q, p=p, msk=msk: q.tensor_tensor(out=PT[p][:].rearrange("p (h t) -> p h t", h=4),
                                                                         in0=PT[p][:].rearrange("p (h t) -> p h t", h=4),
                                                                         in1=msk[:].unsqueeze(1).to_broadcast([128, 4, 128]), op=ALU.mult),
                           reads=[("PT", p), mkey], writes=[("PT", p)])
                        pts.append((p, ks))
                    for hh in range(4):
                        oo = hh * 65
                        for i, (p, ks) in enumerate(pts):
                            op("pe", lambda q, p=p, ks=ks, hh=hh, ob=ob, oo=oo, i=i, n=len(pts), g=g: q.matmul(
                                pab[ob][:, oo:oo + 65], lhsT=PT[p][:, hh * 128:(hh + 1) * 128], rhs=vaug[:, ks, g, :],
                                start=(i == 0), stop=(i == n - 1)),
                               reads=[("PT", p), "vaug%d" % ks], writes=[("pa", ob)])
                    h0 = 4 * grp
                    ov = pab[ob][:, 0:260].rearrange("p (h d) -> p h d", d=65)
                    op("dve", lambda q, ov=ov, h0=h0: q.tensor_tensor(out=sv(DEN + h0, 4), in0=ov[:, :, 64], in1=sv(ESINK + h0, 4), op=ALU.add),
                       reads=[("pa", ob), "esink"], writes=[("den", grp)])
                    op("dve", lambda q, h0=h0: q.reciprocal(out=sv(RDEN + h0, 4), in_=sv(DEN + h0, 4)), reads=[("den", grp)], writes=[("rden", grp)])
                    op("dve", lambda q, ov=ov, h0=h0: q.tensor_tensor(out=F(0).rearrange("p (h d) -> p h d", d=64)[:, h0:h0 + 4, :], in0=ov[:, :, 0:64],
                                                                     in1=sv(RDEN + h0, 4).unsqueeze(2).to_broadcast([128, 4, 64]), op=ALU.mult),
                       reads=[("pa", ob), ("rden", grp)], writes=[("of", grp)])

        def l0_s3(m, j):
            buf = m % 2
            S.tag = "m%d.L0.%d.s3" % (m, j)
            bz = [proj_tm(Z_BANKS[0], j, 1280, 512, W0, "W0"), proj_tm(Z_BANKS[1], j, 1792, 512, W0, "W0")]
            for hf in range(2):
                op("act", lambda q, hf=hf: q.activation(out=Fh(2 + hf), in_=pab[bz[hf]][:], func=AF.Tanh), reads=[("pa", bz[hf])], writes=[FhK(2 + hf)])
                op("dve", lambda q, hf=hf: q.scalar_tensor_tensor(out=Fh(2 + hf), in0=Fh(2 + hf), scalar=1.0, in1=pab[bz[hf]][:], op0=ALU.add, op1=ALU.mult),
                   reads=[FhK(2 + hf), ("pa", bz[hf])], writes=[FhK(2 + hf)])
            op("dve", lambda q: q.tensor_tensor(out=og_all[:, 0, :], in0=F(0), in1=F(1), op=ALU.mult),
               reads=[("of", 0), ("of", 1), ("of", 2), ("of", 3), FhK(0), FhK(1), FhK(2), FhK(3)], writes=[("og", 0), FhK(0), FhK(1)])
            yb = out_proj(og_all[:, 0, :], [("og", 0)], Wo0, "Wo0", True, Y_BANKS, 2)
            post_norm_residual(buf, j, 0, yb)

        def l1_A(m, h):
            ws = h % 2
            Wt = W1h[ws]
            wk = "W1h%d" % ws
            db = h % 2
            S.tag = "m%d.L1.h%d.A" % (m, h)
            bq, bf, bv, bzz = L1_BANKS[0:4]
            hTk = [("hT", j) for j in range(4)]
            for (b, c0) in ((bq, 0), (bf, 128)):
                for kc in range(8):
                    op("pe", lambda q, b=b, c0=c0, kc=kc: q.matmul(pab[b][:], lhsT=Wt[:, kc, c0:c0 + 128], rhs=hT[:, kc, :], start=(kc == 0), stop=(kc == 7)),
                       reads=hTk + [wk], writes=[("pa", b)])
            for (b, c0) in ((bv, 256), (bzz, 384)):
                for j in range(4):
                    for kc in range(8):
                        op("pe", lambda q, b=b, c0=c0, kc=kc, j=j: q.matmul(pab[b][:, j * 128:(j + 1) * 128], lhsT=hT[:, kc, j * 128:(j + 1) * 128],
                                                                            rhs=Wt[:, kc, c0:c0 + 128], start=(kc == 0), stop=(kc == 7)),
                           reads=[("hT", j), wk], writes=[("pa", b)])
            if h + 2 < 8:
                load_w1h(h + 2, ws)
            A0, A1, A2, A3, A4, A5, A6 = [Fh(i) for i in range(7)]
            K0, K1, K2, K3, K4, K5, K6 = [FhK(i) for i in range(7)]
            gz = Fh(7) if db == 0 else GZ[0][:]
            gzk = FhK(7) if db == 0 else "gz0"
            op("act", lambda q: q.activation(out=A0, in_=pab[bq][:], func=AF.Tanh), reads=[("pa", bq)], writes=[K0])
            op("dve", lambda q: q.scalar_tensor_tensor(out=A0, in0=A0, scalar=1.0, in1=pab[bq][:], op0=ALU.add, op1=ALU.mult),
               reads=[K0, ("pa", bq)], writes=[K0])
            op("act", lambda q: q.activation(out=A1, in_=pab[bf][:], func=AF.Tanh), reads=[("pa", bf)], writes=[K1])
            op("act", lambda q: q.activation(out=A2, in_=A1, func=AF.Identity, scale=sv(LBB + h), bias=sv(LBA + h)), reads=[K1] + LBK, writes=[K2])
            op("act", lambda q: q.activation(out=A3, in_=A1, func=AF.Identity, scale=sv(LBNB + h), bias=sv(LBB + h)), reads=[K1] + LBK, writes=[K3])
            op("dve", lambda q: q.tensor_tensor_scan(out=A4, data0=startmask[:], data1=A2, initial=0.0, op0=ALU.max, op1=ALU.mult),
               reads=["startmask", K2], writes=[K4])
            op("dve", lambda q: q.tensor_tensor(out=qeT[db][:], in0=A0, in1=A4, op=ALU.mult), reads=[K0, K4], writes=[("qeT", db)])
            op("dve", lambda q: q.reciprocal(out=A5, in_=A4), reads=[K4], writes=[K5], dur=3.3)
            op("dve", lambda q: q.tensor_tensor(out=A6, in0=A3, in1=A5, op=ALU.mult), reads=[K3, K5], writes=[K6])
            op("act", lambda q: q.activation(out=keT[db][:], in_=A6, func=AF.Copy), reads=[K6], writes=[("keT", db)])
            op("dve", lambda q: q.tensor_tensor(out=kdT[db][:].rearrange("p (c t) -> p c t", t=64), in0=A6.rearrange("p (c t) -> p c t", t=64),
                                                in1=A4.rearrange("p (c t) -> p c t", t=64)[:, :, 63:64].to_broadcast([128, 8, 64]), op=ALU.mult),
               reads=[K6, K4], writes=[("kdT", db)])
            op("act", lambda q: q.activation(out=glast[:, db, :], in_=A4.rearrange("p (c t) -> p c t", t=64)[:, :, 63], func=AF.Copy),
               reads=[K4], writes=[("glast", db)])
            op("act", lambda q: q.activation(out=vb[db][:].rearrange("p j v -> p (j v)"), in_=pab[bv][:], func=AF.Copy), reads=[("pa", bv)], writes=[("vb", db)])
            op("act", lambda q: q.activation(out=gz, in_=pab[bzz][:], func=AF.Tanh), reads=[("pa", bzz)], writes=[gzk])
            op("dve", lambda q: q.scalar_tensor_tensor(out=gz, in0=gz, scalar=1.0, in1=pab[bzz][:], op0=ALU.add, op1=ALU.mult),
               reads=[gzk, ("pa", bzz)], writes=[gzk])

        def l1_B(m, h):
            db = h % 2
            S.tag = "m%d.L1.h%d.B" % (m, h)
            gz = Fh(7) if db == 0 else GZ[0][:]
            gzk = FhK(7) if db == 0 else "gz0"
            ub = [L1_BANKS[4], L1_BANKS[5]]
            bso = L1_BANKS[6]
            for j in range(4):
                op("pe", lambda q, j=j: q.transpose(out=ptv(TB)[:, j * 128:(j + 1) * 128], in_=kdT[db][:, j * 128:(j + 1) * 128], identity=ident[:]),
                   reads=[("kdT", db), "ident"], writes=[("pa", TB)])
            op("act", lambda q: q.activation(out=kd_tm[:].rearrange("p j k -> p (j k)"), in_=ptv(TB)[:, 0:512], func=AF.Copy), reads=[("pa", TB)], writes=["kd_tm"])
            for j in range(4):
                for c in range(2):
                    op("pe", lambda q, j=j, c=c: q.matmul(pab[ub[c]][:, j * 128:(j + 1) * 128], lhsT=kd_tm[64 * c:64 * c + 64, j, :],
                                                          rhs=vb[db][64 * c:64 * c + 64, j, :], start=True, stop=True),
                       reads=["kd_tm", ("vb", db)], writes=[("pa", ub[c])])
            for j in range(4):
                op("pe", lambda q, j=j: q.matmul(pab[bso][:, j * 128:(j + 1) * 128], lhsT=keT[db][:, j * 128:(j + 1) * 128], rhs=qeT[db][:, j * 128:(j + 1) * 128],
                                                 start=True, stop=True),
                   reads=[("keT", db), ("qeT", db)], writes=[("pa", bso)])
            op("dve", lambda q: q.tensor_tensor(out=smk[:], in0=pab[bso][:].rearrange("p (j t) -> p j t", j=4),
                                                in1=maskbd[:].unsqueeze(1).to_broadcast([128, 4, 128]), op=ALU.mult),
               reads=[("pa", bso), "maskbd"], writes=["smk"])
            for k in range(8):
                j, c = k // 2, k % 2
                op("act", lambda q, k=k: q.activation(out=Sbf[:, k, :], in_=S32[:, h, :], func=AF.Copy), reads=[("S32", h)], writes=[("Sbf", k)])
                op("dve", lambda q, k=k, j=j, c=c: q.scalar_tensor_tensor(out=S32[:, h, :], in0=S32[:, h, :], scalar=glast[:, db, k:k + 1],
                                                                             in1=pab[ub[c]][:, j * 128:(j + 1) * 128], op0=ALU.mult, op1=ALU.add),
                   reads=[("S32", h), ("glast", db), ("pa", ub[c])], writes=[("S32", h)])
            for j in range(4):
                op("pe", lambda q, j=j: q.matmul(pab[bso][:, j * 128:(j + 1) * 128], lhsT=smk[:, j, :], rhs=vb[db][:, j, :], start=True, stop=True),
                   reads=["smk", ("vb", db)], writes=[("pa", bso)])
                for c in range(2):
                    k = 2 * j + c
                    op("pe", lambda q, j=j, c=c, k=k: q.matmul(pab[bso][64 * c:64 * c + 64, j * 128:(j + 1) * 128],
                                                               lhsT=qeT[db][:, j * 128 + 64 * c:j * 128 + 64 * c + 64], rhs=Sbf[:, k, :],
                                                               start=False, stop=True, skip_group_check=True),
                       reads=[("qeT", db), ("Sbf", k)], writes=[("pa", bso)])
            for j in range(4):
                op("act", lambda q, j=j: q.activation(out=og_all[:, j, h * 128:(h + 1) * 128], in_=pab[bso][:, j * 128:(j + 1) * 128], func=AF.Square, accum_out=sv(SS3 + j)),
                   reads=[("pa", bso)], writes=[("og", j), ("ss3", j)])
            op("dve", lambda q: q.tensor_scalar(out=sv(SS3, 4), in0=sv(SS3, 4), scalar1=1.0 / 128, scalar2=EPS, op0=ALU.mult, op1=ALU.add),
               reads=[("ss3", j) for j in range(4)], writes=[("ss3", j) for j in range(4)])
            op("pool", lambda q: q.tensor_tensor(out=sv(RSTD3, 4), in0=sv(SS3, 4), in1=sv(NHALF, 4), op=ALU.pow),
               reads=[("ss3", j) for j in range(4)] + ["nhalf"], writes=[("rstd3", j) for j in range(4)])
            for j in range(4):
                op("dve", lambda q, j=j: q.scalar_tensor_tensor(out=og_all[:, j, h * 128:(h + 1) * 128], in0=pab[bso][:, j * 128:(j + 1) * 128],
                                                                 scalar=sv(RSTD3 + j), in1=gz[:, j * 128:(j + 1) * 128], op0=ALU.mult, op1=ALU.mult),
                   reads=[("pa", bso), ("rstd3", j), gzk], writes=[("og", j)])

        load_x(0)
        for m in range(NMT):
            S.new_epoch()
            buf = m % 2
            if m + 1 < NMT:
                load_x(m + 1)
            load_w1h(0, 0)
            load_w1h(1, 1)
            l0_s1(m, 0)
            for j in range(4):
                l0_s2(m, j)
                if j + 1 < 4:
                    l0_s1(m, j + 1)
                l0_s3(m, j)
            S.tag = "m%d.L1.pre" % m
            for j in range(4):
                make_hT(buf, j, j)
            if m % MT_PER_SEQ == 0:
                op("pool", lambda q: q.memset(S32[:], 0.0), writes=[("S32", h) for h in range(8)])
            l1_A(m, 0)
            for h in range(8):
                if h + 1 < 8:
                    l1_A(m, h + 1)
                l1_B(m, h)
            S.tag = "m%d.L1.out" % m
            for j in range(4):
                yb = out_proj(og_all[:, j, :], [("og", j)], Wo1, "Wo1", False, ((L1_BANKS[0], L1_BANKS[1]), (L1_BANKS[2], L1_BANKS[3]))[j % 2])
                post_norm_residual(buf, j, 1, yb)
            op("sp", lambda q, m=m, buf=buf: q.dma_start(out=out_d[m * 512:(m + 1) * 512, :].rearrange("(j p) d -> p j d", p=128), in_=xb[buf][:]),
               reads=[xk(buf, j) for j in range(4)], dma="xs%d" % buf)
        S.emit(st)
    build_program.last_sched = S
    return nc


_PROG_CACHE = {}


def _get_prog(nseq, seq):
    key = (nseq, seq)
    if key not in _PROG_CACHE:
        _PROG_CACHE[key] = build_program(nseq, seq)
    return _PROG_CACHE[key]


def make_in_maps(inputs, n_cores, nseq, seq):
    x = np.ascontiguousarray(inputs["x"], dtype=np.float32)
    pos = np.ascontiguousarray(inputs["positions"], dtype=np.int32)
    cst = make_consts()
    maps = []
    for c in range(n_cores):
        xs = x[c * nseq:(c + 1) * nseq, :seq].reshape(nseq * seq, D)
        ps = pos[c * nseq:(c + 1) * nseq, :seq].reshape(nseq * seq // 128, 128).T
        maps.append({
            "x": np.ascontiguousarray(xs),
            "pos": np.ascontiguousarray(ps),
            "cst": cst,
            "pre_norm_w": np.ascontiguousarray(inputs["pre_norm_w"], dtype=np.float32),
            "post_norm_w": np.ascontiguousarray(inputs["post_norm_w"], dtype=np.float32),
            "attn_w_in": np.ascontiguousarray(inputs["attn_w_in"][0], dtype=np.float32),
            "attn_b_in": np.ascontiguousarray(inputs["attn_b_in"], dtype=np.float32).reshape(1, 2304),
            "attn_sinks": np.ascontiguousarray(inputs["attn_sinks"], dtype=np.float32).reshape(1, 16),
            "attn_w_out": np.ascontiguousarray(inputs["attn_w_out"][0], dtype=np.float32),
            "attn_b_out": np.ascontiguousarray(inputs["attn_b_out"], dtype=np.float32).reshape(1, D),
            "rec_w_in": np.ascontiguousarray(inputs["rec_w_in"][0], dtype=np.float32),
            "rec_lb_logits": np.ascontiguousarray(inputs["rec_lb_logits"], dtype=np.float32),
            "rec_gnorm_w": np.ascontiguousarray(inputs["rec_gnorm_w"], dtype=np.float32).reshape(1, 128),
            "rec_w_out": np.ascontiguousarray(inputs["rec_w_out"][0], dtype=np.float32),
        })
    return maps


def kernel(**inputs):
    B, T, _ = inputs["x"].shape
    nseq = B // N_CORES
    nc = _get_prog(nseq, T)
    maps = make_in_maps(inputs, N_CORES, nseq, T)
    res = run_bass_kernel_spmd(nc, maps, core_ids=list(range(N_CORES)))
    outs = [np.asarray(r["out"], dtype=np.float32).reshape(nseq, T, D) for r in res.results]
    return np.concatenate(outs, axis=0)
```
